# Optimizing a Trainium2 kernel written in Bass

```python
import math
import jax, jax.numpy as jnp
from jax import lax
import numpy as np

D_MODEL = 1024
BATCH = 8
SEQ = 2048
DEPTH = 1
DEC_BATCH = 128
DEC_SEQ = 4
PAST_LEN = 2048
PAGE_SIZE = 128

N_META = 16
ATTN_WIDTH = D_MODEL // 2
SSM_WIDTH = D_MODEL - ATTN_WIDTH
N_HEADS = 4
QK_DIM = 64
HEAD_DIM = 2 * QK_DIM
SSM_GROUP = 16
N_SSM_GROUPS = SSM_WIDTH // SSM_GROUP
SSM_STATE = 64
D_FF = 2816
CONV_WIDTH = 3
Q_BLOCK = 128
EPS = 1e-6
NEG = -1e30
PROJ_WIDTH = 3 * N_HEADS * HEAD_DIM + SSM_WIDTH

kernel_name = "hymba_s5_diffattn_convffn_step"


def _rmsnorm(x, g):
    xf = x.astype(jnp.float32)
    r = lax.rsqrt(jnp.mean(xf * xf, axis=-1, keepdims=True) + EPS)
    return (xf * r).astype(x.dtype) * g


def _project(xn, w_in, q_norm, k_norm):
    b, t, _ = xn.shape
    a = N_HEADS * HEAD_DIM
    proj = xn @ w_in
    q, k, v, u = jnp.split(proj, [a, 2 * a, 3 * a], axis=-1)
    q = _rmsnorm(q.reshape(b, t, N_HEADS, 2, QK_DIM), q_norm)
    k = _rmsnorm(k.reshape(b, t, N_HEADS, 2, QK_DIM), k_norm)
    v = v.reshape(b, t, N_HEADS, HEAD_DIM)
    return q, k, v, u


def _scores(q, k):
    return jnp.einsum('bthcd,bshcd->bhcts', q, k).astype(jnp.float32) * (QK_DIM ** -0.5)


def _diff_weights(s, lam):
    p = jax.nn.softmax(s, axis=-1)
    return p[:, :, 0] - lam * p[:, :, 1]


def _attn_prompt(q, k, v, lam):
    L = q.shape[1]
    bounds = [(0, N_META)] + [(N_META + i * Q_BLOCK, N_META + (i + 1) * Q_BLOCK)
                              for i in range((L - N_META) // Q_BLOCK)]
    outs = []
    for s0, e0 in bounds:
        s = _scores(q[:, s0:e0], k[:, :e0])
        mask = jnp.arange(e0)[None, :] <= jnp.arange(s0, e0)[:, None]
        w = _diff_weights(jnp.where(mask, s, NEG), lam)
        outs.append(jnp.einsum('bhts,bshe->bthe', w.astype(v.dtype), v[:, :e0]))
    return jnp.concatenate(outs, axis=1)


def _attn_sample(q, k_new, v_new, k_past, v_past, lam):
    T = q.shape[1]
    P = k_past.shape[1]
    s_past = _scores(q, k_past)
    s_new = jnp.where(jnp.tril(jnp.ones((T, T), dtype=bool)), _scores(q, k_new), NEG)
    w = _diff_weights(jnp.concatenate([s_past, s_new], axis=-1), lam).astype(v_new.dtype)
    return (jnp.einsum('bhts,bshe->bthe', w[..., :P], v_past)
            + jnp.einsum('bhts,bshe->bthe', w[..., P:], v_new))


def _head_norm(o, sub_g, lam_init):
    b, t = o.shape[:2]
    return (_rmsnorm(o, sub_g) * (1.0 - lam_init)).reshape(b, t, N_HEADS * HEAD_DIM)


def _cplx_combine(e1, e2):
    a1r, a1i, b1r, b1i = e1
    a2r, a2i, b2r, b2i = e2
    return (a2r * a1r - a2i * a1i, a2r * a1i + a2i * a1r,
            a2r * b1r - a2i * b1i + b2r, a2r * b1i + a2i * b1r + b2i)


def _ssm(u, h0r, h0i, a_re, a_im, log_dt, b_re, b_im, c_re, c_im, d_skip, w_glu, b_glu):
    f32 = jnp.float32
    b, t, _ = u.shape
    ug = u.astype(f32).reshape(b, t, N_SSM_GROUPS, SSM_GROUP)
    ar, ai = a_re.astype(f32), a_im.astype(f32)
    dt = jnp.exp(log_dt.astype(f32))[:, None]
    mag = jnp.exp(ar * dt)
    abr, abi = mag * jnp.cos(ai * dt), mag * jnp.sin(ai * dt)
    den = ar * ar + ai * ai
    nr, ni = abr - 1.0, abi
    gr, gi = (nr * ar + ni * ai) / den, (ni * ar - nr * ai) / den
    br, bi = b_re.astype(f32), b_im.astype(f32)
    bbr = gr[..., None] * br - gi[..., None] * bi
    bbi = gr[..., None] * bi + gi[..., None] * br
    xr = jnp.einsum('btgc,gpc->btgp', ug, bbr)
    xi = jnp.einsum('btgc,gpc->btgp', ug, bbi)
    if h0r is not None:
        h0r, h0i = h0r.astype(f32), h0i.astype(f32)
        xr = xr.at[:, 0].add(abr * h0r - abi * h0i)
        xi = xi.at[:, 0].add(abr * h0i + abi * h0r)
    A_r = jnp.broadcast_to(abr, xr.shape)
    A_i = jnp.broadcast_to(abi, xr.shape)
    _, _, hr, hi = lax.associative_scan(_cplx_combine, (A_r, A_i, xr, xi), axis=1)
    y = (jnp.einsum('btgp,gcp->btgc', hr, c_re.astype(f32))
         - jnp.einsum('btgp,gcp->btgc', hi, c_im.astype(f32))
         + d_skip.astype(f32) * ug).reshape(b, t, SSM_WIDTH)
    g = jax.nn.gelu(y)
    out = g * jax.nn.sigmoid(g @ w_glu.astype(f32) + b_glu.astype(f32))
    return out.astype(u.dtype), hr[:, -1], hi[:, -1]


def _conv_ffn(xn, conv_state, w_gate, w_up, conv_w, conv_b, w_down):
    a = xn @ w_gate
    c = xn @ w_up
    b, T, _ = a.shape
    hist = jnp.zeros((b, CONV_WIDTH - 1, D_FF), a.dtype) if conv_state is None else conv_state.astype(a.dtype)
    ap = jnp.concatenate([hist, a], axis=1)
    conv = conv_b + sum(conv_w[j] * ap[:, j:j + T] for j in range(CONV_WIDTH))
    y = (jax.nn.gelu(conv) * c) @ w_down
    return y, ap[:, -(CONV_WIDTH - 1):]


def setup_inputs(seed: int = 0) -> dict:
    key = jax.random.key(seed)
    ks = jax.random.split(key, 40)
    f32 = jnp.float32
    nrm = lambda k, shape, s: jax.random.normal(k, shape, f32) * s
    n_pages = PAST_LEN // PAGE_SIZE
    n_pool = (DEC_BATCH * n_pages * 5 + 3) // 4
    G, P, C = N_SSM_GROUPS, SSM_STATE, SSM_GROUP
    page_table = jax.random.permutation(ks[0], n_pool)[:DEC_BATCH * n_pages].reshape(DEC_BATCH, n_pages).astype(jnp.int32)
    a_im = jnp.broadcast_to(math.pi * jnp.arange(P, dtype=f32), (DEPTH, G, P)) + nrm(ks[1], (DEPTH, G, P), 0.01)
    return {
        "x_prompt": nrm(ks[2], (BATCH, SEQ, D_MODEL), 1.0),
        "x_sample": nrm(ks[3], (DEC_BATCH, DEC_SEQ, D_MODEL), 1.0),
        "cache_k": nrm(ks[4], (DEPTH, n_pool, PAGE_SIZE, N_HEADS, HEAD_DIM), 1.0),
        "cache_v": nrm(ks[5], (DEPTH, n_pool, PAGE_SIZE, N_HEADS, HEAD_DIM), 1.0),
        "state_ssm_re": nrm(ks[6], (DEPTH, DEC_BATCH, G, P), 1.0),
        "state_ssm_im": nrm(ks[7], (DEPTH, DEC_BATCH, G, P), 1.0),
        "state_ffn_conv": nrm(ks[8], (DEPTH, DEC_BATCH, CONV_WIDTH - 1, D_FF), 1.0),
        "page_table": page_table,
        "meta_tokens": nrm(ks[9], (N_META, D_MODEL), 1.0),
        "norm1": 1.0 + nrm(ks[10], (DEPTH, D_MODEL), 0.01),
        "w_in": nrm(ks[11], (DEPTH, D_MODEL, PROJ_WIDTH), D_MODEL ** -0.5),
        "q_norm": 1.0 + nrm(ks[12], (DEPTH, 2, QK_DIM), 0.01),
        "k_norm": 1.0 + nrm(ks[13], (DEPTH, 2, QK_DIM), 0.01),
        "lam_q": nrm(ks[14], (DEPTH, 2, QK_DIM), 0.1),
        "lam_k": nrm(ks[15], (DEPTH, 2, QK_DIM), 0.1),
        "sub_norm": 1.0 + nrm(ks[16], (DEPTH, HEAD_DIM), 0.01),
        "ssm_a_re": -0.5 + nrm(ks[17], (DEPTH, G, P), 0.01),
        "ssm_a_im": a_im,
        "ssm_log_dt": jax.random.uniform(ks[18], (DEPTH, G), f32, math.log(1e-3), math.log(1e-1)),
        "ssm_b_re": nrm(ks[19], (DEPTH, G, P, C), (2 * C) ** -0.5),
        "ssm_b_im": nrm(ks[20], (DEPTH, G, P, C), (2 * C) ** -0.5),
        "ssm_c_re": nrm(ks[21], (DEPTH, G, C, P), P ** -0.5),
        "ssm_c_im": nrm(ks[22], (DEPTH, G, C, P), P ** -0.5),
        "ssm_d": nrm(ks[23], (DEPTH, G, C), 1.0),
        "w_glu": nrm(ks[24], (DEPTH, SSM_WIDTH, SSM_WIDTH), SSM_WIDTH ** -0.5),
        "b_glu": nrm(ks[25], (DEPTH, SSM_WIDTH), 0.01),
        "w_out": nrm(ks[26], (DEPTH, D_MODEL, D_MODEL), D_MODEL ** -0.5),
        "norm2": 1.0 + nrm(ks[27], (DEPTH, D_MODEL), 0.01),
        "w_gate": nrm(ks[28], (DEPTH, D_MODEL, D_FF), D_MODEL ** -0.5),
        "w_up": nrm(ks[29], (DEPTH, D_MODEL, D_FF), D_MODEL ** -0.5),
        "ffn_conv_w": nrm(ks[30], (DEPTH, CONV_WIDTH, D_FF), 0.5),
        "ffn_conv_b": nrm(ks[31], (DEPTH, D_FF), 0.01),
        "w_down": nrm(ks[32], (DEPTH, D_FF, D_MODEL), D_FF ** -0.5),
    }


def reference(x_prompt, x_sample, cache_k, cache_v, state_ssm_re, state_ssm_im, state_ffn_conv,
              page_table, meta_tokens, norm1, w_in, q_norm, k_norm, lam_q, lam_k, sub_norm,
              ssm_a_re, ssm_a_im, ssm_log_dt, ssm_b_re, ssm_b_im, ssm_c_re, ssm_c_im, ssm_d,
              w_glu, b_glu, w_out, norm2, w_gate, w_up, ffn_conv_w, ffn_conv_b, w_down):
    f32 = jnp.float32
    b = x_prompt.shape[0]
    db, T = x_sample.shape[:2]
    meta = jnp.broadcast_to(meta_tokens.astype(x_prompt.dtype), (b, N_META, D_MODEL))
    xp = jnp.concatenate([meta, x_prompt], axis=1)
    xs = x_sample
    kp_l, vp_l, ks_l, vs_l, srp_l, sip_l, srs_l, sis_l, cp_l, cs_l = ([] for _ in range(10))
    for l in range(DEPTH):
        lam_init = 0.8 - 0.6 * math.exp(-0.3 * l)
        lq, lk = lam_q[l].astype(f32), lam_k[l].astype(f32)
        lam = jnp.exp(jnp.sum(lq[0] * lk[0])) - jnp.exp(jnp.sum(lq[1] * lk[1])) + lam_init
        ssm_p = (ssm_a_re[l], ssm_a_im[l], ssm_log_dt[l], ssm_b_re[l], ssm_b_im[l],
                 ssm_c_re[l], ssm_c_im[l], ssm_d[l], w_glu[l], b_glu[l])
        ffn_p = (w_gate[l], w_up[l], ffn_conv_w[l], ffn_conv_b[l], w_down[l])

        q, k, v, u = _project(_rmsnorm(xp, norm1[l]), w_in[l], q_norm[l], k_norm[l])
        att = _head_norm(_attn_prompt(q, k, v, lam), sub_norm[l], lam_init)
        ssm_out, hr, hi = _ssm(u, None, None, *ssm_p)
        xp = xp + jnp.concatenate([att, ssm_out.astype(att.dtype)], axis=-1) @ w_out[l]
        f, cst = _conv_ffn(_rmsnorm(xp, norm2[l]), None, *ffn_p)
        xp = xp + f
        L = xp.shape[1]
        kp_l.append(k.reshape(b, L, N_HEADS, HEAD_DIM)); vp_l.append(v)
        srp_l.append(hr); sip_l.append(hi); cp_l.append(cst)

        q, k, v, u = _project(_rmsnorm(xs, norm1[l]), w_in[l], q_norm[l], k_norm[l])
        k_past = cache_k[l, page_table].reshape(db, -1, N_HEADS, 2, QK_DIM)
        v_past = cache_v[l, page_table].reshape(db, -1, N_HEADS, HEAD_DIM)
        att = _head_norm(_attn_sample(q, k, v, k_past.astype(q.dtype), v_past.astype(v.dtype), lam),
                         sub_norm[l], lam_init)
        ssm_out, hr, hi = _ssm(u, state_ssm_re[l], state_ssm_im[l], *ssm_p)
        xs = xs + jnp.concatenate([att, ssm_out.astype(att.dtype)], axis=-1) @ w_out[l]
        f, cst = _conv_ffn(_rmsnorm(xs, norm2[l]), state_ffn_conv[l], *ffn_p)
        xs = xs + f
        ks_l.append(k.reshape(db, T, N_HEADS, HEAD_DIM)); vs_l.append(v)
        srs_l.append(hr); sis_l.append(hi); cs_l.append(cst)

    y_prompt = xp[:, N_META:]
    y_sample = xs
    return (y_prompt, y_sample,
            jnp.stack(kp_l), jnp.stack(vp_l), jnp.stack(ks_l), jnp.stack(vs_l),
            jnp.stack(srp_l), jnp.stack(sip_l), jnp.stack(srs_l), jnp.stack(sis_l),
            jnp.stack(cp_l), jnp.stack(cs_l))
```

```python
import contextlib
import math
import numpy as np
import concourse.bass as bass
import concourse.mybir as mybir
from concourse.bass_utils import run_bass_kernel_spmd

F32 = mybir.dt.float32
BF16 = mybir.dt.bfloat16
I32 = mybir.dt.int32
AF = mybir.ActivationFunctionType
ALU = mybir.AluOpType
AX = mybir.AxisListType

D = 1024
SEQ = 2048
NMETA = 16
L = SEQ + NMETA
NT = 17
NCOL = NT * 128
NS = 16
TS = 4
NPG = 16
DFF = 2816
NFC = DFF // 128
EPS = 1e-6
LAM_INIT = 0.8 - 0.6 * math.exp(-0.3 * 0)
Q = 8
NJ = L // Q
ENGS = ("pe", "act", "dve", "pool", "sp")


class Res:
    __slots__ = ("name", "writer", "readers")

    def __init__(self, name=""):
        self.name = name
        self.writer = None
        self.readers = []


class DSem:
    def __init__(self, sem):
        self.sem = sem
        self.val = 0


class Prog:
    def __init__(self, nc):
        self.nc = nc
        self.q = {e: [] for e in ENGS}
        self.sem = {e: nc.alloc_semaphore(name="c_" + e) for e in ENGS}
        self.cnt = {e: 0 for e in ENGS}
        self.seen = {e: {} for e in ENGS}
        self.dsems = []
        self.n_inst = 0
        self.dead = False
        import os as _o
        self.cutv = float(_o.environ.get('DBG_S', 1e9))
        self.serial_same = {"act": True, "dve": True, "pool": True, "pe": False, "sp": False}

    def _deps(self, eng, reads, writes):
        need = {}

        def add(sv):
            if sv is None:
                return
            s, v = sv
            if need.get(s, 0) < v:
                need[s] = v
        for r in reads:
            add(r.writer)
        for w in writes:
            add(w.writer)
            for rd in w.readers:
                add(rd)
        waits = []
        for s, v in need.items():
            if s is self.sem[eng] and not self.serial_same[eng]:
                continue
            if self.seen[eng].get(s, 0) >= v:
                continue
            self.seen[eng][s] = v
            waits.append((s, v))
        return waits

    def _mark(self, reads, writes, sv):
        for r in reads:
            r.readers.append(sv)
            if len(r.readers) > 64:
                best = {}
                for s, v in r.readers:
                    if best.get(s, 0) < v:
                        best[s] = v
                r.readers = list(best.items())
        for w in writes:
            w.writer = sv
            w.readers = []

    def cut(self, x):
        if self.cutv < x:
            self.dead = True

    def op(self, eng, fn, reads=(), writes=()):
        if self.dead:
            return
        waits = self._deps(eng, reads, writes)
        self.cnt[eng] += 1
        sem = self.sem[eng]
        sv = (sem, self.cnt[eng])

        def emit(e, fn=fn, waits=waits, sem=sem):
            for s, v in waits:
                e.wait_ge(s, v)
            fn(e).then_inc(sem, 1)
        self.q[eng].append(emit)
        self._mark(reads, writes, sv)
        self.n_inst += 1

    def dma(self, eng, fn, reads=(), writes=(), dsem=None):
        if self.dead:
            return
        waits = self._deps(eng, reads, writes)
        dsem.val += 16
        sv = (dsem.sem, dsem.val)

        def emit(e, fn=fn, waits=waits, s=dsem.sem):
            for ws, wv in waits:
                e.wait_ge(ws, wv)
            fn(e).then_inc(s, 16)
        self.q[eng].append(emit)
        self._mark(reads, writes, sv)
        self.n_inst += 1

    def new_dsem(self, name):
        d = DSem(self.nc.alloc_semaphore(name=name + "_%d" % len(self.dsems)))
        self.dsems.append(d)
        return d

    def barrier(self):
        if self.dead:
            return
        for e in ENGS:
            waits = []
            for e2 in ENGS:
                if e2 != e and self.cnt[e2] > self.seen[e].get(self.sem[e2], 0):
                    self.seen[e][self.sem[e2]] = self.cnt[e2]
                    waits.append((self.sem[e2], self.cnt[e2]))
            for d in self.dsems:
                if d.val > self.seen[e].get(d.sem, 0):
                    self.seen[e][d.sem] = d.val
                    waits.append((d.sem, d.val))

            def emit(en, waits=waits):
                for s, v in waits:
                    en.wait_ge(s, v)
            self.q[e].append(emit)

    def run(self):
        nc = self.nc
        finals = [(d.sem, d.val) for d in self.dsems if d.val > 0]

        def fin(e):
            for s, v in finals:
                e.wait_ge(s, v)
        self.q["sp"].append(fin)
        with nc.Block() as block:
            @block.tensor
            def _(e):
                for f in self.q["pe"]:
                    f(e)

            @block.scalar
            def _(e):
                for f in self.q["act"]:
                    f(e)

            @block.vector
            def _(e):
                for f in self.q["dve"]:
                    f(e)

            @block.gpsimd
            def _(e):
                for f in self.q["pool"]:
                    f(e)

            @block.sync
            def _(e):
                for f in self.q["sp"]:
                    f(e)


class T:
    def __init__(self, t, name):
        self.t = t
        self.r = Res(name)

    def __getitem__(self, k):
        return self.t[k]


def rawap(t, off, dims):
    return bass.AP(t.t if isinstance(t, T) else t, off, [list(d) for d in dims])


def build(npool, nv=1):
    nc = bass.Bass("TRN2", target_bir_lowering=False)
    P = Prog(nc)

    def din(name, shape, dt=F32):
        return nc.dram_tensor(name, list(shape), dt, kind="ExternalInput").ap()

    def dout(name, shape, dt=F32):
        return nc.dram_tensor(name, list(shape), dt, kind="ExternalOutput").ap()

    meta = din("meta", [NMETA, D])
    ck = din("ck", [npool * 128, 512]); cv = din("cv", [npool * 128, 512])
    norm1 = din("norm1", [1, D]); norm2 = din("norm2", [1, D])
    w_in = din("w_in", [D, 2048]); w_out = din("w_out", [D, D])
    q_norm = din("q_norm", [1, 128]); k_norm = din("k_norm", [1, 128])
    lam_q = din("lam_q", [1, 128]); lam_k = din("lam_k", [1, 128])
    sub_norm = din("sub_norm", [1, 128])
    a_re = din("a_re", [16, 128]); a_im = din("a_im", [16, 128]); log_dt = din("log_dt", [16, 2])
    b_re = din("b_re", [16, 2048]); b_im = din("b_im", [16, 2048])
    c_re = din("c_re", [16, 2048]); c_im = din("c_im", [16, 2048])
    ssm_d = din("ssm_d", [4, 128])
    w_glu = din("w_glu", [512, 512]); b_glu = din("b_glu", [4, 128])
    w_gate = din("w_gate", [D, DFF]); w_up = din("w_up", [D, DFF]); w_down = din("w_down", [DFF, D])
    conv_w = din("conv_w", [3, DFF]); conv_b = din("conv_b", [1, DFF])
    SFX = {"s": ""}

    def emit_vc(vc):
        sfx = "_v%d" % vc
        SFX["s"] = sfx
        xp = din("xp" + sfx, [SEQ, D]); xs = din("xs" + sfx, [NS * TS, D])
        pt = din("pt" + sfx, [1, NS * NPG], I32)
        s_re0 = din("s_re0" + sfx, [NS, 2048]); s_im0 = din("s_im0" + sfx, [NS, 2048])
        conv0 = din("conv0" + sfx, [NS * 2, DFF])
        yp = dout("yp" + sfx, [SEQ, D]); ys = dout("ys" + sfx, [NS * TS, D])
        kp = dout("kp" + sfx, [L, 512]); vp = dout("vp" + sfx, [L, 512])
        ks = dout("ks" + sfx, [NS * TS, 512]); vs = dout("vs" + sfx, [NS * TS, 512])
        srp = dout("srp" + sfx, [16, 128]); sip = dout("sip" + sfx, [16, 128])
        srs = dout("srs" + sfx, [NS, 2048]); sis = dout("sis" + sfx, [NS, 2048])
        cp = dout("cp" + sfx, [2, DFF]); cs = dout("cs" + sfx, [NS * 2, DFF])
        es_all = contextlib.ExitStack()

        def S(es, name, shape, dt):
            return T(es.enter_context(nc.sbuf_tensor(name + SFX["s"], list(shape), dt)), name)

        def PS(es, name, shape, dt):
            return T(es.enter_context(nc.psum_tensor(name + SFX["s"], list(shape), dt)), name)

        dq = {"n": 0}

        def ld(eng, out_ap, in_ap, writes, reads=(), name=None):
            w0 = writes[0] if writes else reads[0]
            if not hasattr(P, "_dmap"):
                P._dmap = {}
            key = id(w0)
            if key not in P._dmap:
                fr = getattr(P, "_free", [])
                P._dmap[key] = fr.pop() if fr else P.new_dsem("d%d" % len(P.dsems))
                P._keep = getattr(P, "_keep", []) + [w0]
            P.dma(eng, lambda e: e.dma_start(out=out_ap, in_=in_ap), reads=reads, writes=writes, dsem=P._dmap[key])


        def TT(eng, out, a, b, op, rd, wr):
            P.op(eng, lambda e: e.tensor_tensor(out=out, in0=a, in1=b, op=op), reads=rd, writes=wr)

        def TSC(eng, out, a, s1, s2, op0, op1, rd, wr):
            if s2 is None:
                P.op(eng, lambda e: e.tensor_scalar(out=out, in0=a, scalar1=s1, scalar2=None, op0=op0), reads=rd, writes=wr)
            else:
                P.op(eng, lambda e: e.tensor_scalar(out=out, in0=a, scalar1=s1, scalar2=s2, op0=op0, op1=op1), reads=rd, writes=wr)

        def STT(out, a, sc, b, op0, op1, rd, wr):
            P.op("dve", lambda e: e.scalar_tensor_tensor(out=out, in0=a, scalar=sc, in1=b, op0=op0, op1=op1), reads=rd, writes=wr)

        def ACT(out, in_, func, rd, wr, bias=None, scale=None):
            kw = {}
            if bias is not None:
                kw["bias"] = bias
            if scale is not None:
                kw["scale"] = scale
            P.op("act", lambda e: e.activation(out=out, in_=in_, func=func, **kw), reads=rd, writes=wr)

        def CP(eng, out, in_, rd, wr):
            if eng == "act":
                P.op("act", lambda e: e.copy(out=out, in_=in_), reads=rd, writes=wr)
            else:
                P.op(eng, lambda e: e.tensor_copy(out=out, in_=in_), reads=rd, writes=wr)

        def MM(out, lhsT, rhs, start, stop, rd, wr, tp=None):
            if tp is None:
                P.op("pe", lambda e: e.matmul(out, lhsT=lhsT, rhs=rhs, start=start, stop=stop), reads=rd, writes=wr)
            else:
                P.op("pe", lambda e: e.matmul(out, lhsT=lhsT, rhs=rhs, start=start, stop=stop, tile_position=tp), reads=rd, writes=wr)

        def TR(out, in_, ident, rd, wr):
            P.op("pe", lambda e: e.transpose(out=out, in_=in_, identity=ident), reads=rd, writes=wr)

        def MS(eng, out, val, wr):
            P.op(eng, lambda e: e.memset(out, val), writes=wr)

        with es_all:
            ident_f = S(es_all, "ident_f", [128, 128], F32)
            ident_b = S(es_all, "ident_b", [128, 128], BF16)
            ones_b = S(es_all, "ones_b", [128, 128], BF16)
            P.op("pool", lambda e: e.memset(ident_f[:], 1.0), writes=[ident_f.r])
            P.op("pool", lambda e: e.affine_select(out=ident_f[:], in_=ident_f[:], pattern=[[-1, 128]],
                                                   compare_op=ALU.is_equal, fill=0.0, base=0, channel_multiplier=1),
                 reads=[ident_f.r], writes=[ident_f.r])
            P.op("pool", lambda e: e.tensor_copy(out=ident_b[:], in_=ident_f[:]), reads=[ident_f.r], writes=[ident_b.r])
            P.op("pool", lambda e: e.memset(ones_b[:], 1.0), writes=[ones_b.r])

            mixT = S(es_all, "mixT", [128, 8, NCOL], BF16)
            mix_r = [Res("mix%d" % i) for i in range(8)]
            P.op("pool", lambda e: e.memset(mixT[:], 0.0), writes=mix_r)
            lamt = S(es_all, "lamt", [128, 8], F32)
            sgp = S(es_all, "sgp", [128, 2], F32)
            maskT = S(es_all, "maskT", [128, 128], BF16)
            ones_f = S(es_all, "ones_f", [128, 128], F32)
            es_B = contextlib.ExitStack()
            es_all.enter_context(es_B)
            qT = S(es_B, "qT", [128, 4, NCOL], BF16)
            kT = S(es_B, "kT", [128, 4, NCOL], BF16)
            vb = S(es_B, "vb", [128, NT, 512], BF16)
            uT = S(es_B, "uT", [128, 4, L + NS * TS], BF16)

            es_A = contextlib.ExitStack()
            with es_A:
                g1b = S(es_A, "g1b", [128, D], F32)
                gqb = S(es_A, "gqb", [128, 512], F32)
                gkb = S(es_A, "gkb", [128, 512], F32)
                wi = S(es_A, "wi", [128, 8, 2048], BF16)
                ld("sp", g1b[:], rawap(norm1.tensor, 0, [[0, 128], [1, D]]), [g1b.r])
                ld("sp", gqb[:].rearrange("p (h e) -> p h e", h=4), rawap(q_norm.tensor, 0, [[0, 128], [0, 4], [1, 128]]), [gqb.r])
                ld("sp", gkb[:].rearrange("p (h e) -> p h e", h=4), rawap(k_norm.tensor, 0, [[0, 128], [0, 4], [1, 128]]), [gkb.r])
                P.op("act", lambda e: e.mul(out=gqb[:], in_=gqb[:], mul=0.125), reads=[gqb.r], writes=[gqb.r])
                wi_r = [Res("wi%d" % k) for k in range(8)]
                w_in_v = w_in.rearrange("(kc p) n -> p kc n", p=128)
                for kc in range(8):
                    ld("pool", wi[:, kc, :], w_in_v[:, kc, :], [wi_r[kc]])

                NB = 2
                xt = [S(es_A, "xt%d" % i, [128, D], F32) for i in range(NB)]
                junk = S(es_A, "junk", [128, D], BF16)
                ss = [S(es_A, "ss%d" % i, [128, 4], F32) for i in range(NB)]
                xn = [S(es_A, "xn%d" % i, [128, D], BF16) for i in range(NB)]
                xnT = [S(es_A, "xnT%d" % i, [128, 8, 128], BF16) for i in range(NB)]
                sqb = [S(es_A, "sqb%d" % i, [128, 1024], F32) for i in range(NB)]
                s8 = [S(es_A, "s8%d" % i, [128, 16 * 3], F32) for i in range(NB)]
                qk32 = [S(es_A, "qk32%d" % i, [128, 1024], F32) for i in range(NB)]
                k32 = [S(es_A, "k32%d" % i, [128, 512], F32) for i in range(NB)]
                qkb = [S(es_A, "qkb%d" % i, [128, 1024], BF16) for i in range(NB)]
                v32 = [S(es_A, "v32%d" % i, [128, 512], F32) for i in range(NB)]
                pT = [PS(es_A, "pT%d" % i, [128, 8, 128], BF16) for i in range(2)]
                ps_qk = PS(es_A, "ps_qk", [128, 1024], F32)
                ps_v = PS(es_A, "ps_v", [128, 512], F32)
                ps_u = PS(es_A, "ps_u", [128, 4, 128], F32)
                pQK = PS(es_A, "pQK", [128, 8, 128], BF16)
                qT_r = [Res("qT%d" % i) for i in range(NT)]
                kT_r = [Res("kT%d" % i) for i in range(NT)]
                vb_r = [Res("vb%d" % i) for i in range(NT)]
                uT_r = [Res("uT%d" % i) for i in range(NT)]

                import os as _os
                _NTR = int(_os.environ.get('DBG_NT', NT)); _CUT = float(_os.environ.get('DBG_CUT', 99))
                for tt in range(_NTR):
                    b = tt % NB
                    X = xt[b]
                    if tt == 0:
                        P.op("pool", lambda e, X=X: e.memset(X[:], 0.0), writes=[X.r])
                        ld("sp", X[0:16, :], meta, [X.r], name="x0a")
                        r2 = Res("x0b")
                        ld("sp", X[32:96, :], xs, [r2], reads=[X.r])
                        xr = [X.r, r2]
                    else:
                        ld("sp", X[:], xp[(tt - 1) * 128:tt * 128, :], [X.r])
                        xr = [X.r]
                    P.op("act", lambda e, X=X, b=b: e.activation(out=junk[:], in_=X[:], func=AF.Square, accum_out=ss[b][:, 0:1]),
                         reads=xr, writes=[junk.r, ss[b].r])
                    P.op("dve", lambda e, b=b: e.tensor_scalar(out=ss[b][:, 1:2], in0=ss[b][:, 0:1], scalar1=1.0 / D, scalar2=EPS,
                                                               op0=ALU.mult, op1=ALU.add), reads=[ss[b].r], writes=[ss[b].r])
                    P.op("act", lambda e, b=b: e.sqrt(out=ss[b][:, 2:3], in_=ss[b][:, 1:2]), reads=[ss[b].r], writes=[ss[b].r])
                    P.op("dve", lambda e, b=b: e.reciprocal(out=ss[b][:, 3:4], in_=ss[b][:, 2:3]), reads=[ss[b].r], writes=[ss[b].r])
                    P.op("dve", lambda e, X=X, b=b: e.scalar_tensor_tensor(out=xn[b][:], in0=X[:], scalar=ss[b][:, 3:4], in1=g1b[:],
                                                                          op0=ALU.mult, op1=ALU.mult),
                         reads=xr + [ss[b].r, g1b.r], writes=[xn[b].r])
                    if _CUT < 1:
                        continue
                    pt_ = pT[tt % 2]
                    for kc in range(8):
                        P.op("pe", lambda e, b=b, kc=kc, pt_=pt_: e.transpose(out=pt_[:, kc, :], in_=xn[b][:, kc * 128:(kc + 1) * 128],
                                                                              identity=ident_b[:]),
                             reads=[xn[b].r, ident_b.r], writes=[pt_.r])
                    P.op("act", lambda e, b=b, pt_=pt_: e.copy(out=xnT[b][:], in_=pt_[:]), reads=[pt_.r], writes=[xnT[b].r])
                    if _CUT < 2:
                        continue
                    for nb in range(2):
                        for kc in range(8):
                            P.op("pe", lambda e, b=b, kc=kc, nb=nb: e.matmul(ps_qk[:, nb * 512:(nb + 1) * 512], lhsT=xnT[b][:, kc, :],
                                                                              rhs=wi[:, kc, nb * 512:(nb + 1) * 512],
                                                                              start=(kc == 0), stop=(kc == 7)),
                                 reads=[xnT[b].r, wi_r[kc]], writes=[ps_qk.r])
                    for kc in range(8):
                        P.op("pe", lambda e, b=b, kc=kc: e.matmul(ps_v[:], lhsT=xnT[b][:, kc, :], rhs=wi[:, kc, 1024:1536],
                                                                  start=(kc == 0), stop=(kc == 7)),
                             reads=[xnT[b].r, wi_r[kc]], writes=[ps_v.r])
                    for c in range(4):
                        for kc in range(8):
                            P.op("pe", lambda e, b=b, kc=kc, c=c: e.matmul(ps_u[:, c, :], lhsT=wi[:, kc, 1536 + c * 128:1536 + (c + 1) * 128],
                                                                            rhs=xnT[b][:, kc, :], start=(kc == 0), stop=(kc == 7)),
                                 reads=[xnT[b].r, wi_r[kc]], writes=[ps_u.r])
                    if _CUT < 3:
                        continue
                    for hf in range(2):
                        sl = slice(hf * 512, (hf + 1) * 512)
                        P.op("act", lambda e, b=b, sl=sl: e.activation(out=sqb[b][:, sl], in_=ps_qk[:, sl], func=AF.Square),
                             reads=[ps_qk.r, sqb[b].r], writes=[sqb[b].r])
                    if _CUT < 3.1:
                        continue
                    P.op("dve", lambda e, b=b: e.tensor_reduce(out=s8[b][:, 0:16], in_=sqb[b][:].rearrange("p (c d) -> p c d", d=64),
                                                               axis=AX.X, op=ALU.add), reads=[sqb[b].r], writes=[s8[b].r])
                    P.op("dve", lambda e, b=b: e.tensor_scalar(out=s8[b][:, 16:32], in0=s8[b][:, 0:16], scalar1=1.0 / 64, scalar2=EPS,
                                                               op0=ALU.mult, op1=ALU.add), reads=[s8[b].r], writes=[s8[b].r])
                    P.op("act", lambda e, b=b: e.sqrt(out=s8[b][:, 32:48], in_=s8[b][:, 16:32]), reads=[s8[b].r], writes=[s8[b].r])
                    P.op("dve", lambda e, b=b: e.reciprocal(out=s8[b][:, 0:16], in_=s8[b][:, 32:48]), reads=[s8[b].r], writes=[s8[b].r])
                    if _CUT < 3.2:
                        continue
                    for hf in range(2):
                        sl = slice(hf * 512, (hf + 1) * 512)
                        P.op("dve", lambda e, b=b, sl=sl, hf=hf: e.tensor_tensor(
                            out=qk32[b][:, sl].rearrange("p (c d) -> p c d", d=64),
                            in0=ps_qk[:, sl].rearrange("p (c d) -> p c d", d=64),
                            in1=s8[b][:, hf * 8:hf * 8 + 8].unsqueeze(2).to_broadcast([128, 8, 64]), op=ALU.mult),
                            reads=[ps_qk.r, s8[b].r, qk32[b].r], writes=[qk32[b].r])
                    if _CUT < 3.3:
                        continue
                    P.op("pool", lambda e, b=b: e.tensor_tensor(out=qkb[b][:, 0:512], in0=qk32[b][:, 0:512], in1=gqb[:], op=ALU.mult),
                         reads=[qk32[b].r, gqb.r], writes=[qkb[b].r])
                    P.op("dve", lambda e, b=b: e.tensor_tensor(out=k32[b][:], in0=qk32[b][:, 512:1024], in1=gkb[:], op=ALU.mult),
                         reads=[qk32[b].r, gkb.r], writes=[k32[b].r])
                    rkb = Res("kb")
                    P.op("pool", lambda e, b=b: e.tensor_copy(out=qkb[b][:, 512:1024], in_=k32[b][:]), reads=[k32[b].r, qkb[b].r], writes=[qkb[b].r])
                    if _CUT < 3.4:
                        continue
                    for h in range(8):
                        P.op("pe", lambda e, b=b, h=h: e.transpose(out=pQK[:, h, :], in_=qkb[b][:, h * 128:(h + 1) * 128], identity=ident_b[:]),
                             reads=[qkb[b].r, ident_b.r], writes=[pQK.r])
                    if _CUT < 3.5:
                        continue
                    c0 = tt * 128
                    if _CUT >= 3.6:
                        P.op("act", lambda e, c0=c0: e.copy(out=qT[:, :, c0:c0 + 128], in_=pQK[:, 0:4, :]), reads=[pQK.r], writes=[qT_r[tt]])
                    if _CUT >= 3.7:
                        P.op("act", lambda e, c0=c0: e.copy(out=kT[:, :, c0:c0 + 128], in_=pQK[:, 4:8, :]), reads=[pQK.r], writes=[kT_r[tt]])
                    if _CUT < 4:
                        continue
                    P.op("act", lambda e, b=b: e.copy(out=v32[b][:], in_=ps_v[:]), reads=[ps_v.r], writes=[v32[b].r])
                    P.op("pool", lambda e, b=b, tt=tt: e.tensor_copy(out=vb[:, tt, :], in_=v32[b][:]), reads=[v32[b].r], writes=[vb_r[tt]])
                    if tt == 0:
                        P.op("dve", lambda e: e.tensor_copy(out=uT[:, :, 0:16], in_=ps_u[:, :, 0:16]), reads=[ps_u.r], writes=[uT_r[0]])
                        P.op("dve", lambda e: e.tensor_copy(out=uT[:, :, L:L + 64], in_=ps_u[:, :, 32:96]), reads=[ps_u.r, uT_r[0]], writes=[uT_r[0]])
                    else:
                        o0 = 16 + (tt - 1) * 128
                        P.op("act", lambda e, o0=o0: e.copy(out=uT[:, :, o0:o0 + 128], in_=ps_u[:]), reads=[ps_u.r], writes=[uT_r[tt]])
                    if tt == 0:
                        ld("sp", kp[0:16, :], k32[b][0:16, :], [], reads=[k32[b].r])
                        ld("sp", ks[:, :], k32[b][32:96, :], [], reads=[k32[b].r])
                        ld("sp", vp[0:16, :], v32[b][0:16, :], [], reads=[v32[b].r])
                        ld("sp", vs[:, :], v32[b][32:96, :], [], reads=[v32[b].r])
                    else:
                        o0 = 16 + (tt - 1) * 128
                        ld("sp", kp[o0:o0 + 128, :], k32[b][:], [], reads=[k32[b].r])
                        ld("sp", vp[o0:o0 + 128, :], v32[b][:], [], reads=[v32[b].r])
                P.barrier()
            TC = 129
            NM = L // TC
            PI = math.pi
            es_S = contextlib.ExitStack()
            with es_S:
                prm = S(es_S, "prm", [128, 3, 16], F32)
                sc = S(es_S, "sc", [128, 24, 16], F32)
                sci = S(es_S, "sci", [128, 16], I32)
                BT = S(es_S, "BT", [128, 4, 2, 128], BF16)
                CT = S(es_S, "CT", [128, 16, 2, 128], BF16)
                Dp = S(es_S, "Dp", [128, 8], F32)
                wg = S(es_S, "wg", [128, 4, 512], BF16)
                TCs = S(es_S, "TCs", [128, 16, TC], F32)
                TSn = S(es_S, "TSn", [128, 16, TC], F32)
                rq = S(es_S, "rq", [128, 16], F32)
                y32 = S(es_S, "y32", [128, TC], F32)
                g32 = [S(es_S, "g32_%d" % i, [128, TC], F32) for i in range(4)]
                gb = [S(es_S, "gb_%d" % i, [128, TC], BF16) for i in range(4)]
                sg = S(es_S, "sg", [128, TC], F32)
                es_P = contextlib.ExitStack()
                es_P.__enter__()
                pa = S(es_P, "pa", [16, 3, 128], F32)
                ldt = S(es_P, "ldt", [16, 2], F32)
                ld("sp", pa[:, 0, :], a_re, [pa.r])
                ld("sp", pa[:, 1, :], a_im, [pa.r])
                ld("sp", ldt[:], log_dt, [ldt.r])
                CP("dve", pa[:, 2, :].rearrange("p (g q) -> p g q", g=2), ldt[:].unsqueeze(2).to_broadcast([16, 2, 64]), [ldt.r, pa.r], [pa.r])
                ps0 = PS(es_S, "ps0", [128, 512], F32)
                for j in range(3):
                    TR(ps0[:, j * 16:(j + 1) * 16], pa[:, j, :], ident_f[0:16, 0:16], [pa.r, ident_f.r], [ps0.r])
                CP("act", prm[:].rearrange("p a b -> p (a b)"), ps0[:, 0:48], [ps0.r], [prm.r])
                P.cut(1)
                R = lambda k: sc[:, k, :]
                are, aim, ldtp = prm[:, 0, :], prm[:, 1, :], prm[:, 2, :]
                scr = [sc.r, prm.r]
                ACT(R(0), ldtp, AF.Exp, scr, [sc.r])
                TT("dve", R(1), are, R(0), ALU.mult, scr, [sc.r])
                ACT(R(2), R(1), AF.Exp, scr, [sc.r])
                TT("dve", R(3), aim, R(0), ALU.mult, scr, [sc.r])

                def sincos(x, s_out, c_out, t1, t2, ti, rd, wr):
                    C1 = 6.28125
                    C2 = 2 * PI - C1
                    TSC("dve", t1, x, 1.0 / (2 * PI), None, ALU.mult, None, rd, wr)
                    CP("dve", ti, t1, rd, wr)
                    CP("dve", t1, ti, rd, wr)
                    STT(t2, t1, -C1, x, ALU.mult, ALU.add, rd, wr)
                    STT(t2, t1, -C2, t2, ALU.mult, ALU.add, rd, wr)
                    TSC("dve", t1, t2, PI, -2 * PI, ALU.is_gt, ALU.mult, rd, wr)
                    TT("dve", t2, t2, t1, ALU.add, rd, wr)
                    TSC("dve", t1, t2, -PI, 2 * PI, ALU.is_lt, ALU.mult, rd, wr)
                    TT("dve", t2, t2, t1, ALU.add, rd, wr)
                    ACT(s_out, t2, AF.Sin, rd, wr)
                    TSC("dve", t2, t2, PI / 2, None, ALU.add, None, rd, wr)
                    TSC("dve", t1, t2, PI, -2 * PI, ALU.is_gt, ALU.mult, rd, wr)
                    TT("dve", t2, t2, t1, ALU.add, rd, wr)
                    ACT(c_out, t2, AF.Sin, rd, wr)
                sincos(R(3), R(4), R(5), R(6), R(7), sci[:], scr + [sci.r], [sc.r, sci.r])
                AR, AI = R(8), R(9)
                TT("dve", AR, R(2), R(5), ALU.mult, scr, [sc.r])
                TT("dve", AI, R(2), R(4), ALU.mult, scr, [sc.r])
                TT("dve", R(10), are, are, ALU.mult, scr, [sc.r])
                TT("dve", R(11), aim, aim, ALU.mult, scr, [sc.r])
                TT("dve", R(10), R(10), R(11), ALU.add, scr, [sc.r])
                P.op("dve", lambda e: e.reciprocal(out=R(10), in_=R(10)), reads=scr, writes=[sc.r])
                TSC("dve", R(11), AR, -1.0, None, ALU.add, None, scr, [sc.r])
                TT("dve", R(12), R(11), are, ALU.mult, scr, [sc.r])
                TT("dve", R(13), AI, aim, ALU.mult, scr, [sc.r])
                TT("dve", R(12), R(12), R(13), ALU.add, scr, [sc.r])
                TT("dve", R(12), R(12), R(10), ALU.mult, scr, [sc.r])
                TT("dve", R(13), AI, are, ALU.mult, scr, [sc.r])
                TT("dve", R(14), R(11), aim, ALU.mult, scr, [sc.r])
                TT("dve", R(13), R(13), R(14), ALU.subtract, scr, [sc.r])
                TT("dve", R(13), R(13), R(10), ALU.mult, scr, [sc.r])
                GR, GI = R(12), R(13)

                P.cut(2)
                BN = S(es_P, "BN", [128, 4, 256], F32)
                es_L = contextlib.ExitStack()
                with es_L:
                    Bld = S(es_L, "Bld", [16, 4, 2048], F32)
                    Cl2 = S(es_L, "Cl2", [16, 2, 2048], F32)
                    for j, src in enumerate((b_re, b_im, c_re, c_im)):
                        ld("sp", Bld[:, j, :], src, [Bld.r])
                    for ri in range(2):
                        CP("pool", Cl2[:, ri, :].rearrange("p (c g q) -> p c g q", c=16, g=2),
                           Bld[:, 2 + ri, :].rearrange("p (g c q) -> p c g q", g=2, c=16), [Bld.r, Cl2.r], [Cl2.r])
                    for j in range(4):
                        for c in range(16):
                            if j < 2:
                                src_ap = rawap(Bld, j * 2048 + c, [[4 * 2048, 16], [16, 128]])
                            else:
                                src_ap = Cl2[:, j - 2, c * 128:(c + 1) * 128]
                            TR(ps0[:, c * 16:(c + 1) * 16], src_ap, ident_f[0:16, 0:16], [Bld.r, Cl2.r, ident_f.r], [ps0.r])
                        CP("act", BN[:, j, :], ps0[:, 0:256], [ps0.r], [BN.r])
                    P.barrier()
                P.cut(3)
                Bv = lambda j: BN[:, j, :].rearrange("p (c i) -> p c i", c=16)
                BB = S(es_P, "BB", [128, 4, 256], F32)
                BBv = lambda j: BB[:, j, :].rearrange("p (c i) -> p c i", c=16)
                gbc = lambda g: g.unsqueeze(1).to_broadcast([128, 16, 16])
                rdB = [BN.r, BB.r, sc.r]
                TT("dve", BBv(0), Bv(0), gbc(GR), ALU.mult, rdB, [BB.r])
                TT("dve", BBv(2), Bv(1), gbc(GI), ALU.mult, rdB, [BB.r])
                TT("dve", BBv(0), BBv(0), BBv(2), ALU.subtract, rdB, [BB.r])
                TT("dve", BBv(1), Bv(1), gbc(GR), ALU.mult, rdB, [BB.r])
                TT("dve", BBv(2), Bv(0), gbc(GI), ALU.mult, rdB, [BB.r])
                TT("dve", BBv(1), BBv(1), BBv(2), ALU.add, rdB, [BB.r])
                MASK = S(es_P, "MASK", [128, 16, 2, 16], F32)
                MS("pool", MASK[:], 0.0, [MASK.r])
                MS("pool", MASK[0:64, :, 0, :], 1.0, [MASK.r])
                MS("pool", MASK[64:128, :, 1, :], 1.0, [MASK.r])
                Z4 = S(es_P, "Z4", [128, 16, 2, 16], F32)
                for ri in range(2):
                    TT("dve", Z4[:], rawap(BB, ri * 256, [[1024, 128], [1, 16], [0, 2], [16, 16]]), MASK[:], ALU.mult,
                       [BB.r, MASK.r, Z4.r], [Z4.r])
                    for k in range(4):
                        TR(ps0[:, k * 128:(k + 1) * 128], Z4[:, 4 * k:4 * k + 4, :, :].rearrange("p a b c -> p (a b c)"), ident_f[:],
                           [Z4.r, ident_f.r], [ps0.r])
                    CP("act", BT[:, :, ri, :], ps0[:].rearrange("p (k m) -> p k m", k=4), [ps0.r], [BT.r])
                P.cut(4)
                CTf = S(es_P, "CTf", [128, 16, 2, 128], F32)
                MS("pool", CTf[:], 0.0, [CTf.r])
                for ri in range(2):
                    for il in range(4):
                        outv = rawap(CTf, ri * 128 + il * 32 + il * 256, [[4096, 128], [4 * 256, 4], [16, 2], [1, 16]])
                        inv = rawap(BN, (2 + ri) * 256 + il, [[1024, 128], [4, 4], [0, 2], [16, 16]])
                        mk = MASK[:, 0:4, :, :]
                        TT("dve", outv, inv, mk, ALU.mult, [BN.r, MASK.r, CTf.r], [CTf.r])
                CP("pool", CT[:, :, 0, :], CTf[:, :, 0, :], [CTf.r], [CT.r])
                TSC("dve", CT[:, :, 1, :], CTf[:, :, 1, :], -1.0, None, ALU.mult, None, [CTf.r, CT.r], [CT.r])
                P.cut(5)
                dld = S(es_P, "dld", [4, 2, 128], F32)
                ld("sp", dld[:, 0, :], ssm_d, [dld.r])
                ld("sp", dld[:, 1, :], b_glu, [dld.r])
                TR(ps0[:, 0:4], dld[:, 0, :], ident_f[0:4, 0:4], [dld.r, ident_f.r], [ps0.r])
                TR(ps0[:, 4:8], dld[:, 1, :], ident_f[0:4, 0:4], [dld.r, ident_f.r], [ps0.r])
                CP("act", Dp[:], ps0[:, 0:8], [ps0.r], [Dp.r])
                ld("pool", wg[:], w_glu.rearrange("(kc p) n -> p kc n", p=128), [wg.r])
                P.cut(6)
                es_T = contextlib.ExitStack()
                with es_T:
                    NI = S(es_T, "NI", [128, TC], I32)
                    NF = S(es_T, "NF", [128, TC], F32)
                    P.op("pool", lambda e: e.iota(NI[:], pattern=[[1, TC]], base=1, channel_multiplier=0), writes=[NI.r])
                    CP("dve", NF[:], NI[:], [NI.r], [NF.r])
                    TA = S(es_T, "TA", [128, 16, TC], F32)
                    T1 = S(es_T, "T1", [128, 16, TC], F32)
                    T2 = S(es_T, "T2", [128, 16, TC], F32)
                    TI = S(es_T, "TI", [128, 16, TC], I32)
                    TT("dve", TA[:], R(3).unsqueeze(2).to_broadcast([128, 16, TC]), NF[:].unsqueeze(1).to_broadcast([128, 16, TC]), ALU.mult,
                       [sc.r, NF.r], [TA.r])
                    rr = [TA.r, T1.r, T2.r, TI.r, TCs.r, TSn.r]
                    sincos(TA[:], TSn[:], TCs[:], T1[:], T2[:], TI[:], rr, rr)
                    P.barrier()

                P.cut(7)
                P.barrier()
                es_P.close()
                es_M = contextlib.ExitStack()
                es_M.__enter__()
                NB2 = 2
                pXr = [PS(es_S, "pXr%d" % i, [128, 512], F32) for i in range(NB2)]
                pXi = [PS(es_S, "pXi%d" % i, [128, 512], F32) for i in range(NB2)]
                pY = PS(es_S, "pY", [128, 512], F32)
                pZ = PS(es_S, "pZ", [128, 512], F32)
                wk = [S(es_M, "wk%d" % i, [128, 8, TC], F32) for i in range(NB2)]
                wk2 = [S(es_M, "wk2%d" % i, [128, 4, TC], F32) for i in range(NB2)]
                H32 = [S(es_M, "H32_%d" % i, [128, 2, TC], F32) for i in range(16)]
                Hb = [S(es_M, "Hb%d" % i, [128, 2, TC], BF16) for i in range(4)]
                CP("dve", rq[:], R(2), [sc.r], [rq.r])

                def glu(N, outs):
                    for kq in range(4):
                        for kc in range(4):
                            MM(pZ[:, 0:N], wg[:, kc, kq * 128:(kq + 1) * 128], gb[kc][:, 0:N], kc == 0, kc == 3, [wg.r, gb[kc].r], [pZ.r])
                        ACT(sg[:, 0:N], pZ[:, 0:N], AF.Sigmoid, [pZ.r, Dp.r], [sg.r], bias=Dp[:, 4 + kq:5 + kq])
                        for (sl, dst, dres) in outs[kq]:
                            TT("pool", dst, g32[kq][:, sl], sg[:, sl], ALU.mult, [g32[kq].r, sg.r], [dres])

                def y_finish(k, N, ucols):
                    STT(y32[:, 0:N], uT[:, k, ucols], Dp[:, k:k + 1], pY[:, 0:N], ALU.mult, ALU.add, [uT_r[0], pY.r, Dp.r], [y32.r])
                    ACT(g32[k][:, 0:N], y32[:, 0:N], AF.Gelu_apprx_tanh, [y32.r], [g32[k].r])
                    CP("pool", gb[k][:, 0:N], g32[k][:, 0:N], [g32[k].r], [gb[k].r])

                uT_all = list(uT_r)
                for m in range(NM):
                    c0 = m * TC
                    for k in range(4):
                        for il in range(4):
                            i = 4 * k + il
                            b = i % NB2
                            W = wk[b]
                            W2 = wk2[b]
                            MM(pXr[b][:, 0:TC], BT[32 * il:32 * il + 32, k, 0, :], uT[32 * il:32 * il + 32, k, c0:c0 + TC], True, True,
                               [BT.r] + uT_all, [pXr[b].r], tp=(32 * il, 0))
                            MM(pXi[b][:, 0:TC], BT[32 * il:32 * il + 32, k, 1, :], uT[32 * il:32 * il + 32, k, c0:c0 + TC], True, True,
                               [BT.r] + uT_all, [pXi[b].r], tp=(32 * il, 0))
                            cs_, sn_ = TCs[:, i, :], TSn[:, i, :]
                            rd = [pXr[b].r, pXi[b].r, TCs.r, TSn.r, W.r]
                            TT("dve", W[:, 0, :], pXr[b][:, 0:TC], cs_, ALU.mult, rd, [W.r])
                            TT("dve", W[:, 1, :], pXi[b][:, 0:TC], sn_, ALU.mult, rd, [W.r])
                            TT("dve", W[:, 4, :], W[:, 0, :], W[:, 1, :], ALU.add, rd, [W.r])
                            TT("dve", W[:, 2, :], pXi[b][:, 0:TC], cs_, ALU.mult, rd, [W.r])
                            TT("dve", W[:, 3, :], pXr[b][:, 0:TC], sn_, ALU.mult, rd, [W.r])
                            TT("dve", W[:, 5, :], W[:, 2, :], W[:, 3, :], ALU.subtract, rd, [W.r])
                            for ri in range(2):
                                init = 0.0 if m == 0 else H32[i][:, ri, TC - 1:TC]
                                P.op("dve", lambda e, W=W, ri=ri, init=init, i=i: e.tensor_tensor_scan(
                                    out=W[:, 6 + ri, :], data0=rq[:, i:i + 1].to_broadcast([128, TC]), data1=W[:, 4 + ri, :],
                                    initial=init, op0=ALU.mult, op1=ALU.add), reads=[W.r, rq.r, H32[i].r], writes=[W.r])
                            rd2 = [W.r, W2.r, TCs.r, TSn.r]
                            TT("pool", W2[:, 0, :], W[:, 6, :], cs_, ALU.mult, rd2, [W2.r])
                            TT("pool", W2[:, 1, :], W[:, 7, :], sn_, ALU.mult, rd2, [W2.r])
                            TT("pool", H32[i][:, 0, :], W2[:, 0, :], W2[:, 1, :], ALU.subtract, rd2 + [H32[i].r], [H32[i].r])
                            TT("pool", W2[:, 2, :], W[:, 7, :], cs_, ALU.mult, rd2, [W2.r])
                            TT("pool", W2[:, 3, :], W[:, 6, :], sn_, ALU.mult, rd2, [W2.r])
                            TT("pool", H32[i][:, 1, :], W2[:, 2, :], W2[:, 3, :], ALU.add, rd2 + [H32[i].r], [H32[i].r])
                            CP("act", Hb[il][:], H32[i][:], [H32[i].r], [Hb[il].r])
                        n = 0
                        for il in range(4):
                            for ri in range(2):
                                MM(pY[:, 0:TC], CT[:, 4 * k + il, ri, :], Hb[il][:, ri, :], n == 0, n == 7, [CT.r, Hb[il].r], [pY.r])
                                n += 1
                        y_finish(k, TC, slice(c0, c0 + TC))
                    outs = []
                    for kq in range(4):
                        if m == 0:
                            o = [(slice(0, 16), mixT[:, 4 + kq, 0:16], mix_r[4 + kq]),
                                 (slice(16, TC), mixT[:, 4 + kq, 128:128 + TC - 16], mix_r[4 + kq])]
                        else:
                            d0 = 128 + c0 - 16
                            o = [(slice(0, TC), mixT[:, 4 + kq, d0:d0 + TC], mix_r[4 + kq])]
                        outs.append(o)
                    glu(TC, outs)
                P.cut(9)
                FP = S(es_M, "FP", [128, 2, 16], F32)
                for i in range(16):
                    CP("dve", FP[:, :, i:i + 1], H32[i][:, :, TC - 1:TC], [H32[i].r, FP.r], [FP.r])
                fpo = S(es_M, "fpo", [16, 2, 128], F32)
                for ri in range(2):
                    TR(ps0[0:16, ri * 128:(ri + 1) * 128], FP[:, ri, :], ident_f[:], [FP.r, ident_f.r], [ps0.r])
                CP("act", fpo[:].rearrange("p a b -> p (a b)"), ps0[0:16, 0:256], [ps0.r], [fpo.r])
                ld("sp", srp, fpo[:, 0, :], [], reads=[fpo.r])
                ld("sp", sip, fpo[:, 1, :], [], reads=[fpo.r])

                P.cut(10)
                P.barrier()
                es_M.close()
                NSC = NS * TS
                XS = S(es_S, "XS", [128, 2, 16, NSC], F32)
                banks = [pXr[0], pXr[1], pXi[0], pXi[1]]
                for ri in range(2):
                    for il in range(4):
                        for k in range(4):
                            MM(banks[il][:, k * NSC:(k + 1) * NSC], BT[32 * il:32 * il + 32, k, ri, :], uT[32 * il:32 * il + 32, k, L:L + NSC],
                               True, True, [BT.r] + uT_all, [banks[il].r], tp=(32 * il, 0))
                    for il in range(4):
                        CP("act", XS[:, ri, :, :].rearrange("p (k il) c -> p k il c", il=4)[:, :, il, :],
                           banks[il][:, 0:4 * NSC].rearrange("p (a b) -> p a b", a=4), [banks[il].r, XS.r], [XS.r])
                P.cut(10.1)
                H0l = S(es_S, "H0l", [16, 2, 2048], F32)
                ld("sp", H0l[:, 0, :], s_re0, [H0l.r])
                ld("sp", H0l[:, 1, :], s_im0, [H0l.r])
                HS = S(es_S, "HS", [128, 2, 16, NS, TS + 1], F32)
                for ri in range(2):
                    for i in range(16):
                        TR(ps0[:, i * 16:(i + 1) * 16], H0l[:, ri, i * 128:(i + 1) * 128], ident_f[0:16, 0:16], [H0l.r, ident_f.r], [ps0.r])
                    CP("act", HS[:, ri, :, :, 0], ps0[:, 0:256].rearrange("p (a b) -> p a b", a=16), [ps0.r, HS.r], [HS.r])
                P.cut(10.2)
                M4 = S(es_S, "M4", [128, 4, 16, NS], F32)
                abc = lambda a: a.unsqueeze(2).to_broadcast([128, 16, NS])
                XSv = lambda ri, t: XS[:, ri, :, :].rearrange("p a (s t) -> p a s t", t=TS)[:, :, :, t]
                rdh = [HS.r, M4.r, XS.r, sc.r]
                for t in range(TS):
                    hr, hi = HS[:, 0, :, :, t], HS[:, 1, :, :, t]
                    TT("dve", M4[:, 0], hr, abc(AR), ALU.mult, rdh, [M4.r])
                    TT("dve", M4[:, 1], hi, abc(AI), ALU.mult, rdh, [M4.r])
                    TT("dve", M4[:, 0], M4[:, 0], M4[:, 1], ALU.subtract, rdh, [M4.r])
                    TT("dve", HS[:, 0, :, :, t + 1], M4[:, 0], XSv(0, t), ALU.add, rdh, [HS.r])
                    TT("dve", M4[:, 2], hi, abc(AR), ALU.mult, rdh, [M4.r])
                    TT("dve", M4[:, 3], hr, abc(AI), ALU.mult, rdh, [M4.r])
                    TT("dve", M4[:, 2], M4[:, 2], M4[:, 3], ALU.add, rdh, [M4.r])
                    TT("dve", HS[:, 1, :, :, t + 1], M4[:, 2], XSv(1, t), ALU.add, rdh, [HS.r])
                P.cut(10.3)
                HSb = S(es_S, "HSb", [128, 2, 16, NS, TS], BF16)
                for ri in range(2):
                    CP("pool", HSb[:, ri], HS[:, ri, :, :, 1:TS + 1], [HS.r, HSb.r], [HSb.r])
                P.cut(10.4)
                for k in range(4):
                    n = 0
                    for il in range(4):
                        for ri in range(2):
                            MM(pY[:, 0:NSC], CT[:, 4 * k + il, ri, :], HSb[:, ri, 4 * k + il].rearrange("p s t -> p (s t)"), n == 0, n == 7,
                               [CT.r, HSb.r], [pY.r])
                            n += 1
                    y_finish(k, NSC, slice(L, L + NSC))
                glu(NSC, [[(slice(0, NSC), mixT[:, 4 + kq, 32:32 + NSC], mix_r[4 + kq])] for kq in range(4)])
                P.cut(10.5)
                fso = S(es_S, "fso", [16, 2, 2048], F32)
                for ri in range(2):
                    for q4 in range(4):
                        for i4 in range(4):
                            i = q4 * 4 + i4
                            TR(ps0[0:16, i4 * 128:(i4 + 1) * 128], HS[:, ri, i, :, TS], ident_f[:], [HS.r, ident_f.r], [ps0.r])
                        CP("act", fso[:, ri, q4 * 512:(q4 + 1) * 512], ps0[0:16, :], [ps0.r, fso.r], [fso.r])
                ld("sp", srs, fso[:, 0, :], [], reads=[fso.r])
                ld("sp", sis, fso[:, 1, :], [], reads=[fso.r])
                P.barrier()
            es_L2 = contextlib.ExitStack()
            with es_L2:
                lq = S(es_L2, "lq", [128, 2, 128], F32)
                ld("sp", lq[:, 0, :], rawap(lam_q.tensor, 0, [[0, 128], [1, 128]]), [lq.r])
                ld("sp", lq[:, 1, :], rawap(lam_k.tensor, 0, [[0, 128], [1, 128]]), [lq.r])
                TT("dve", lq[:, 0, :], lq[:, 0, :], lq[:, 1, :], ALU.mult, [lq.r], [lq.r])
                P.op("dve", lambda e: e.tensor_reduce(out=lamt[:, 0:2], in_=lq[:, 0, :].rearrange("p (c d) -> p c d", c=2), axis=AX.X, op=ALU.add),
                     reads=[lq.r], writes=[lamt.r])
                ACT(lamt[:, 2:4], lamt[:, 0:2], AF.Exp, [lamt.r], [lamt.r])
                TT("dve", lamt[:, 5:6], lamt[:, 2:3], lamt[:, 3:4], ALU.subtract, [lamt.r], [lamt.r])
                TSC("dve", lamt[:, 5:6], lamt[:, 5:6], LAM_INIT, None, ALU.add, None, [lamt.r], [lamt.r])
                TSC("dve", lamt[:, 4:5], lamt[:, 5:6], -1.0, None, ALU.mult, None, [lamt.r], [lamt.r])
                ld("sp", sgp[:, 0:1], rawap(sub_norm.tensor, 0, [[1, 128], [1, 1]]), [sgp.r])
                TSC("dve", sgp[:, 1:2], sgp[:, 0:1], 1.0 - LAM_INIT, None, ALU.mult, None, [sgp.r], [sgp.r])
                MS("pool", maskT[:], 1.0, [maskT.r])
                P.op("pool", lambda e: e.affine_select(out=maskT[:], in_=maskT[:], pattern=[[1, 128]], compare_op=ALU.is_ge, fill=0.0,
                                                       base=0, channel_multiplier=-1), reads=[maskT.r], writes=[maskT.r])
                MS("pool", ones_f[:], 1.0, [ones_f.r])
                P.barrier()
            NLAM = lamt[:, 4:5]
            es_At = contextlib.ExitStack()
            with es_At:
                sA = [PS(es_At, "sA%d" % i, [128, 512], F32) for i in range(2)]
                sB = [PS(es_At, "sB%d" % i, [128, 512], F32) for i in range(2)]
                po = [PS(es_At, "po%d" % i, [128, 512], F32) for i in range(2)]
                pss = [PS(es_At, "pss%d" % i, [128, 512], F32) for i in range(2)]
                ptb = [[S(es_At, "ptb%d_%d" % (c, i), [128, 512], BF16) for i in range(2)] for c in range(2)]
                wa = [S(es_At, "wa%d" % i, [128, 512], F32) for i in range(4)]
                kv_all = list(kT_r) + list(qT_r) + list(vb_r)
                groups = [(0, 16, [0])] + [(128 * (4 * g + 1), 512, list(range(0, 4 * g + 5))) for g in range(4)]
                it = 0
                for (q0, NQ, kts) in groups:
                    for h in range(4):
                        for kt in kts:
                            if NQ == 16:
                                off, N, diag = 0, 16, True
                            elif kt * 128 < q0:
                                off, N, diag = 0, 512, False
                            else:
                                off = kt * 128 - q0
                                N, diag = 512 - off, True
                            b = it % 2
                            it += 1
                            sb = (sA[b], sB[b])
                            for c in range(2):
                                MM(sb[c][:, 0:N], kT[64 * c:64 * c + 64, h, kt * 128:(kt + 1) * 128], qT[64 * c:64 * c + 64, h, q0 + off:q0 + off + N],
                                   True, True, kv_all, [sb[c].r])
                            for c in range(2):
                                pt_ = ptb[c][b]
                                ACT(pt_[:, 0:N], sb[c][:, 0:N], AF.Exp, [sb[c].r], [pt_.r])
                                if diag:
                                    nd = min(N, 128)
                                    TT("dve" if c == 0 else "pool", pt_[:, 0:nd], pt_[:, 0:nd], maskT[:, 0:nd], ALU.mult, [pt_.r, maskT.r], [pt_.r])
                                elif kt == 0:
                                    TSC("dve" if c == 0 else "pool", pt_[:, 0:N], pt_[:, 0:N], maskT[:, 15:16], None, ALU.mult, None,
                                        [pt_.r, maskT.r], [pt_.r])
                            first, last = (kt == kts[0]), (kt == kts[-1])
                            for c in range(2):
                                pt_ = ptb[c][b]
                                MM(po[c][:, off:off + N], vb[:, kt, h * 128:(h + 1) * 128], pt_[:, 0:N], first, last, [pt_.r] + kv_all, [po[c].r])
                                MM(pss[c][:, off:off + N], ones_b[:], pt_[:, 0:N], first, last, [pt_.r, ones_b.r], [pss[c].r])
                        W0, W1, W2, W3 = wa
                        P.op("dve", lambda e, W0=W0, NQ=NQ: e.reciprocal(out=W0[:, 0:NQ], in_=pss[0][:, 0:NQ]), reads=[pss[0].r, W0.r], writes=[W0.r])
                        TT("dve", W0[:, 0:NQ], po[0][:, 0:NQ], W0[:, 0:NQ], ALU.mult, [po[0].r, W0.r], [W0.r])
                        P.op("dve", lambda e, W1=W1, NQ=NQ: e.reciprocal(out=W1[:, 0:NQ], in_=pss[1][:, 0:NQ]), reads=[pss[1].r, W1.r], writes=[W1.r])
                        TT("dve", W1[:, 0:NQ], po[1][:, 0:NQ], W1[:, 0:NQ], ALU.mult, [po[1].r, W1.r], [W1.r])
                        STT(W2[:, 0:NQ], W1[:, 0:NQ], NLAM, W0[:, 0:NQ], ALU.mult, ALU.add, [W0.r, W1.r, lamt.r, W2.r], [W2.r])
                        ACT(W3[:, 0:NQ], W2[:, 0:NQ], AF.Square, [W2.r, W3.r], [W3.r])
                        bq = it % 2
                        P.op("pe", lambda e, W3=W3, NQ=NQ, bq=bq: e.matmul(sA[bq][:, 0:NQ], lhsT=ones_f[:], rhs=W3[:, 0:NQ], start=True, stop=True),
                             reads=[W3.r, ones_f.r], writes=[sA[bq].r])
                        TSC("dve", W0[:, 0:NQ], sA[bq][:, 0:NQ], 1.0 / 128, EPS, ALU.mult, ALU.add, [sA[bq].r, W0.r], [W0.r])
                        P.op("act", lambda e, W0=W0, NQ=NQ: e.sqrt(out=W0[:, 0:NQ], in_=W0[:, 0:NQ]), reads=[W0.r], writes=[W0.r])
                        P.op("dve", lambda e, W0=W0, NQ=NQ: e.reciprocal(out=W0[:, 0:NQ], in_=W0[:, 0:NQ]), reads=[W0.r], writes=[W0.r])
                        STT(mixT[:, h, q0:q0 + NQ], W2[:, 0:NQ], sgp[:, 1:2], W0[:, 0:NQ], ALU.mult, ALU.mult, [W2.r, W0.r, sgp.r], [mix_r[h]])
                P.barrier()
            es_Q = contextlib.ExitStack()
            with es_Q:
                ptbc = S(es_Q, "ptbc", [128, NS * NPG], I32)
                iop = S(es_Q, "iop", [128, 1], I32)
                idx = S(es_Q, "idx", [128, NS * NPG], I32)
                ld("sp", ptbc[:], rawap(pt.tensor, 0, [[0, 128], [1, NS * NPG]]), [ptbc.r])
                P.op("pool", lambda e: e.iota(iop[:], pattern=[[0, 1]], base=0, channel_multiplier=1), writes=[iop.r])
                TSC("dve", idx[:], ptbc[:], 128, None, ALU.mult, None, [ptbc.r], [idx.r])
                TSC("dve", idx[:], idx[:], iop[:, 0:1], None, ALU.add, None, [idx.r, iop.r], [idx.r])
                Qblk = S(es_Q, "Qblk", [128, 4, 2, NS * TS], BF16)
                MS("pool", Qblk[:], 0.0, [Qblk.r])
                CP("act", Qblk[0:64, :, 0, :], qT[0:64, :, 32:96], [Qblk.r] + list(qT_r), [Qblk.r])
                CP("act", Qblk[64:128, :, 1, :], qT[64:128, :, 32:96], [Qblk.r] + list(qT_r), [Qblk.r])
                vnew = S(es_Q, "vnew", [128, NS, 512], BF16)
                MS("pool", vnew[:], 0.0, [vnew.r])
                pnew = S(es_Q, "pnew", [128, 32], BF16)
                MS("pool", pnew[:], 0.0, [pnew.r])
                msk4 = S(es_Q, "msk4", [4, 32], F32)
                MS("pool", msk4[:], 1.0, [msk4.r])
                P.op("pool", lambda e: e.affine_select(out=msk4[:], in_=msk4[:], pattern=[[0, 2], [0, 4], [1, 4]], compare_op=ALU.is_ge, fill=0.0,
                                                       base=0, channel_multiplier=-1), reads=[msk4.r], writes=[msk4.r])
                e4 = S(es_Q, "e4", [4, 32], F32)
                att = S(es_Q, "att", [NS * TS, 512], F32)
                kpg = [S(es_Q, "kpg%d" % i, [128, 8, 512], F32) for i in range(2)]
                vpg = [S(es_Q, "vpg%d" % i, [128, 8, 512], BF16) for i in range(3)]
                kTs = [S(es_Q, "kTs%d" % i, [128, 4, 128], BF16) for i in range(3)]
                pexp = [S(es_Q, "pexp%d" % i, [128, 512], BF16) for i in range(2)]
                rs = [S(es_Q, "rs%d" % i, [16, 4], F32) for i in range(2)]
                t0 = [S(es_Q, "t0_%d" % i, [16, 512], F32) for i in range(2)]
                o16 = [S(es_Q, "o16_%d" % i, [16, 512], F32) for i in range(2)]
                kd = [P.new_dsem("kd%d" % i) for i in range(2)]
                vd = [P.new_dsem("vd%d" % i) for i in range(3)]
                es_Q1 = contextlib.ExitStack()
                with es_Q1:
                    pk = [PS(es_Q1, "pk%d" % i, [128, 512], F32) for i in range(2)]
                    pS = [PS(es_Q1, "pS%d" % i, [128, 512], F32) for i in range(2)]
                    pSn = PS(es_Q1, "pSn", [128, 512], F32)
                    poc = [PS(es_Q1, "poc%d" % i, [128, 512], F32) for i in range(2)]
                    psm = PS(es_Q1, "psm", [128, 512], F32)
                    for s in range(NS):
                        MM(pk[0][0:4, :], ident_b[:, 32 + 4 * s:36 + 4 * s], vb[:, 0, :], True, True, [ident_b.r] + list(vb_r), [pk[0].r])
                        CP("act", vnew[0:4, s, :], pk[0][0:4, :], [pk[0].r, vnew.r], [vnew.r])

                    def gather(dst, sem, src, cols):
                        if P.dead:
                            return
                        waits = P._deps("pool", [idx.r], [dst.r])
                        fns = []
                        for n, col in enumerate(cols):
                            fns.append((n, col))
                        sem.val += 16 * len(cols)

                        def emit(e, waits=waits, fns=fns, dst=dst, sem=sem, src=src):
                            for ws, wv in waits:
                                e.wait_ge(ws, wv)
                            for n, col in fns:
                                e.indirect_dma_start(out=dst[:, n, :], out_offset=None, in_=src,
                                                     in_offset=bass.IndirectOffsetOnAxis(ap=idx[:, col:col + 1], axis=0)).then_inc(sem.sem, 16)
                        P.q["pool"].append(emit)
                        P._mark([idx.r], [dst.r], (sem.sem, sem.val))

                    gi_ = 0
                    kn = 0
                    for s in range(NS):
                        ps_ = pS[s % 2]
                        vbufs = []
                        for hf in range(2):
                            kb = kpg[gi_ % 2]
                            vbf = vpg[gi_ % 3]
                            cols = [s * NPG + hf * 8 + n for n in range(8)]
                            gather(kb, kd[gi_ % 2], ck, cols)
                            gather(vbf, vd[gi_ % 3], cv, cols)
                            gi_ += 1
                            vbufs.append(vbf)
                            for n in range(8):
                                pk_ = pk[kn % 2]
                                kt_ = kTs[kn % 3]
                                for h in range(4):
                                    TR(pk_[:, h * 128:(h + 1) * 128], kb[:, n, h * 128:(h + 1) * 128], ident_f[:], [kb.r, ident_f.r], [pk_.r])
                                CP("act" if kn % 2 == 0 else "dve", kt_[:].rearrange("p h k -> p (h k)"), pk_[:], [pk_.r], [kt_.r])
                                kn += 1
                                base = (hf * 8 + n) * 32
                                for h in range(4):
                                    for c in range(2):
                                        o0 = base + c * 16 + h * 4
                                        MM(ps_[:, o0:o0 + 4], kt_[:, h, :], Qblk[:, h, c, 4 * s:4 * s + 4], True, True, [kt_.r, Qblk.r], [ps_.r])
                        for h in range(4):
                            for c in range(2):
                                o0 = c * 16 + h * 4
                                MM(pSn[0:4, o0:o0 + 4], kT[:, h, 32 + 4 * s:36 + 4 * s], Qblk[:, h, c, 4 * s:4 * s + 4],
                                   True, True, [Qblk.r] + list(kT_r), [pSn.r])
                        ACT(e4[:], pSn[0:4, 0:32], AF.Exp, [pSn.r, e4.r], [e4.r])
                        TT("dve", pnew[0:4, :], e4[:], msk4[:], ALU.mult, [e4.r, msk4.r, pnew.r], [pnew.r])
                        px = pexp[s % 2]
                        ACT(px[:], ps_[:], AF.Exp, [ps_.r], [px.r])
                        for c in range(2):
                            for n in range(16):
                                vbf = vbufs[n // 8]
                                lh = px[:, n * 32 + c * 16:n * 32 + c * 16 + 16]
                                MM(poc[c][0:16, :], lh, vbf[:, n % 8, :], n == 0, False, [px.r, vbf.r], [poc[c].r])
                                MM(psm[0:16, c:c + 1], lh, ones_b[:, 0:1], n == 0, False, [px.r, ones_b.r], [psm.r])
                            lh = pnew[:, c * 16:(c + 1) * 16]
                            MM(poc[c][0:16, :], lh, vnew[:, s, :], False, True, [pnew.r, vnew.r], [poc[c].r])
                            MM(psm[0:16, c:c + 1], lh, ones_b[:, 0:1], False, True, [pnew.r, ones_b.r], [psm.r])
                        r_, t_, o_ = rs[s % 2], t0[s % 2], o16[s % 2]
                        P.op("dve", lambda e, r_=r_: e.reciprocal(out=r_[:, 0:2], in_=psm[0:16, 0:2]), reads=[psm.r, r_.r], writes=[r_.r])
                        TT("dve", r_[:, 2:3], r_[:, 1:2], lamt[0:16, 4:5], ALU.mult, [r_.r, lamt.r], [r_.r])
                        TSC("dve", t_[:], poc[0][0:16, :], r_[:, 0:1], None, ALU.mult, None, [poc[0].r, r_.r, t_.r], [t_.r])
                        STT(o_[:], poc[1][0:16, :], r_[:, 2:3], t_[:], ALU.mult, ALU.add, [poc[1].r, r_.r, t_.r, o_.r], [o_.r])
                        for h in range(4):
                            ld("sp", att[4 * s:4 * s + 4, h * 128:(h + 1) * 128], o_[4 * h:4 * h + 4, h * 128:(h + 1) * 128], [], reads=[o_.r])
                    P.barrier()
                es_Q2 = contextlib.ExitStack()
                with es_Q2:
                    NQ4 = NS * TS
                    sq4 = S(es_Q2, "sq4", [NQ4, 4, 128], F32)
                    s4 = S(es_Q2, "s4", [NQ4, 3, 4], F32)
                    sg4 = S(es_Q2, "sg4", [NQ4, 128], F32)
                    attb = S(es_Q2, "attb", [NQ4, 512], BF16)
                    pT4 = PS(es_Q2, "pT4", [128, 4, NQ4], BF16)
                    ld("sp", sg4[:], rawap(sub_norm.tensor, 0, [[0, NQ4], [1, 128]]), [sg4.r])
                    TSC("dve", sg4[:], sg4[:], 1.0 - LAM_INIT, None, ALU.mult, None, [sg4.r], [sg4.r])
                    attv = att[:].rearrange("p (h e) -> p h e", h=4)
                    ACT(sq4[:], attv, AF.Square, [att.r], [sq4.r])
                    P.op("dve", lambda e: e.tensor_reduce(out=s4[:, 0, :], in_=sq4[:], axis=AX.X, op=ALU.add), reads=[sq4.r], writes=[s4.r])
                    TSC("dve", s4[:, 1, :], s4[:, 0, :], 1.0 / 128, EPS, ALU.mult, ALU.add, [s4.r], [s4.r])
                    P.op("act", lambda e: e.sqrt(out=s4[:, 2, :], in_=s4[:, 1, :]), reads=[s4.r], writes=[s4.r])
                    P.op("dve", lambda e: e.reciprocal(out=s4[:, 0, :], in_=s4[:, 2, :]), reads=[s4.r], writes=[s4.r])
                    TT("dve", sq4[:], attv, s4[:, 0, :].unsqueeze(2).to_broadcast([NQ4, 4, 128]), ALU.mult, [att.r, s4.r, sq4.r], [sq4.r])
                    TT("dve", attb[:].rearrange("p (h e) -> p h e", h=4), sq4[:], sg4[:].unsqueeze(1).to_broadcast([NQ4, 4, 128]), ALU.mult,
                       [sq4.r, sg4.r], [attb.r])
                    for h in range(4):
                        TR(pT4[:, h, :], attb[:, h * 128:(h + 1) * 128], ident_b[0:NQ4, 0:NQ4], [attb.r, ident_b.r], [pT4.r])
                    CP("act", mixT[:, 0:4, 32:96], pT4[:], [pT4.r], mix_r[0:4])
                    P.barrier()
            es_B.close()
            es_C = contextlib.ExitStack()
            with es_C:
                x1 = S(es_C, "x1", [128, NT, D], F32)
                x1_r = [Res("x1_%d" % i) for i in range(NT)]
                xn2T = S(es_C, "xn2T", [128, 8, NCOL], BF16)
                xn2_r = [Res("xn2_%d" % i) for i in range(NT)]
                cwb = S(es_C, "cwb", [128, 4, NFC], F32)
                es_C1 = contextlib.ExitStack()
                with es_C1:
                    g2b = S(es_C1, "g2b", [128, D], F32)
                    ld("sp", g2b[:], rawap(norm2.tensor, 0, [[0, 128], [1, D]]), [g2b.r])
                    wo = S(es_C1, "wo", [128, 8, D], BF16)
                    ld("pool", wo[:], w_out.rearrange("(kc p) n -> p kc n", p=128), [wo.r])
                    cwl = S(es_C1, "cwl", [NFC, 4, 128], F32)
                    for j in range(3):
                        ld("sp", cwl[:, j, :], conv_w[j:j + 1, :].rearrange("o (c f) -> (o c) f", f=128), [cwl.r])
                    ld("sp", cwl[:, 3, :], conv_b.rearrange("o (c f) -> (o c) f", f=128), [cwl.r])
                    pc0 = PS(es_C1, "pcw0", [128, 512], F32)
                    for j in range(4):
                        TR(pc0[:, j * NFC:(j + 1) * NFC], cwl[:, j, :], ident_f[0:NFC, 0:NFC], [cwl.r, ident_f.r], [pc0.r])
                    CP("act", cwb[:].rearrange("p a b -> p (a b)"), pc0[:, 0:4 * NFC], [pc0.r], [cwb.r])
                    NB = 2
                    xt = [S(es_C1, "cxt%d" % i, [128, D], F32) for i in range(NB)]
                    junk = S(es_C1, "cjunk", [128, D], BF16)
                    ss = [S(es_C1, "cssq%d" % i, [128, 4], F32) for i in range(NB)]
                    xn = [S(es_C1, "cxn%d" % i, [128, D], BF16) for i in range(NB)]
                    pw = [PS(es_C1, "pw%d" % i, [128, 1024], F32) for i in range(2)]
                    pT2 = [PS(es_C1, "pT2%d" % i, [128, 8, 128], BF16) for i in range(2)]
                    for tt in range(NT):
                        b = tt % NB
                        X = xt[b]
                        if tt == 0:
                            MS("pool", X[:], 0.0, [X.r])
                            ld("sp", X[0:16, :], meta, [X.r])
                            r2 = Res("cx0b")
                            ld("sp", X[32:96, :], xs, [r2], reads=[X.r])
                            xr = [X.r, r2]
                        else:
                            ld("sp", X[:], xp[(tt - 1) * 128:tt * 128, :], [X.r])
                            xr = [X.r]
                        pw_ = pw[tt % 2]
                        for nb in range(2):
                            for kc in range(8):
                                MM(pw_[:, nb * 512:(nb + 1) * 512], mixT[:, kc, tt * 128:(tt + 1) * 128], wo[:, kc, nb * 512:(nb + 1) * 512],
                                   kc == 0, kc == 7, [wo.r] + mix_r, [pw_.r])
                        for nb in range(2):
                            sl = slice(nb * 512, (nb + 1) * 512)
                            TT("dve", x1[:, tt, sl], pw_[:, sl], X[:, sl], ALU.add, [pw_.r] + xr + [x1_r[tt]], [x1_r[tt]])
                        P.op("act", lambda e, tt=tt, b=b: e.activation(out=junk[:], in_=x1[:, tt, :], func=AF.Square, accum_out=ss[b][:, 0:1]),
                             reads=[x1_r[tt]], writes=[junk.r, ss[b].r])
                        TSC("dve", ss[b][:, 1:2], ss[b][:, 0:1], 1.0 / D, EPS, ALU.mult, ALU.add, [ss[b].r], [ss[b].r])
                        P.op("act", lambda e, b=b: e.sqrt(out=ss[b][:, 2:3], in_=ss[b][:, 1:2]), reads=[ss[b].r], writes=[ss[b].r])
                        P.op("dve", lambda e, b=b: e.reciprocal(out=ss[b][:, 3:4], in_=ss[b][:, 2:3]), reads=[ss[b].r], writes=[ss[b].r])
                        STT(xn[b][:], x1[:, tt, :], ss[b][:, 3:4], g2b[:], ALU.mult, ALU.mult, [x1_r[tt], ss[b].r, g2b.r], [xn[b].r])
                        pt_ = pT2[tt % 2]
                        for kc in range(8):
                            TR(pt_[:, kc, :], xn[b][:, kc * 128:(kc + 1) * 128], ident_b[:], [xn[b].r, ident_b.r], [pt_.r])
                        CP("act", xn2T[:, :, tt * 128:(tt + 1) * 128], pt_[:], [pt_.r], [xn2_r[tt]])
                    P.barrier()
                parts = [list(range(0, 8)), list(range(8, 15)), list(range(15, 22))]
                hT = mixT
                wd = S(es_C, "wd", [128, 8, D], BF16)
                Ab = S(es_C, "Ab", [128, L + 2], F32)
                As = S(es_C, "As", [128, NS, TS + 2], F32)
                Gt = S(es_C, "Gt", [128, NCOL], F32)
                wgu = [S(es_C, "wgu%d" % i, [128, 2, 8, 128], BF16) for i in range(2)]
                cst = S(es_C, "cst", [128, NFC, 2], F32)
                css = S(es_C, "css", [128, NFC, NS, 2], F32)
                hist = S(es_C, "hist", [NS * 2, 128], F32)
                MS("pool", Ab[:, 0:2], 0.0, [Ab.r])
                MS("pool", Gt[:], 0.0, [Gt.r])
                pa_ = [PS(es_C, "pa%d" % i, [128, 512], F32) for i in range(2)]
                pc_ = [PS(es_C, "pc%d" % i, [128, 512], F32) for i in range(2)]
                pd = [PS(es_C, "pd%d" % i, [128, 1024], F32) for i in range(1)]
                ph = PS(es_C, "ph", [128, 512], F32)
                ost = [S(es_C, "ost%d" % i, [128, D], F32) for i in range(1)]
                stg = [S(es_C, "stg%d" % i, [NS * 2, 512], F32) for i in range(2)]
                w_gate_v = w_gate.rearrange("(kc p) n -> p kc n", p=128)
                w_up_v = w_up.rearrange("(kc p) n -> p kc n", p=128)
                w_down_v = w_down.rearrange("(fc p) n -> p fc n", p=128)
                xn2_all = list(xn2_r)
                hT_r = Res("hT")
                colgroups = [(0, 512), (512, 512), (1024, 512), (1536, 512), (2048, 128)]
                for pi, fcs in enumerate(parts):
                    for j, fc in enumerate(fcs):
                        ld("pool", wd[:, j, :], w_down_v[:, fc, :], [wd.r])
                    for j, fc in enumerate(fcs):
                        wb = wgu[fc % 2]
                        ld("pool", wb[:, 0], w_gate_v[:, :, fc * 128:(fc + 1) * 128], [wb.r])
                        rwb2 = Res("wb2")
                        ld("pool", wb[:, 1], w_up_v[:, :, fc * 128:(fc + 1) * 128], [rwb2], reads=[wb.r])
                        wrd = [wb.r, rwb2]
                        ld("sp", hist[:], conv0[:, fc * 128:(fc + 1) * 128], [hist.r])
                        TR(ph[:, 0:NS * 2], hist[:], ident_f[0:NS * 2, 0:NS * 2], [hist.r, ident_f.r], [ph.r])
                        CP("act", As[:, :, 0:2], ph[:, 0:NS * 2].rearrange("p (s j) -> p s j", j=2), [ph.r, As.r], [As.r])
                        for gi, (c0, N) in enumerate(colgroups):
                            pa = pa_[gi % 2]
                            for kc in range(8):
                                MM(pa[:, 0:N], wb[:, 0, kc, :], xn2T[:, kc, c0:c0 + N], kc == 0, kc == 7, wrd + xn2_all, [pa.r])
                            if gi == 0:
                                CP("act", Ab[:, 2:18], pa[:, 0:16], [pa.r, Ab.r], [Ab.r])
                                CP("act", As[:, :, 2:2 + TS], pa[:, 32:96].rearrange("p (s t) -> p s t", t=TS), [pa.r, As.r], [As.r])
                                CP("act", Ab[:, 18:18 + 384], pa[:, 128:512], [pa.r, Ab.r], [Ab.r])
                            else:
                                CP("act", Ab[:, c0 - 110:c0 - 110 + N], pa[:, 0:N], [pa.r, Ab.r], [Ab.r])
                        w0, w1, w2, bb = (cwb[:, q, fc:fc + 1] for q in range(4))
                        rdc = [Ab.r, Gt.r, cwb.r]
                        ACT(Gt[:, 112:112 + L], Ab[:, 0:L], AF.Identity, rdc, [Gt.r], bias=bb, scale=w0)
                        STT(Gt[:, 112:112 + L], Ab[:, 1:L + 1], w1, Gt[:, 112:112 + L], ALU.mult, ALU.add, rdc, [Gt.r])
                        STT(Gt[:, 112:112 + L], Ab[:, 2:L + 2], w2, Gt[:, 112:112 + L], ALU.mult, ALU.add, rdc, [Gt.r])
                        ACT(Gt[:, 112:112 + L], Gt[:, 112:112 + L], AF.Gelu_apprx_tanh, rdc, [Gt.r])
                        CP("pool", Gt[:, 0:16], Gt[:, 112:128], [Gt.r], [Gt.r])
                        Gs = Gt[:, 32:96].rearrange("p (s t) -> p s t", t=TS)
                        rds = [As.r, Gt.r, cwb.r]
                        ACT(Gs, As[:, :, 0:TS], AF.Identity, rds, [Gt.r], bias=bb, scale=w0)
                        STT(Gs, As[:, :, 1:TS + 1], w1, Gs, ALU.mult, ALU.add, rds, [Gt.r])
                        STT(Gs, As[:, :, 2:TS + 2], w2, Gs, ALU.mult, ALU.add, rds, [Gt.r])
                        ACT(Gs, Gs, AF.Gelu_apprx_tanh, rds, [Gt.r])
                        for gi, (c0, N) in enumerate(colgroups):
                            pc = pc_[gi % 2]
                            for kc in range(8):
                                MM(pc[:, 0:N], wb[:, 1, kc, :], xn2T[:, kc, c0:c0 + N], kc == 0, kc == 7, wrd + xn2_all, [pc.r])
                            TT("dve", hT[:, j, c0:c0 + N], Gt[:, c0:c0 + N], pc[:, 0:N], ALU.mult, [Gt.r, pc.r, hT_r], [hT_r])
                        CP("pool", cst[:, fc, :], Ab[:, L:L + 2], [Ab.r, cst.r], [cst.r])
                        CP("pool", css[:, fc, :, :], As[:, :, TS:TS + 2], [As.r, css.r], [css.r])
                    lastp = (pi == len(parts) - 1)
                    for tt in range(NT):
                        pd_ = pd[0]
                        for nb in range(2):
                            for j in range(len(fcs)):
                                MM(pd_[:, nb * 512:(nb + 1) * 512], hT[:, j, tt * 128:(tt + 1) * 128], wd[:, j, nb * 512:(nb + 1) * 512],
                                   j == 0, j == len(fcs) - 1, [hT_r, wd.r], [pd_.r])
                        if not lastp:
                            for nb in range(2):
                                sl = slice(nb * 512, (nb + 1) * 512)
                                TT("dve", x1[:, tt, sl], pd_[:, sl], x1[:, tt, sl], ALU.add, [pd_.r, x1_r[tt]], [x1_r[tt]])
                        else:
                            o_ = ost[0]
                            for nb in range(2):
                                sl = slice(nb * 512, (nb + 1) * 512)
                                TT("dve", o_[:, sl], pd_[:, sl], x1[:, tt, sl], ALU.add, [pd_.r, x1_r[tt], o_.r], [o_.r])
                            if tt == 0:
                                ld("sp", ys[:, :], o_[32:96, :], [], reads=[o_.r])
                            else:
                                ld("sp", yp[(tt - 1) * 128:tt * 128, :], o_[:], [], reads=[o_.r])
                for q4 in range(6):
                    nch = min(4, NFC - 4 * q4)
                    for j in range(nch):
                        fc = 4 * q4 + j
                        TR(ph[0:2, j * 128:(j + 1) * 128], cst[:, fc, :], ident_f[:], [cst.r, ident_f.r], [ph.r])
                    CP("act", stg[0][0:2, 0:nch * 128], ph[0:2, 0:nch * 128], [ph.r, stg[0].r], [stg[0].r])
                    ld("sp", cp[:, q4 * 512:q4 * 512 + nch * 128], stg[0][0:2, 0:nch * 128], [], reads=[stg[0].r])
                    for j in range(nch):
                        fc = 4 * q4 + j
                        TR(pa_[0][0:NS * 2, j * 128:(j + 1) * 128], css[:, fc, :, :].rearrange("p s j -> p (s j)"), ident_f[:],
                           [css.r, ident_f.r], [pa_[0].r])
                    CP("act", stg[1][:, 0:nch * 128], pa_[0][0:NS * 2, 0:nch * 128], [pa_[0].r, stg[1].r], [stg[1].r])
                    ld("sp", cs[:, q4 * 512:q4 * 512 + nch * 128], stg[1][:, 0:nch * 128], [], reads=[stg[1].r])
                P.barrier()
        P.barrier()
        P._free = [d for d in P._dmap.values()]
        P._dmap = {}

    for vc in range(nv):
        emit_vc(vc)
    if True:
        P.run()
    return nc


def make_in_maps(inp, n_cores, npool, nv=1):
    f = lambda a: np.ascontiguousarray(np.asarray(a))
    ckf = f(inp["cache_k"]).reshape(npool * 128, 512)
    cvf = f(inp["cache_v"]).reshape(npool * 128, 512)
    shared = {
        "meta": f(inp["meta_tokens"]), "ck": ckf, "cv": cvf,
        "norm1": f(inp["norm1"]).reshape(1, D), "norm2": f(inp["norm2"]).reshape(1, D),
        "w_in": f(inp["w_in"])[0], "w_out": f(inp["w_out"])[0],
        "q_norm": f(inp["q_norm"]).reshape(1, 128), "k_norm": f(inp["k_norm"]).reshape(1, 128),
        "lam_q": f(inp["lam_q"]).reshape(1, 128), "lam_k": f(inp["lam_k"]).reshape(1, 128),
        "sub_norm": f(inp["sub_norm"]).reshape(1, 128),
        "a_re": f(inp["ssm_a_re"]).reshape(16, 128), "a_im": f(inp["ssm_a_im"]).reshape(16, 128),
        "log_dt": f(inp["ssm_log_dt"]).reshape(16, 2),
        "b_re": f(inp["ssm_b_re"]).reshape(16, 2048), "b_im": f(inp["ssm_b_im"]).reshape(16, 2048),
        "c_re": f(inp["ssm_c_re"]).reshape(16, 2048), "c_im": f(inp["ssm_c_im"]).reshape(16, 2048),
        "ssm_d": f(inp["ssm_d"]).reshape(4, 128),
        "w_glu": f(inp["w_glu"])[0], "b_glu": f(inp["b_glu"]).reshape(4, 128),
        "w_gate": f(inp["w_gate"])[0], "w_up": f(inp["w_up"])[0], "w_down": f(inp["w_down"])[0],
        "conv_w": f(inp["ffn_conv_w"])[0], "conv_b": f(inp["ffn_conv_b"]).reshape(1, DFF),
    }
    maps = []
    for c in range(n_cores):
        m = dict(shared)
        for vc in range(nv):
            g = c * nv + vc
            sfx = "_v%d" % vc
            m["xp" + sfx] = f(inp["x_prompt"][g])
            m["xs" + sfx] = f(inp["x_sample"][g * NS:(g + 1) * NS]).reshape(NS * TS, D)
            m["pt" + sfx] = f(inp["page_table"][g * NS:(g + 1) * NS]).reshape(1, NS * NPG).astype(np.int32)
            m["s_re0" + sfx] = f(inp["state_ssm_re"][0, g * NS:(g + 1) * NS]).reshape(NS, 2048)
            m["s_im0" + sfx] = f(inp["state_ssm_im"][0, g * NS:(g + 1) * NS]).reshape(NS, 2048)
            m["conv0" + sfx] = f(inp["state_ffn_conv"][0, g * NS:(g + 1) * NS]).reshape(NS * 2, DFF)
        maps.append(m)
    return maps


def assemble(results, n_cores, nv=1):
    n = n_cores * nv
    cat = lambda k: np.stack([np.asarray(results[c][k + "_v%d" % vc]) for c in range(n_cores) for vc in range(nv)])
    y_prompt = cat("yp").reshape(n, SEQ, D)
    y_sample = cat("ys").reshape(n * NS, TS, D)
    k_prompt = cat("kp").reshape(1, n, L, 4, 128)
    v_prompt = cat("vp").reshape(1, n, L, 4, 128)
    k_sample = cat("ks").reshape(1, n * NS, TS, 4, 128)
    v_sample = cat("vs").reshape(1, n * NS, TS, 4, 128)
    srp = cat("srp").reshape(1, n, 32, 64)
    sip = cat("sip").reshape(1, n, 32, 64)
    srs = cat("srs").reshape(1, n * NS, 32, 64)
    sis = cat("sis").reshape(1, n * NS, 32, 64)
    cpo = cat("cp").reshape(1, n, 2, DFF)
    cso = cat("cs").reshape(1, n * NS, 2, DFF)
    return tuple(np.ascontiguousarray(a.astype(np.float32)) for a in
                 (y_prompt, y_sample, k_prompt, v_prompt, k_sample, v_sample, srp, sip, srs, sis, cpo, cso))


N_CORES = 4
N_VC = 2


def kernel(**inputs):
    npool = int(np.asarray(inputs["cache_k"]).shape[1])
    nc = build(npool, N_VC)
    maps = make_in_maps(inputs, N_CORES, npool, N_VC)
    res = run_bass_kernel_spmd(nc, maps, core_ids=list(range(N_CORES)))
    return assemble(res.results, N_CORES, N_VC)
```

```python
import contextlib
import math
import numpy as np
import concourse.bass as bass
import concourse.mybir as mybir
from concourse.bass_utils import run_bass_kernel_spmd

F32 = mybir.dt.float32
BF16 = mybir.dt.bfloat16
I32 = mybir.dt.int32
AF = mybir.ActivationFunctionType
ALU = mybir.AluOpType
AX = mybir.AxisListType

D = 1024
SEQ = 2048
NMETA = 16
L = SEQ + NMETA
NT = 17
NCOL = NT * 128
NS = 16
TS = 4
NPG = 16
DFF = 2816
NFC = DFF // 128
EPS = 1e-6
LAM_INIT = 0.8 - 0.6 * math.exp(-0.3 * 0)
Q = 8
NJ = L // Q
ENGS = ("pe", "act", "dve", "pool", "sp")


class Res:
    __slots__ = ("name", "writer", "readers")

    def __init__(self, name=""):
        self.name = name
        self.writer = None
        self.readers = []


class DSem:
    def __init__(self, sem):
        self.sem = sem
        self.val = 0


class Prog:
    def __init__(self, nc):
        self.nc = nc
        self.q = {e: [] for e in ENGS}
        self.sem = {e: nc.alloc_semaphore(name="c_" + e) for e in ENGS}
        self.cnt = {e: 0 for e in ENGS}
        self.seen = {e: {} for e in ENGS}
        self.dsems = []
        self.n_inst = 0
        self.dead = False
        import os as _o
        self.cutv = float(_o.environ.get('DBG_S', 1e9))
        self.serial_same = {"act": True, "dve": True, "pool": True, "pe": False, "sp": False}

    def _deps(self, eng, reads, writes):
        need = {}

        def add(sv):
            if sv is None:
                return
            s, v = sv
            if need.get(s, 0) < v:
                need[s] = v
        for r in reads:
            add(r.writer)
        for w in writes:
            add(w.writer)
            for rd in w.readers:
                add(rd)
        waits = []
        for s, v in need.items():
            if s is self.sem[eng] and not self.serial_same[eng]:
                continue
            if self.seen[eng].get(s, 0) >= v:
                continue
            self.seen[eng][s] = v
            waits.append((s, v))
        return waits

    def _mark(self, reads, writes, sv):
        for r in reads:
            r.readers.append(sv)
            if len(r.readers) > 64:
                best = {}
                for s, v in r.readers:
                    if best.get(s, 0) < v:
                        best[s] = v
                r.readers = list(best.items())
        for w in writes:
            w.writer = sv
            w.readers = []

    def cut(self, x):
        if self.cutv < x:
            self.dead = True

    def op(self, eng, fn, reads=(), writes=()):
        if self.dead:
            return
        waits = self._deps(eng, reads, writes)
        self.cnt[eng] += 1
        sem = self.sem[eng]
        sv = (sem, self.cnt[eng])

        def emit(e, fn=fn, waits=waits, sem=sem):
            for s, v in waits:
                e.wait_ge(s, v)
            fn(e).then_inc(sem, 1)
        self.q[eng].append(emit)
        self._mark(reads, writes, sv)
        self.n_inst += 1

    def dma(self, eng, fn, reads=(), writes=(), dsem=None):
        if self.dead:
            return
        waits = self._deps(eng, reads, writes)
        dsem.val += 16
        sv = (dsem.sem, dsem.val)

        def emit(e, fn=fn, waits=waits, s=dsem.sem):
            for ws, wv in waits:
                e.wait_ge(ws, wv)
            fn(e).then_inc(s, 16)
        self.q[eng].append(emit)
        self._mark(reads, writes, sv)
        self.n_inst += 1

    def new_dsem(self, name):
        d = DSem(self.nc.alloc_semaphore(name=name + "_%d" % len(self.dsems)))
        self.dsems.append(d)
        return d

    def barrier(self):
        if self.dead:
            return
        for e in ENGS:
            waits = []
            for e2 in ENGS:
                if e2 != e and self.cnt[e2] > self.seen[e].get(self.sem[e2], 0):
                    self.seen[e][self.sem[e2]] = self.cnt[e2]
                    waits.append((self.sem[e2], self.cnt[e2]))
            for d in self.dsems:
                if d.val > self.seen[e].get(d.sem, 0):
                    self.seen[e][d.sem] = d.val
                    waits.append((d.sem, d.val))

            def emit(en, waits=waits):
                for s, v in waits:
                    en.wait_ge(s, v)
            self.q[e].append(emit)

    def run(self):
        nc = self.nc
        finals = [(d.sem, d.val) for d in self.dsems if d.val > 0]

        def fin(e):
            for s, v in finals:
                e.wait_ge(s, v)
        self.q["sp"].append(fin)
        with nc.Block() as block:
            @block.tensor
            def _(e):
                for f in self.q["pe"]:
                    f(e)

            @block.scalar
            def _(e):
                for f in self.q["act"]:
                    f(e)

            @block.vector
            def _(e):
                for f in self.q["dve"]:
                    f(e)

            @block.gpsimd
            def _(e):
                for f in self.q["pool"]:
                    f(e)

            @block.sync
            def _(e):
                for f in self.q["sp"]:
                    f(e)


class T:
    def __init__(self, t, name):
        self.t = t
        self.r = Res(name)

    def __getitem__(self, k):
        return self.t[k]


def rawap(t, off, dims):
    return bass.AP(t.t if isinstance(t, T) else t, off, [list(d) for d in dims])


def build(npool, nv=1):
    nc = bass.Bass("TRN2", target_bir_lowering=False)
    P = Prog(nc)

    def din(name, shape, dt=F32):
        return nc.dram_tensor(name, list(shape), dt, kind="ExternalInput").ap()

    def dout(name, shape, dt=F32):
        return nc.dram_tensor(name, list(shape), dt, kind="ExternalOutput").ap()

    meta = din("meta", [NMETA, D])
    ck = din("ck", [npool * 128, 512]); cv = din("cv", [npool * 128, 512])
    norm1 = din("norm1", [1, D]); norm2 = din("norm2", [1, D])
    w_in = din("w_in", [D, 2048]); w_out = din("w_out", [D, D])
    q_norm = din("q_norm", [1, 128]); k_norm = din("k_norm", [1, 128])
    lam_q = din("lam_q", [1, 128]); lam_k = din("lam_k", [1, 128])
    sub_norm = din("sub_norm", [1, 128])
    a_re = din("a_re", [16, 128]); a_im = din("a_im", [16, 128]); log_dt = din("log_dt", [16, 2])
    b_re = din("b_re", [16, 2048]); b_im = din("b_im", [16, 2048])
    c_re = din("c_re", [16, 2048]); c_im = din("c_im", [16, 2048])
    ssm_d = din("ssm_d", [4, 128])
    w_glu = din("w_glu", [512, 512]); b_glu = din("b_glu", [4, 128])
    w_gate = din("w_gate", [D, DFF]); w_up = din("w_up", [D, DFF]); w_down = din("w_down", [DFF, D])
    conv_w = din("conv_w", [3, DFF]); conv_b = din("conv_b", [1, DFF])
    SFX = {"s": ""}

    def emit_vc(vc):
        sfx = "_v%d" % vc
        SFX["s"] = sfx
        xp = din("xp" + sfx, [SEQ, D]); xs = din("xs" + sfx, [NS * TS, D])
        pt = din("pt" + sfx, [1, NS * NPG], I32)
        s_re0 = din("s_re0" + sfx, [NS, 2048]); s_im0 = din("s_im0" + sfx, [NS, 2048])
        conv0 = din("conv0" + sfx, [NS * 2, DFF])
        yp = dout("yp" + sfx, [SEQ, D]); ys = dout("ys" + sfx, [NS * TS, D])
        kp = dout("kp" + sfx, [L, 512]); vp = dout("vp" + sfx, [L, 512])
        ks = dout("ks" + sfx, [NS * TS, 512]); vs = dout("vs" + sfx, [NS * TS, 512])
        srp = dout("srp" + sfx, [16, 128]); sip = dout("sip" + sfx, [16, 128])
        srs = dout("srs" + sfx, [NS, 2048]); sis = dout("sis" + sfx, [NS, 2048])
        cp = dout("cp" + sfx, [2, DFF]); cs = dout("cs" + sfx, [NS * 2, DFF])
        es_all = contextlib.ExitStack()

        def S(es, name, shape, dt):
            return T(es.enter_context(nc.sbuf_tensor(name + SFX["s"], list(shape), dt)), name)

        def PS(es, name, shape, dt):
            return T(es.enter_context(nc.psum_tensor(name + SFX["s"], list(shape), dt)), name)

        dq = {"n": 0}

        def ld(eng, out_ap, in_ap, writes, reads=(), name=None):
            w0 = writes[0] if writes else reads[0]
            if not hasattr(P, "_dmap"):
                P._dmap = {}
            key = id(w0)
            if key not in P._dmap:
                fr = getattr(P, "_free", [])
                P._dmap[key] = fr.pop() if fr else P.new_dsem("d%d" % len(P.dsems))
                P._keep = getattr(P, "_keep", []) + [w0]
            P.dma(eng, lambda e: e.dma_start(out=out_ap, in_=in_ap), reads=reads, writes=writes, dsem=P._dmap[key])


        def TT(eng, out, a, b, op, rd, wr):
            P.op(eng, lambda e: e.tensor_tensor(out=out, in0=a, in1=b, op=op), reads=rd, writes=wr)

        def TSC(eng, out, a, s1, s2, op0, op1, rd, wr):
            if s2 is None:
                P.op(eng, lambda e: e.tensor_scalar(out=out, in0=a, scalar1=s1, scalar2=None, op0=op0), reads=rd, writes=wr)
            else:
                P.op(eng, lambda e: e.tensor_scalar(out=out, in0=a, scalar1=s1, scalar2=s2, op0=op0, op1=op1), reads=rd, writes=wr)

        def STT(out, a, sc, b, op0, op1, rd, wr):
            P.op("dve", lambda e: e.scalar_tensor_tensor(out=out, in0=a, scalar=sc, in1=b, op0=op0, op1=op1), reads=rd, writes=wr)

        def ACT(out, in_, func, rd, wr, bias=None, scale=None):
            kw = {}
            if bias is not None:
                kw["bias"] = bias
            if scale is not None:
                kw["scale"] = scale
            P.op("act", lambda e: e.activation(out=out, in_=in_, func=func, **kw), reads=rd, writes=wr)

        def CP(eng, out, in_, rd, wr):
            if eng == "act":
                P.op("act", lambda e: e.copy(out=out, in_=in_), reads=rd, writes=wr)
            else:
                P.op(eng, lambda e: e.tensor_copy(out=out, in_=in_), reads=rd, writes=wr)

        def MM(out, lhsT, rhs, start, stop, rd, wr, tp=None):
            if tp is None:
                P.op("pe", lambda e: e.matmul(out, lhsT=lhsT, rhs=rhs, start=start, stop=stop), reads=rd, writes=wr)
            else:
                P.op("pe", lambda e: e.matmul(out, lhsT=lhsT, rhs=rhs, start=start, stop=stop, tile_position=tp), reads=rd, writes=wr)

        def TR(out, in_, ident, rd, wr):
            P.op("pe", lambda e: e.transpose(out=out, in_=in_, identity=ident), reads=rd, writes=wr)

        def MS(eng, out, val, wr):
            P.op(eng, lambda e: e.memset(out, val), writes=wr)

        with es_all:
            ident_f = S(es_all, "ident_f", [128, 128], F32)
            ident_b = S(es_all, "ident_b", [128, 128], BF16)
            ones_b = S(es_all, "ones_b", [128, 128], BF16)
            P.op("pool", lambda e: e.memset(ident_f[:], 1.0), writes=[ident_f.r])
            P.op("pool", lambda e: e.affine_select(out=ident_f[:], in_=ident_f[:], pattern=[[-1, 128]],
                                                   compare_op=ALU.is_equal, fill=0.0, base=0, channel_multiplier=1),
                 reads=[ident_f.r], writes=[ident_f.r])
            P.op("pool", lambda e: e.tensor_copy(out=ident_b[:], in_=ident_f[:]), reads=[ident_f.r], writes=[ident_b.r])
            P.op("pool", lambda e: e.memset(ones_b[:], 1.0), writes=[ones_b.r])

            mixT = S(es_all, "mixT", [128, 8, NCOL], BF16)
            mix_r = [Res("mix%d" % i) for i in range(8)]
            P.op("pool", lambda e: e.memset(mixT[:], 0.0), writes=mix_r)
            lamt = S(es_all, "lamt", [128, 8], F32)
            sgp = S(es_all, "sgp", [128, 2], F32)
            maskT = S(es_all, "maskT", [128, 128], BF16)
            ones_f = S(es_all, "ones_f", [128, 128], F32)
            es_U = contextlib.ExitStack()
            es_all.enter_context(es_U)
            uT = S(es_U, "uT", [128, 4, L + NS * TS], BF16)
            es_B = contextlib.ExitStack()
            es_all.enter_context(es_B)
            qT = S(es_B, "qT", [128, 4, NCOL], BF16)
            kT = S(es_B, "kT", [128, 4, NCOL], BF16)
            vb = S(es_B, "vb", [128, NT, 512], BF16)

            es_A = contextlib.ExitStack()
            with es_A:
                g1b = S(es_A, "g1b", [128, D], F32)
                gqb = S(es_A, "gqb", [128, 512], F32)
                gkb = S(es_A, "gkb", [128, 512], F32)
                wi = S(es_A, "wi", [128, 8, 2048], BF16)
                ld("sp", g1b[:], rawap(norm1.tensor, 0, [[0, 128], [1, D]]), [g1b.r])
                ld("sp", gqb[:].rearrange("p (h e) -> p h e", h=4), rawap(q_norm.tensor, 0, [[0, 128], [0, 4], [1, 128]]), [gqb.r])
                ld("sp", gkb[:].rearrange("p (h e) -> p h e", h=4), rawap(k_norm.tensor, 0, [[0, 128], [0, 4], [1, 128]]), [gkb.r])
                P.op("act", lambda e: e.mul(out=gqb[:], in_=gqb[:], mul=0.125), reads=[gqb.r], writes=[gqb.r])
                wi_r = [Res("wi%d" % k) for k in range(8)]
                w_in_v = w_in.rearrange("(kc p) n -> p kc n", p=128)
                for kc in range(8):
                    ld("pool", wi[:, kc, :], w_in_v[:, kc, :], [wi_r[kc]])

                NB = 2
                xt = [S(es_A, "xt%d" % i, [128, D], F32) for i in range(NB)]
                junk = S(es_A, "junk", [128, D], BF16)
                ss = [S(es_A, "ss%d" % i, [128, 4], F32) for i in range(NB)]
                xn = [S(es_A, "xn%d" % i, [128, D], BF16) for i in range(NB)]
                xnT = [S(es_A, "xnT%d" % i, [128, 8, 128], BF16) for i in range(NB)]
                sqb = [S(es_A, "sqb%d" % i, [128, 1024], F32) for i in range(NB)]
                s8 = [S(es_A, "s8%d" % i, [128, 16 * 3], F32) for i in range(NB)]
                qk32 = [S(es_A, "qk32%d" % i, [128, 1024], F32) for i in range(NB)]
                k32 = [S(es_A, "k32%d" % i, [128, 512], F32) for i in range(NB)]
                qkb = [S(es_A, "qkb%d" % i, [128, 1024], BF16) for i in range(NB)]
                v32 = [S(es_A, "v32%d" % i, [128, 512], F32) for i in range(NB)]
                pT = [PS(es_A, "pT%d" % i, [128, 8, 128], BF16) for i in range(2)]
                ps_qk = PS(es_A, "ps_qk", [128, 1024], F32)
                ps_v = PS(es_A, "ps_v", [128, 512], F32)
                ps_u = PS(es_A, "ps_u", [128, 4, 128], F32)
                pQK = PS(es_A, "pQK", [128, 8, 128], BF16)
                qT_r = [Res("qT%d" % i) for i in range(NT)]
                kT_r = [Res("kT%d" % i) for i in range(NT)]
                vb_r = [Res("vb%d" % i) for i in range(NT)]
                uT_r = [Res("uT%d" % i) for i in range(NT)]

                import os as _os
                _NTR = int(_os.environ.get('DBG_NT', NT)); _CUT = float(_os.environ.get('DBG_CUT', 99))
                for tt in range(_NTR):
                    b = tt % NB
                    X = xt[b]
                    if tt == 0:
                        P.op("pool", lambda e, X=X: e.memset(X[:], 0.0), writes=[X.r])
                        ld("sp", X[0:16, :], meta, [X.r], name="x0a")
                        r2 = Res("x0b")
                        ld("sp", X[32:96, :], xs, [r2], reads=[X.r])
                        xr = [X.r, r2]
                    else:
                        ld("sp", X[:], xp[(tt - 1) * 128:tt * 128, :], [X.r])
                        xr = [X.r]
                    P.op("act", lambda e, X=X, b=b: e.activation(out=junk[:], in_=X[:], func=AF.Square, accum_out=ss[b][:, 0:1]),
                         reads=xr, writes=[junk.r, ss[b].r])
                    P.op("dve", lambda e, b=b: e.tensor_scalar(out=ss[b][:, 1:2], in0=ss[b][:, 0:1], scalar1=1.0 / D, scalar2=EPS,
                                                               op0=ALU.mult, op1=ALU.add), reads=[ss[b].r], writes=[ss[b].r])
                    P.op("act", lambda e, b=b: e.sqrt(out=ss[b][:, 2:3], in_=ss[b][:, 1:2]), reads=[ss[b].r], writes=[ss[b].r])
                    P.op("dve", lambda e, b=b: e.reciprocal(out=ss[b][:, 3:4], in_=ss[b][:, 2:3]), reads=[ss[b].r], writes=[ss[b].r])
                    P.op("dve", lambda e, X=X, b=b: e.scalar_tensor_tensor(out=xn[b][:], in0=X[:], scalar=ss[b][:, 3:4], in1=g1b[:],
                                                                          op0=ALU.mult, op1=ALU.mult),
                         reads=xr + [ss[b].r, g1b.r], writes=[xn[b].r])
                    if _CUT < 1:
                        continue
                    pt_ = pT[tt % 2]
                    for kc in range(8):
                        P.op("pe", lambda e, b=b, kc=kc, pt_=pt_: e.transpose(out=pt_[:, kc, :], in_=xn[b][:, kc * 128:(kc + 1) * 128],
                                                                              identity=ident_b[:]),
                             reads=[xn[b].r, ident_b.r], writes=[pt_.r])
                    P.op("act", lambda e, b=b, pt_=pt_: e.copy(out=xnT[b][:], in_=pt_[:]), reads=[pt_.r], writes=[xnT[b].r])
                    if _CUT < 2:
                        continue
                    for nb in range(2):
                        for kc in range(8):
                            P.op("pe", lambda e, b=b, kc=kc, nb=nb: e.matmul(ps_qk[:, nb * 512:(nb + 1) * 512], lhsT=xnT[b][:, kc, :],
                                                                              rhs=wi[:, kc, nb * 512:(nb + 1) * 512],
                                                                              start=(kc == 0), stop=(kc == 7)),
                                 reads=[xnT[b].r, wi_r[kc]], writes=[ps_qk.r])
                    for kc in range(8):
                        P.op("pe", lambda e, b=b, kc=kc: e.matmul(ps_v[:], lhsT=xnT[b][:, kc, :], rhs=wi[:, kc, 1024:1536],
                                                                  start=(kc == 0), stop=(kc == 7)),
                             reads=[xnT[b].r, wi_r[kc]], writes=[ps_v.r])
                    for c in range(4):
                        for kc in range(8):
                            P.op("pe", lambda e, b=b, kc=kc, c=c: e.matmul(ps_u[:, c, :], lhsT=wi[:, kc, 1536 + c * 128:1536 + (c + 1) * 128],
                                                                            rhs=xnT[b][:, kc, :], start=(kc == 0), stop=(kc == 7)),
                                 reads=[xnT[b].r, wi_r[kc]], writes=[ps_u.r])
                    if _CUT < 3:
                        continue
                    for hf in range(2):
                        sl = slice(hf * 512, (hf + 1) * 512)
                        P.op("act", lambda e, b=b, sl=sl: e.activation(out=sqb[b][:, sl], in_=ps_qk[:, sl], func=AF.Square),
                             reads=[ps_qk.r, sqb[b].r], writes=[sqb[b].r])
                    if _CUT < 3.1:
                        continue
                    P.op("dve", lambda e, b=b: e.tensor_reduce(out=s8[b][:, 0:16], in_=sqb[b][:].rearrange("p (c d) -> p c d", d=64),
                                                               axis=AX.X, op=ALU.add), reads=[sqb[b].r], writes=[s8[b].r])
                    P.op("dve", lambda e, b=b: e.tensor_scalar(out=s8[b][:, 16:32], in0=s8[b][:, 0:16], scalar1=1.0 / 64, scalar2=EPS,
                                                               op0=ALU.mult, op1=ALU.add), reads=[s8[b].r], writes=[s8[b].r])
                    P.op("act", lambda e, b=b: e.sqrt(out=s8[b][:, 32:48], in_=s8[b][:, 16:32]), reads=[s8[b].r], writes=[s8[b].r])
                    P.op("dve", lambda e, b=b: e.reciprocal(out=s8[b][:, 0:16], in_=s8[b][:, 32:48]), reads=[s8[b].r], writes=[s8[b].r])
                    if _CUT < 3.2:
                        continue
                    for hf in range(2):
                        sl = slice(hf * 512, (hf + 1) * 512)
                        P.op("dve", lambda e, b=b, sl=sl, hf=hf: e.tensor_tensor(
                            out=qk32[b][:, sl].rearrange("p (c d) -> p c d", d=64),
                            in0=ps_qk[:, sl].rearrange("p (c d) -> p c d", d=64),
                            in1=s8[b][:, hf * 8:hf * 8 + 8].unsqueeze(2).to_broadcast([128, 8, 64]), op=ALU.mult),
                            reads=[ps_qk.r, s8[b].r, qk32[b].r], writes=[qk32[b].r])
                    if _CUT < 3.3:
                        continue
                    P.op("pool", lambda e, b=b: e.tensor_tensor(out=qkb[b][:, 0:512], in0=qk32[b][:, 0:512], in1=gqb[:], op=ALU.mult),
                         reads=[qk32[b].r, gqb.r], writes=[qkb[b].r])
                    P.op("dve", lambda e, b=b: e.tensor_tensor(out=k32[b][:], in0=qk32[b][:, 512:1024], in1=gkb[:], op=ALU.mult),
                         reads=[qk32[b].r, gkb.r], writes=[k32[b].r])
                    rkb = Res("kb")
                    P.op("pool", lambda e, b=b: e.tensor_copy(out=qkb[b][:, 512:1024], in_=k32[b][:]), reads=[k32[b].r, qkb[b].r], writes=[qkb[b].r])
                    if _CUT < 3.4:
                        continue
                    for h in range(8):
                        P.op("pe", lambda e, b=b, h=h: e.transpose(out=pQK[:, h, :], in_=qkb[b][:, h * 128:(h + 1) * 128], identity=ident_b[:]),
                             reads=[qkb[b].r, ident_b.r], writes=[pQK.r])
                    if _CUT < 3.5:
                        continue
                    c0 = tt * 128
                    if _CUT >= 3.6:
                        P.op("act", lambda e, c0=c0: e.copy(out=qT[:, :, c0:c0 + 128], in_=pQK[:, 0:4, :]), reads=[pQK.r], writes=[qT_r[tt]])
                    if _CUT >= 3.7:
                        P.op("act", lambda e, c0=c0: e.copy(out=kT[:, :, c0:c0 + 128], in_=pQK[:, 4:8, :]), reads=[pQK.r], writes=[kT_r[tt]])
                    if _CUT < 4:
                        continue
                    P.op("act", lambda e, b=b: e.copy(out=v32[b][:], in_=ps_v[:]), reads=[ps_v.r], writes=[v32[b].r])
                    P.op("pool", lambda e, b=b, tt=tt: e.tensor_copy(out=vb[:, tt, :], in_=v32[b][:]), reads=[v32[b].r], writes=[vb_r[tt]])
                    if tt == 0:
                        P.op("dve", lambda e: e.tensor_copy(out=uT[:, :, 0:16], in_=ps_u[:, :, 0:16]), reads=[ps_u.r], writes=[uT_r[0]])
                        P.op("dve", lambda e: e.tensor_copy(out=uT[:, :, L:L + 64], in_=ps_u[:, :, 32:96]), reads=[ps_u.r, uT_r[0]], writes=[uT_r[0]])
                    else:
                        o0 = 16 + (tt - 1) * 128
                        P.op("act", lambda e, o0=o0: e.copy(out=uT[:, :, o0:o0 + 128], in_=ps_u[:]), reads=[ps_u.r], writes=[uT_r[tt]])
                    if tt == 0:
                        ld("sp", kp[0:16, :], k32[b][0:16, :], [], reads=[k32[b].r])
                        ld("sp", ks[:, :], k32[b][32:96, :], [], reads=[k32[b].r])
                        ld("sp", vp[0:16, :], v32[b][0:16, :], [], reads=[v32[b].r])
                        ld("sp", vs[:, :], v32[b][32:96, :], [], reads=[v32[b].r])
                    else:
                        o0 = 16 + (tt - 1) * 128
                        ld("sp", kp[o0:o0 + 128, :], k32[b][:], [], reads=[k32[b].r])
                        ld("sp", vp[o0:o0 + 128, :], v32[b][:], [], reads=[v32[b].r])
                P.barrier()
            P.cut(100)
            es_L2 = contextlib.ExitStack()
            with es_L2:
                lq = S(es_L2, "lq", [128, 2, 128], F32)
                ld("sp", lq[:, 0, :], rawap(lam_q.tensor, 0, [[0, 128], [1, 128]]), [lq.r])
                ld("sp", lq[:, 1, :], rawap(lam_k.tensor, 0, [[0, 128], [1, 128]]), [lq.r])
                TT("dve", lq[:, 0, :], lq[:, 0, :], lq[:, 1, :], ALU.mult, [lq.r], [lq.r])
                P.op("dve", lambda e: e.tensor_reduce(out=lamt[:, 0:2], in_=lq[:, 0, :].rearrange("p (c d) -> p c d", c=2), axis=AX.X, op=ALU.add),
                     reads=[lq.r], writes=[lamt.r])
                ACT(lamt[:, 2:4], lamt[:, 0:2], AF.Exp, [lamt.r], [lamt.r])
                TT("dve", lamt[:, 5:6], lamt[:, 2:3], lamt[:, 3:4], ALU.subtract, [lamt.r], [lamt.r])
                TSC("dve", lamt[:, 5:6], lamt[:, 5:6], LAM_INIT, None, ALU.add, None, [lamt.r], [lamt.r])
                TSC("dve", lamt[:, 4:5], lamt[:, 5:6], -1.0, None, ALU.mult, None, [lamt.r], [lamt.r])
                ld("sp", sgp[:, 0:1], rawap(sub_norm.tensor, 0, [[1, 128], [1, 1]]), [sgp.r])
                TSC("dve", sgp[:, 1:2], sgp[:, 0:1], 1.0 - LAM_INIT, None, ALU.mult, None, [sgp.r], [sgp.r])
                MS("pool", maskT[:], 1.0, [maskT.r])
                P.op("pool", lambda e: e.affine_select(out=maskT[:], in_=maskT[:], pattern=[[1, 128]], compare_op=ALU.is_ge, fill=0.0,
                                                       base=0, channel_multiplier=-1), reads=[maskT.r], writes=[maskT.r])
                MS("pool", ones_f[:], 1.0, [ones_f.r])
                P.barrier()
            NLAM = lamt[:, 4:5]
            es_At = contextlib.ExitStack()
            with es_At:
                sA = [PS(es_At, "sA%d" % i, [128, 512], F32) for i in range(2)]
                sB = [PS(es_At, "sB%d" % i, [128, 512], F32) for i in range(2)]
                po = [PS(es_At, "po%d" % i, [128, 512], F32) for i in range(2)]
                pss = [PS(es_At, "pss%d" % i, [128, 512], F32) for i in range(2)]
                ptb = [[S(es_At, "ptb%d_%d" % (c, i), [128, 512], BF16) for i in range(2)] for c in range(2)]
                wa = [S(es_At, "wa%d" % i, [128, 512], F32) for i in range(4)]
                kv_all = list(kT_r) + list(qT_r) + list(vb_r)
                groups = [(0, 16, [0])] + [(128 * (4 * g + 1), 512, list(range(0, 4 * g + 5))) for g in range(4)]
                it = 0
                for (q0, NQ, kts) in groups:
                    for h in range(4):
                        for kt in kts:
                            if NQ == 16:
                                off, N, diag = 0, 16, True
                            elif kt * 128 < q0:
                                off, N, diag = 0, 512, False
                            else:
                                off = kt * 128 - q0
                                N, diag = 512 - off, True
                            b = it % 2
                            it += 1
                            sb = (sA[b], sB[b])
                            for c in range(2):
                                MM(sb[c][:, 0:N], kT[64 * c:64 * c + 64, h, kt * 128:(kt + 1) * 128], qT[64 * c:64 * c + 64, h, q0 + off:q0 + off + N],
                                   True, True, kv_all, [sb[c].r])
                            for c in range(2):
                                pt_ = ptb[c][b]
                                ACT(pt_[:, 0:N], sb[c][:, 0:N], AF.Exp, [sb[c].r], [pt_.r])
                                if diag:
                                    nd = min(N, 128)
                                    TT("dve" if c == 0 else "pool", pt_[:, 0:nd], pt_[:, 0:nd], maskT[:, 0:nd], ALU.mult, [pt_.r, maskT.r], [pt_.r])
                                elif kt == 0:
                                    TSC("dve" if c == 0 else "pool", pt_[:, 0:N], pt_[:, 0:N], maskT[:, 15:16], None, ALU.mult, None,
                                        [pt_.r, maskT.r], [pt_.r])
                            first, last = (kt == kts[0]), (kt == kts[-1])
                            for c in range(2):
                                pt_ = ptb[c][b]
                                MM(po[c][:, off:off + N], vb[:, kt, h * 128:(h + 1) * 128], pt_[:, 0:N], first, last, [pt_.r] + kv_all, [po[c].r])
                                MM(pss[c][:, off:off + N], ones_b[:], pt_[:, 0:N], first, last, [pt_.r, ones_b.r], [pss[c].r])
                        W0, W1, W2, W3 = wa
                        P.op("dve", lambda e, W0=W0, NQ=NQ: e.reciprocal(out=W0[:, 0:NQ], in_=pss[0][:, 0:NQ]), reads=[pss[0].r, W0.r], writes=[W0.r])
                        TT("dve", W0[:, 0:NQ], po[0][:, 0:NQ], W0[:, 0:NQ], ALU.mult, [po[0].r, W0.r], [W0.r])
                        P.op("dve", lambda e, W1=W1, NQ=NQ: e.reciprocal(out=W1[:, 0:NQ], in_=pss[1][:, 0:NQ]), reads=[pss[1].r, W1.r], writes=[W1.r])
                        TT("dve", W1[:, 0:NQ], po[1][:, 0:NQ], W1[:, 0:NQ], ALU.mult, [po[1].r, W1.r], [W1.r])
                        STT(W2[:, 0:NQ], W1[:, 0:NQ], NLAM, W0[:, 0:NQ], ALU.mult, ALU.add, [W0.r, W1.r, lamt.r, W2.r], [W2.r])
                        ACT(W3[:, 0:NQ], W2[:, 0:NQ], AF.Square, [W2.r, W3.r], [W3.r])
                        bq = it % 2
                        P.op("pe", lambda e, W3=W3, NQ=NQ, bq=bq: e.matmul(sA[bq][:, 0:NQ], lhsT=ones_f[:], rhs=W3[:, 0:NQ], start=True, stop=True),
                             reads=[W3.r, ones_f.r], writes=[sA[bq].r])
                        TSC("dve", W0[:, 0:NQ], sA[bq][:, 0:NQ], 1.0 / 128, EPS, ALU.mult, ALU.add, [sA[bq].r, W0.r], [W0.r])
                        P.op("act", lambda e, W0=W0, NQ=NQ: e.sqrt(out=W0[:, 0:NQ], in_=W0[:, 0:NQ]), reads=[W0.r], writes=[W0.r])
                        P.op("dve", lambda e, W0=W0, NQ=NQ: e.reciprocal(out=W0[:, 0:NQ], in_=W0[:, 0:NQ]), reads=[W0.r], writes=[W0.r])
                        STT(mixT[:, h, q0:q0 + NQ], W2[:, 0:NQ], sgp[:, 1:2], W0[:, 0:NQ], ALU.mult, ALU.mult, [W2.r, W0.r, sgp.r], [mix_r[h]])
                P.barrier()
            P.cut(200)
            es_Q = contextlib.ExitStack()
            with es_Q:
                ptbc = S(es_Q, "ptbc", [128, NS * NPG], I32)
                iop = S(es_Q, "iop", [128, 1], I32)
                idx = S(es_Q, "idx", [128, NS * NPG], I32)
                ld("sp", ptbc[:], rawap(pt.tensor, 0, [[0, 128], [1, NS * NPG]]), [ptbc.r])
                P.op("pool", lambda e: e.iota(iop[:], pattern=[[0, 1]], base=0, channel_multiplier=1), writes=[iop.r])
                TSC("dve", idx[:], ptbc[:], 128, None, ALU.mult, None, [ptbc.r], [idx.r])
                TSC("dve", idx[:], idx[:], iop[:, 0:1], None, ALU.add, None, [idx.r, iop.r], [idx.r])
                Qblk = S(es_Q, "Qblk", [128, 4, 2, NS * TS], BF16)
                MS("pool", Qblk[:], 0.0, [Qblk.r])
                CP("act", Qblk[0:64, :, 0, :], qT[0:64, :, 32:96], [Qblk.r] + list(qT_r), [Qblk.r])
                CP("act", Qblk[64:128, :, 1, :], qT[64:128, :, 32:96], [Qblk.r] + list(qT_r), [Qblk.r])
                vnew = S(es_Q, "vnew", [128, NS, 512], BF16)
                MS("pool", vnew[:], 0.0, [vnew.r])
                pnew = S(es_Q, "pnew", [128, 32], BF16)
                MS("pool", pnew[:], 0.0, [pnew.r])
                msk4 = S(es_Q, "msk4", [4, 32], F32)
                MS("pool", msk4[:], 1.0, [msk4.r])
                P.op("pool", lambda e: e.affine_select(out=msk4[:], in_=msk4[:], pattern=[[0, 2], [0, 4], [1, 4]], compare_op=ALU.is_ge, fill=0.0,
                                                       base=0, channel_multiplier=-1), reads=[msk4.r], writes=[msk4.r])
                e4 = S(es_Q, "e4", [4, 32], F32)
                att = S(es_Q, "att", [NS * TS, 512], F32)
                kpg = [S(es_Q, "kpg%d" % i, [128, 8, 512], F32) for i in range(2)]
                vpg = [S(es_Q, "vpg%d" % i, [128, 8, 512], BF16) for i in range(3)]
                kTs = [S(es_Q, "kTs%d" % i, [128, 4, 128], BF16) for i in range(3)]
                pexp = [S(es_Q, "pexp%d" % i, [128, 512], BF16) for i in range(2)]
                rs = [S(es_Q, "rs%d" % i, [16, 4], F32) for i in range(2)]
                t0 = [S(es_Q, "t0_%d" % i, [16, 512], F32) for i in range(2)]
                o16 = [S(es_Q, "o16_%d" % i, [16, 512], F32) for i in range(2)]
                kd = [P.new_dsem("kd%d" % i) for i in range(2)]
                vd = [P.new_dsem("vd%d" % i) for i in range(3)]
                es_Q1 = contextlib.ExitStack()
                with es_Q1:
                    pk = [PS(es_Q1, "pk%d" % i, [128, 512], F32) for i in range(2)]
                    pS = [PS(es_Q1, "pS%d" % i, [128, 512], F32) for i in range(2)]
                    pSn = PS(es_Q1, "pSn", [128, 512], F32)
                    poc = [PS(es_Q1, "poc%d" % i, [128, 512], F32) for i in range(2)]
                    psm = PS(es_Q1, "psm", [128, 512], F32)
                    for s in range(NS):
                        MM(pk[0][0:4, :], ident_b[:, 32 + 4 * s:36 + 4 * s], vb[:, 0, :], True, True, [ident_b.r] + list(vb_r), [pk[0].r])
                        CP("act", vnew[0:4, s, :], pk[0][0:4, :], [pk[0].r, vnew.r], [vnew.r])

                    def gather(dst, sem, src, cols):
                        if P.dead:
                            return
                        waits = P._deps("pool", [idx.r], [dst.r])
                        fns = []
                        for n, col in enumerate(cols):
                            fns.append((n, col))
                        sem.val += 16 * len(cols)

                        def emit(e, waits=waits, fns=fns, dst=dst, sem=sem, src=src):
                            for ws, wv in waits:
                                e.wait_ge(ws, wv)
                            for n, col in fns:
                                e.indirect_dma_start(out=dst[:, n, :], out_offset=None, in_=src,
                                                     in_offset=bass.IndirectOffsetOnAxis(ap=idx[:, col:col + 1], axis=0)).then_inc(sem.sem, 16)
                        P.q["pool"].append(emit)
                        P._mark([idx.r], [dst.r], (sem.sem, sem.val))

                    gi_ = 0
                    kn = 0
                    for s in range(NS):
                        ps_ = pS[s % 2]
                        vbufs = []
                        for hf in range(2):
                            kb = kpg[gi_ % 2]
                            vbf = vpg[gi_ % 3]
                            cols = [s * NPG + hf * 8 + n for n in range(8)]
                            gather(kb, kd[gi_ % 2], ck, cols)
                            gather(vbf, vd[gi_ % 3], cv, cols)
                            gi_ += 1
                            vbufs.append(vbf)
                            for n in range(8):
                                pk_ = pk[kn % 2]
                                kt_ = kTs[kn % 3]
                                for h in range(4):
                                    TR(pk_[:, h * 128:(h + 1) * 128], kb[:, n, h * 128:(h + 1) * 128], ident_f[:], [kb.r, ident_f.r], [pk_.r])
                                CP("act" if kn % 2 == 0 else "dve", kt_[:].rearrange("p h k -> p (h k)"), pk_[:], [pk_.r], [kt_.r])
                                kn += 1
                                base = (hf * 8 + n) * 32
                                for h in range(4):
                                    for c in range(2):
                                        o0 = base + c * 16 + h * 4
                                        MM(ps_[:, o0:o0 + 4], kt_[:, h, :], Qblk[:, h, c, 4 * s:4 * s + 4], True, True, [kt_.r, Qblk.r], [ps_.r])
                        for h in range(4):
                            for c in range(2):
                                o0 = c * 16 + h * 4
                                MM(pSn[0:4, o0:o0 + 4], kT[:, h, 32 + 4 * s:36 + 4 * s], Qblk[:, h, c, 4 * s:4 * s + 4],
                                   True, True, [Qblk.r] + list(kT_r), [pSn.r])
                        ACT(e4[:], pSn[0:4, 0:32], AF.Exp, [pSn.r, e4.r], [e4.r])
                        TT("dve", pnew[0:4, :], e4[:], msk4[:], ALU.mult, [e4.r, msk4.r, pnew.r], [pnew.r])
                        px = pexp[s % 2]
                        ACT(px[:], ps_[:], AF.Exp, [ps_.r], [px.r])
                        for c in range(2):
                            for n in range(16):
                                vbf = vbufs[n // 8]
                                lh = px[:, n * 32 + c * 16:n * 32 + c * 16 + 16]
                                MM(poc[c][0:16, :], lh, vbf[:, n % 8, :], n == 0, False, [px.r, vbf.r], [poc[c].r])
                                MM(psm[0:16, c:c + 1], lh, ones_b[:, 0:1], n == 0, False, [px.r, ones_b.r], [psm.r])
                            lh = pnew[:, c * 16:(c + 1) * 16]
                            MM(poc[c][0:16, :], lh, vnew[:, s, :], False, True, [pnew.r, vnew.r], [poc[c].r])
                            MM(psm[0:16, c:c + 1], lh, ones_b[:, 0:1], False, True, [pnew.r, ones_b.r], [psm.r])
                        r_, t_, o_ = rs[s % 2], t0[s % 2], o16[s % 2]
                        P.op("dve", lambda e, r_=r_: e.reciprocal(out=r_[:, 0:2], in_=psm[0:16, 0:2]), reads=[psm.r, r_.r], writes=[r_.r])
                        TT("dve", r_[:, 2:3], r_[:, 1:2], lamt[0:16, 4:5], ALU.mult, [r_.r, lamt.r], [r_.r])
                        TSC("dve", t_[:], poc[0][0:16, :], r_[:, 0:1], None, ALU.mult, None, [poc[0].r, r_.r, t_.r], [t_.r])
                        STT(o_[:], poc[1][0:16, :], r_[:, 2:3], t_[:], ALU.mult, ALU.add, [poc[1].r, r_.r, t_.r, o_.r], [o_.r])
                        for h in range(4):
                            ld("sp", att[4 * s:4 * s + 4, h * 128:(h + 1) * 128], o_[4 * h:4 * h + 4, h * 128:(h + 1) * 128], [], reads=[o_.r])
                    P.barrier()
                es_Q2 = contextlib.ExitStack()
                with es_Q2:
                    NQ4 = NS * TS
                    sq4 = S(es_Q2, "sq4", [NQ4, 4, 128], F32)
                    s4 = S(es_Q2, "s4", [NQ4, 3, 4], F32)
                    sg4 = S(es_Q2, "sg4", [NQ4, 128], F32)
                    attb = S(es_Q2, "attb", [NQ4, 512], BF16)
                    pT4 = PS(es_Q2, "pT4", [128, 4, NQ4], BF16)
                    ld("sp", sg4[:], rawap(sub_norm.tensor, 0, [[0, NQ4], [1, 128]]), [sg4.r])
                    TSC("dve", sg4[:], sg4[:], 1.0 - LAM_INIT, None, ALU.mult, None, [sg4.r], [sg4.r])
                    attv = att[:].rearrange("p (h e) -> p h e", h=4)
                    ACT(sq4[:], attv, AF.Square, [att.r], [sq4.r])
                    P.op("dve", lambda e: e.tensor_reduce(out=s4[:, 0, :], in_=sq4[:], axis=AX.X, op=ALU.add), reads=[sq4.r], writes=[s4.r])
                    TSC("dve", s4[:, 1, :], s4[:, 0, :], 1.0 / 128, EPS, ALU.mult, ALU.add, [s4.r], [s4.r])
                    P.op("act", lambda e: e.sqrt(out=s4[:, 2, :], in_=s4[:, 1, :]), reads=[s4.r], writes=[s4.r])
                    P.op("dve", lambda e: e.reciprocal(out=s4[:, 0, :], in_=s4[:, 2, :]), reads=[s4.r], writes=[s4.r])
                    TT("dve", sq4[:], attv, s4[:, 0, :].unsqueeze(2).to_broadcast([NQ4, 4, 128]), ALU.mult, [att.r, s4.r, sq4.r], [sq4.r])
                    TT("dve", attb[:].rearrange("p (h e) -> p h e", h=4), sq4[:], sg4[:].unsqueeze(1).to_broadcast([NQ4, 4, 128]), ALU.mult,
                       [sq4.r, sg4.r], [attb.r])
                    for h in range(4):
                        TR(pT4[:, h, :], attb[:, h * 128:(h + 1) * 128], ident_b[0:NQ4, 0:NQ4], [attb.r, ident_b.r], [pT4.r])
                    CP("act", mixT[:, 0:4, 32:96], pT4[:], [pT4.r], mix_r[0:4])
                    P.barrier()
            P.cut(50)
            es_B.close()
            TC = 258
            NM = L // TC
            PI = math.pi
            es_S = contextlib.ExitStack()
            with es_S:
                prm = S(es_S, "prm", [128, 3, 16], F32)
                sc = S(es_S, "sc", [128, 24, 16], F32)
                sci = S(es_S, "sci", [128, 16], I32)
                BT = S(es_S, "BT", [128, 4, 2, 128], BF16)
                CT = S(es_S, "CT", [128, 16, 2, 128], BF16)
                Dp = S(es_S, "Dp", [128, 8], F32)
                wg = S(es_S, "wg", [128, 4, 512], BF16)
                TCs = S(es_S, "TCs", [128, 16, TC], F32)
                TSn = S(es_S, "TSn", [128, 16, TC], F32)
                rq = S(es_S, "rq", [128, 16], F32)
                y32 = S(es_S, "y32", [128, TC], F32)
                g32 = [S(es_S, "g32_%d" % i, [128, TC], F32) for i in range(4)]
                gb = [S(es_S, "gb_%d" % i, [128, TC], BF16) for i in range(4)]
                sg = S(es_S, "sg", [128, TC], F32)
                es_P = contextlib.ExitStack()
                es_P.__enter__()
                pa = S(es_P, "pa", [16, 3, 128], F32)
                ldt = S(es_P, "ldt", [16, 2], F32)
                ld("sp", pa[:, 0, :], a_re, [pa.r])
                ld("sp", pa[:, 1, :], a_im, [pa.r])
                ld("sp", ldt[:], log_dt, [ldt.r])
                CP("dve", pa[:, 2, :].rearrange("p (g q) -> p g q", g=2), ldt[:].unsqueeze(2).to_broadcast([16, 2, 64]), [ldt.r, pa.r], [pa.r])
                ps0 = PS(es_S, "ps0", [128, 512], F32)
                for j in range(3):
                    TR(ps0[:, j * 16:(j + 1) * 16], pa[:, j, :], ident_f[0:16, 0:16], [pa.r, ident_f.r], [ps0.r])
                CP("act", prm[:].rearrange("p a b -> p (a b)"), ps0[:, 0:48], [ps0.r], [prm.r])
                P.cut(1)
                R = lambda k: sc[:, k, :]
                are, aim, ldtp = prm[:, 0, :], prm[:, 1, :], prm[:, 2, :]
                scr = [sc.r, prm.r]
                ACT(R(0), ldtp, AF.Exp, scr, [sc.r])
                TT("dve", R(1), are, R(0), ALU.mult, scr, [sc.r])
                ACT(R(2), R(1), AF.Exp, scr, [sc.r])
                TT("dve", R(3), aim, R(0), ALU.mult, scr, [sc.r])

                def sincos(x, s_out, c_out, t1, t2, ti, rd, wr):
                    C1 = 6.28125
                    C2 = 2 * PI - C1
                    TSC("dve", t1, x, 1.0 / (2 * PI), None, ALU.mult, None, rd, wr)
                    CP("dve", ti, t1, rd, wr)
                    CP("dve", t1, ti, rd, wr)
                    STT(t2, t1, -C1, x, ALU.mult, ALU.add, rd, wr)
                    STT(t2, t1, -C2, t2, ALU.mult, ALU.add, rd, wr)
                    TSC("dve", t1, t2, PI, -2 * PI, ALU.is_gt, ALU.mult, rd, wr)
                    TT("dve", t2, t2, t1, ALU.add, rd, wr)
                    TSC("dve", t1, t2, -PI, 2 * PI, ALU.is_lt, ALU.mult, rd, wr)
                    TT("dve", t2, t2, t1, ALU.add, rd, wr)
                    ACT(s_out, t2, AF.Sin, rd, wr)
                    TSC("dve", t2, t2, PI / 2, None, ALU.add, None, rd, wr)
                    TSC("dve", t1, t2, PI, -2 * PI, ALU.is_gt, ALU.mult, rd, wr)
                    TT("dve", t2, t2, t1, ALU.add, rd, wr)
                    ACT(c_out, t2, AF.Sin, rd, wr)
                sincos(R(3), R(4), R(5), R(6), R(7), sci[:], scr + [sci.r], [sc.r, sci.r])
                AR, AI = R(8), R(9)
                TT("dve", AR, R(2), R(5), ALU.mult, scr, [sc.r])
                TT("dve", AI, R(2), R(4), ALU.mult, scr, [sc.r])
                TT("dve", R(10), are, are, ALU.mult, scr, [sc.r])
                TT("dve", R(11), aim, aim, ALU.mult, scr, [sc.r])
                TT("dve", R(10), R(10), R(11), ALU.add, scr, [sc.r])
                P.op("dve", lambda e: e.reciprocal(out=R(10), in_=R(10)), reads=scr, writes=[sc.r])
                TSC("dve", R(11), AR, -1.0, None, ALU.add, None, scr, [sc.r])
                TT("dve", R(12), R(11), are, ALU.mult, scr, [sc.r])
                TT("dve", R(13), AI, aim, ALU.mult, scr, [sc.r])
                TT("dve", R(12), R(12), R(13), ALU.add, scr, [sc.r])
                TT("dve", R(12), R(12), R(10), ALU.mult, scr, [sc.r])
                TT("dve", R(13), AI, are, ALU.mult, scr, [sc.r])
                TT("dve", R(14), R(11), aim, ALU.mult, scr, [sc.r])
                TT("dve", R(13), R(13), R(14), ALU.subtract, scr, [sc.r])
                TT("dve", R(13), R(13), R(10), ALU.mult, scr, [sc.r])
                GR, GI = R(12), R(13)

                P.cut(2)
                BN = S(es_P, "BN", [128, 4, 256], F32)
                es_L = contextlib.ExitStack()
                with es_L:
                    Bld = S(es_L, "Bld", [16, 4, 2048], F32)
                    Cl2 = S(es_L, "Cl2", [16, 2, 2048], F32)
                    for j, src in enumerate((b_re, b_im, c_re, c_im)):
                        ld("sp", Bld[:, j, :], src, [Bld.r])
                    for ri in range(2):
                        CP("pool", Cl2[:, ri, :].rearrange("p (c g q) -> p c g q", c=16, g=2),
                           Bld[:, 2 + ri, :].rearrange("p (g c q) -> p c g q", g=2, c=16), [Bld.r, Cl2.r], [Cl2.r])
                    for j in range(4):
                        for c in range(16):
                            if j < 2:
                                src_ap = rawap(Bld, j * 2048 + c, [[4 * 2048, 16], [16, 128]])
                            else:
                                src_ap = Cl2[:, j - 2, c * 128:(c + 1) * 128]
                            TR(ps0[:, c * 16:(c + 1) * 16], src_ap, ident_f[0:16, 0:16], [Bld.r, Cl2.r, ident_f.r], [ps0.r])
                        CP("act", BN[:, j, :], ps0[:, 0:256], [ps0.r], [BN.r])
                    P.barrier()
                P.cut(3)
                Bv = lambda j: BN[:, j, :].rearrange("p (c i) -> p c i", c=16)
                BB = S(es_P, "BB", [128, 4, 256], F32)
                BBv = lambda j: BB[:, j, :].rearrange("p (c i) -> p c i", c=16)
                gbc = lambda g: g.unsqueeze(1).to_broadcast([128, 16, 16])
                rdB = [BN.r, BB.r, sc.r]
                TT("dve", BBv(0), Bv(0), gbc(GR), ALU.mult, rdB, [BB.r])
                TT("dve", BBv(2), Bv(1), gbc(GI), ALU.mult, rdB, [BB.r])
                TT("dve", BBv(0), BBv(0), BBv(2), ALU.subtract, rdB, [BB.r])
                TT("dve", BBv(1), Bv(1), gbc(GR), ALU.mult, rdB, [BB.r])
                TT("dve", BBv(2), Bv(0), gbc(GI), ALU.mult, rdB, [BB.r])
                TT("dve", BBv(1), BBv(1), BBv(2), ALU.add, rdB, [BB.r])
                MASK = S(es_P, "MASK", [128, 16, 2, 16], F32)
                MS("pool", MASK[:], 0.0, [MASK.r])
                MS("pool", MASK[0:64, :, 0, :], 1.0, [MASK.r])
                MS("pool", MASK[64:128, :, 1, :], 1.0, [MASK.r])
                Z4 = S(es_P, "Z4", [128, 16, 2, 16], F32)
                for ri in range(2):
                    TT("dve", Z4[:], rawap(BB, ri * 256, [[1024, 128], [1, 16], [0, 2], [16, 16]]), MASK[:], ALU.mult,
                       [BB.r, MASK.r, Z4.r], [Z4.r])
                    for k in range(4):
                        TR(ps0[:, k * 128:(k + 1) * 128], Z4[:, 4 * k:4 * k + 4, :, :].rearrange("p a b c -> p (a b c)"), ident_f[:],
                           [Z4.r, ident_f.r], [ps0.r])
                    CP("act", BT[:, :, ri, :], ps0[:].rearrange("p (k m) -> p k m", k=4), [ps0.r], [BT.r])
                P.cut(4)
                CTf = S(es_P, "CTf", [128, 16, 2, 128], F32)
                MS("pool", CTf[:], 0.0, [CTf.r])
                for ri in range(2):
                    for il in range(4):
                        outv = rawap(CTf, ri * 128 + il * 32 + il * 256, [[4096, 128], [4 * 256, 4], [16, 2], [1, 16]])
                        inv = rawap(BN, (2 + ri) * 256 + il, [[1024, 128], [4, 4], [0, 2], [16, 16]])
                        mk = MASK[:, 0:4, :, :]
                        TT("dve", outv, inv, mk, ALU.mult, [BN.r, MASK.r, CTf.r], [CTf.r])
                CP("pool", CT[:, :, 0, :], CTf[:, :, 0, :], [CTf.r], [CT.r])
                TSC("dve", CT[:, :, 1, :], CTf[:, :, 1, :], -1.0, None, ALU.mult, None, [CTf.r, CT.r], [CT.r])
                P.cut(5)
                dld = S(es_P, "dld", [4, 2, 128], F32)
                ld("sp", dld[:, 0, :], ssm_d, [dld.r])
                ld("sp", dld[:, 1, :], b_glu, [dld.r])
                TR(ps0[:, 0:4], dld[:, 0, :], ident_f[0:4, 0:4], [dld.r, ident_f.r], [ps0.r])
                TR(ps0[:, 4:8], dld[:, 1, :], ident_f[0:4, 0:4], [dld.r, ident_f.r], [ps0.r])
                CP("act", Dp[:], ps0[:, 0:8], [ps0.r], [Dp.r])
                ld("pool", wg[:], w_glu.rearrange("(kc p) n -> p kc n", p=128), [wg.r])
                P.cut(6)
                es_T = contextlib.ExitStack()
                with es_T:
                    NI = S(es_T, "NI", [128, TC], I32)
                    NF = S(es_T, "NF", [128, TC], F32)
                    P.op("pool", lambda e: e.iota(NI[:], pattern=[[1, TC]], base=1, channel_multiplier=0), writes=[NI.r])
                    CP("dve", NF[:], NI[:], [NI.r], [NF.r])
                    TA = S(es_T, "TA", [128, 16, TC], F32)
                    T1 = S(es_T, "T1", [128, 16, TC], F32)
                    T2 = S(es_T, "T2", [128, 16, TC], F32)
                    TI = S(es_T, "TI", [128, 16, TC], I32)
                    TT("dve", TA[:], R(3).unsqueeze(2).to_broadcast([128, 16, TC]), NF[:].unsqueeze(1).to_broadcast([128, 16, TC]), ALU.mult,
                       [sc.r, NF.r], [TA.r])
                    rr = [TA.r, T1.r, T2.r, TI.r, TCs.r, TSn.r]
                    sincos(TA[:], TSn[:], TCs[:], T1[:], T2[:], TI[:], rr, rr)
                    P.barrier()

                P.cut(7)
                P.barrier()
                es_P.close()
                es_M = contextlib.ExitStack()
                es_M.__enter__()
                NB2 = 2
                pXr = [PS(es_S, "pXr%d" % i, [128, 512], F32) for i in range(NB2)]
                pXi = [PS(es_S, "pXi%d" % i, [128, 512], F32) for i in range(NB2)]
                pY = PS(es_S, "pY", [128, 512], F32)
                pZ = PS(es_S, "pZ", [128, 512], F32)
                wk = [S(es_M, "wk%d" % i, [128, 8, TC], F32) for i in range(NB2)]
                wk2 = [S(es_M, "wk2%d" % i, [128, 4, TC], F32) for i in range(NB2)]
                H32 = [S(es_M, "H32_%d" % i, [128, 2, TC], F32) for i in range(16)]
                Hb = [S(es_M, "Hb%d" % i, [128, 2, TC], BF16) for i in range(4)]
                CP("dve", rq[:], R(2), [sc.r], [rq.r])

                def glu(N, outs):
                    for kq in range(4):
                        for kc in range(4):
                            MM(pZ[:, 0:N], wg[:, kc, kq * 128:(kq + 1) * 128], gb[kc][:, 0:N], kc == 0, kc == 3, [wg.r, gb[kc].r], [pZ.r])
                        ACT(sg[:, 0:N], pZ[:, 0:N], AF.Sigmoid, [pZ.r, Dp.r], [sg.r], bias=Dp[:, 4 + kq:5 + kq])
                        for (sl, dst, dres) in outs[kq]:
                            TT("pool", dst, g32[kq][:, sl], sg[:, sl], ALU.mult, [g32[kq].r, sg.r], [dres])

                def y_finish(k, N, ucols):
                    STT(y32[:, 0:N], uT[:, k, ucols], Dp[:, k:k + 1], pY[:, 0:N], ALU.mult, ALU.add, [uT_r[0], pY.r, Dp.r], [y32.r])
                    ACT(g32[k][:, 0:N], y32[:, 0:N], AF.Gelu_apprx_tanh, [y32.r], [g32[k].r])
                    CP("pool", gb[k][:, 0:N], g32[k][:, 0:N], [g32[k].r], [gb[k].r])

                uT_all = list(uT_r)
                for m in range(NM):
                    c0 = m * TC
                    for k in range(4):
                        for il in range(4):
                            i = 4 * k + il
                            b = i % NB2
                            W = wk[b]
                            W2 = wk2[b]
                            MM(pXr[b][:, 0:TC], BT[32 * il:32 * il + 32, k, 0, :], uT[32 * il:32 * il + 32, k, c0:c0 + TC], True, True,
                               [BT.r] + uT_all, [pXr[b].r], tp=(32 * il, 0))
                            MM(pXi[b][:, 0:TC], BT[32 * il:32 * il + 32, k, 1, :], uT[32 * il:32 * il + 32, k, c0:c0 + TC], True, True,
                               [BT.r] + uT_all, [pXi[b].r], tp=(32 * il, 0))
                            cs_, sn_ = TCs[:, i, :], TSn[:, i, :]
                            rd = [pXr[b].r, pXi[b].r, TCs.r, TSn.r, W.r]
                            TT("dve", W[:, 0, :], pXr[b][:, 0:TC], cs_, ALU.mult, rd, [W.r])
                            TT("dve", W[:, 1, :], pXi[b][:, 0:TC], sn_, ALU.mult, rd, [W.r])
                            TT("dve", W[:, 4, :], W[:, 0, :], W[:, 1, :], ALU.add, rd, [W.r])
                            TT("dve", W[:, 2, :], pXi[b][:, 0:TC], cs_, ALU.mult, rd, [W.r])
                            TT("dve", W[:, 3, :], pXr[b][:, 0:TC], sn_, ALU.mult, rd, [W.r])
                            TT("dve", W[:, 5, :], W[:, 2, :], W[:, 3, :], ALU.subtract, rd, [W.r])
                            for ri in range(2):
                                init = 0.0 if m == 0 else H32[i][:, ri, TC - 1:TC]
                                P.op("dve", lambda e, W=W, ri=ri, init=init, i=i: e.tensor_tensor_scan(
                                    out=W[:, 6 + ri, :], data0=rq[:, i:i + 1].to_broadcast([128, TC]), data1=W[:, 4 + ri, :],
                                    initial=init, op0=ALU.mult, op1=ALU.add), reads=[W.r, rq.r, H32[i].r], writes=[W.r])
                            rd2 = [W.r, W2.r, TCs.r, TSn.r]
                            TT("pool", W2[:, 0, :], W[:, 6, :], cs_, ALU.mult, rd2, [W2.r])
                            TT("pool", W2[:, 1, :], W[:, 7, :], sn_, ALU.mult, rd2, [W2.r])
                            TT("pool", H32[i][:, 0, :], W2[:, 0, :], W2[:, 1, :], ALU.subtract, rd2 + [H32[i].r], [H32[i].r])
                            TT("pool", W2[:, 2, :], W[:, 7, :], cs_, ALU.mult, rd2, [W2.r])
                            TT("pool", W2[:, 3, :], W[:, 6, :], sn_, ALU.mult, rd2, [W2.r])
                            TT("pool", H32[i][:, 1, :], W2[:, 2, :], W2[:, 3, :], ALU.add, rd2 + [H32[i].r], [H32[i].r])
                            CP("act", Hb[il][:], H32[i][:], [H32[i].r], [Hb[il].r])
                        n = 0
                        for il in range(4):
                            for ri in range(2):
                                MM(pY[:, 0:TC], CT[:, 4 * k + il, ri, :], Hb[il][:, ri, :], n == 0, n == 7, [CT.r, Hb[il].r], [pY.r])
                                n += 1
                        y_finish(k, TC, slice(c0, c0 + TC))
                    outs = []
                    for kq in range(4):
                        if m == 0:
                            o = [(slice(0, 16), mixT[:, 4 + kq, 0:16], mix_r[4 + kq]),
                                 (slice(16, TC), mixT[:, 4 + kq, 128:128 + TC - 16], mix_r[4 + kq])]
                        else:
                            d0 = 128 + c0 - 16
                            o = [(slice(0, TC), mixT[:, 4 + kq, d0:d0 + TC], mix_r[4 + kq])]
                        outs.append(o)
                    glu(TC, outs)
                P.cut(9)
                FP = S(es_M, "FP", [128, 2, 16], F32)
                for i in range(16):
                    CP("dve", FP[:, :, i:i + 1], H32[i][:, :, TC - 1:TC], [H32[i].r, FP.r], [FP.r])
                fpo = S(es_M, "fpo", [16, 2, 128], F32)
                for ri in range(2):
                    TR(ps0[0:16, ri * 128:(ri + 1) * 128], FP[:, ri, :], ident_f[:], [FP.r, ident_f.r], [ps0.r])
                CP("act", fpo[:].rearrange("p a b -> p (a b)"), ps0[0:16, 0:256], [ps0.r], [fpo.r])
                ld("sp", srp, fpo[:, 0, :], [], reads=[fpo.r])
                ld("sp", sip, fpo[:, 1, :], [], reads=[fpo.r])

                P.cut(10)
                P.barrier()
                es_M.close()
                NSC = NS * TS
                XS = S(es_S, "XS", [128, 2, 16, NSC], F32)
                banks = [pXr[0], pXr[1], pXi[0], pXi[1]]
                for ri in range(2):
                    for il in range(4):
                        for k in range(4):
                            MM(banks[il][:, k * NSC:(k + 1) * NSC], BT[32 * il:32 * il + 32, k, ri, :], uT[32 * il:32 * il + 32, k, L:L + NSC],
                               True, True, [BT.r] + uT_all, [banks[il].r], tp=(32 * il, 0))
                    for il in range(4):
                        CP("act", XS[:, ri, :, :].rearrange("p (k il) c -> p k il c", il=4)[:, :, il, :],
                           banks[il][:, 0:4 * NSC].rearrange("p (a b) -> p a b", a=4), [banks[il].r, XS.r], [XS.r])
                P.cut(10.1)
                H0l = S(es_S, "H0l", [16, 2, 2048], F32)
                ld("sp", H0l[:, 0, :], s_re0, [H0l.r])
                ld("sp", H0l[:, 1, :], s_im0, [H0l.r])
                HS = S(es_S, "HS", [128, 2, 16, NS, TS + 1], F32)
                for ri in range(2):
                    for i in range(16):
                        TR(ps0[:, i * 16:(i + 1) * 16], H0l[:, ri, i * 128:(i + 1) * 128], ident_f[0:16, 0:16], [H0l.r, ident_f.r], [ps0.r])
                    CP("act", HS[:, ri, :, :, 0], ps0[:, 0:256].rearrange("p (a b) -> p a b", a=16), [ps0.r, HS.r], [HS.r])
                P.cut(10.2)
                M4 = S(es_S, "M4", [128, 4, 16, NS], F32)
                abc = lambda a: a.unsqueeze(2).to_broadcast([128, 16, NS])
                XSv = lambda ri, t: XS[:, ri, :, :].rearrange("p a (s t) -> p a s t", t=TS)[:, :, :, t]
                rdh = [HS.r, M4.r, XS.r, sc.r]
                for t in range(TS):
                    hr, hi = HS[:, 0, :, :, t], HS[:, 1, :, :, t]
                    TT("dve", M4[:, 0], hr, abc(AR), ALU.mult, rdh, [M4.r])
                    TT("dve", M4[:, 1], hi, abc(AI), ALU.mult, rdh, [M4.r])
                    TT("dve", M4[:, 0], M4[:, 0], M4[:, 1], ALU.subtract, rdh, [M4.r])
                    TT("dve", HS[:, 0, :, :, t + 1], M4[:, 0], XSv(0, t), ALU.add, rdh, [HS.r])
                    TT("dve", M4[:, 2], hi, abc(AR), ALU.mult, rdh, [M4.r])
                    TT("dve", M4[:, 3], hr, abc(AI), ALU.mult, rdh, [M4.r])
                    TT("dve", M4[:, 2], M4[:, 2], M4[:, 3], ALU.add, rdh, [M4.r])
                    TT("dve", HS[:, 1, :, :, t + 1], M4[:, 2], XSv(1, t), ALU.add, rdh, [HS.r])
                P.cut(10.3)
                HSb = S(es_S, "HSb", [128, 2, 16, NS, TS], BF16)
                for ri in range(2):
                    CP("pool", HSb[:, ri], HS[:, ri, :, :, 1:TS + 1], [HS.r, HSb.r], [HSb.r])
                P.cut(10.4)
                for k in range(4):
                    n = 0
                    for il in range(4):
                        for ri in range(2):
                            MM(pY[:, 0:NSC], CT[:, 4 * k + il, ri, :], HSb[:, ri, 4 * k + il].rearrange("p s t -> p (s t)"), n == 0, n == 7,
                               [CT.r, HSb.r], [pY.r])
                            n += 1
                    y_finish(k, NSC, slice(L, L + NSC))
                glu(NSC, [[(slice(0, NSC), mixT[:, 4 + kq, 32:32 + NSC], mix_r[4 + kq])] for kq in range(4)])
                P.cut(10.5)
                fso = S(es_S, "fso", [16, 2, 2048], F32)
                for ri in range(2):
                    for q4 in range(4):
                        for i4 in range(4):
                            i = q4 * 4 + i4
                            TR(ps0[0:16, i4 * 128:(i4 + 1) * 128], HS[:, ri, i, :, TS], ident_f[:], [HS.r, ident_f.r], [ps0.r])
                        CP("act", fso[:, ri, q4 * 512:(q4 + 1) * 512], ps0[0:16, :], [ps0.r, fso.r], [fso.r])
                ld("sp", srs, fso[:, 0, :], [], reads=[fso.r])
                ld("sp", sis, fso[:, 1, :], [], reads=[fso.r])
                P.barrier()
            P.cut(300)
            es_U.close()
            es_C = contextlib.ExitStack()
            with es_C:
                x1 = S(es_C, "x1", [128, NT, D], F32)
                x1_r = [Res("x1_%d" % i) for i in range(NT)]
                xn2T = S(es_C, "xn2T", [128, 8, NCOL], BF16)
                xn2_r = [Res("xn2_%d" % i) for i in range(NT)]
                cwb = S(es_C, "cwb", [128, 4, NFC], F32)
                es_C1 = contextlib.ExitStack()
                with es_C1:
                    g2b = S(es_C1, "g2b", [128, D], F32)
                    ld("sp", g2b[:], rawap(norm2.tensor, 0, [[0, 128], [1, D]]), [g2b.r])
                    wo = S(es_C1, "wo", [128, 8, D], BF16)
                    ld("pool", wo[:], w_out.rearrange("(kc p) n -> p kc n", p=128), [wo.r])
                    cwl = S(es_C1, "cwl", [NFC, 4, 128], F32)
                    for j in range(3):
                        ld("sp", cwl[:, j, :], conv_w[j:j + 1, :].rearrange("o (c f) -> (o c) f", f=128), [cwl.r])
                    ld("sp", cwl[:, 3, :], conv_b.rearrange("o (c f) -> (o c) f", f=128), [cwl.r])
                    pc0 = PS(es_C1, "pcw0", [128, 512], F32)
                    for j in range(4):
                        TR(pc0[:, j * NFC:(j + 1) * NFC], cwl[:, j, :], ident_f[0:NFC, 0:NFC], [cwl.r, ident_f.r], [pc0.r])
                    CP("act", cwb[:].rearrange("p a b -> p (a b)"), pc0[:, 0:4 * NFC], [pc0.r], [cwb.r])
                    NB = 2
                    xt = [S(es_C1, "cxt%d" % i, [128, D], F32) for i in range(NB)]
                    junk = S(es_C1, "cjunk", [128, D], BF16)
                    ss = [S(es_C1, "cssq%d" % i, [128, 4], F32) for i in range(NB)]
                    xn = [S(es_C1, "cxn%d" % i, [128, D], BF16) for i in range(NB)]
                    pw = [PS(es_C1, "pw%d" % i, [128, 1024], F32) for i in range(2)]
                    pT2 = [PS(es_C1, "pT2%d" % i, [128, 8, 128], BF16) for i in range(2)]
                    for tt in range(NT):
                        b = tt % NB
                        X = xt[b]
                        if tt == 0:
                            MS("pool", X[:], 0.0, [X.r])
                            ld("sp", X[0:16, :], meta, [X.r])
                            r2 = Res("cx0b")
                            ld("sp", X[32:96, :], xs, [r2], reads=[X.r])
                            xr = [X.r, r2]
                        else:
                            ld("sp", X[:], xp[(tt - 1) * 128:tt * 128, :], [X.r])
                            xr = [X.r]
                        pw_ = pw[tt % 2]
                        for nb in range(2):
                            for kc in range(8):
                                MM(pw_[:, nb * 512:(nb + 1) * 512], mixT[:, kc, tt * 128:(tt + 1) * 128], wo[:, kc, nb * 512:(nb + 1) * 512],
                                   kc == 0, kc == 7, [wo.r] + mix_r, [pw_.r])
                        for nb in range(2):
                            sl = slice(nb * 512, (nb + 1) * 512)
                            TT("dve", x1[:, tt, sl], pw_[:, sl], X[:, sl], ALU.add, [pw_.r] + xr + [x1_r[tt]], [x1_r[tt]])
                        P.op("act", lambda e, tt=tt, b=b: e.activation(out=junk[:], in_=x1[:, tt, :], func=AF.Square, accum_out=ss[b][:, 0:1]),
                             reads=[x1_r[tt]], writes=[junk.r, ss[b].r])
                        TSC("dve", ss[b][:, 1:2], ss[b][:, 0:1], 1.0 / D, EPS, ALU.mult, ALU.add, [ss[b].r], [ss[b].r])
                        P.op("act", lambda e, b=b: e.sqrt(out=ss[b][:, 2:3], in_=ss[b][:, 1:2]), reads=[ss[b].r], writes=[ss[b].r])
                        P.op("dve", lambda e, b=b: e.reciprocal(out=ss[b][:, 3:4], in_=ss[b][:, 2:3]), reads=[ss[b].r], writes=[ss[b].r])
                        STT(xn[b][:], x1[:, tt, :], ss[b][:, 3:4], g2b[:], ALU.mult, ALU.mult, [x1_r[tt], ss[b].r, g2b.r], [xn[b].r])
                        pt_ = pT2[tt % 2]
                        for kc in range(8):
                            TR(pt_[:, kc, :], xn[b][:, kc * 128:(kc + 1) * 128], ident_b[:], [xn[b].r, ident_b.r], [pt_.r])
                        CP("act", xn2T[:, :, tt * 128:(tt + 1) * 128], pt_[:], [pt_.r], [xn2_r[tt]])
                    P.barrier()
                P.cut(350)
                parts = [list(range(0, 8)), list(range(8, 15)), list(range(15, 22))]
                hT = mixT
                wd = S(es_C, "wd", [128, 8, D], BF16)
                Ab = S(es_C, "Ab", [128, L + 2], F32)
                As = S(es_C, "As", [128, NS, TS + 2], F32)
                Gt = S(es_C, "Gt", [128, NCOL], F32)
                wgu = [S(es_C, "wgu%d" % i, [128, 2, 8, 128], BF16) for i in range(2)]
                cst = S(es_C, "cst", [128, NFC, 2], F32)
                css = S(es_C, "css", [128, NFC, NS, 2], F32)
                hist = S(es_C, "hist", [NS * 2, 128], F32)
                MS("pool", Ab[:, 0:2], 0.0, [Ab.r])
                MS("pool", Gt[:], 0.0, [Gt.r])
                pa_ = [PS(es_C, "pa%d" % i, [128, 512], F32) for i in range(2)]
                pc_ = [PS(es_C, "pc%d" % i, [128, 512], F32) for i in range(2)]
                pd = [PS(es_C, "pd%d" % i, [128, 1024], F32) for i in range(1)]
                ph = PS(es_C, "ph", [128, 512], F32)
                ost = [S(es_C, "ost%d" % i, [128, D], F32) for i in range(1)]
                stg = [S(es_C, "stg%d" % i, [NS * 2, 512], F32) for i in range(2)]
                w_gate_v = w_gate.rearrange("(kc p) n -> p kc n", p=128)
                w_up_v = w_up.rearrange("(kc p) n -> p kc n", p=128)
                w_down_v = w_down.rearrange("(fc p) n -> p fc n", p=128)
                xn2_all = list(xn2_r)
                hT_r = Res("hT")
                colgroups = [(0, 512), (512, 512), (1024, 512), (1536, 512), (2048, 128)]
                for pi, fcs in enumerate(parts):
                    for j, fc in enumerate(fcs):
                        ld("pool", wd[:, j, :], w_down_v[:, fc, :], [wd.r])
                    for j, fc in enumerate(fcs):
                        wb = wgu[fc % 2]
                        ld("pool", wb[:, 0], w_gate_v[:, :, fc * 128:(fc + 1) * 128], [wb.r])
                        rwb2 = Res("wb2")
                        ld("pool", wb[:, 1], w_up_v[:, :, fc * 128:(fc + 1) * 128], [rwb2], reads=[wb.r])
                        wrd = [wb.r, rwb2]
                        ld("sp", hist[:], conv0[:, fc * 128:(fc + 1) * 128], [hist.r])
                        TR(ph[:, 0:NS * 2], hist[:], ident_f[0:NS * 2, 0:NS * 2], [hist.r, ident_f.r], [ph.r])
                        CP("act", As[:, :, 0:2], ph[:, 0:NS * 2].rearrange("p (s j) -> p s j", j=2), [ph.r, As.r], [As.r])
                        for gi, (c0, N) in enumerate(colgroups):
                            pa = pa_[gi % 2]
                            for kc in range(8):
                                MM(pa[:, 0:N], wb[:, 0, kc, :], xn2T[:, kc, c0:c0 + N], kc == 0, kc == 7, wrd + xn2_all, [pa.r])
                            if gi == 0:
                                CP("act", Ab[:, 2:18], pa[:, 0:16], [pa.r, Ab.r], [Ab.r])
                                CP("act", As[:, :, 2:2 + TS], pa[:, 32:96].rearrange("p (s t) -> p s t", t=TS), [pa.r, As.r], [As.r])
                                CP("act", Ab[:, 18:18 + 384], pa[:, 128:512], [pa.r, Ab.r], [Ab.r])
                            else:
                                CP("act", Ab[:, c0 - 110:c0 - 110 + N], pa[:, 0:N], [pa.r, Ab.r], [Ab.r])
                        w0, w1, w2, bb = (cwb[:, q, fc:fc + 1] for q in range(4))
                        rdc = [Ab.r, Gt.r, cwb.r]
                        ACT(Gt[:, 112:112 + L], Ab[:, 0:L], AF.Identity, rdc, [Gt.r], bias=bb, scale=w0)
                        STT(Gt[:, 112:112 + L], Ab[:, 1:L + 1], w1, Gt[:, 112:112 + L], ALU.mult, ALU.add, rdc, [Gt.r])
                        STT(Gt[:, 112:112 + L], Ab[:, 2:L + 2], w2, Gt[:, 112:112 + L], ALU.mult, ALU.add, rdc, [Gt.r])
                        ACT(Gt[:, 112:112 + L], Gt[:, 112:112 + L], AF.Gelu_apprx_tanh, rdc, [Gt.r])
                        CP("pool", Gt[:, 0:16], Gt[:, 112:128], [Gt.r], [Gt.r])
                        Gs = Gt[:, 32:96].rearrange("p (s t) -> p s t", t=TS)
                        rds = [As.r, Gt.r, cwb.r]
                        ACT(Gs, As[:, :, 0:TS], AF.Identity, rds, [Gt.r], bias=bb, scale=w0)
                        STT(Gs, As[:, :, 1:TS + 1], w1, Gs, ALU.mult, ALU.add, rds, [Gt.r])
                        STT(Gs, As[:, :, 2:TS + 2], w2, Gs, ALU.mult, ALU.add, rds, [Gt.r])
                        ACT(Gs, Gs, AF.Gelu_apprx_tanh, rds, [Gt.r])
                        for gi, (c0, N) in enumerate(colgroups):
                            pc = pc_[gi % 2]
                            for kc in range(8):
                                MM(pc[:, 0:N], wb[:, 1, kc, :], xn2T[:, kc, c0:c0 + N], kc == 0, kc == 7, wrd + xn2_all, [pc.r])
                            TT("dve", hT[:, j, c0:c0 + N], Gt[:, c0:c0 + N], pc[:, 0:N], ALU.mult, [Gt.r, pc.r, hT_r], [hT_r])
                        CP("pool", cst[:, fc, :], Ab[:, L:L + 2], [Ab.r, cst.r], [cst.r])
                        CP("pool", css[:, fc, :, :], As[:, :, TS:TS + 2], [As.r, css.r], [css.r])
                    lastp = (pi == len(parts) - 1)
                    for tt in range(NT):
                        pd_ = pd[0]
                        for nb in range(2):
                            for j in range(len(fcs)):
                                MM(pd_[:, nb * 512:(nb + 1) * 512], hT[:, j, tt * 128:(tt + 1) * 128], wd[:, j, nb * 512:(nb + 1) * 512],
                                   j == 0, j == len(fcs) - 1, [hT_r, wd.r], [pd_.r])
                        if not lastp:
                            for nb in range(2):
                                sl = slice(nb * 512, (nb + 1) * 512)
                                TT("dve", x1[:, tt, sl], pd_[:, sl], x1[:, tt, sl], ALU.add, [pd_.r, x1_r[tt]], [x1_r[tt]])
                        else:
                            o_ = ost[0]
                            for nb in range(2):
                                sl = slice(nb * 512, (nb + 1) * 512)
                                TT("dve", o_[:, sl], pd_[:, sl], x1[:, tt, sl], ALU.add, [pd_.r, x1_r[tt], o_.r], [o_.r])
                            if tt == 0:
                                ld("sp", ys[:, :], o_[32:96, :], [], reads=[o_.r])
                            else:
                                ld("sp", yp[(tt - 1) * 128:tt * 128, :], o_[:], [], reads=[o_.r])
                for q4 in range(6):
                    nch = min(4, NFC - 4 * q4)
                    for j in range(nch):
                        fc = 4 * q4 + j
                        TR(ph[0:2, j * 128:(j + 1) * 128], cst[:, fc, :], ident_f[:], [cst.r, ident_f.r], [ph.r])
                    CP("act", stg[0][0:2, 0:nch * 128], ph[0:2, 0:nch * 128], [ph.r, stg[0].r], [stg[0].r])
                    ld("sp", cp[:, q4 * 512:q4 * 512 + nch * 128], stg[0][0:2, 0:nch * 128], [], reads=[stg[0].r])
                    for j in range(nch):
                        fc = 4 * q4 + j
                        TR(pa_[0][0:NS * 2, j * 128:(j + 1) * 128], css[:, fc, :, :].rearrange("p s j -> p (s j)"), ident_f[:],
                           [css.r, ident_f.r], [pa_[0].r])
                    CP("act", stg[1][:, 0:nch * 128], pa_[0][0:NS * 2, 0:nch * 128], [pa_[0].r, stg[1].r], [stg[1].r])
                    ld("sp", cs[:, q4 * 512:q4 * 512 + nch * 128], stg[1][:, 0:nch * 128], [], reads=[stg[1].r])
                P.barrier()
        P.barrier()
        P._free = [d for d in P._dmap.values()]
        P._dmap = {}

    for vc in range(nv):
        emit_vc(vc)
    if True:
        P.run()
    return nc


def make_in_maps(inp, n_cores, npool, nv=1):
    f = lambda a: np.ascontiguousarray(np.asarray(a))
    ckf = f(inp["cache_k"]).reshape(npool * 128, 512)
    cvf = f(inp["cache_v"]).reshape(npool * 128, 512)
    shared = {
        "meta": f(inp["meta_tokens"]), "ck": ckf, "cv": cvf,
        "norm1": f(inp["norm1"]).reshape(1, D), "norm2": f(inp["norm2"]).reshape(1, D),
        "w_in": f(inp["w_in"])[0], "w_out": f(inp["w_out"])[0],
        "q_norm": f(inp["q_norm"]).reshape(1, 128), "k_norm": f(inp["k_norm"]).reshape(1, 128),
        "lam_q": f(inp["lam_q"]).reshape(1, 128), "lam_k": f(inp["lam_k"]).reshape(1, 128),
        "sub_norm": f(inp["sub_norm"]).reshape(1, 128),
        "a_re": f(inp["ssm_a_re"]).reshape(16, 128), "a_im": f(inp["ssm_a_im"]).reshape(16, 128),
        "log_dt": f(inp["ssm_log_dt"]).reshape(16, 2),
        "b_re": f(inp["ssm_b_re"]).reshape(16, 2048), "b_im": f(inp["ssm_b_im"]).reshape(16, 2048),
        "c_re": f(inp["ssm_c_re"]).reshape(16, 2048), "c_im": f(inp["ssm_c_im"]).reshape(16, 2048),
        "ssm_d": f(inp["ssm_d"]).reshape(4, 128),
        "w_glu": f(inp["w_glu"])[0], "b_glu": f(inp["b_glu"]).reshape(4, 128),
        "w_gate": f(inp["w_gate"])[0], "w_up": f(inp["w_up"])[0], "w_down": f(inp["w_down"])[0],
        "conv_w": f(inp["ffn_conv_w"])[0], "conv_b": f(inp["ffn_conv_b"]).reshape(1, DFF),
    }
    maps = []
    for c in range(n_cores):
        m = dict(shared)
        for vc in range(nv):
            g = c * nv + vc
            sfx = "_v%d" % vc
            m["xp" + sfx] = f(inp["x_prompt"][g])
            m["xs" + sfx] = f(inp["x_sample"][g * NS:(g + 1) * NS]).reshape(NS * TS, D)
            m["pt" + sfx] = f(inp["page_table"][g * NS:(g + 1) * NS]).reshape(1, NS * NPG).astype(np.int32)
            m["s_re0" + sfx] = f(inp["state_ssm_re"][0, g * NS:(g + 1) * NS]).reshape(NS, 2048)
            m["s_im0" + sfx] = f(inp["state_ssm_im"][0, g * NS:(g + 1) * NS]).reshape(NS, 2048)
            m["conv0" + sfx] = f(inp["state_ffn_conv"][0, g * NS:(g + 1) * NS]).reshape(NS * 2, DFF)
        maps.append(m)
    return maps


def assemble(results, n_cores, nv=1):
    n = n_cores * nv
    cat = lambda k: np.stack([np.asarray(results[c][k + "_v%d" % vc]) for c in range(n_cores) for vc in range(nv)])
    y_prompt = cat("yp").reshape(n, SEQ, D)
    y_sample = cat("ys").reshape(n * NS, TS, D)
    k_prompt = cat("kp").reshape(1, n, L, 4, 128)
    v_prompt = cat("vp").reshape(1, n, L, 4, 128)
    k_sample = cat("ks").reshape(1, n * NS, TS, 4, 128)
    v_sample = cat("vs").reshape(1, n * NS, TS, 4, 128)
    srp = cat("srp").reshape(1, n, 32, 64)
    sip = cat("sip").reshape(1, n, 32, 64)
    srs = cat("srs").reshape(1, n * NS, 32, 64)
    sis = cat("sis").reshape(1, n * NS, 32, 64)
    cpo = cat("cp").reshape(1, n, 2, DFF)
    cso = cat("cs").reshape(1, n * NS, 2, DFF)
    return tuple(np.ascontiguousarray(a.astype(np.float32)) for a in
                 (y_prompt, y_sample, k_prompt, v_prompt, k_sample, v_sample, srp, sip, srs, sis, cpo, cso))


N_CORES = 8
N_VC = 1


def kernel(**inputs):
    npool = int(np.asarray(inputs["cache_k"]).shape[1])
    nc = build(npool, N_VC)
    maps = make_in_maps(inputs, N_CORES, npool, N_VC)
    res = run_bass_kernel_spmd(nc, maps, core_ids=list(range(N_CORES)))
    return assemble(res.results, N_CORES, N_VC)
```

```python
import contextlib
import math
import numpy as np
import concourse.bass as bass
import concourse.mybir as mybir
from concourse.bass_utils import run_bass_kernel_spmd

F32 = mybir.dt.float32
BF16 = mybir.dt.bfloat16
I32 = mybir.dt.int32
AF = mybir.ActivationFunctionType
ALU = mybir.AluOpType
AX = mybir.AxisListType

D = 1024
SEQ = 2048
NMETA = 16
L = SEQ + NMETA
NT = 17
NCOL = NT * 128
NS = 16
TS = 4
NPG = 16
DFF = 2816
NFC = DFF // 128
EPS = 1e-6
LAM_INIT = 0.8 - 0.6 * math.exp(-0.3 * 0)
Q = 8
NJ = L // Q
ENGS = ("pe", "act", "dve", "pool", "sp")


class Res:
    __slots__ = ("name", "writer", "readers")

    def __init__(self, name=""):
        self.name = name
        self.writer = None
        self.readers = []


class DSem:
    def __init__(self, sem):
        self.sem = sem
        self.val = 0


class Prog:
    def __init__(self, nc):
        self.nc = nc
        self.q = {e: [] for e in ENGS}
        self.sem = {e: nc.alloc_semaphore(name="c_" + e) for e in ENGS}
        self.cnt = {e: 0 for e in ENGS}
        self.seen = {e: {} for e in ENGS}
        self.dsems = []
        self.n_inst = 0
        self.dead = False
        import os as _o
        self.cutv = float(_o.environ.get('DBG_S', 1e9))
        self.serial_same = {"act": True, "dve": True, "pool": True, "pe": False, "sp": False}

    def _deps(self, eng, reads, writes):
        need = {}

        def add(sv):
            if sv is None:
                return
            s, v = sv
            if need.get(s, 0) < v:
                need[s] = v
        for r in reads:
            add(r.writer)
        for w in writes:
            add(w.writer)
            for rd in w.readers:
                add(rd)
        waits = []
        for s, v in need.items():
            if s is self.sem[eng] and not self.serial_same[eng]:
                continue
            if self.seen[eng].get(s, 0) >= v:
                continue
            self.seen[eng][s] = v
            waits.append((s, v))
        return waits

    def _mark(self, reads, writes, sv):
        for r in reads:
            r.readers.append(sv)
            if len(r.readers) > 64:
                best = {}
                for s, v in r.readers:
                    if best.get(s, 0) < v:
                        best[s] = v
                r.readers = list(best.items())
        for w in writes:
            w.writer = sv
            w.readers = []

    def cut(self, x):
        if self.cutv < x:
            self.dead = True

    def op(self, eng, fn, reads=(), writes=()):
        if self.dead:
            return
        waits = self._deps(eng, reads, writes)
        self.cnt[eng] += 1
        sem = self.sem[eng]
        sv = (sem, self.cnt[eng])

        def emit(e, fn=fn, waits=waits, sem=sem):
            for s, v in waits:
                e.wait_ge(s, v)
            fn(e).then_inc(sem, 1)
        self.q[eng].append(emit)
        self._mark(reads, writes, sv)
        self.n_inst += 1

    def dma(self, eng, fn, reads=(), writes=(), dsem=None):
        if self.dead:
            return
        waits = self._deps(eng, reads, writes)
        dsem.val += 16
        sv = (dsem.sem, dsem.val)

        def emit(e, fn=fn, waits=waits, s=dsem.sem):
            for ws, wv in waits:
                e.wait_ge(ws, wv)
            fn(e).then_inc(s, 16)
        self.q[eng].append(emit)
        self._mark(reads, writes, sv)
        self.n_inst += 1

    def new_dsem(self, name):
        d = DSem(self.nc.alloc_semaphore(name=name + "_%d" % len(self.dsems)))
        self.dsems.append(d)
        return d

    def barrier(self):
        if self.dead:
            return
        for e in ENGS:
            waits = []
            for e2 in ENGS:
                if e2 != e and self.cnt[e2] > self.seen[e].get(self.sem[e2], 0):
                    self.seen[e][self.sem[e2]] = self.cnt[e2]
                    waits.append((self.sem[e2], self.cnt[e2]))
            for d in self.dsems:
                if d.val > self.seen[e].get(d.sem, 0):
                    self.seen[e][d.sem] = d.val
                    waits.append((d.sem, d.val))

            def emit(en, waits=waits):
                for s, v in waits:
                    en.wait_ge(s, v)
            self.q[e].append(emit)

    def run(self):
        nc = self.nc
        finals = [(d.sem, d.val) for d in self.dsems if d.val > 0]

        def fin(e):
            for s, v in finals:
                e.wait_ge(s, v)
        self.q["sp"].append(fin)
        with nc.Block() as block:
            @block.tensor
            def _(e):
                for f in self.q["pe"]:
                    f(e)

            @block.scalar
            def _(e):
                for f in self.q["act"]:
                    f(e)

            @block.vector
            def _(e):
                for f in self.q["dve"]:
                    f(e)

            @block.gpsimd
            def _(e):
                for f in self.q["pool"]:
                    f(e)

            @block.sync
            def _(e):
                for f in self.q["sp"]:
                    f(e)


class T:
    def __init__(self, t, name):
        self.t = t
        self.r = Res(name)

    def __getitem__(self, k):
        return self.t[k]


def rawap(t, off, dims):
    return bass.AP(t.t if isinstance(t, T) else t, off, [list(d) for d in dims])


def build(npool, nv=1):
    nc = bass.Bass("TRN2", target_bir_lowering=False)
    P = Prog(nc)

    def din(name, shape, dt=F32):
        return nc.dram_tensor(name, list(shape), dt, kind="ExternalInput").ap()

    def dout(name, shape, dt=F32):
        return nc.dram_tensor(name, list(shape), dt, kind="ExternalOutput").ap()

    meta = din("meta", [NMETA, D])
    ckv = din("ckv", [npool * 128, 1024])
    norm1 = din("norm1", [1, D]); norm2 = din("norm2", [1, D])
    w_in = din("w_in", [D, 2048]); w_out = din("w_out", [D, D])
    q_norm = din("q_norm", [1, 128]); k_norm = din("k_norm", [1, 128])
    lam_q = din("lam_q", [1, 128]); lam_k = din("lam_k", [1, 128])
    sub_norm = din("sub_norm", [1, 128])
    a_re = din("a_re", [16, 128]); a_im = din("a_im", [16, 128]); log_dt = din("log_dt", [16, 2])
    b_re = din("b_re", [16, 2048]); b_im = din("b_im", [16, 2048])
    c_re = din("c_re", [16, 2048]); c_im = din("c_im", [16, 2048])
    ssm_d = din("ssm_d", [4, 128])
    w_glu = din("w_glu", [512, 512]); b_glu = din("b_glu", [4, 128])
    w_gate = din("w_gate", [D, DFF]); w_up = din("w_up", [D, DFF]); w_down = din("w_down", [DFF, D])
    conv_w = din("conv_w", [3, DFF]); conv_b = din("conv_b", [1, DFF])
    SFX = {"s": ""}

    def emit_vc(vc):
        sfx = "_v%d" % vc
        SFX["s"] = sfx
        xp = din("xp" + sfx, [SEQ, D]); xs = din("xs" + sfx, [NS * TS, D])
        pt = din("pt" + sfx, [1, NS * NPG], I32)
        s_re0 = din("s_re0" + sfx, [NS, 2048]); s_im0 = din("s_im0" + sfx, [NS, 2048])
        conv0 = din("conv0" + sfx, [NS * 2, DFF])
        yp = dout("yp" + sfx, [SEQ, D]); ys = dout("ys" + sfx, [NS * TS, D])
        kp = dout("kp" + sfx, [L, 512]); vp = dout("vp" + sfx, [L, 512])
        ks = dout("ks" + sfx, [NS * TS, 512]); vs = dout("vs" + sfx, [NS * TS, 512])
        srp = dout("srp" + sfx, [16, 128]); sip = dout("sip" + sfx, [16, 128])
        srs = dout("srs" + sfx, [NS, 2048]); sis = dout("sis" + sfx, [NS, 2048])
        cp = dout("cp" + sfx, [2, DFF]); cs = dout("cs" + sfx, [NS * 2, DFF])
        es_all = contextlib.ExitStack()

        def S(es, name, shape, dt):
            return T(es.enter_context(nc.sbuf_tensor(name + SFX["s"], list(shape), dt)), name)

        def PS(es, name, shape, dt):
            return T(es.enter_context(nc.psum_tensor(name + SFX["s"], list(shape), dt)), name)

        dq = {"n": 0}

        def ld(eng, out_ap, in_ap, writes, reads=(), name=None):
            w0 = writes[0] if writes else reads[0]
            if not hasattr(P, "_dmap"):
                P._dmap = {}
            key = id(w0)
            if key not in P._dmap:
                fr = getattr(P, "_free", [])
                P._dmap[key] = fr.pop() if fr else P.new_dsem("d%d" % len(P.dsems))
                P._keep = getattr(P, "_keep", []) + [w0]
            P.dma(eng, lambda e: e.dma_start(out=out_ap, in_=in_ap), reads=reads, writes=writes, dsem=P._dmap[key])


        def TT(eng, out, a, b, op, rd, wr):
            P.op(eng, lambda e: e.tensor_tensor(out=out, in0=a, in1=b, op=op), reads=rd, writes=wr)

        def TSC(eng, out, a, s1, s2, op0, op1, rd, wr):
            if s2 is None:
                P.op(eng, lambda e: e.tensor_scalar(out=out, in0=a, scalar1=s1, scalar2=None, op0=op0), reads=rd, writes=wr)
            else:
                P.op(eng, lambda e: e.tensor_scalar(out=out, in0=a, scalar1=s1, scalar2=s2, op0=op0, op1=op1), reads=rd, writes=wr)

        def STT(out, a, sc, b, op0, op1, rd, wr):
            P.op("dve", lambda e: e.scalar_tensor_tensor(out=out, in0=a, scalar=sc, in1=b, op0=op0, op1=op1), reads=rd, writes=wr)

        def ACT(out, in_, func, rd, wr, bias=None, scale=None):
            kw = {}
            if bias is not None:
                kw["bias"] = bias
            if scale is not None:
                kw["scale"] = scale
            P.op("act", lambda e: e.activation(out=out, in_=in_, func=func, **kw), reads=rd, writes=wr)

        def CP(eng, out, in_, rd, wr):
            if eng == "act":
                P.op("act", lambda e: e.copy(out=out, in_=in_), reads=rd, writes=wr)
            else:
                P.op(eng, lambda e: e.tensor_copy(out=out, in_=in_), reads=rd, writes=wr)

        def MM(out, lhsT, rhs, start, stop, rd, wr, tp=None):
            if tp is None:
                P.op("pe", lambda e: e.matmul(out, lhsT=lhsT, rhs=rhs, start=start, stop=stop), reads=rd, writes=wr)
            else:
                P.op("pe", lambda e: e.matmul(out, lhsT=lhsT, rhs=rhs, start=start, stop=stop, tile_position=tp), reads=rd, writes=wr)

        def TR(out, in_, ident, rd, wr):
            P.op("pe", lambda e: e.transpose(out=out, in_=in_, identity=ident), reads=rd, writes=wr)

        def MS(eng, out, val, wr):
            P.op(eng, lambda e: e.memset(out, val), writes=wr)

        with es_all:
            ident_f = S(es_all, "ident_f", [128, 128], F32)
            ident_b = S(es_all, "ident_b", [128, 128], BF16)
            ones_b = S(es_all, "ones_b", [128, 128], BF16)
            P.op("pool", lambda e: e.memset(ident_f[:], 1.0), writes=[ident_f.r])
            P.op("pool", lambda e: e.affine_select(out=ident_f[:], in_=ident_f[:], pattern=[[-1, 128]],
                                                   compare_op=ALU.is_equal, fill=0.0, base=0, channel_multiplier=1),
                 reads=[ident_f.r], writes=[ident_f.r])
            P.op("pool", lambda e: e.tensor_copy(out=ident_b[:], in_=ident_f[:]), reads=[ident_f.r], writes=[ident_b.r])
            P.op("pool", lambda e: e.memset(ones_b[:], 1.0), writes=[ones_b.r])

            mixT = S(es_all, "mixT", [128, 8, NCOL], BF16)
            mix_r = [Res("mix%d" % i) for i in range(8)]
            P.op("pool", lambda e: e.memset(mixT[:], 0.0), writes=mix_r)
            lamt = S(es_all, "lamt", [128, 8], F32)
            sgp = S(es_all, "sgp", [128, 2], F32)
            maskT = S(es_all, "maskT", [128, 128], BF16)
            ones_f = S(es_all, "ones_f", [128, 128], F32)
            es_U = contextlib.ExitStack()
            es_all.enter_context(es_U)
            uT = S(es_U, "uT", [128, 4, L + NS * TS], BF16)
            es_B = contextlib.ExitStack()
            es_all.enter_context(es_B)
            qT = S(es_B, "qT", [128, 4, NCOL], BF16)
            kT = S(es_B, "kT", [128, 4, NCOL], BF16)
            vb = S(es_B, "vb", [128, NT, 512], BF16)

            es_A = contextlib.ExitStack()
            with es_A:
                g1b = S(es_A, "g1b", [128, D], F32)
                gqb = S(es_A, "gqb", [128, 512], F32)
                gkb = S(es_A, "gkb", [128, 512], F32)
                wi = S(es_A, "wi", [128, 8, 2048], BF16)
                ld("sp", g1b[:], rawap(norm1.tensor, 0, [[0, 128], [1, D]]), [g1b.r])
                ld("sp", gqb[:].rearrange("p (h e) -> p h e", h=4), rawap(q_norm.tensor, 0, [[0, 128], [0, 4], [1, 128]]), [gqb.r])
                ld("sp", gkb[:].rearrange("p (h e) -> p h e", h=4), rawap(k_norm.tensor, 0, [[0, 128], [0, 4], [1, 128]]), [gkb.r])
                P.op("act", lambda e: e.mul(out=gqb[:], in_=gqb[:], mul=0.125), reads=[gqb.r], writes=[gqb.r])
                wi_r = [Res("wi%d" % k) for k in range(8)]
                w_in_v = w_in.rearrange("(kc p) n -> p kc n", p=128)
                for kc in range(8):
                    ld("pool", wi[:, kc, :], w_in_v[:, kc, :], [wi_r[kc]])

                NB = 2
                xt = [S(es_A, "xt%d" % i, [128, D], F32) for i in range(NB)]
                junk = S(es_A, "junk", [128, D], BF16)
                ss = [S(es_A, "ss%d" % i, [128, 4], F32) for i in range(NB)]
                xn = [S(es_A, "xn%d" % i, [128, D], BF16) for i in range(NB)]
                xnT = [S(es_A, "xnT%d" % i, [128, 8, 128], BF16) for i in range(NB)]
                sqb = [S(es_A, "sqb%d" % i, [128, 1024], F32) for i in range(NB)]
                s8 = [S(es_A, "s8%d" % i, [128, 16 * 3], F32) for i in range(NB)]
                qk32 = [S(es_A, "qk32%d" % i, [128, 1024], F32) for i in range(NB)]
                k32 = [S(es_A, "k32%d" % i, [128, 512], F32) for i in range(NB)]
                qkb = [S(es_A, "qkb%d" % i, [128, 1024], BF16) for i in range(NB)]
                v32 = [S(es_A, "v32%d" % i, [128, 512], F32) for i in range(NB)]
                pT = [PS(es_A, "pT%d" % i, [128, 8, 128], BF16) for i in range(2)]
                ps_qk = PS(es_A, "ps_qk", [128, 1024], F32)
                ps_v = PS(es_A, "ps_v", [128, 512], F32)
                ps_u = PS(es_A, "ps_u", [128, 4, 128], F32)
                pQK = PS(es_A, "pQK", [128, 8, 128], BF16)
                qT_r = [Res("qT%d" % i) for i in range(NT)]
                kT_r = [Res("kT%d" % i) for i in range(NT)]
                vb_r = [Res("vb%d" % i) for i in range(NT)]
                uT_r = [Res("uT%d" % i) for i in range(NT)]

                import os as _os
                _NTR = int(_os.environ.get('DBG_NT', NT)); _CUT = float(_os.environ.get('DBG_CUT', 99))
                for tt in range(_NTR):
                    b = tt % NB
                    X = xt[b]
                    if tt == 0:
                        P.op("pool", lambda e, X=X: e.memset(X[:], 0.0), writes=[X.r])
                        ld("sp", X[0:16, :], meta, [X.r], name="x0a")
                        r2 = Res("x0b")
                        ld("sp", X[32:96, :], xs, [r2], reads=[X.r])
                        xr = [X.r, r2]
                    else:
                        ld("sp", X[:], xp[(tt - 1) * 128:tt * 128, :], [X.r])
                        xr = [X.r]
                    P.op("act", lambda e, X=X, b=b: e.activation(out=junk[:], in_=X[:], func=AF.Square, accum_out=ss[b][:, 0:1]),
                         reads=xr, writes=[junk.r, ss[b].r])
                    P.op("dve", lambda e, b=b: e.tensor_scalar(out=ss[b][:, 1:2], in0=ss[b][:, 0:1], scalar1=1.0 / D, scalar2=EPS,
                                                               op0=ALU.mult, op1=ALU.add), reads=[ss[b].r], writes=[ss[b].r])
                    P.op("act", lambda e, b=b: e.sqrt(out=ss[b][:, 2:3], in_=ss[b][:, 1:2]), reads=[ss[b].r], writes=[ss[b].r])
                    P.op("dve", lambda e, b=b: e.reciprocal(out=ss[b][:, 3:4], in_=ss[b][:, 2:3]), reads=[ss[b].r], writes=[ss[b].r])
                    P.op("dve", lambda e, X=X, b=b: e.scalar_tensor_tensor(out=xn[b][:], in0=X[:], scalar=ss[b][:, 3:4], in1=g1b[:],
                                                                          op0=ALU.mult, op1=ALU.mult),
                         reads=xr + [ss[b].r, g1b.r], writes=[xn[b].r])
                    if _CUT < 1:
                        continue
                    pt_ = pT[tt % 2]
                    for kc in range(8):
                        P.op("pe", lambda e, b=b, kc=kc, pt_=pt_: e.transpose(out=pt_[:, kc, :], in_=xn[b][:, kc * 128:(kc + 1) * 128],
                                                                              identity=ident_b[:]),
                             reads=[xn[b].r, ident_b.r], writes=[pt_.r])
                    P.op("act", lambda e, b=b, pt_=pt_: e.copy(out=xnT[b][:], in_=pt_[:]), reads=[pt_.r], writes=[xnT[b].r])
                    if _CUT < 2:
                        continue
                    for nb in range(2):
                        for kc in range(8):
                            P.op("pe", lambda e, b=b, kc=kc, nb=nb: e.matmul(ps_qk[:, nb * 512:(nb + 1) * 512], lhsT=xnT[b][:, kc, :],
                                                                              rhs=wi[:, kc, nb * 512:(nb + 1) * 512],
                                                                              start=(kc == 0), stop=(kc == 7)),
                                 reads=[xnT[b].r, wi_r[kc]], writes=[ps_qk.r])
                    for kc in range(8):
                        P.op("pe", lambda e, b=b, kc=kc: e.matmul(ps_v[:], lhsT=xnT[b][:, kc, :], rhs=wi[:, kc, 1024:1536],
                                                                  start=(kc == 0), stop=(kc == 7)),
                             reads=[xnT[b].r, wi_r[kc]], writes=[ps_v.r])
                    for c in range(4):
                        for kc in range(8):
                            P.op("pe", lambda e, b=b, kc=kc, c=c: e.matmul(ps_u[:, c, :], lhsT=wi[:, kc, 1536 + c * 128:1536 + (c + 1) * 128],
                                                                            rhs=xnT[b][:, kc, :], start=(kc == 0), stop=(kc == 7)),
                                 reads=[xnT[b].r, wi_r[kc]], writes=[ps_u.r])
                    if _CUT < 3:
                        continue
                    for hf in range(2):
                        sl = slice(hf * 512, (hf + 1) * 512)
                        P.op("act", lambda e, b=b, sl=sl: e.activation(out=sqb[b][:, sl], in_=ps_qk[:, sl], func=AF.Square),
                             reads=[ps_qk.r, sqb[b].r], writes=[sqb[b].r])
                    if _CUT < 3.1:
                        continue
                    P.op("dve", lambda e, b=b: e.tensor_reduce(out=s8[b][:, 0:16], in_=sqb[b][:].rearrange("p (c d) -> p c d", d=64),
                                                               axis=AX.X, op=ALU.add), reads=[sqb[b].r], writes=[s8[b].r])
                    P.op("dve", lambda e, b=b: e.tensor_scalar(out=s8[b][:, 16:32], in0=s8[b][:, 0:16], scalar1=1.0 / 64, scalar2=EPS,
                                                               op0=ALU.mult, op1=ALU.add), reads=[s8[b].r], writes=[s8[b].r])
                    P.op("act", lambda e, b=b: e.sqrt(out=s8[b][:, 32:48], in_=s8[b][:, 16:32]), reads=[s8[b].r], writes=[s8[b].r])
                    P.op("dve", lambda e, b=b: e.reciprocal(out=s8[b][:, 0:16], in_=s8[b][:, 32:48]), reads=[s8[b].r], writes=[s8[b].r])
                    if _CUT < 3.2:
                        continue
                    for hf in range(2):
                        sl = slice(hf * 512, (hf + 1) * 512)
                        P.op("dve", lambda e, b=b, sl=sl, hf=hf: e.tensor_tensor(
                            out=qk32[b][:, sl].rearrange("p (c d) -> p c d", d=64),
                            in0=ps_qk[:, sl].rearrange("p (c d) -> p c d", d=64),
                            in1=s8[b][:, hf * 8:hf * 8 + 8].unsqueeze(2).to_broadcast([128, 8, 64]), op=ALU.mult),
                            reads=[ps_qk.r, s8[b].r, qk32[b].r], writes=[qk32[b].r])
                    if _CUT < 3.3:
                        continue
                    P.op("pool", lambda e, b=b: e.tensor_tensor(out=qkb[b][:, 0:512], in0=qk32[b][:, 0:512], in1=gqb[:], op=ALU.mult),
                         reads=[qk32[b].r, gqb.r], writes=[qkb[b].r])
                    P.op("dve", lambda e, b=b: e.tensor_tensor(out=k32[b][:], in0=qk32[b][:, 512:1024], in1=gkb[:], op=ALU.mult),
                         reads=[qk32[b].r, gkb.r], writes=[k32[b].r])
                    rkb = Res("kb")
                    P.op("pool", lambda e, b=b: e.tensor_copy(out=qkb[b][:, 512:1024], in_=k32[b][:]), reads=[k32[b].r, qkb[b].r], writes=[qkb[b].r])
                    if _CUT < 3.4:
                        continue
                    for h in range(8):
                        P.op("pe", lambda e, b=b, h=h: e.transpose(out=pQK[:, h, :], in_=qkb[b][:, h * 128:(h + 1) * 128], identity=ident_b[:]),
                             reads=[qkb[b].r, ident_b.r], writes=[pQK.r])
                    if _CUT < 3.5:
                        continue
                    c0 = tt * 128
                    if _CUT >= 3.6:
                        P.op("act", lambda e, c0=c0: e.copy(out=qT[:, :, c0:c0 + 128], in_=pQK[:, 0:4, :]), reads=[pQK.r], writes=[qT_r[tt]])
                    if _CUT >= 3.7:
                        P.op("act", lambda e, c0=c0: e.copy(out=kT[:, :, c0:c0 + 128], in_=pQK[:, 4:8, :]), reads=[pQK.r], writes=[kT_r[tt]])
                    if _CUT < 4:
                        continue
                    P.op("act", lambda e, b=b: e.copy(out=v32[b][:], in_=ps_v[:]), reads=[ps_v.r], writes=[v32[b].r])
                    P.op("pool", lambda e, b=b, tt=tt: e.tensor_copy(out=vb[:, tt, :], in_=v32[b][:]), reads=[v32[b].r], writes=[vb_r[tt]])
                    if tt == 0:
                        P.op("dve", lambda e: e.tensor_copy(out=uT[:, :, 0:16], in_=ps_u[:, :, 0:16]), reads=[ps_u.r], writes=[uT_r[0]])
                        P.op("dve", lambda e: e.tensor_copy(out=uT[:, :, L:L + 64], in_=ps_u[:, :, 32:96]), reads=[ps_u.r, uT_r[0]], writes=[uT_r[0]])
                    else:
                        o0 = 16 + (tt - 1) * 128
                        P.op("act", lambda e, o0=o0: e.copy(out=uT[:, :, o0:o0 + 128], in_=ps_u[:]), reads=[ps_u.r], writes=[uT_r[tt]])
                    if tt == 0:
                        ld("sp", kp[0:16, :], k32[b][0:16, :], [], reads=[k32[b].r])
                        ld("sp", ks[:, :], k32[b][32:96, :], [], reads=[k32[b].r])
                        ld("sp", vp[0:16, :], v32[b][0:16, :], [], reads=[v32[b].r])
                        ld("sp", vs[:, :], v32[b][32:96, :], [], reads=[v32[b].r])
                    else:
                        o0 = 16 + (tt - 1) * 128
                        ld("sp", kp[o0:o0 + 128, :], k32[b][:], [], reads=[k32[b].r])
                        ld("sp", vp[o0:o0 + 128, :], v32[b][:], [], reads=[v32[b].r])
                P.barrier()
            P.cut(100)
            es_L2 = contextlib.ExitStack()
            with es_L2:
                lq = S(es_L2, "lq", [128, 2, 128], F32)
                ld("sp", lq[:, 0, :], rawap(lam_q.tensor, 0, [[0, 128], [1, 128]]), [lq.r])
                ld("sp", lq[:, 1, :], rawap(lam_k.tensor, 0, [[0, 128], [1, 128]]), [lq.r])
                TT("dve", lq[:, 0, :], lq[:, 0, :], lq[:, 1, :], ALU.mult, [lq.r], [lq.r])
                P.op("dve", lambda e: e.tensor_reduce(out=lamt[:, 0:2], in_=lq[:, 0, :].rearrange("p (c d) -> p c d", c=2), axis=AX.X, op=ALU.add),
                     reads=[lq.r], writes=[lamt.r])
                ACT(lamt[:, 2:4], lamt[:, 0:2], AF.Exp, [lamt.r], [lamt.r])
                TT("dve", lamt[:, 5:6], lamt[:, 2:3], lamt[:, 3:4], ALU.subtract, [lamt.r], [lamt.r])
                TSC("dve", lamt[:, 5:6], lamt[:, 5:6], LAM_INIT, None, ALU.add, None, [lamt.r], [lamt.r])
                TSC("dve", lamt[:, 4:5], lamt[:, 5:6], -1.0, None, ALU.mult, None, [lamt.r], [lamt.r])
                ld("sp", sgp[:, 0:1], rawap(sub_norm.tensor, 0, [[1, 128], [1, 1]]), [sgp.r])
                TSC("dve", sgp[:, 1:2], sgp[:, 0:1], 1.0 - LAM_INIT, None, ALU.mult, None, [sgp.r], [sgp.r])
                MS("pool", maskT[:], 1.0, [maskT.r])
                P.op("pool", lambda e: e.affine_select(out=maskT[:], in_=maskT[:], pattern=[[1, 128]], compare_op=ALU.is_ge, fill=0.0,
                                                       base=0, channel_multiplier=-1), reads=[maskT.r], writes=[maskT.r])
                MS("pool", ones_f[:], 1.0, [ones_f.r])
                P.barrier()
            NLAM = lamt[:, 4:5]
            es_At = contextlib.ExitStack()
            with es_At:
                sA = [PS(es_At, "sA%d" % i, [128, 512], F32) for i in range(2)]
                sB = [PS(es_At, "sB%d" % i, [128, 512], F32) for i in range(2)]
                po = [PS(es_At, "po%d" % i, [128, 512], F32) for i in range(2)]
                pss = [PS(es_At, "pss%d" % i, [128, 512], F32) for i in range(2)]
                ptb = [[S(es_At, "ptb%d_%d" % (c, i), [128, 512], BF16) for i in range(2)] for c in range(2)]
                wa = [S(es_At, "wa%d" % i, [128, 512], F32) for i in range(4)]
                kv_all = list(kT_r) + list(qT_r) + list(vb_r)
                groups = [(0, 16, [0])] + [(128 * (4 * g + 1), 512, list(range(0, 4 * g + 5))) for g in range(4)]
                steps = []
                for (q0, NQ, kts) in groups:
                    for h in range(4):
                        for kt in kts:
                            if NQ == 16:
                                off, N, diag = 0, 16, True
                            elif kt * 128 < q0:
                                off, N, diag = 0, 512, False
                            else:
                                off = kt * 128 - q0
                                N, diag = 512 - off, True
                            steps.append((q0, NQ, h, kt, off, N, diag, kt == kts[0], kt == kts[-1]))

                def stage1(i):
                    (q0, NQ, h, kt, off, N, diag, first, last) = steps[i]
                    b = i % 2
                    sb = (sA[b], sB[b])
                    for c in range(2):
                        MM(sb[c][:, 0:N], kT[64 * c:64 * c + 64, h, kt * 128:(kt + 1) * 128], qT[64 * c:64 * c + 64, h, q0 + off:q0 + off + N],
                           True, True, kv_all, [sb[c].r])
                    for c in range(2):
                        pt_ = ptb[c][b]
                        ACT(pt_[:, 0:N], sb[c][:, 0:N], AF.Exp, [sb[c].r], [pt_.r])
                        if diag:
                            nd = min(N, 128)
                            TT("dve" if c == 0 else "pool", pt_[:, 0:nd], pt_[:, 0:nd], maskT[:, 0:nd], ALU.mult, [pt_.r, maskT.r], [pt_.r])
                        elif kt == 0:
                            TSC("dve" if c == 0 else "pool", pt_[:, 0:N], pt_[:, 0:N], maskT[:, 15:16], None, ALU.mult, None,
                                [pt_.r, maskT.r], [pt_.r])

                def stage2(i):
                    (q0, NQ, h, kt, off, N, diag, first, last) = steps[i]
                    b = i % 2
                    for c in range(2):
                        pt_ = ptb[c][b]
                        MM(po[c][:, off:off + N], vb[:, kt, h * 128:(h + 1) * 128], pt_[:, 0:N], first, last, [pt_.r] + kv_all, [po[c].r])
                        MM(pss[c][:, off:off + N], ones_b[:], pt_[:, 0:N], first, last, [pt_.r, ones_b.r], [pss[c].r])
                    if not last:
                        return
                    W0, W1, W2, W3 = wa
                    P.op("dve", lambda e: e.reciprocal(out=W0[:, 0:NQ], in_=pss[0][:, 0:NQ]), reads=[pss[0].r, W0.r], writes=[W0.r])
                    TT("dve", W0[:, 0:NQ], po[0][:, 0:NQ], W0[:, 0:NQ], ALU.mult, [po[0].r, W0.r], [W0.r])
                    P.op("dve", lambda e: e.reciprocal(out=W1[:, 0:NQ], in_=pss[1][:, 0:NQ]), reads=[pss[1].r, W1.r], writes=[W1.r])
                    TT("dve", W1[:, 0:NQ], po[1][:, 0:NQ], W1[:, 0:NQ], ALU.mult, [po[1].r, W1.r], [W1.r])
                    STT(W2[:, 0:NQ], W1[:, 0:NQ], NLAM, W0[:, 0:NQ], ALU.mult, ALU.add, [W0.r, W1.r, lamt.r, W2.r], [W2.r])
                    ACT(W3[:, 0:NQ], W2[:, 0:NQ], AF.Square, [W2.r, W3.r], [W3.r])
                    P.op("pe", lambda e: e.matmul(pss[0][:, 0:NQ], lhsT=ones_f[:], rhs=W3[:, 0:NQ], start=True, stop=True),
                         reads=[W3.r, ones_f.r], writes=[pss[0].r])
                    TSC("dve", W0[:, 0:NQ], pss[0][:, 0:NQ], 1.0 / 128, EPS, ALU.mult, ALU.add, [pss[0].r, W0.r], [W0.r])
                    P.op("act", lambda e: e.sqrt(out=W0[:, 0:NQ], in_=W0[:, 0:NQ]), reads=[W0.r], writes=[W0.r])
                    P.op("dve", lambda e: e.reciprocal(out=W0[:, 0:NQ], in_=W0[:, 0:NQ]), reads=[W0.r], writes=[W0.r])
                    STT(mixT[:, h, q0:q0 + NQ], W2[:, 0:NQ], sgp[:, 1:2], W0[:, 0:NQ], ALU.mult, ALU.mult, [W2.r, W0.r, sgp.r], [mix_r[h]])

                stage1(0)
                for i in range(len(steps)):
                    if i + 1 < len(steps):
                        stage1(i + 1)
                    stage2(i)
                P.barrier()
            P.cut(200)
            es_Q = contextlib.ExitStack()
            with es_Q:
                ptbc = S(es_Q, "ptbc", [128, NS * NPG], I32)
                iop = S(es_Q, "iop", [128, 1], I32)
                idx = S(es_Q, "idx", [128, NS * NPG], I32)
                ld("sp", ptbc[:], rawap(pt.tensor, 0, [[0, 128], [1, NS * NPG]]), [ptbc.r])
                P.op("pool", lambda e: e.iota(iop[:], pattern=[[0, 1]], base=0, channel_multiplier=1), writes=[iop.r])
                TSC("dve", idx[:], ptbc[:], 128, None, ALU.mult, None, [ptbc.r], [idx.r])
                TSC("dve", idx[:], idx[:], iop[:, 0:1], None, ALU.add, None, [idx.r, iop.r], [idx.r])
                Qblk = S(es_Q, "Qblk", [128, 4, 2, NS * TS], BF16)
                MS("pool", Qblk[:], 0.0, [Qblk.r])
                CP("act", Qblk[0:64, :, 0, :], qT[0:64, :, 32:96], [Qblk.r] + list(qT_r), [Qblk.r])
                CP("act", Qblk[64:128, :, 1, :], qT[64:128, :, 32:96], [Qblk.r] + list(qT_r), [Qblk.r])
                vnew2 = [S(es_Q, "vnew%d" % i, [128, 512], BF16) for i in range(2)]
                for _v in vnew2:
                    MS("pool", _v[:], 0.0, [_v.r])
                pnew = S(es_Q, "pnew", [128, 32], BF16)
                MS("pool", pnew[:], 0.0, [pnew.r])
                msk4 = S(es_Q, "msk4", [4, 32], F32)
                MS("pool", msk4[:], 1.0, [msk4.r])
                P.op("pool", lambda e: e.affine_select(out=msk4[:], in_=msk4[:], pattern=[[0, 2], [0, 4], [1, 4]], compare_op=ALU.is_ge, fill=0.0,
                                                       base=0, channel_multiplier=-1), reads=[msk4.r], writes=[msk4.r])
                e4 = S(es_Q, "e4", [4, 32], F32)
                att = S(es_Q, "att", [NS * TS, 512], F32)
                kvpg = [S(es_Q, "kvpg%d" % i, [128, 4, 1024], F32) for i in range(3)]
                vpg = [S(es_Q, "vpg%d" % i, [128, 4, 512], BF16) for i in range(6)]
                kTs = [S(es_Q, "kTs%d" % i, [128, 4, 128], BF16) for i in range(3)]
                pexp = [S(es_Q, "pexp%d" % i, [128, 512], BF16) for i in range(2)]
                rs = [S(es_Q, "rs%d" % i, [16, 4], F32) for i in range(2)]
                t0 = [S(es_Q, "t0_%d" % i, [16, 512], F32) for i in range(2)]
                o16 = [S(es_Q, "o16_%d" % i, [16, 512], F32) for i in range(2)]
                kvd = [P.new_dsem("kvd%d" % i) for i in range(3)]
                es_Q1 = contextlib.ExitStack()
                with es_Q1:
                    pk = [PS(es_Q1, "pk%d" % i, [128, 512], F32) for i in range(2)]
                    pS = [PS(es_Q1, "pS%d" % i, [128, 512], F32) for i in range(2)]
                    pSn = PS(es_Q1, "pSn", [128, 512], F32)
                    poc = [PS(es_Q1, "poc%d" % i, [128, 512], F32) for i in range(2)]
                    psm = PS(es_Q1, "psm", [128, 512], F32)

                    def gather(dst, sem, src, cols):
                        if P.dead:
                            return
                        waits = P._deps("pool", [idx.r], [dst.r])
                        fns = []
                        for n, col in enumerate(cols):
                            fns.append((n, col))
                        sem.val += 16 * len(cols)

                        def emit(e, waits=waits, fns=fns, dst=dst, sem=sem, src=src):
                            for ws, wv in waits:
                                e.wait_ge(ws, wv)
                            for n, col in fns:
                                e.indirect_dma_start(out=dst[:, n, :], out_offset=None, in_=src,
                                                     in_offset=bass.IndirectOffsetOnAxis(ap=idx[:, col:col + 1], axis=0)).then_inc(sem.sem, 16)
                        P.q["pool"].append(emit)
                        P._mark([idx.r], [dst.r], (sem.sem, sem.val))

                    pages = []
                    for s in range(NS):
                        for g4 in range(4):
                            gi_ = s * 4 + g4
                            for n in range(4):
                                pages.append((s, g4, n, gi_))

                    def stage1(j):
                        (s, g4, n, gi_) = pages[j]
                        kvb = kvpg[gi_ % 3]
                        if n == 0:
                            gather(kvb, kvd[gi_ % 3], ckv, [s * NPG + g4 * 4 + q for q in range(4)])
                        pk_ = pk[j % 2]
                        kt_ = kTs[j % 3]
                        for h in range(4):
                            TR(pk_[:, h * 128:(h + 1) * 128], kvb[:, n, h * 128:(h + 1) * 128], ident_f[:], [kvb.r, ident_f.r], [pk_.r])
                        CP("act" if j % 2 == 0 else "dve", kt_[:].rearrange("p h k -> p (h k)"), pk_[:], [pk_.r], [kt_.r])

                    def stage2(j):
                        (s, g4, n, gi_) = pages[j]
                        ps_ = pS[s % 2]
                        kt_ = kTs[j % 3]
                        base = (g4 * 4 + n) * 32
                        for h in range(4):
                            for c in range(2):
                                o0 = base + c * 16 + h * 4
                                MM(ps_[:, o0:o0 + 4], kt_[:, h, :], Qblk[:, h, c, 4 * s:4 * s + 4], True, True, [kt_.r, Qblk.r], [ps_.r])
                        if n == 3:
                            CP("dve", vpg[gi_ % 6][:], kvpg[gi_ % 3][:, :, 512:1024], [kvpg[gi_ % 3].r], [vpg[gi_ % 6].r])

                    def tail(s):
                        ps_ = pS[s % 2]
                        vnew = vnew2[s % 2]
                        MM(pSn[0:4, :], ident_b[:, 32 + 4 * s:36 + 4 * s], vb[:, 0, :], True, True, [ident_b.r] + list(vb_r), [pSn.r])
                        CP("act", vnew[0:4, :], pSn[0:4, :], [pSn.r, vnew.r], [vnew.r])
                        for h in range(4):
                            for c in range(2):
                                o0 = c * 16 + h * 4
                                MM(pSn[0:4, o0:o0 + 4], kT[:, h, 32 + 4 * s:36 + 4 * s], Qblk[:, h, c, 4 * s:4 * s + 4],
                                   True, True, [Qblk.r] + list(kT_r), [pSn.r])
                        ACT(e4[:], pSn[0:4, 0:32], AF.Exp, [pSn.r, e4.r], [e4.r])
                        TT("dve", pnew[0:4, :], e4[:], msk4[:], ALU.mult, [e4.r, msk4.r, pnew.r], [pnew.r])
                        px = pexp[s % 2]
                        ACT(px[:], ps_[:], AF.Exp, [ps_.r], [px.r])
                        for c in range(2):
                            for n in range(16):
                                vbf = vpg[(s * 4 + n // 4) % 6]
                                lh = px[:, n * 32 + c * 16:n * 32 + c * 16 + 16]
                                MM(poc[c][0:16, :], lh, vbf[:, n % 4, :], n == 0, False, [px.r, vbf.r], [poc[c].r])
                                MM(psm[0:16, c:c + 1], lh, ones_b[:, 0:1], n == 0, False, [px.r, ones_b.r], [psm.r])
                            lh = pnew[:, c * 16:(c + 1) * 16]
                            MM(poc[c][0:16, :], lh, vnew[:], False, True, [pnew.r, vnew.r], [poc[c].r])
                            MM(psm[0:16, c:c + 1], lh, ones_b[:, 0:1], False, True, [pnew.r, ones_b.r], [psm.r])
                        r_, t_, o_ = rs[s % 2], t0[s % 2], o16[s % 2]
                        P.op("dve", lambda e: e.reciprocal(out=r_[:, 0:2], in_=psm[0:16, 0:2]), reads=[psm.r, r_.r], writes=[r_.r])
                        TT("dve", r_[:, 2:3], r_[:, 1:2], lamt[0:16, 4:5], ALU.mult, [r_.r, lamt.r], [r_.r])
                        TSC("dve", t_[:], poc[0][0:16, :], r_[:, 0:1], None, ALU.mult, None, [poc[0].r, r_.r, t_.r], [t_.r])
                        STT(o_[:], poc[1][0:16, :], r_[:, 2:3], t_[:], ALU.mult, ALU.add, [poc[1].r, r_.r, t_.r, o_.r], [o_.r])
                        for h in range(4):
                            ld("sp", att[4 * s:4 * s + 4, h * 128:(h + 1) * 128], o_[4 * h:4 * h + 4, h * 128:(h + 1) * 128], [], reads=[o_.r])

                    stage1(0)
                    for j in range(len(pages)):
                        if j + 1 < len(pages):
                            stage1(j + 1)
                        stage2(j)
                        s, g4, n, gi_ = pages[j]
                        if g4 == 0 and n == 3 and s > 0:
                            tail(s - 1)
                    tail(NS - 1)
                    P.barrier()
                es_Q2 = contextlib.ExitStack()
                with es_Q2:
                    NQ4 = NS * TS
                    sq4 = S(es_Q2, "sq4", [NQ4, 4, 128], F32)
                    s4 = S(es_Q2, "s4", [NQ4, 3, 4], F32)
                    sg4 = S(es_Q2, "sg4", [NQ4, 128], F32)
                    attb = S(es_Q2, "attb", [NQ4, 512], BF16)
                    pT4 = PS(es_Q2, "pT4", [128, 4, NQ4], BF16)
                    ld("sp", sg4[:], rawap(sub_norm.tensor, 0, [[0, NQ4], [1, 128]]), [sg4.r])
                    TSC("dve", sg4[:], sg4[:], 1.0 - LAM_INIT, None, ALU.mult, None, [sg4.r], [sg4.r])
                    attv = att[:].rearrange("p (h e) -> p h e", h=4)
                    ACT(sq4[:], attv, AF.Square, [att.r], [sq4.r])
                    P.op("dve", lambda e: e.tensor_reduce(out=s4[:, 0, :], in_=sq4[:], axis=AX.X, op=ALU.add), reads=[sq4.r], writes=[s4.r])
                    TSC("dve", s4[:, 1, :], s4[:, 0, :], 1.0 / 128, EPS, ALU.mult, ALU.add, [s4.r], [s4.r])
                    P.op("act", lambda e: e.sqrt(out=s4[:, 2, :], in_=s4[:, 1, :]), reads=[s4.r], writes=[s4.r])
                    P.op("dve", lambda e: e.reciprocal(out=s4[:, 0, :], in_=s4[:, 2, :]), reads=[s4.r], writes=[s4.r])
                    TT("dve", sq4[:], attv, s4[:, 0, :].unsqueeze(2).to_broadcast([NQ4, 4, 128]), ALU.mult, [att.r, s4.r, sq4.r], [sq4.r])
                    TT("dve", attb[:].rearrange("p (h e) -> p h e", h=4), sq4[:], sg4[:].unsqueeze(1).to_broadcast([NQ4, 4, 128]), ALU.mult,
                       [sq4.r, sg4.r], [attb.r])
                    for h in range(4):
                        TR(pT4[:, h, :], attb[:, h * 128:(h + 1) * 128], ident_b[0:NQ4, 0:NQ4], [attb.r, ident_b.r], [pT4.r])
                    CP("act", mixT[:, 0:4, 32:96], pT4[:], [pT4.r], mix_r[0:4])
                    P.barrier()
            P.cut(50)
            es_B.close()
            TC = 258
            NM = L // TC
            PI = math.pi
            es_S = contextlib.ExitStack()
            with es_S:
                prm = S(es_S, "prm", [128, 3, 16], F32)
                sc = S(es_S, "sc", [128, 24, 16], F32)
                sci = S(es_S, "sci", [128, 16], I32)
                BT = S(es_S, "BT", [128, 4, 2, 128], BF16)
                CT = S(es_S, "CT", [128, 16, 2, 128], BF16)
                Dp = S(es_S, "Dp", [128, 8], F32)
                wg = S(es_S, "wg", [128, 4, 512], BF16)
                TCs = S(es_S, "TCs", [128, 16, TC], F32)
                TSn = S(es_S, "TSn", [128, 16, TC], F32)
                rq = S(es_S, "rq", [128, 16], F32)
                y32 = S(es_S, "y32", [128, TC], F32)
                g32 = [S(es_S, "g32_%d" % i, [128, TC], F32) for i in range(4)]
                gb = [S(es_S, "gb_%d" % i, [128, TC], BF16) for i in range(4)]
                sg = S(es_S, "sg", [128, TC], F32)
                es_P = contextlib.ExitStack()
                es_P.__enter__()
                pa = S(es_P, "pa", [16, 3, 128], F32)
                ldt = S(es_P, "ldt", [16, 2], F32)
                ld("sp", pa[:, 0, :], a_re, [pa.r])
                ld("sp", pa[:, 1, :], a_im, [pa.r])
                ld("sp", ldt[:], log_dt, [ldt.r])
                CP("dve", pa[:, 2, :].rearrange("p (g q) -> p g q", g=2), ldt[:].unsqueeze(2).to_broadcast([16, 2, 64]), [ldt.r, pa.r], [pa.r])
                ps0 = PS(es_S, "ps0", [128, 512], F32)
                for j in range(3):
                    TR(ps0[:, j * 16:(j + 1) * 16], pa[:, j, :], ident_f[0:16, 0:16], [pa.r, ident_f.r], [ps0.r])
                CP("act", prm[:].rearrange("p a b -> p (a b)"), ps0[:, 0:48], [ps0.r], [prm.r])
                P.cut(1)
                R = lambda k: sc[:, k, :]
                are, aim, ldtp = prm[:, 0, :], prm[:, 1, :], prm[:, 2, :]
                scr = [sc.r, prm.r]
                ACT(R(0), ldtp, AF.Exp, scr, [sc.r])
                TT("dve", R(1), are, R(0), ALU.mult, scr, [sc.r])
                ACT(R(2), R(1), AF.Exp, scr, [sc.r])
                TT("dve", R(3), aim, R(0), ALU.mult, scr, [sc.r])

                def sincos(x, s_out, c_out, t1, t2, ti, rd, wr):
                    C1 = 6.28125
                    C2 = 2 * PI - C1
                    TSC("dve", t1, x, 1.0 / (2 * PI), None, ALU.mult, None, rd, wr)
                    CP("dve", ti, t1, rd, wr)
                    CP("dve", t1, ti, rd, wr)
                    STT(t2, t1, -C1, x, ALU.mult, ALU.add, rd, wr)
                    STT(t2, t1, -C2, t2, ALU.mult, ALU.add, rd, wr)
                    TSC("dve", t1, t2, PI, -2 * PI, ALU.is_gt, ALU.mult, rd, wr)
                    TT("dve", t2, t2, t1, ALU.add, rd, wr)
                    TSC("dve", t1, t2, -PI, 2 * PI, ALU.is_lt, ALU.mult, rd, wr)
                    TT("dve", t2, t2, t1, ALU.add, rd, wr)
                    ACT(s_out, t2, AF.Sin, rd, wr)
                    TSC("dve", t2, t2, PI / 2, None, ALU.add, None, rd, wr)
                    TSC("dve", t1, t2, PI, -2 * PI, ALU.is_gt, ALU.mult, rd, wr)
                    TT("dve", t2, t2, t1, ALU.add, rd, wr)
                    ACT(c_out, t2, AF.Sin, rd, wr)
                sincos(R(3), R(4), R(5), R(6), R(7), sci[:], scr + [sci.r], [sc.r, sci.r])
                AR, AI = R(8), R(9)
                TT("dve", AR, R(2), R(5), ALU.mult, scr, [sc.r])
                TT("dve", AI, R(2), R(4), ALU.mult, scr, [sc.r])
                TT("dve", R(10), are, are, ALU.mult, scr, [sc.r])
                TT("dve", R(11), aim, aim, ALU.mult, scr, [sc.r])
                TT("dve", R(10), R(10), R(11), ALU.add, scr, [sc.r])
                P.op("dve", lambda e: e.reciprocal(out=R(10), in_=R(10)), reads=scr, writes=[sc.r])
                TSC("dve", R(11), AR, -1.0, None, ALU.add, None, scr, [sc.r])
                TT("dve", R(12), R(11), are, ALU.mult, scr, [sc.r])
                TT("dve", R(13), AI, aim, ALU.mult, scr, [sc.r])
                TT("dve", R(12), R(12), R(13), ALU.add, scr, [sc.r])
                TT("dve", R(12), R(12), R(10), ALU.mult, scr, [sc.r])
                TT("dve", R(13), AI, are, ALU.mult, scr, [sc.r])
                TT("dve", R(14), R(11), aim, ALU.mult, scr, [sc.r])
                TT("dve", R(13), R(13), R(14), ALU.subtract, scr, [sc.r])
                TT("dve", R(13), R(13), R(10), ALU.mult, scr, [sc.r])
                GR, GI = R(12), R(13)

                P.cut(2)
                BN = S(es_P, "BN", [128, 4, 256], F32)
                es_L = contextlib.ExitStack()
                with es_L:
                    Bld = S(es_L, "Bld", [16, 4, 2048], F32)
                    Cl2 = S(es_L, "Cl2", [16, 2, 2048], F32)
                    for j, src in enumerate((b_re, b_im, c_re, c_im)):
                        ld("sp", Bld[:, j, :], src, [Bld.r])
                    for ri in range(2):
                        CP("pool", Cl2[:, ri, :].rearrange("p (c g q) -> p c g q", c=16, g=2),
                           Bld[:, 2 + ri, :].rearrange("p (g c q) -> p c g q", g=2, c=16), [Bld.r, Cl2.r], [Cl2.r])
                    for j in range(4):
                        for c in range(16):
                            if j < 2:
                                src_ap = rawap(Bld, j * 2048 + c, [[4 * 2048, 16], [16, 128]])
                            else:
                                src_ap = Cl2[:, j - 2, c * 128:(c + 1) * 128]
                            TR(ps0[:, c * 16:(c + 1) * 16], src_ap, ident_f[0:16, 0:16], [Bld.r, Cl2.r, ident_f.r], [ps0.r])
                        CP("act", BN[:, j, :], ps0[:, 0:256], [ps0.r], [BN.r])
                    P.barrier()
                P.cut(3)
                Bv = lambda j: BN[:, j, :].rearrange("p (c i) -> p c i", c=16)
                BB = S(es_P, "BB", [128, 4, 256], F32)
                BBv = lambda j: BB[:, j, :].rearrange("p (c i) -> p c i", c=16)
                gbc = lambda g: g.unsqueeze(1).to_broadcast([128, 16, 16])
                rdB = [BN.r, BB.r, sc.r]
                TT("dve", BBv(0), Bv(0), gbc(GR), ALU.mult, rdB, [BB.r])
                TT("dve", BBv(2), Bv(1), gbc(GI), ALU.mult, rdB, [BB.r])
                TT("dve", BBv(0), BBv(0), BBv(2), ALU.subtract, rdB, [BB.r])
                TT("dve", BBv(1), Bv(1), gbc(GR), ALU.mult, rdB, [BB.r])
                TT("dve", BBv(2), Bv(0), gbc(GI), ALU.mult, rdB, [BB.r])
                TT("dve", BBv(1), BBv(1), BBv(2), ALU.add, rdB, [BB.r])
                MASK = S(es_P, "MASK", [128, 16, 2, 16], F32)
                MS("pool", MASK[:], 0.0, [MASK.r])
                MS("pool", MASK[0:64, :, 0, :], 1.0, [MASK.r])
                MS("pool", MASK[64:128, :, 1, :], 1.0, [MASK.r])
                Z4 = S(es_P, "Z4", [128, 16, 2, 16], F32)
                for ri in range(2):
                    TT("dve", Z4[:], rawap(BB, ri * 256, [[1024, 128], [1, 16], [0, 2], [16, 16]]), MASK[:], ALU.mult,
                       [BB.r, MASK.r, Z4.r], [Z4.r])
                    for k in range(4):
                        TR(ps0[:, k * 128:(k + 1) * 128], Z4[:, 4 * k:4 * k + 4, :, :].rearrange("p a b c -> p (a b c)"), ident_f[:],
                           [Z4.r, ident_f.r], [ps0.r])
                    CP("act", BT[:, :, ri, :], ps0[:].rearrange("p (k m) -> p k m", k=4), [ps0.r], [BT.r])
                P.cut(4)
                CTf = S(es_P, "CTf", [128, 16, 2, 128], F32)
                MS("pool", CTf[:], 0.0, [CTf.r])
                for ri in range(2):
                    for il in range(4):
                        outv = rawap(CTf, ri * 128 + il * 32 + il * 256, [[4096, 128], [4 * 256, 4], [16, 2], [1, 16]])
                        inv = rawap(BN, (2 + ri) * 256 + il, [[1024, 128], [4, 4], [0, 2], [16, 16]])
                        mk = MASK[:, 0:4, :, :]
                        TT("dve", outv, inv, mk, ALU.mult, [BN.r, MASK.r, CTf.r], [CTf.r])
                CP("pool", CT[:, :, 0, :], CTf[:, :, 0, :], [CTf.r], [CT.r])
                TSC("dve", CT[:, :, 1, :], CTf[:, :, 1, :], -1.0, None, ALU.mult, None, [CTf.r, CT.r], [CT.r])
                P.cut(5)
                dld = S(es_P, "dld", [4, 2, 128], F32)
                ld("sp", dld[:, 0, :], ssm_d, [dld.r])
                ld("sp", dld[:, 1, :], b_glu, [dld.r])
                TR(ps0[:, 0:4], dld[:, 0, :], ident_f[0:4, 0:4], [dld.r, ident_f.r], [ps0.r])
                TR(ps0[:, 4:8], dld[:, 1, :], ident_f[0:4, 0:4], [dld.r, ident_f.r], [ps0.r])
                CP("act", Dp[:], ps0[:, 0:8], [ps0.r], [Dp.r])
                ld("pool", wg[:], w_glu.rearrange("(kc p) n -> p kc n", p=128), [wg.r])
                P.cut(6)
                es_T = contextlib.ExitStack()
                with es_T:
                    NI = S(es_T, "NI", [128, TC], I32)
                    NF = S(es_T, "NF", [128, TC], F32)
                    P.op("pool", lambda e: e.iota(NI[:], pattern=[[1, TC]], base=1, channel_multiplier=0), writes=[NI.r])
                    CP("dve", NF[:], NI[:], [NI.r], [NF.r])
                    TA = S(es_T, "TA", [128, 16, TC], F32)
                    T1 = S(es_T, "T1", [128, 16, TC], F32)
                    T2 = S(es_T, "T2", [128, 16, TC], F32)
                    TI = S(es_T, "TI", [128, 16, TC], I32)
                    TT("dve", TA[:], R(3).unsqueeze(2).to_broadcast([128, 16, TC]), NF[:].unsqueeze(1).to_broadcast([128, 16, TC]), ALU.mult,
                       [sc.r, NF.r], [TA.r])
                    rr = [TA.r, T1.r, T2.r, TI.r, TCs.r, TSn.r]
                    sincos(TA[:], TSn[:], TCs[:], T1[:], T2[:], TI[:], rr, rr)
                    P.barrier()

                P.cut(7)
                P.barrier()
                es_P.close()
                es_M = contextlib.ExitStack()
                es_M.__enter__()
                NB2 = 2
                pXr = [PS(es_S, "pXr%d" % i, [128, 512], F32) for i in range(NB2)]
                pXi = [PS(es_S, "pXi%d" % i, [128, 512], F32) for i in range(NB2)]
                pY = PS(es_S, "pY", [128, 512], F32)
                pZ = PS(es_S, "pZ", [128, 512], F32)
                wk = [S(es_M, "wk%d" % i, [128, 8, TC], F32) for i in range(NB2)]
                wk2 = [S(es_M, "wk2%d" % i, [128, 4, TC], F32) for i in range(NB2)]
                H32 = [S(es_M, "H32_%d" % i, [128, 2, TC], F32) for i in range(16)]
                Hb = [S(es_M, "Hb%d" % i, [128, 2, TC], BF16) for i in range(4)]
                CP("dve", rq[:], R(2), [sc.r], [rq.r])

                def glu(N, outs):
                    for kq in range(4):
                        for kc in range(4):
                            MM(pZ[:, 0:N], wg[:, kc, kq * 128:(kq + 1) * 128], gb[kc][:, 0:N], kc == 0, kc == 3, [wg.r, gb[kc].r], [pZ.r])
                        ACT(sg[:, 0:N], pZ[:, 0:N], AF.Sigmoid, [pZ.r, Dp.r], [sg.r], bias=Dp[:, 4 + kq:5 + kq])
                        for (sl, dst, dres) in outs[kq]:
                            TT("pool", dst, g32[kq][:, sl], sg[:, sl], ALU.mult, [g32[kq].r, sg.r], [dres])

                def y_finish(k, N, ucols):
                    STT(y32[:, 0:N], uT[:, k, ucols], Dp[:, k:k + 1], pY[:, 0:N], ALU.mult, ALU.add, [uT_r[0], pY.r, Dp.r], [y32.r])
                    ACT(g32[k][:, 0:N], y32[:, 0:N], AF.Gelu_apprx_tanh, [y32.r], [g32[k].r])
                    CP("pool", gb[k][:, 0:N], g32[k][:, 0:N], [g32[k].r], [gb[k].r])

                uT_all = list(uT_r)
                for m in range(NM):
                    c0 = m * TC
                    for k in range(4):
                        for il in range(4):
                            i = 4 * k + il
                            b = i % NB2
                            W = wk[b]
                            W2 = wk2[b]
                            MM(pXr[b][:, 0:TC], BT[32 * il:32 * il + 32, k, 0, :], uT[32 * il:32 * il + 32, k, c0:c0 + TC], True, True,
                               [BT.r] + uT_all, [pXr[b].r], tp=(32 * il, 0))
                            MM(pXi[b][:, 0:TC], BT[32 * il:32 * il + 32, k, 1, :], uT[32 * il:32 * il + 32, k, c0:c0 + TC], True, True,
                               [BT.r] + uT_all, [pXi[b].r], tp=(32 * il, 0))
                            cs_, sn_ = TCs[:, i, :], TSn[:, i, :]
                            rd = [pXr[b].r, pXi[b].r, TCs.r, TSn.r, W.r]
                            TT("dve", W[:, 0, :], pXr[b][:, 0:TC], cs_, ALU.mult, rd, [W.r])
                            TT("dve", W[:, 1, :], pXi[b][:, 0:TC], sn_, ALU.mult, rd, [W.r])
                            TT("dve", W[:, 4, :], W[:, 0, :], W[:, 1, :], ALU.add, rd, [W.r])
                            TT("dve", W[:, 2, :], pXi[b][:, 0:TC], cs_, ALU.mult, rd, [W.r])
                            TT("dve", W[:, 3, :], pXr[b][:, 0:TC], sn_, ALU.mult, rd, [W.r])
                            TT("dve", W[:, 5, :], W[:, 2, :], W[:, 3, :], ALU.subtract, rd, [W.r])
                            for ri in range(2):
                                init = 0.0 if m == 0 else H32[i][:, ri, TC - 1:TC]
                                P.op("dve", lambda e, W=W, ri=ri, init=init, i=i: e.tensor_tensor_scan(
                                    out=W[:, 6 + ri, :], data0=rq[:, i:i + 1].to_broadcast([128, TC]), data1=W[:, 4 + ri, :],
                                    initial=init, op0=ALU.mult, op1=ALU.add), reads=[W.r, rq.r, H32[i].r], writes=[W.r])
                            rd2 = [W.r, W2.r, TCs.r, TSn.r]
                            TT("pool", W2[:, 0, :], W[:, 6, :], cs_, ALU.mult, rd2, [W2.r])
                            TT("pool", W2[:, 1, :], W[:, 7, :], sn_, ALU.mult, rd2, [W2.r])
                            TT("pool", H32[i][:, 0, :], W2[:, 0, :], W2[:, 1, :], ALU.subtract, rd2 + [H32[i].r], [H32[i].r])
                            TT("pool", W2[:, 2, :], W[:, 7, :], cs_, ALU.mult, rd2, [W2.r])
                            TT("pool", W2[:, 3, :], W[:, 6, :], sn_, ALU.mult, rd2, [W2.r])
                            TT("pool", H32[i][:, 1, :], W2[:, 2, :], W2[:, 3, :], ALU.add, rd2 + [H32[i].r], [H32[i].r])
                            CP("act", Hb[il][:], H32[i][:], [H32[i].r], [Hb[il].r])
                        n = 0
                        for il in range(4):
                            for ri in range(2):
                                MM(pY[:, 0:TC], CT[:, 4 * k + il, ri, :], Hb[il][:, ri, :], n == 0, n == 7, [CT.r, Hb[il].r], [pY.r])
                                n += 1
                        y_finish(k, TC, slice(c0, c0 + TC))
                    outs = []
                    for kq in range(4):
                        if m == 0:
                            o = [(slice(0, 16), mixT[:, 4 + kq, 0:16], mix_r[4 + kq]),
                                 (slice(16, TC), mixT[:, 4 + kq, 128:128 + TC - 16], mix_r[4 + kq])]
                        else:
                            d0 = 128 + c0 - 16
                            o = [(slice(0, TC), mixT[:, 4 + kq, d0:d0 + TC], mix_r[4 + kq])]
                        outs.append(o)
                    glu(TC, outs)
                P.cut(9)
                FP = S(es_M, "FP", [128, 2, 16], F32)
                for i in range(16):
                    CP("dve", FP[:, :, i:i + 1], H32[i][:, :, TC - 1:TC], [H32[i].r, FP.r], [FP.r])
                fpo = S(es_M, "fpo", [16, 2, 128], F32)
                for ri in range(2):
                    TR(ps0[0:16, ri * 128:(ri + 1) * 128], FP[:, ri, :], ident_f[:], [FP.r, ident_f.r], [ps0.r])
                CP("act", fpo[:].rearrange("p a b -> p (a b)"), ps0[0:16, 0:256], [ps0.r], [fpo.r])
                ld("sp", srp, fpo[:, 0, :], [], reads=[fpo.r])
                ld("sp", sip, fpo[:, 1, :], [], reads=[fpo.r])

                P.cut(10)
                P.barrier()
                es_M.close()
                NSC = NS * TS
                XS = S(es_S, "XS", [128, 2, 16, NSC], F32)
                banks = [pXr[0], pXr[1], pXi[0], pXi[1]]
                for ri in range(2):
                    for il in range(4):
                        for k in range(4):
                            MM(banks[il][:, k * NSC:(k + 1) * NSC], BT[32 * il:32 * il + 32, k, ri, :], uT[32 * il:32 * il + 32, k, L:L + NSC],
                               True, True, [BT.r] + uT_all, [banks[il].r], tp=(32 * il, 0))
                    for il in range(4):
                        CP("act", XS[:, ri, :, :].rearrange("p (k il) c -> p k il c", il=4)[:, :, il, :],
                           banks[il][:, 0:4 * NSC].rearrange("p (a b) -> p a b", a=4), [banks[il].r, XS.r], [XS.r])
                P.cut(10.1)
                H0l = S(es_S, "H0l", [16, 2, 2048], F32)
                ld("sp", H0l[:, 0, :], s_re0, [H0l.r])
                ld("sp", H0l[:, 1, :], s_im0, [H0l.r])
                HS = S(es_S, "HS", [128, 2, 16, NS, TS + 1], F32)
                for ri in range(2):
                    for i in range(16):
                        TR(ps0[:, i * 16:(i + 1) * 16], H0l[:, ri, i * 128:(i + 1) * 128], ident_f[0:16, 0:16], [H0l.r, ident_f.r], [ps0.r])
                    CP("act", HS[:, ri, :, :, 0], ps0[:, 0:256].rearrange("p (a b) -> p a b", a=16), [ps0.r, HS.r], [HS.r])
                P.cut(10.2)
                M4 = S(es_S, "M4", [128, 4, 16, NS], F32)
                abc = lambda a: a.unsqueeze(2).to_broadcast([128, 16, NS])
                XSv = lambda ri, t: XS[:, ri, :, :].rearrange("p a (s t) -> p a s t", t=TS)[:, :, :, t]
                rdh = [HS.r, M4.r, XS.r, sc.r]
                for t in range(TS):
                    hr, hi = HS[:, 0, :, :, t], HS[:, 1, :, :, t]
                    TT("dve", M4[:, 0], hr, abc(AR), ALU.mult, rdh, [M4.r])
                    TT("dve", M4[:, 1], hi, abc(AI), ALU.mult, rdh, [M4.r])
                    TT("dve", M4[:, 0], M4[:, 0], M4[:, 1], ALU.subtract, rdh, [M4.r])
                    TT("dve", HS[:, 0, :, :, t + 1], M4[:, 0], XSv(0, t), ALU.add, rdh, [HS.r])
                    TT("dve", M4[:, 2], hi, abc(AR), ALU.mult, rdh, [M4.r])
                    TT("dve", M4[:, 3], hr, abc(AI), ALU.mult, rdh, [M4.r])
                    TT("dve", M4[:, 2], M4[:, 2], M4[:, 3], ALU.add, rdh, [M4.r])
                    TT("dve", HS[:, 1, :, :, t + 1], M4[:, 2], XSv(1, t), ALU.add, rdh, [HS.r])
                P.cut(10.3)
                HSb = S(es_S, "HSb", [128, 2, 16, NS, TS], BF16)
                for ri in range(2):
                    CP("pool", HSb[:, ri], HS[:, ri, :, :, 1:TS + 1], [HS.r, HSb.r], [HSb.r])
                P.cut(10.4)
                for k in range(4):
                    n = 0
                    for il in range(4):
                        for ri in range(2):
                            MM(pY[:, 0:NSC], CT[:, 4 * k + il, ri, :], HSb[:, ri, 4 * k + il].rearrange("p s t -> p (s t)"), n == 0, n == 7,
                               [CT.r, HSb.r], [pY.r])
                            n += 1
                    y_finish(k, NSC, slice(L, L + NSC))
                glu(NSC, [[(slice(0, NSC), mixT[:, 4 + kq, 32:32 + NSC], mix_r[4 + kq])] for kq in range(4)])
                P.cut(10.5)
                fso = S(es_S, "fso", [16, 2, 2048], F32)
                for ri in range(2):
                    for q4 in range(4):
                        for i4 in range(4):
                            i = q4 * 4 + i4
                            TR(ps0[0:16, i4 * 128:(i4 + 1) * 128], HS[:, ri, i, :, TS], ident_f[:], [HS.r, ident_f.r], [ps0.r])
                        CP("act", fso[:, ri, q4 * 512:(q4 + 1) * 512], ps0[0:16, :], [ps0.r, fso.r], [fso.r])
                ld("sp", srs, fso[:, 0, :], [], reads=[fso.r])
                ld("sp", sis, fso[:, 1, :], [], reads=[fso.r])
                P.barrier()
            P.cut(300)
            es_U.close()
            es_C = contextlib.ExitStack()
            with es_C:
                x1 = S(es_C, "x1", [128, NT, D], F32)
                x1_r = [Res("x1_%d" % i) for i in range(NT)]
                xn2T = S(es_C, "xn2T", [128, 8, NCOL], BF16)
                xn2_r = [Res("xn2_%d" % i) for i in range(NT)]
                cwb = S(es_C, "cwb", [128, 4, NFC], F32)
                es_C1 = contextlib.ExitStack()
                with es_C1:
                    g2b = S(es_C1, "g2b", [128, D], F32)
                    ld("sp", g2b[:], rawap(norm2.tensor, 0, [[0, 128], [1, D]]), [g2b.r])
                    wo = S(es_C1, "wo", [128, 8, D], BF16)
                    ld("pool", wo[:], w_out.rearrange("(kc p) n -> p kc n", p=128), [wo.r])
                    cwl = S(es_C1, "cwl", [NFC, 4, 128], F32)
                    for j in range(3):
                        ld("sp", cwl[:, j, :], conv_w[j:j + 1, :].rearrange("o (c f) -> (o c) f", f=128), [cwl.r])
                    ld("sp", cwl[:, 3, :], conv_b.rearrange("o (c f) -> (o c) f", f=128), [cwl.r])
                    pc0 = PS(es_C1, "pcw0", [128, 512], F32)
                    for j in range(4):
                        TR(pc0[:, j * NFC:(j + 1) * NFC], cwl[:, j, :], ident_f[0:NFC, 0:NFC], [cwl.r, ident_f.r], [pc0.r])
                    CP("act", cwb[:].rearrange("p a b -> p (a b)"), pc0[:, 0:4 * NFC], [pc0.r], [cwb.r])
                    NB = 2
                    xt = [S(es_C1, "cxt%d" % i, [128, D], F32) for i in range(NB)]
                    junk = S(es_C1, "cjunk", [128, D], BF16)
                    ss = [S(es_C1, "cssq%d" % i, [128, 4], F32) for i in range(NB)]
                    xn = [S(es_C1, "cxn%d" % i, [128, D], BF16) for i in range(NB)]
                    pw = [PS(es_C1, "pw%d" % i, [128, 1024], F32) for i in range(2)]
                    pT2 = [PS(es_C1, "pT2%d" % i, [128, 8, 128], BF16) for i in range(2)]
                    for tt in range(NT):
                        b = tt % NB
                        X = xt[b]
                        if tt == 0:
                            MS("pool", X[:], 0.0, [X.r])
                            ld("sp", X[0:16, :], meta, [X.r])
                            r2 = Res("cx0b")
                            ld("sp", X[32:96, :], xs, [r2], reads=[X.r])
                            xr = [X.r, r2]
                        else:
                            ld("sp", X[:], xp[(tt - 1) * 128:tt * 128, :], [X.r])
                            xr = [X.r]
                        pw_ = pw[tt % 2]
                        for nb in range(2):
                            for kc in range(8):
                                MM(pw_[:, nb * 512:(nb + 1) * 512], mixT[:, kc, tt * 128:(tt + 1) * 128], wo[:, kc, nb * 512:(nb + 1) * 512],
                                   kc == 0, kc == 7, [wo.r] + mix_r, [pw_.r])
                        for nb in range(2):
                            sl = slice(nb * 512, (nb + 1) * 512)
                            TT("dve", x1[:, tt, sl], pw_[:, sl], X[:, sl], ALU.add, [pw_.r] + xr + [x1_r[tt]], [x1_r[tt]])
                        P.op("act", lambda e, tt=tt, b=b: e.activation(out=junk[:], in_=x1[:, tt, :], func=AF.Square, accum_out=ss[b][:, 0:1]),
                             reads=[x1_r[tt]], writes=[junk.r, ss[b].r])
                        TSC("dve", ss[b][:, 1:2], ss[b][:, 0:1], 1.0 / D, EPS, ALU.mult, ALU.add, [ss[b].r], [ss[b].r])
                        P.op("act", lambda e, b=b: e.sqrt(out=ss[b][:, 2:3], in_=ss[b][:, 1:2]), reads=[ss[b].r], writes=[ss[b].r])
                        P.op("dve", lambda e, b=b: e.reciprocal(out=ss[b][:, 3:4], in_=ss[b][:, 2:3]), reads=[ss[b].r], writes=[ss[b].r])
                        STT(xn[b][:], x1[:, tt, :], ss[b][:, 3:4], g2b[:], ALU.mult, ALU.mult, [x1_r[tt], ss[b].r, g2b.r], [xn[b].r])
                        pt_ = pT2[tt % 2]
                        for kc in range(8):
                            TR(pt_[:, kc, :], xn[b][:, kc * 128:(kc + 1) * 128], ident_b[:], [xn[b].r, ident_b.r], [pt_.r])
                        CP("act", xn2T[:, :, tt * 128:(tt + 1) * 128], pt_[:], [pt_.r], [xn2_r[tt]])
                    P.barrier()
                P.cut(350)
                parts = [list(range(0, 8)), list(range(8, 15)), list(range(15, 22))]
                hT = mixT
                wd = S(es_C, "wd", [128, 8, D], BF16)
                Ab = S(es_C, "Ab", [128, L + 2], F32)
                As = S(es_C, "As", [128, NS, TS + 2], F32)
                Gt = S(es_C, "Gt", [128, NCOL], F32)
                wgu = [S(es_C, "wgu%d" % i, [128, 2, 8, 128], BF16) for i in range(2)]
                cst = S(es_C, "cst", [128, NFC, 2], F32)
                css = S(es_C, "css", [128, NFC, NS, 2], F32)
                hist = S(es_C, "hist", [NS * 2, 128], F32)
                MS("pool", Ab[:, 0:2], 0.0, [Ab.r])
                MS("pool", Gt[:], 0.0, [Gt.r])
                pa_ = [PS(es_C, "pa%d" % i, [128, 512], F32) for i in range(2)]
                pc_ = [PS(es_C, "pc%d" % i, [128, 512], F32) for i in range(2)]
                pd = [PS(es_C, "pd%d" % i, [128, 1024], F32) for i in range(1)]
                ph = PS(es_C, "ph", [128, 512], F32)
                ost = [S(es_C, "ost%d" % i, [128, D], F32) for i in range(1)]
                stg = [S(es_C, "stg%d" % i, [NS * 2, 512], F32) for i in range(2)]
                w_gate_v = w_gate.rearrange("(kc p) n -> p kc n", p=128)
                w_up_v = w_up.rearrange("(kc p) n -> p kc n", p=128)
                w_down_v = w_down.rearrange("(fc p) n -> p fc n", p=128)
                xn2_all = list(xn2_r)
                hT_r = Res("hT")
                colgroups = [(0, 512), (512, 512), (1024, 512), (1536, 512), (2048, 128)]
                for pi, fcs in enumerate(parts):
                    for j, fc in enumerate(fcs):
                        ld("pool", wd[:, j, :], w_down_v[:, fc, :], [wd.r])
                    for j, fc in enumerate(fcs):
                        wb = wgu[fc % 2]
                        ld("pool", wb[:, 0], w_gate_v[:, :, fc * 128:(fc + 1) * 128], [wb.r])
                        rwb2 = Res("wb2")
                        ld("pool", wb[:, 1], w_up_v[:, :, fc * 128:(fc + 1) * 128], [rwb2], reads=[wb.r])
                        wrd = [wb.r, rwb2]
                        ld("sp", hist[:], conv0[:, fc * 128:(fc + 1) * 128], [hist.r])
                        TR(ph[:, 0:NS * 2], hist[:], ident_f[0:NS * 2, 0:NS * 2], [hist.r, ident_f.r], [ph.r])
                        CP("act", As[:, :, 0:2], ph[:, 0:NS * 2].rearrange("p (s j) -> p s j", j=2), [ph.r, As.r], [As.r])
                        for gi, (c0, N) in enumerate(colgroups):
                            pa = pa_[gi % 2]
                            for kc in range(8):
                                MM(pa[:, 0:N], wb[:, 0, kc, :], xn2T[:, kc, c0:c0 + N], kc == 0, kc == 7, wrd + xn2_all, [pa.r])
                            if gi == 0:
                                CP("act", Ab[:, 2:18], pa[:, 0:16], [pa.r, Ab.r], [Ab.r])
                                CP("act", As[:, :, 2:2 + TS], pa[:, 32:96].rearrange("p (s t) -> p s t", t=TS), [pa.r, As.r], [As.r])
                                CP("act", Ab[:, 18:18 + 384], pa[:, 128:512], [pa.r, Ab.r], [Ab.r])
                            else:
                                CP("act", Ab[:, c0 - 110:c0 - 110 + N], pa[:, 0:N], [pa.r, Ab.r], [Ab.r])
                        w0, w1, w2, bb = (cwb[:, q, fc:fc + 1] for q in range(4))
                        rdc = [Ab.r, Gt.r, cwb.r]
                        ACT(Gt[:, 112:112 + L], Ab[:, 0:L], AF.Identity, rdc, [Gt.r], bias=bb, scale=w0)
                        STT(Gt[:, 112:112 + L], Ab[:, 1:L + 1], w1, Gt[:, 112:112 + L], ALU.mult, ALU.add, rdc, [Gt.r])
                        STT(Gt[:, 112:112 + L], Ab[:, 2:L + 2], w2, Gt[:, 112:112 + L], ALU.mult, ALU.add, rdc, [Gt.r])
                        ACT(Gt[:, 112:112 + L], Gt[:, 112:112 + L], AF.Gelu_apprx_tanh, rdc, [Gt.r])
                        CP("pool", Gt[:, 0:16], Gt[:, 112:128], [Gt.r], [Gt.r])
                        Gs = Gt[:, 32:96].rearrange("p (s t) -> p s t", t=TS)
                        rds = [As.r, Gt.r, cwb.r]
                        ACT(Gs, As[:, :, 0:TS], AF.Identity, rds, [Gt.r], bias=bb, scale=w0)
                        STT(Gs, As[:, :, 1:TS + 1], w1, Gs, ALU.mult, ALU.add, rds, [Gt.r])
                        STT(Gs, As[:, :, 2:TS + 2], w2, Gs, ALU.mult, ALU.add, rds, [Gt.r])
                        ACT(Gs, Gs, AF.Gelu_apprx_tanh, rds, [Gt.r])
                        for gi, (c0, N) in enumerate(colgroups):
                            pc = pc_[gi % 2]
                            for kc in range(8):
                                MM(pc[:, 0:N], wb[:, 1, kc, :], xn2T[:, kc, c0:c0 + N], kc == 0, kc == 7, wrd + xn2_all, [pc.r])
                            TT("dve", hT[:, j, c0:c0 + N], Gt[:, c0:c0 + N], pc[:, 0:N], ALU.mult, [Gt.r, pc.r, hT_r], [hT_r])
                        CP("pool", cst[:, fc, :], Ab[:, L:L + 2], [Ab.r, cst.r], [cst.r])
                        CP("pool", css[:, fc, :, :], As[:, :, TS:TS + 2], [As.r, css.r], [css.r])
                    lastp = (pi == len(parts) - 1)
                    for tt in range(NT):
                        pd_ = pd[0]
                        for nb in range(2):
                            for j in range(len(fcs)):
                                MM(pd_[:, nb * 512:(nb + 1) * 512], hT[:, j, tt * 128:(tt + 1) * 128], wd[:, j, nb * 512:(nb + 1) * 512],
                                   j == 0, j == len(fcs) - 1, [hT_r, wd.r], [pd_.r])
                        if not lastp:
                            for nb in range(2):
                                sl = slice(nb * 512, (nb + 1) * 512)
                                TT("dve", x1[:, tt, sl], pd_[:, sl], x1[:, tt, sl], ALU.add, [pd_.r, x1_r[tt]], [x1_r[tt]])
                        else:
                            o_ = ost[0]
                            for nb in range(2):
                                sl = slice(nb * 512, (nb + 1) * 512)
                                TT("dve", o_[:, sl], pd_[:, sl], x1[:, tt, sl], ALU.add, [pd_.r, x1_r[tt], o_.r], [o_.r])
                            if tt == 0:
                                ld("sp", ys[:, :], o_[32:96, :], [], reads=[o_.r])
                            else:
                                ld("sp", yp[(tt - 1) * 128:tt * 128, :], o_[:], [], reads=[o_.r])
                for q4 in range(6):
                    nch = min(4, NFC - 4 * q4)
                    for j in range(nch):
                        fc = 4 * q4 + j
                        TR(ph[0:2, j * 128:(j + 1) * 128], cst[:, fc, :], ident_f[:], [cst.r, ident_f.r], [ph.r])
                    CP("act", stg[0][0:2, 0:nch * 128], ph[0:2, 0:nch * 128], [ph.r, stg[0].r], [stg[0].r])
                    ld("sp", cp[:, q4 * 512:q4 * 512 + nch * 128], stg[0][0:2, 0:nch * 128], [], reads=[stg[0].r])
                    for j in range(nch):
                        fc = 4 * q4 + j
                        TR(pa_[0][0:NS * 2, j * 128:(j + 1) * 128], css[:, fc, :, :].rearrange("p s j -> p (s j)"), ident_f[:],
                           [css.r, ident_f.r], [pa_[0].r])
                    CP("act", stg[1][:, 0:nch * 128], pa_[0][0:NS * 2, 0:nch * 128], [pa_[0].r, stg[1].r], [stg[1].r])
                    ld("sp", cs[:, q4 * 512:q4 * 512 + nch * 128], stg[1][:, 0:nch * 128], [], reads=[stg[1].r])
                P.barrier()
        P.barrier()
        P._free = [d for d in P._dmap.values()]
        P._dmap = {}

    for vc in range(nv):
        emit_vc(vc)
    if True:
        P.run()
    return nc


def make_in_maps(inp, n_cores, npool, nv=1):
    f = lambda a: np.ascontiguousarray(np.asarray(a))
    ckv = np.concatenate([np.asarray(inp["cache_k"]).reshape(npool * 128, 512),
                          np.asarray(inp["cache_v"]).reshape(npool * 128, 512)], axis=1)
    shared = {
        "meta": f(inp["meta_tokens"]), "ckv": ckv,
        "norm1": f(inp["norm1"]).reshape(1, D), "norm2": f(inp["norm2"]).reshape(1, D),
        "w_in": f(inp["w_in"])[0], "w_out": f(inp["w_out"])[0],
        "q_norm": f(inp["q_norm"]).reshape(1, 128), "k_norm": f(inp["k_norm"]).reshape(1, 128),
        "lam_q": f(inp["lam_q"]).reshape(1, 128), "lam_k": f(inp["lam_k"]).reshape(1, 128),
        "sub_norm": f(inp["sub_norm"]).reshape(1, 128),
        "a_re": f(inp["ssm_a_re"]).reshape(16, 128), "a_im": f(inp["ssm_a_im"]).reshape(16, 128),
        "log_dt": f(inp["ssm_log_dt"]).reshape(16, 2),
        "b_re": f(inp["ssm_b_re"]).reshape(16, 2048), "b_im": f(inp["ssm_b_im"]).reshape(16, 2048),
        "c_re": f(inp["ssm_c_re"]).reshape(16, 2048), "c_im": f(inp["ssm_c_im"]).reshape(16, 2048),
        "ssm_d": f(inp["ssm_d"]).reshape(4, 128),
        "w_glu": f(inp["w_glu"])[0], "b_glu": f(inp["b_glu"]).reshape(4, 128),
        "w_gate": f(inp["w_gate"])[0], "w_up": f(inp["w_up"])[0], "w_down": f(inp["w_down"])[0],
        "conv_w": f(inp["ffn_conv_w"])[0], "conv_b": f(inp["ffn_conv_b"]).reshape(1, DFF),
    }
    maps = []
    for c in range(n_cores):
        m = dict(shared)
        for vc in range(nv):
            g = c * nv + vc
            sfx = "_v%d" % vc
            m["xp" + sfx] = f(inp["x_prompt"][g])
            m["xs" + sfx] = f(inp["x_sample"][g * NS:(g + 1) * NS]).reshape(NS * TS, D)
            m["pt" + sfx] = f(inp["page_table"][g * NS:(g + 1) * NS]).reshape(1, NS * NPG).astype(np.int32)
            m["s_re0" + sfx] = f(inp["state_ssm_re"][0, g * NS:(g + 1) * NS]).reshape(NS, 2048)
            m["s_im0" + sfx] = f(inp["state_ssm_im"][0, g * NS:(g + 1) * NS]).reshape(NS, 2048)
            m["conv0" + sfx] = f(inp["state_ffn_conv"][0, g * NS:(g + 1) * NS]).reshape(NS * 2, DFF)
        maps.append(m)
    return maps


def assemble(results, n_cores, nv=1):
    n = n_cores * nv
    cat = lambda k: np.stack([np.asarray(results[c][k + "_v%d" % vc]) for c in range(n_cores) for vc in range(nv)])
    y_prompt = cat("yp").reshape(n, SEQ, D)
    y_sample = cat("ys").reshape(n * NS, TS, D)
    k_prompt = cat("kp").reshape(1, n, L, 4, 128)
    v_prompt = cat("vp").reshape(1, n, L, 4, 128)
    k_sample = cat("ks").reshape(1, n * NS, TS, 4, 128)
    v_sample = cat("vs").reshape(1, n * NS, TS, 4, 128)
    srp = cat("srp").reshape(1, n, 32, 64)
    sip = cat("sip").reshape(1, n, 32, 64)
    srs = cat("srs").reshape(1, n * NS, 32, 64)
    sis = cat("sis").reshape(1, n * NS, 32, 64)
    cpo = cat("cp").reshape(1, n, 2, DFF)
    cso = cat("cs").reshape(1, n * NS, 2, DFF)
    return tuple(np.ascontiguousarray(a.astype(np.float32)) for a in
                 (y_prompt, y_sample, k_prompt, v_prompt, k_sample, v_sample, srp, sip, srs, sis, cpo, cso))


N_CORES = 8
N_VC = 1


def kernel(**inputs):
    npool = int(np.asarray(inputs["cache_k"]).shape[1])
    nc = build(npool, N_VC)
    maps = make_in_maps(inputs, N_CORES, npool, N_VC)
    res = run_bass_kernel_spmd(nc, maps, core_ids=list(range(N_CORES)))
    return assemble(res.results, N_CORES, N_VC)
```

```python
import contextlib
import math
import numpy as np
import concourse.bass as bass
import concourse.mybir as mybir
from concourse.bass_utils import run_bass_kernel_spmd

F32 = mybir.dt.float32
BF16 = mybir.dt.bfloat16
I32 = mybir.dt.int32
AF = mybir.ActivationFunctionType
ALU = mybir.AluOpType
AX = mybir.AxisListType

D = 1024
SEQ = 2048
NMETA = 16
L = SEQ + NMETA
NT = 17
NCOL = NT * 128
NS = 16
TS = 4
NPG = 16
DFF = 2816
NFC = DFF // 128
EPS = 1e-6
LAM_INIT = 0.8 - 0.6 * math.exp(-0.3 * 0)
Q = 8
NJ = L // Q
ENGS = ("pe", "act", "dve", "pool", "sp")


class Res:
    __slots__ = ("name", "writer", "readers")

    def __init__(self, name=""):
        self.name = name
        self.writer = None
        self.readers = []


class DSem:
    def __init__(self, sem):
        self.sem = sem
        self.val = 0


class Prog:
    def __init__(self, nc):
        self.nc = nc
        self.q = {e: [] for e in ENGS}
        self.sem = {e: nc.alloc_semaphore(name="c_" + e) for e in ENGS}
        self.cnt = {e: 0 for e in ENGS}
        self.seen = {e: {} for e in ENGS}
        self.dsems = []
        self.n_inst = 0
        self.dead = False
        import os as _o
        self.cutv = float(_o.environ.get('DBG_S', 1e9))
        _ss = _o.environ.get("DBG_NOSER", "") == ""
        self.serial_same = {"act": _ss, "dve": _ss, "pool": _ss, "pe": False, "sp": False}

    def _deps(self, eng, reads, writes):
        need = {}

        def add(sv):
            if sv is None:
                return
            s, v = sv
            if need.get(s, 0) < v:
                need[s] = v
        for r in reads:
            add(r.writer)
        for w in writes:
            add(w.writer)
            for rd in w.readers:
                add(rd)
        waits = []
        for s, v in need.items():
            if s is self.sem[eng] and not self.serial_same[eng]:
                continue
            if self.seen[eng].get(s, 0) >= v:
                continue
            self.seen[eng][s] = v
            waits.append((s, v))
        return waits

    def _mark(self, reads, writes, sv):
        for r in reads:
            r.readers.append(sv)
            if len(r.readers) > 64:
                best = {}
                for s, v in r.readers:
                    if best.get(s, 0) < v:
                        best[s] = v
                r.readers = list(best.items())
        for w in writes:
            w.writer = sv
            w.readers = []

    def cut(self, x):
        if self.cutv < x:
            self.dead = True

    def op(self, eng, fn, reads=(), writes=()):
        if self.dead:
            return
        waits = self._deps(eng, reads, writes)
        self.cnt[eng] += 1
        sem = self.sem[eng]
        sv = (sem, self.cnt[eng])

        def emit(e, fn=fn, waits=waits, sem=sem):
            for s, v in waits:
                e.wait_ge(s, v)
            fn(e).then_inc(sem, 1)
        self.q[eng].append(emit)
        self._mark(reads, writes, sv)
        self.n_inst += 1

    def dma(self, eng, fn, reads=(), writes=(), dsem=None):
        if self.dead:
            return
        waits = self._deps(eng, reads, writes)
        dsem.val += 16
        sv = (dsem.sem, dsem.val)

        def emit(e, fn=fn, waits=waits, s=dsem.sem):
            for ws, wv in waits:
                e.wait_ge(ws, wv)
            fn(e).then_inc(s, 16)
        self.q[eng].append(emit)
        self._mark(reads, writes, sv)
        self.n_inst += 1

    def new_dsem(self, name):
        d = DSem(self.nc.alloc_semaphore(name=name + "_%d" % len(self.dsems)))
        self.dsems.append(d)
        return d

    def barrier(self):
        if self.dead:
            return
        for e in ENGS:
            waits = []
            for e2 in ENGS:
                if e2 != e and self.cnt[e2] > self.seen[e].get(self.sem[e2], 0):
                    self.seen[e][self.sem[e2]] = self.cnt[e2]
                    waits.append((self.sem[e2], self.cnt[e2]))
            for d in self.dsems:
                if d.val > self.seen[e].get(d.sem, 0):
                    self.seen[e][d.sem] = d.val
                    waits.append((d.sem, d.val))

            def emit(en, waits=waits):
                for s, v in waits:
                    en.wait_ge(s, v)
            self.q[e].append(emit)

    def run(self):
        nc = self.nc
        finals = [(d.sem, d.val) for d in self.dsems if d.val > 0]

        def fin(e):
            for s, v in finals:
                e.wait_ge(s, v)
        self.q["sp"].append(fin)
        with nc.Block() as block:
            @block.tensor
            def _(e):
                for f in self.q["pe"]:
                    f(e)

            @block.scalar
            def _(e):
                for f in self.q["act"]:
                    f(e)

            @block.vector
            def _(e):
                for f in self.q["dve"]:
                    f(e)

            @block.gpsimd
            def _(e):
                for f in self.q["pool"]:
                    f(e)

            @block.sync
            def _(e):
                for f in self.q["sp"]:
                    f(e)


class T:
    def __init__(self, t, name):
        self.t = t
        self.r = Res(name)

    def __getitem__(self, k):
        return self.t[k]


def rawap(t, off, dims):
    return bass.AP(t.t if isinstance(t, T) else t, off, [list(d) for d in dims])


def build(npool, nv=1):
    nc = bass.Bass("TRN2", target_bir_lowering=False)
    P = Prog(nc)

    def din(name, shape, dt=F32):
        return nc.dram_tensor(name, list(shape), dt, kind="ExternalInput").ap()

    def dout(name, shape, dt=F32):
        return nc.dram_tensor(name, list(shape), dt, kind="ExternalOutput").ap()

    meta = din("meta", [NMETA, D])
    ckv = din("ckv", [npool * 128, 1024])
    norm1 = din("norm1", [1, D]); norm2 = din("norm2", [1, D])
    w_in = din("w_in", [D, 2048]); w_out = din("w_out", [D, D])
    q_norm = din("q_norm", [1, 128]); k_norm = din("k_norm", [1, 128])
    lam_q = din("lam_q", [1, 128]); lam_k = din("lam_k", [1, 128])
    sub_norm = din("sub_norm", [1, 128])
    a_re = din("a_re", [16, 128]); a_im = din("a_im", [16, 128]); log_dt = din("log_dt", [16, 2])
    b_re = din("b_re", [16, 2048]); b_im = din("b_im", [16, 2048])
    c_re = din("c_re", [16, 2048]); c_im = din("c_im", [16, 2048])
    ssm_d = din("ssm_d", [4, 128])
    w_glu = din("w_glu", [512, 512]); b_glu = din("b_glu", [4, 128])
    w_gate = din("w_gate", [D, DFF]); w_up = din("w_up", [D, DFF]); w_down = din("w_down", [DFF, D])
    conv_w = din("conv_w", [3, DFF]); conv_b = din("conv_b", [1, DFF])
    SFX = {"s": ""}

    def emit_vc(vc):
        sfx = "_v%d" % vc
        SFX["s"] = sfx
        xp = din("xp" + sfx, [SEQ, D]); xs = din("xs" + sfx, [NS * TS, D])
        pt = din("pt" + sfx, [1, NS * NPG], I32)
        s_re0 = din("s_re0" + sfx, [NS, 2048]); s_im0 = din("s_im0" + sfx, [NS, 2048])
        conv0 = din("conv0" + sfx, [NS * 2, DFF])
        yp = dout("yp" + sfx, [SEQ, D]); ys = dout("ys" + sfx, [NS * TS, D])
        kp = dout("kp" + sfx, [L, 512]); vp = dout("vp" + sfx, [L, 512])
        ks = dout("ks" + sfx, [NS * TS, 512]); vs = dout("vs" + sfx, [NS * TS, 512])
        srp = dout("srp" + sfx, [16, 128]); sip = dout("sip" + sfx, [16, 128])
        srs = dout("srs" + sfx, [NS, 2048]); sis = dout("sis" + sfx, [NS, 2048])
        cp = dout("cp" + sfx, [2, DFF]); cs = dout("cs" + sfx, [NS * 2, DFF])
        es_all = contextlib.ExitStack()

        def S(es, name, shape, dt):
            return T(es.enter_context(nc.sbuf_tensor(name + SFX["s"], list(shape), dt)), name)

        def PS(es, name, shape, dt):
            return T(es.enter_context(nc.psum_tensor(name + SFX["s"], list(shape), dt)), name)

        dq = {"n": 0}

        def ld(eng, out_ap, in_ap, writes, reads=(), name=None):
            w0 = writes[0] if writes else reads[0]
            if not hasattr(P, "_dmap"):
                P._dmap = {}
            key = id(w0)
            if key not in P._dmap:
                fr = getattr(P, "_free", [])
                P._dmap[key] = fr.pop() if fr else P.new_dsem("d%d" % len(P.dsems))
                P._keep = getattr(P, "_keep", []) + [w0]
            P.dma(eng, lambda e: e.dma_start(out=out_ap, in_=in_ap), reads=reads, writes=writes, dsem=P._dmap[key])


        def TT(eng, out, a, b, op, rd, wr):
            P.op(eng, lambda e: e.tensor_tensor(out=out, in0=a, in1=b, op=op), reads=rd, writes=wr)

        def TSC(eng, out, a, s1, s2, op0, op1, rd, wr):
            if s2 is None:
                P.op(eng, lambda e: e.tensor_scalar(out=out, in0=a, scalar1=s1, scalar2=None, op0=op0), reads=rd, writes=wr)
            else:
                P.op(eng, lambda e: e.tensor_scalar(out=out, in0=a, scalar1=s1, scalar2=s2, op0=op0, op1=op1), reads=rd, writes=wr)

        def STT(out, a, sc, b, op0, op1, rd, wr):
            P.op("dve", lambda e: e.scalar_tensor_tensor(out=out, in0=a, scalar=sc, in1=b, op0=op0, op1=op1), reads=rd, writes=wr)

        def ACT(out, in_, func, rd, wr, bias=None, scale=None):
            kw = {}
            if bias is not None:
                kw["bias"] = bias
            if scale is not None:
                kw["scale"] = scale
            P.op("act", lambda e: e.activation(out=out, in_=in_, func=func, **kw), reads=rd, writes=wr)

        def CP(eng, out, in_, rd, wr):
            if eng == "act":
                P.op("act", lambda e: e.copy(out=out, in_=in_), reads=rd, writes=wr)
            else:
                P.op(eng, lambda e: e.tensor_copy(out=out, in_=in_), reads=rd, writes=wr)

        def MM(out, lhsT, rhs, start, stop, rd, wr, tp=None):
            if tp is None:
                P.op("pe", lambda e: e.matmul(out, lhsT=lhsT, rhs=rhs, start=start, stop=stop), reads=rd, writes=wr)
            else:
                P.op("pe", lambda e: e.matmul(out, lhsT=lhsT, rhs=rhs, start=start, stop=stop, tile_position=tp), reads=rd, writes=wr)

        def TR(out, in_, ident, rd, wr):
            P.op("pe", lambda e: e.transpose(out=out, in_=in_, identity=ident), reads=rd, writes=wr)

        def MS(eng, out, val, wr):
            P.op(eng, lambda e: e.memset(out, val), writes=wr)

        with es_all:
            ident_f = S(es_all, "ident_f", [128, 128], F32)
            ident_b = S(es_all, "ident_b", [128, 128], BF16)
            ones_b = S(es_all, "ones_b", [128, 128], BF16)
            P.op("pool", lambda e: e.memset(ident_f[:], 1.0), writes=[ident_f.r])
            P.op("pool", lambda e: e.affine_select(out=ident_f[:], in_=ident_f[:], pattern=[[-1, 128]],
                                                   compare_op=ALU.is_equal, fill=0.0, base=0, channel_multiplier=1),
                 reads=[ident_f.r], writes=[ident_f.r])
            P.op("pool", lambda e: e.tensor_copy(out=ident_b[:], in_=ident_f[:]), reads=[ident_f.r], writes=[ident_b.r])
            P.op("pool", lambda e: e.memset(ones_b[:], 1.0), writes=[ones_b.r])

            mixT = S(es_all, "mixT", [128, 8, NCOL], BF16)
            mix_r = [Res("mix%d" % i) for i in range(8)]
            P.op("pool", lambda e: e.memset(mixT[:], 0.0), writes=mix_r)
            lamt = S(es_all, "lamt", [128, 8], F32)
            sgp = S(es_all, "sgp", [128, 2], F32)
            maskT = S(es_all, "maskT", [128, 128], BF16)
            ones_f = S(es_all, "ones_f", [128, 128], F32)
            es_U = contextlib.ExitStack()
            es_all.enter_context(es_U)
            uT = S(es_U, "uT", [128, 4, L + NS * TS], BF16)
            es_B = contextlib.ExitStack()
            es_all.enter_context(es_B)
            qT = S(es_B, "qT", [128, 4, NCOL], BF16)
            kT = S(es_B, "kT", [128, 4, NCOL], BF16)
            vb = S(es_B, "vb", [128, NT, 512], BF16)

            es_A = contextlib.ExitStack()
            with es_A:
                g1b = S(es_A, "g1b", [128, D], F32)
                gqb = S(es_A, "gqb", [128, 512], F32)
                gkb = S(es_A, "gkb", [128, 512], F32)
                wi = S(es_A, "wi", [128, 8, 2048], BF16)
                ld("sp", g1b[:], rawap(norm1.tensor, 0, [[0, 128], [1, D]]), [g1b.r])
                ld("sp", gqb[:].rearrange("p (h e) -> p h e", h=4), rawap(q_norm.tensor, 0, [[0, 128], [0, 4], [1, 128]]), [gqb.r])
                ld("sp", gkb[:].rearrange("p (h e) -> p h e", h=4), rawap(k_norm.tensor, 0, [[0, 128], [0, 4], [1, 128]]), [gkb.r])
                P.op("act", lambda e: e.mul(out=gqb[:], in_=gqb[:], mul=0.125), reads=[gqb.r], writes=[gqb.r])
                wi_r = [Res("wi%d" % k) for k in range(8)]
                w_in_v = w_in.rearrange("(kc p) n -> p kc n", p=128)
                for kc in range(8):
                    ld("pool", wi[:, kc, :], w_in_v[:, kc, :], [wi_r[kc]])

                NB = 2
                xt = [S(es_A, "xt%d" % i, [128, D], F32) for i in range(NB)]
                junk = S(es_A, "junk", [128, D], BF16)
                ss = [S(es_A, "ss%d" % i, [128, 4], F32) for i in range(NB)]
                xn = [S(es_A, "xn%d" % i, [128, D], BF16) for i in range(NB)]
                xnT = [S(es_A, "xnT%d" % i, [128, 8, 128], BF16) for i in range(NB)]
                sqb = [S(es_A, "sqb%d" % i, [128, 1024], F32) for i in range(NB)]
                s8 = [S(es_A, "s8%d" % i, [128, 16 * 3], F32) for i in range(NB)]
                qk32 = [S(es_A, "qk32%d" % i, [128, 1024], F32) for i in range(NB)]
                k32 = [S(es_A, "k32%d" % i, [128, 512], F32) for i in range(NB)]
                qkb = [S(es_A, "qkb%d" % i, [128, 1024], BF16) for i in range(NB)]
                v32 = [S(es_A, "v32%d" % i, [128, 512], F32) for i in range(NB)]
                pT = [PS(es_A, "pT%d" % i, [128, 8, 128], BF16) for i in range(2)]
                ps_qk = PS(es_A, "ps_qk", [128, 1024], F32)
                psqk_r = [Res("psq"), Res("psk")]
                ps_v = PS(es_A, "ps_v", [128, 512], F32)
                ps_u = PS(es_A, "ps_u", [128, 4, 128], F32)
                pQK = PS(es_A, "pQK", [128, 8, 128], BF16)
                qT_r = [Res("qT%d" % i) for i in range(NT)]
                kT_r = [Res("kT%d" % i) for i in range(NT)]
                vb_r = [Res("vb%d" % i) for i in range(NT)]
                uT_r = [Res("uT%d" % i) for i in range(NT)]

                import os as _os
                _NTR = int(_os.environ.get('DBG_NT', NT)); _CUT = float(_os.environ.get('DBG_CUT', 99))
                for tt in range(_NTR):
                    b = tt % NB
                    X = xt[b]
                    if tt == 0:
                        P.op("pool", lambda e, X=X: e.memset(X[:], 0.0), writes=[X.r])
                        ld("sp", X[0:16, :], meta, [X.r], name="x0a")
                        r2 = Res("x0b")
                        ld("sp", X[32:96, :], xs, [r2], reads=[X.r])
                        xr = [X.r, r2]
                    else:
                        ld("sp", X[:], xp[(tt - 1) * 128:tt * 128, :], [X.r])
                        xr = [X.r]
                    P.op("act", lambda e, X=X, b=b: e.activation(out=junk[:], in_=X[:], func=AF.Square, accum_out=ss[b][:, 0:1]),
                         reads=xr, writes=[junk.r, ss[b].r])
                    P.op("dve", lambda e, b=b: e.tensor_scalar(out=ss[b][:, 1:2], in0=ss[b][:, 0:1], scalar1=1.0 / D, scalar2=EPS,
                                                               op0=ALU.mult, op1=ALU.add), reads=[ss[b].r], writes=[ss[b].r])
                    P.op("act", lambda e, b=b: e.sqrt(out=ss[b][:, 2:3], in_=ss[b][:, 1:2]), reads=[ss[b].r], writes=[ss[b].r])
                    P.op("dve", lambda e, b=b: e.reciprocal(out=ss[b][:, 3:4], in_=ss[b][:, 2:3]), reads=[ss[b].r], writes=[ss[b].r])
                    P.op("dve", lambda e, X=X, b=b: e.scalar_tensor_tensor(out=xn[b][:], in0=X[:], scalar=ss[b][:, 3:4], in1=g1b[:],
                                                                          op0=ALU.mult, op1=ALU.mult),
                         reads=xr + [ss[b].r, g1b.r], writes=[xn[b].r])
                    if _CUT < 1:
                        continue
                    pt_ = pT[tt % 2]
                    for kc in range(8):
                        P.op("pe", lambda e, b=b, kc=kc, pt_=pt_: e.transpose(out=pt_[:, kc, :], in_=xn[b][:, kc * 128:(kc + 1) * 128],
                                                                              identity=ident_b[:]),
                             reads=[xn[b].r, ident_b.r], writes=[pt_.r])
                    P.op("act", lambda e, b=b, pt_=pt_: e.copy(out=xnT[b][:], in_=pt_[:]), reads=[pt_.r], writes=[xnT[b].r])
                    if _CUT < 2:
                        continue
                    for nb in range(2):
                        for kc in range(8):
                            P.op("pe", lambda e, b=b, kc=kc, nb=nb: e.matmul(ps_qk[:, nb * 512:(nb + 1) * 512], lhsT=xnT[b][:, kc, :],
                                                                              rhs=wi[:, kc, nb * 512:(nb + 1) * 512],
                                                                              start=(kc == 0), stop=(kc == 7)),
                                 reads=[xnT[b].r, wi_r[kc]], writes=[psqk_r[nb]])
                    for kc in range(8):
                        P.op("pe", lambda e, b=b, kc=kc: e.matmul(ps_v[:], lhsT=xnT[b][:, kc, :], rhs=wi[:, kc, 1024:1536],
                                                                  start=(kc == 0), stop=(kc == 7)),
                             reads=[xnT[b].r, wi_r[kc]], writes=[ps_v.r])
                    for c in range(4):
                        for kc in range(8):
                            P.op("pe", lambda e, b=b, kc=kc, c=c: e.matmul(ps_u[:, c, :], lhsT=wi[:, kc, 1536 + c * 128:1536 + (c + 1) * 128],
                                                                            rhs=xnT[b][:, kc, :], start=(kc == 0), stop=(kc == 7)),
                                 reads=[xnT[b].r, wi_r[kc]], writes=[ps_u.r])
                    if _CUT < 3:
                        continue
                    for hf in range(2):
                        sl = slice(hf * 512, (hf + 1) * 512)
                        P.op("act", lambda e, b=b, sl=sl: e.activation(out=sqb[b][:, sl], in_=ps_qk[:, sl], func=AF.Square),
                             reads=[psqk_r[hf], sqb[b].r], writes=[sqb[b].r])
                    if _CUT < 3.1:
                        continue
                    P.op("dve", lambda e, b=b: e.tensor_reduce(out=s8[b][:, 0:16], in_=sqb[b][:].rearrange("p (c d) -> p c d", d=64),
                                                               axis=AX.X, op=ALU.add), reads=[sqb[b].r], writes=[s8[b].r])
                    P.op("dve", lambda e, b=b: e.tensor_scalar(out=s8[b][:, 16:32], in0=s8[b][:, 0:16], scalar1=1.0 / 64, scalar2=EPS,
                                                               op0=ALU.mult, op1=ALU.add), reads=[s8[b].r], writes=[s8[b].r])
                    P.op("act", lambda e, b=b: e.sqrt(out=s8[b][:, 32:48], in_=s8[b][:, 16:32]), reads=[s8[b].r], writes=[s8[b].r])
                    P.op("dve", lambda e, b=b: e.reciprocal(out=s8[b][:, 0:16], in_=s8[b][:, 32:48]), reads=[s8[b].r], writes=[s8[b].r])
                    if _CUT < 3.2:
                        continue
                    for hf in range(2):
                        sl = slice(hf * 512, (hf + 1) * 512)
                        P.op("dve", lambda e, b=b, sl=sl, hf=hf: e.tensor_tensor(
                            out=qk32[b][:, sl].rearrange("p (c d) -> p c d", d=64),
                            in0=ps_qk[:, sl].rearrange("p (c d) -> p c d", d=64),
                            in1=s8[b][:, hf * 8:hf * 8 + 8].unsqueeze(2).to_broadcast([128, 8, 64]), op=ALU.mult),
                            reads=[psqk_r[hf], s8[b].r, qk32[b].r], writes=[qk32[b].r])
                    if _CUT < 3.3:
                        continue
                    P.op("pool", lambda e, b=b: e.tensor_tensor(out=qkb[b][:, 0:512], in0=qk32[b][:, 0:512], in1=gqb[:], op=ALU.mult),
                         reads=[qk32[b].r, gqb.r], writes=[qkb[b].r])
                    P.op("dve", lambda e, b=b: e.tensor_tensor(out=k32[b][:], in0=qk32[b][:, 512:1024], in1=gkb[:], op=ALU.mult),
                         reads=[qk32[b].r, gkb.r], writes=[k32[b].r])
                    rkb = Res("kb")
                    P.op("pool", lambda e, b=b: e.tensor_copy(out=qkb[b][:, 512:1024], in_=k32[b][:]), reads=[k32[b].r, qkb[b].r], writes=[qkb[b].r])
                    if _CUT < 3.4:
                        continue
                    for h in range(8):
                        P.op("pe", lambda e, b=b, h=h: e.transpose(out=pQK[:, h, :], in_=qkb[b][:, h * 128:(h + 1) * 128], identity=ident_b[:]),
                             reads=[qkb[b].r, ident_b.r], writes=[pQK.r])
                    if _CUT < 3.5:
                        continue
                    c0 = tt * 128
                    if _CUT >= 3.6:
                        P.op("act", lambda e, c0=c0: e.copy(out=qT[:, :, c0:c0 + 128], in_=pQK[:, 0:4, :]), reads=[pQK.r], writes=[qT_r[tt]])
                    if _CUT >= 3.7:
                        P.op("act", lambda e, c0=c0: e.copy(out=kT[:, :, c0:c0 + 128], in_=pQK[:, 4:8, :]), reads=[pQK.r], writes=[kT_r[tt]])
                    if _CUT < 4:
                        continue
                    P.op("act", lambda e, b=b: e.copy(out=v32[b][:], in_=ps_v[:]), reads=[ps_v.r], writes=[v32[b].r])
                    P.op("pool", lambda e, b=b, tt=tt: e.tensor_copy(out=vb[:, tt, :], in_=v32[b][:]), reads=[v32[b].r], writes=[vb_r[tt]])
                    if tt == 0:
                        P.op("dve", lambda e: e.tensor_copy(out=uT[:, :, 0:16], in_=ps_u[:, :, 0:16]), reads=[ps_u.r], writes=[uT_r[0]])
                        P.op("dve", lambda e: e.tensor_copy(out=uT[:, :, L:L + 64], in_=ps_u[:, :, 32:96]), reads=[ps_u.r, uT_r[0]], writes=[uT_r[0]])
                    else:
                        o0 = 16 + (tt - 1) * 128
                        P.op("act", lambda e, o0=o0: e.copy(out=uT[:, :, o0:o0 + 128], in_=ps_u[:]), reads=[ps_u.r], writes=[uT_r[tt]])
                    if tt == 0:
                        ld("sp", kp[0:16, :], k32[b][0:16, :], [], reads=[k32[b].r])
                        ld("sp", ks[:, :], k32[b][32:96, :], [], reads=[k32[b].r])
                        ld("sp", vp[0:16, :], v32[b][0:16, :], [], reads=[v32[b].r])
                        ld("sp", vs[:, :], v32[b][32:96, :], [], reads=[v32[b].r])
                    else:
                        o0 = 16 + (tt - 1) * 128
                        ld("sp", kp[o0:o0 + 128, :], k32[b][:], [], reads=[k32[b].r])
                        ld("sp", vp[o0:o0 + 128, :], v32[b][:], [], reads=[v32[b].r])
                P.barrier()
            P.cut(100)
            es_L2 = contextlib.ExitStack()
            with es_L2:
                lq = S(es_L2, "lq", [128, 2, 128], F32)
                ld("sp", lq[:, 0, :], rawap(lam_q.tensor, 0, [[0, 128], [1, 128]]), [lq.r])
                ld("sp", lq[:, 1, :], rawap(lam_k.tensor, 0, [[0, 128], [1, 128]]), [lq.r])
                TT("dve", lq[:, 0, :], lq[:, 0, :], lq[:, 1, :], ALU.mult, [lq.r], [lq.r])
                P.op("dve", lambda e: e.tensor_reduce(out=lamt[:, 0:2], in_=lq[:, 0, :].rearrange("p (c d) -> p c d", c=2), axis=AX.X, op=ALU.add),
                     reads=[lq.r], writes=[lamt.r])
                ACT(lamt[:, 2:4], lamt[:, 0:2], AF.Exp, [lamt.r], [lamt.r])
                TT("dve", lamt[:, 5:6], lamt[:, 2:3], lamt[:, 3:4], ALU.subtract, [lamt.r], [lamt.r])
                TSC("dve", lamt[:, 5:6], lamt[:, 5:6], LAM_INIT, None, ALU.add, None, [lamt.r], [lamt.r])
                TSC("dve", lamt[:, 4:5], lamt[:, 5:6], -1.0, None, ALU.mult, None, [lamt.r], [lamt.r])
                ld("sp", sgp[:, 0:1], rawap(sub_norm.tensor, 0, [[1, 128], [1, 1]]), [sgp.r])
                TSC("dve", sgp[:, 1:2], sgp[:, 0:1], 1.0 - LAM_INIT, None, ALU.mult, None, [sgp.r], [sgp.r])
                MS("pool", maskT[:], 1.0, [maskT.r])
                P.op("pool", lambda e: e.affine_select(out=maskT[:], in_=maskT[:], pattern=[[1, 128]], compare_op=ALU.is_ge, fill=0.0,
                                                       base=0, channel_multiplier=-1), reads=[maskT.r], writes=[maskT.r])
                MS("pool", ones_f[:], 1.0, [ones_f.r])
                P.barrier()
            NLAM = lamt[:, 4:5]
            es_At = contextlib.ExitStack()
            with es_At:
                sA = [PS(es_At, "sA%d" % i, [128, 512], F32) for i in range(2)]
                sB = [PS(es_At, "sB%d" % i, [128, 512], F32) for i in range(2)]
                po = [PS(es_At, "po%d" % i, [128, 512], F32) for i in range(2)]
                pss = [PS(es_At, "pss%d" % i, [128, 512], F32) for i in range(2)]
                ptb = [[S(es_At, "ptb%d_%d" % (c, i), [128, 512], BF16) for i in range(2)] for c in range(2)]
                wa = [S(es_At, "wa%d" % i, [128, 512], F32) for i in range(4)]
                kv_all = list(kT_r) + list(qT_r) + list(vb_r)
                groups = [(0, 16, [0])] + [(128 * (4 * g + 1), 512, list(range(0, 4 * g + 5))) for g in range(4)]
                steps = []
                for (q0, NQ, kts) in groups:
                    for h in range(4):
                        for kt in kts:
                            if NQ == 16:
                                off, N, diag = 0, 16, True
                            elif kt * 128 < q0:
                                off, N, diag = 0, 512, False
                            else:
                                off = kt * 128 - q0
                                N, diag = 512 - off, True
                            steps.append((q0, NQ, h, kt, off, N, diag, kt == kts[0], kt == kts[-1]))

                def stage1(i):
                    (q0, NQ, h, kt, off, N, diag, first, last) = steps[i]
                    b = i % 2
                    sb = (sA[b], sB[b])
                    for c in range(2):
                        MM(sb[c][:, 0:N], kT[64 * c:64 * c + 64, h, kt * 128:(kt + 1) * 128], qT[64 * c:64 * c + 64, h, q0 + off:q0 + off + N],
                           True, True, kv_all, [sb[c].r])
                    for c in range(2):
                        pt_ = ptb[c][b]
                        ACT(pt_[:, 0:N], sb[c][:, 0:N], AF.Exp, [sb[c].r], [pt_.r])
                        if diag:
                            nd = min(N, 128)
                            TT("dve" if c == 0 else "pool", pt_[:, 0:nd], pt_[:, 0:nd], maskT[:, 0:nd], ALU.mult, [pt_.r, maskT.r], [pt_.r])
                        elif kt == 0:
                            TSC("dve" if c == 0 else "pool", pt_[:, 0:N], pt_[:, 0:N], maskT[:, 15:16], None, ALU.mult, None,
                                [pt_.r, maskT.r], [pt_.r])

                def stage2(i):
                    (q0, NQ, h, kt, off, N, diag, first, last) = steps[i]
                    b = i % 2
                    for c in range(2):
                        pt_ = ptb[c][b]
                        MM(po[c][:, off:off + N], vb[:, kt, h * 128:(h + 1) * 128], pt_[:, 0:N], first, last, [pt_.r] + kv_all, [po[c].r])
                        MM(pss[c][:, off:off + N], ones_b[:], pt_[:, 0:N], first, last, [pt_.r, ones_b.r], [pss[c].r])
                    if not last:
                        return
                    W0, W1, W2, W3 = wa
                    P.op("dve", lambda e: e.reciprocal(out=W0[:, 0:NQ], in_=pss[0][:, 0:NQ]), reads=[pss[0].r, W0.r], writes=[W0.r])
                    TT("dve", W0[:, 0:NQ], po[0][:, 0:NQ], W0[:, 0:NQ], ALU.mult, [po[0].r, W0.r], [W0.r])
                    P.op("dve", lambda e: e.reciprocal(out=W1[:, 0:NQ], in_=pss[1][:, 0:NQ]), reads=[pss[1].r, W1.r], writes=[W1.r])
                    TT("dve", W1[:, 0:NQ], po[1][:, 0:NQ], W1[:, 0:NQ], ALU.mult, [po[1].r, W1.r], [W1.r])
                    STT(W2[:, 0:NQ], W1[:, 0:NQ], NLAM, W0[:, 0:NQ], ALU.mult, ALU.add, [W0.r, W1.r, lamt.r, W2.r], [W2.r])
                    ACT(W3[:, 0:NQ], W2[:, 0:NQ], AF.Square, [W2.r, W3.r], [W3.r])
                    P.op("pe", lambda e: e.matmul(pss[0][:, 0:NQ], lhsT=ones_f[:], rhs=W3[:, 0:NQ], start=True, stop=True),
                         reads=[W3.r, ones_f.r], writes=[pss[0].r])
                    TSC("dve", W0[:, 0:NQ], pss[0][:, 0:NQ], 1.0 / 128, EPS, ALU.mult, ALU.add, [pss[0].r, W0.r], [W0.r])
                    P.op("act", lambda e: e.sqrt(out=W0[:, 0:NQ], in_=W0[:, 0:NQ]), reads=[W0.r], writes=[W0.r])
                    P.op("dve", lambda e: e.reciprocal(out=W0[:, 0:NQ], in_=W0[:, 0:NQ]), reads=[W0.r], writes=[W0.r])
                    STT(mixT[:, h, q0:q0 + NQ], W2[:, 0:NQ], sgp[:, 1:2], W0[:, 0:NQ], ALU.mult, ALU.mult, [W2.r, W0.r, sgp.r], [mix_r[h]])

                stage1(0)
                for i in range(len(steps)):
                    if i + 1 < len(steps):
                        stage1(i + 1)
                    stage2(i)
                P.barrier()
            P.cut(200)
            es_Q = contextlib.ExitStack()
            with es_Q:
                ptbc = S(es_Q, "ptbc", [128, NS * NPG], I32)
                iop = S(es_Q, "iop", [128, 1], I32)
                idx = S(es_Q, "idx", [128, NS * NPG], I32)
                ld("sp", ptbc[:], rawap(pt.tensor, 0, [[0, 128], [1, NS * NPG]]), [ptbc.r])
                P.op("pool", lambda e: e.iota(iop[:], pattern=[[0, 1]], base=0, channel_multiplier=1), writes=[iop.r])
                TSC("dve", idx[:], ptbc[:], 128, None, ALU.mult, None, [ptbc.r], [idx.r])
                TSC("dve", idx[:], idx[:], iop[:, 0:1], None, ALU.add, None, [idx.r, iop.r], [idx.r])
                Qblk = S(es_Q, "Qblk", [128, 4, 2, NS * TS], BF16)
                MS("pool", Qblk[:], 0.0, [Qblk.r])
                CP("act", Qblk[0:64, :, 0, :], qT[0:64, :, 32:96], [Qblk.r] + list(qT_r), [Qblk.r])
                CP("act", Qblk[64:128, :, 1, :], qT[64:128, :, 32:96], [Qblk.r] + list(qT_r), [Qblk.r])
                vnew2 = [S(es_Q, "vnew%d" % i, [128, 512], BF16) for i in range(2)]
                for _v in vnew2:
                    MS("pool", _v[:], 0.0, [_v.r])
                pnew = S(es_Q, "pnew", [128, 32], BF16)
                MS("pool", pnew[:], 0.0, [pnew.r])
                msk4 = S(es_Q, "msk4", [4, 32], F32)
                MS("pool", msk4[:], 1.0, [msk4.r])
                P.op("pool", lambda e: e.affine_select(out=msk4[:], in_=msk4[:], pattern=[[0, 2], [0, 4], [1, 4]], compare_op=ALU.is_ge, fill=0.0,
                                                       base=0, channel_multiplier=-1), reads=[msk4.r], writes=[msk4.r])
                e4 = S(es_Q, "e4", [4, 32], F32)
                att = S(es_Q, "att", [NS * TS, 512], F32)
                kvpg = [S(es_Q, "kvpg%d" % i, [128, 4, 1024], F32) for i in range(3)]
                vpg = [S(es_Q, "vpg%d" % i, [128, 4, 512], BF16) for i in range(6)]
                kTs = [S(es_Q, "kTs%d" % i, [128, 4, 128], BF16) for i in range(3)]
                pexp = [S(es_Q, "pexp%d" % i, [128, 512], BF16) for i in range(2)]
                rs = [S(es_Q, "rs%d" % i, [16, 4], F32) for i in range(2)]
                t0 = [S(es_Q, "t0_%d" % i, [16, 512], F32) for i in range(2)]
                o16 = [S(es_Q, "o16_%d" % i, [16, 512], F32) for i in range(2)]
                kvd = [P.new_dsem("kvd%d" % i) for i in range(3)]
                es_Q1 = contextlib.ExitStack()
                with es_Q1:
                    pk = [PS(es_Q1, "pk%d" % i, [128, 512], F32) for i in range(2)]
                    pS = [PS(es_Q1, "pS%d" % i, [128, 512], F32) for i in range(2)]
                    pSn = PS(es_Q1, "pSn", [128, 512], F32)
                    poc = [PS(es_Q1, "poc%d" % i, [128, 512], F32) for i in range(2)]
                    psm = PS(es_Q1, "psm", [128, 512], F32)

                    def gather(dst, sem, src, cols):
                        if P.dead:
                            return
                        waits = P._deps("pool", [idx.r], [dst.r])
                        fns = []
                        for n, col in enumerate(cols):
                            fns.append((n, col))
                        sem.val += 16 * len(cols)

                        def emit(e, waits=waits, fns=fns, dst=dst, sem=sem, src=src):
                            for ws, wv in waits:
                                e.wait_ge(ws, wv)
                            for n, col in fns:
                                e.indirect_dma_start(out=dst[:, n, :], out_offset=None, in_=src,
                                                     in_offset=bass.IndirectOffsetOnAxis(ap=idx[:, col:col + 1], axis=0)).then_inc(sem.sem, 16)
                        P.q["pool"].append(emit)
                        P._mark([idx.r], [dst.r], (sem.sem, sem.val))

                    pages = []
                    for s in range(NS):
                        for g4 in range(4):
                            gi_ = s * 4 + g4
                            for n in range(4):
                                pages.append((s, g4, n, gi_))

                    def stage1(j):
                        (s, g4, n, gi_) = pages[j]
                        kvb = kvpg[gi_ % 3]
                        if n == 0:
                            gather(kvb, kvd[gi_ % 3], ckv, [s * NPG + g4 * 4 + q for q in range(4)])
                        pk_ = pk[j % 2]
                        kt_ = kTs[j % 3]
                        for h in range(4):
                            TR(pk_[:, h * 128:(h + 1) * 128], kvb[:, n, h * 128:(h + 1) * 128], ident_f[:], [kvb.r, ident_f.r], [pk_.r])
                        CP("act" if j % 2 == 0 else "dve", kt_[:].rearrange("p h k -> p (h k)"), pk_[:], [pk_.r], [kt_.r])

                    def stage2(j):
                        (s, g4, n, gi_) = pages[j]
                        ps_ = pS[s % 2]
                        kt_ = kTs[j % 3]
                        base = (g4 * 4 + n) * 32
                        for h in range(4):
                            for c in range(2):
                                o0 = base + c * 16 + h * 4
                                MM(ps_[:, o0:o0 + 4], kt_[:, h, :], Qblk[:, h, c, 4 * s:4 * s + 4], True, True, [kt_.r, Qblk.r], [ps_.r])
                        if n == 3:
                            CP("dve", vpg[gi_ % 6][:], kvpg[gi_ % 3][:, :, 512:1024], [kvpg[gi_ % 3].r], [vpg[gi_ % 6].r])

                    def tail(s):
                        ps_ = pS[s % 2]
                        vnew = vnew2[s % 2]
                        MM(pSn[0:4, :], ident_b[:, 32 + 4 * s:36 + 4 * s], vb[:, 0, :], True, True, [ident_b.r] + list(vb_r), [pSn.r])
                        CP("act", vnew[0:4, :], pSn[0:4, :], [pSn.r, vnew.r], [vnew.r])
                        for h in range(4):
                            for c in range(2):
                                o0 = c * 16 + h * 4
                                MM(pSn[0:4, o0:o0 + 4], kT[:, h, 32 + 4 * s:36 + 4 * s], Qblk[:, h, c, 4 * s:4 * s + 4],
                                   True, True, [Qblk.r] + list(kT_r), [pSn.r])
                        ACT(e4[:], pSn[0:4, 0:32], AF.Exp, [pSn.r, e4.r], [e4.r])
                        TT("dve", pnew[0:4, :], e4[:], msk4[:], ALU.mult, [e4.r, msk4.r, pnew.r], [pnew.r])
                        px = pexp[s % 2]
                        ACT(px[:], ps_[:], AF.Exp, [ps_.r], [px.r])
                        for c in range(2):
                            for n in range(16):
                                vbf = vpg[(s * 4 + n // 4) % 6]
                                lh = px[:, n * 32 + c * 16:n * 32 + c * 16 + 16]
                                MM(poc[c][0:16, :], lh, vbf[:, n % 4, :], n == 0, False, [px.r, vbf.r], [poc[c].r])
                                MM(psm[0:16, c:c + 1], lh, ones_b[:, 0:1], n == 0, False, [px.r, ones_b.r], [psm.r])
                            lh = pnew[:, c * 16:(c + 1) * 16]
                            MM(poc[c][0:16, :], lh, vnew[:], False, True, [pnew.r, vnew.r], [poc[c].r])
                            MM(psm[0:16, c:c + 1], lh, ones_b[:, 0:1], False, True, [pnew.r, ones_b.r], [psm.r])
                        r_, t_, o_ = rs[s % 2], t0[s % 2], o16[s % 2]
                        P.op("dve", lambda e: e.reciprocal(out=r_[:, 0:2], in_=psm[0:16, 0:2]), reads=[psm.r, r_.r], writes=[r_.r])
                        TT("dve", r_[:, 2:3], r_[:, 1:2], lamt[0:16, 4:5], ALU.mult, [r_.r, lamt.r], [r_.r])
                        TSC("dve", t_[:], poc[0][0:16, :], r_[:, 0:1], None, ALU.mult, None, [poc[0].r, r_.r, t_.r], [t_.r])
                        STT(o_[:], poc[1][0:16, :], r_[:, 2:3], t_[:], ALU.mult, ALU.add, [poc[1].r, r_.r, t_.r, o_.r], [o_.r])
                        for h in range(4):
                            ld("sp", att[4 * s:4 * s + 4, h * 128:(h + 1) * 128], o_[4 * h:4 * h + 4, h * 128:(h + 1) * 128], [], reads=[o_.r])

                    stage1(0)
                    for j in range(len(pages)):
                        if j + 1 < len(pages):
                            stage1(j + 1)
                        stage2(j)
                        s, g4, n, gi_ = pages[j]
                        if g4 == 0 and n == 3 and s > 0:
                            tail(s - 1)
                    tail(NS - 1)
                    P.barrier()
                es_Q2 = contextlib.ExitStack()
                with es_Q2:
                    NQ4 = NS * TS
                    sq4 = S(es_Q2, "sq4", [NQ4, 4, 128], F32)
                    s4 = S(es_Q2, "s4", [NQ4, 3, 4], F32)
                    sg4 = S(es_Q2, "sg4", [NQ4, 128], F32)
                    attb = S(es_Q2, "attb", [NQ4, 512], BF16)
                    pT4 = PS(es_Q2, "pT4", [128, 4, NQ4], BF16)
                    ld("sp", sg4[:], rawap(sub_norm.tensor, 0, [[0, NQ4], [1, 128]]), [sg4.r])
                    TSC("dve", sg4[:], sg4[:], 1.0 - LAM_INIT, None, ALU.mult, None, [sg4.r], [sg4.r])
                    attv = att[:].rearrange("p (h e) -> p h e", h=4)
                    ACT(sq4[:], attv, AF.Square, [att.r], [sq4.r])
                    P.op("dve", lambda e: e.tensor_reduce(out=s4[:, 0, :], in_=sq4[:], axis=AX.X, op=ALU.add), reads=[sq4.r], writes=[s4.r])
                    TSC("dve", s4[:, 1, :], s4[:, 0, :], 1.0 / 128, EPS, ALU.mult, ALU.add, [s4.r], [s4.r])
                    P.op("act", lambda e: e.sqrt(out=s4[:, 2, :], in_=s4[:, 1, :]), reads=[s4.r], writes=[s4.r])
                    P.op("dve", lambda e: e.reciprocal(out=s4[:, 0, :], in_=s4[:, 2, :]), reads=[s4.r], writes=[s4.r])
                    TT("dve", sq4[:], attv, s4[:, 0, :].unsqueeze(2).to_broadcast([NQ4, 4, 128]), ALU.mult, [att.r, s4.r, sq4.r], [sq4.r])
                    TT("dve", attb[:].rearrange("p (h e) -> p h e", h=4), sq4[:], sg4[:].unsqueeze(1).to_broadcast([NQ4, 4, 128]), ALU.mult,
                       [sq4.r, sg4.r], [attb.r])
                    for h in range(4):
                        TR(pT4[:, h, :], attb[:, h * 128:(h + 1) * 128], ident_b[0:NQ4, 0:NQ4], [attb.r, ident_b.r], [pT4.r])
                    CP("act", mixT[:, 0:4, 32:96], pT4[:], [pT4.r], mix_r[0:4])
                    P.barrier()
            P.cut(50)
            es_B.close()
            TC = 258
            NM = L // TC
            PI = math.pi
            es_S = contextlib.ExitStack()
            with es_S:
                prm = S(es_S, "prm", [128, 3, 16], F32)
                sc = S(es_S, "sc", [128, 24, 16], F32)
                sci = S(es_S, "sci", [128, 16], I32)
                BT = S(es_S, "BT", [128, 4, 2, 128], BF16)
                CT = S(es_S, "CT", [128, 16, 2, 128], BF16)
                Dp = S(es_S, "Dp", [128, 8], F32)
                wg = S(es_S, "wg", [128, 4, 512], BF16)
                TCs = S(es_S, "TCs", [128, 16, TC], F32)
                TSn = S(es_S, "TSn", [128, 16, TC], F32)
                rq = S(es_S, "rq", [128, 16], F32)
                y32 = S(es_S, "y32", [128, TC], F32)
                g32 = [S(es_S, "g32_%d" % i, [128, TC], F32) for i in range(4)]
                gb = [S(es_S, "gb_%d" % i, [128, TC], BF16) for i in range(4)]
                sg = S(es_S, "sg", [128, TC], F32)
                es_P = contextlib.ExitStack()
                es_P.__enter__()
                pa = S(es_P, "pa", [16, 3, 128], F32)
                ldt = S(es_P, "ldt", [16, 2], F32)
                ld("sp", pa[:, 0, :], a_re, [pa.r])
                ld("sp", pa[:, 1, :], a_im, [pa.r])
                ld("sp", ldt[:], log_dt, [ldt.r])
                CP("dve", pa[:, 2, :].rearrange("p (g q) -> p g q", g=2), ldt[:].unsqueeze(2).to_broadcast([16, 2, 64]), [ldt.r, pa.r], [pa.r])
                ps0 = PS(es_S, "ps0", [128, 512], F32)
                for j in range(3):
                    TR(ps0[:, j * 16:(j + 1) * 16], pa[:, j, :], ident_f[0:16, 0:16], [pa.r, ident_f.r], [ps0.r])
                CP("act", prm[:].rearrange("p a b -> p (a b)"), ps0[:, 0:48], [ps0.r], [prm.r])
                P.cut(1)
                R = lambda k: sc[:, k, :]
                are, aim, ldtp = prm[:, 0, :], prm[:, 1, :], prm[:, 2, :]
                scr = [sc.r, prm.r]
                ACT(R(0), ldtp, AF.Exp, scr, [sc.r])
                TT("dve", R(1), are, R(0), ALU.mult, scr, [sc.r])
                ACT(R(2), R(1), AF.Exp, scr, [sc.r])
                TT("dve", R(3), aim, R(0), ALU.mult, scr, [sc.r])

                def sincos(x, s_out, c_out, t1, t2, ti, rd, wr):
                    C1 = 6.28125
                    C2 = 2 * PI - C1
                    TSC("dve", t1, x, 1.0 / (2 * PI), None, ALU.mult, None, rd, wr)
                    CP("dve", ti, t1, rd, wr)
                    CP("dve", t1, ti, rd, wr)
                    STT(t2, t1, -C1, x, ALU.mult, ALU.add, rd, wr)
                    STT(t2, t1, -C2, t2, ALU.mult, ALU.add, rd, wr)
                    TSC("dve", t1, t2, PI, -2 * PI, ALU.is_gt, ALU.mult, rd, wr)
                    TT("dve", t2, t2, t1, ALU.add, rd, wr)
                    TSC("dve", t1, t2, -PI, 2 * PI, ALU.is_lt, ALU.mult, rd, wr)
                    TT("dve", t2, t2, t1, ALU.add, rd, wr)
                    ACT(s_out, t2, AF.Sin, rd, wr)
                    TSC("dve", t2, t2, PI / 2, None, ALU.add, None, rd, wr)
                    TSC("dve", t1, t2, PI, -2 * PI, ALU.is_gt, ALU.mult, rd, wr)
                    TT("dve", t2, t2, t1, ALU.add, rd, wr)
                    ACT(c_out, t2, AF.Sin, rd, wr)
                sincos(R(3), R(4), R(5), R(6), R(7), sci[:], scr + [sci.r], [sc.r, sci.r])
                AR, AI = R(8), R(9)
                TT("dve", AR, R(2), R(5), ALU.mult, scr, [sc.r])
                TT("dve", AI, R(2), R(4), ALU.mult, scr, [sc.r])
                TT("dve", R(10), are, are, ALU.mult, scr, [sc.r])
                TT("dve", R(11), aim, aim, ALU.mult, scr, [sc.r])
                TT("dve", R(10), R(10), R(11), ALU.add, scr, [sc.r])
                P.op("dve", lambda e: e.reciprocal(out=R(10), in_=R(10)), reads=scr, writes=[sc.r])
                TSC("dve", R(11), AR, -1.0, None, ALU.add, None, scr, [sc.r])
                TT("dve", R(12), R(11), are, ALU.mult, scr, [sc.r])
                TT("dve", R(13), AI, aim, ALU.mult, scr, [sc.r])
                TT("dve", R(12), R(12), R(13), ALU.add, scr, [sc.r])
                TT("dve", R(12), R(12), R(10), ALU.mult, scr, [sc.r])
                TT("dve", R(13), AI, are, ALU.mult, scr, [sc.r])
                TT("dve", R(14), R(11), aim, ALU.mult, scr, [sc.r])
                TT("dve", R(13), R(13), R(14), ALU.subtract, scr, [sc.r])
                TT("dve", R(13), R(13), R(10), ALU.mult, scr, [sc.r])
                GR, GI = R(12), R(13)

                P.cut(2)
                BN = S(es_P, "BN", [128, 4, 256], F32)
                es_L = contextlib.ExitStack()
                with es_L:
                    Bld = S(es_L, "Bld", [16, 4, 2048], F32)
                    Cl2 = S(es_L, "Cl2", [16, 2, 2048], F32)
                    for j, src in enumerate((b_re, b_im, c_re, c_im)):
                        ld("sp", Bld[:, j, :], src, [Bld.r])
                    for ri in range(2):
                        CP("pool", Cl2[:, ri, :].rearrange("p (c g q) -> p c g q", c=16, g=2),
                           Bld[:, 2 + ri, :].rearrange("p (g c q) -> p c g q", g=2, c=16), [Bld.r, Cl2.r], [Cl2.r])
                    for j in range(4):
                        for c in range(16):
                            if j < 2:
                                src_ap = rawap(Bld, j * 2048 + c, [[4 * 2048, 16], [16, 128]])
                            else:
                                src_ap = Cl2[:, j - 2, c * 128:(c + 1) * 128]
                            TR(ps0[:, c * 16:(c + 1) * 16], src_ap, ident_f[0:16, 0:16], [Bld.r, Cl2.r, ident_f.r], [ps0.r])
                        CP("act", BN[:, j, :], ps0[:, 0:256], [ps0.r], [BN.r])
                    P.barrier()
                P.cut(3)
                Bv = lambda j: BN[:, j, :].rearrange("p (c i) -> p c i", c=16)
                BB = S(es_P, "BB", [128, 4, 256], F32)
                BBv = lambda j: BB[:, j, :].rearrange("p (c i) -> p c i", c=16)
                gbc = lambda g: g.unsqueeze(1).to_broadcast([128, 16, 16])
                rdB = [BN.r, BB.r, sc.r]
                TT("dve", BBv(0), Bv(0), gbc(GR), ALU.mult, rdB, [BB.r])
                TT("dve", BBv(2), Bv(1), gbc(GI), ALU.mult, rdB, [BB.r])
                TT("dve", BBv(0), BBv(0), BBv(2), ALU.subtract, rdB, [BB.r])
                TT("dve", BBv(1), Bv(1), gbc(GR), ALU.mult, rdB, [BB.r])
                TT("dve", BBv(2), Bv(0), gbc(GI), ALU.mult, rdB, [BB.r])
                TT("dve", BBv(1), BBv(1), BBv(2), ALU.add, rdB, [BB.r])
                MASK = S(es_P, "MASK", [128, 16, 2, 16], F32)
                MS("pool", MASK[:], 0.0, [MASK.r])
                MS("pool", MASK[0:64, :, 0, :], 1.0, [MASK.r])
                MS("pool", MASK[64:128, :, 1, :], 1.0, [MASK.r])
                Z4 = S(es_P, "Z4", [128, 16, 2, 16], F32)
                for ri in range(2):
                    TT("dve", Z4[:], rawap(BB, ri * 256, [[1024, 128], [1, 16], [0, 2], [16, 16]]), MASK[:], ALU.mult,
                       [BB.r, MASK.r, Z4.r], [Z4.r])
                    for k in range(4):
                        TR(ps0[:, k * 128:(k + 1) * 128], Z4[:, 4 * k:4 * k + 4, :, :].rearrange("p a b c -> p (a b c)"), ident_f[:],
                           [Z4.r, ident_f.r], [ps0.r])
                    CP("act", BT[:, :, ri, :], ps0[:].rearrange("p (k m) -> p k m", k=4), [ps0.r], [BT.r])
                P.cut(4)
                CTf = S(es_P, "CTf", [128, 16, 2, 128], F32)
                MS("pool", CTf[:], 0.0, [CTf.r])
                for ri in range(2):
                    for il in range(4):
                        outv = rawap(CTf, ri * 128 + il * 32 + il * 256, [[4096, 128], [4 * 256, 4], [16, 2], [1, 16]])
                        inv = rawap(BN, (2 + ri) * 256 + il, [[1024, 128], [4, 4], [0, 2], [16, 16]])
                        mk = MASK[:, 0:4, :, :]
                        TT("dve", outv, inv, mk, ALU.mult, [BN.r, MASK.r, CTf.r], [CTf.r])
                CP("pool", CT[:, :, 0, :], CTf[:, :, 0, :], [CTf.r], [CT.r])
                TSC("dve", CT[:, :, 1, :], CTf[:, :, 1, :], -1.0, None, ALU.mult, None, [CTf.r, CT.r], [CT.r])
                P.cut(5)
                dld = S(es_P, "dld", [4, 2, 128], F32)
                ld("sp", dld[:, 0, :], ssm_d, [dld.r])
                ld("sp", dld[:, 1, :], b_glu, [dld.r])
                TR(ps0[:, 0:4], dld[:, 0, :], ident_f[0:4, 0:4], [dld.r, ident_f.r], [ps0.r])
                TR(ps0[:, 4:8], dld[:, 1, :], ident_f[0:4, 0:4], [dld.r, ident_f.r], [ps0.r])
                CP("act", Dp[:], ps0[:, 0:8], [ps0.r], [Dp.r])
                ld("pool", wg[:], w_glu.rearrange("(kc p) n -> p kc n", p=128), [wg.r])
                P.cut(6)
                es_T = contextlib.ExitStack()
                with es_T:
                    NI = S(es_T, "NI", [128, TC], I32)
                    NF = S(es_T, "NF", [128, TC], F32)
                    P.op("pool", lambda e: e.iota(NI[:], pattern=[[1, TC]], base=1, channel_multiplier=0), writes=[NI.r])
                    CP("dve", NF[:], NI[:], [NI.r], [NF.r])
                    TA = S(es_T, "TA", [128, 16, TC], F32)
                    T1 = S(es_T, "T1", [128, 16, TC], F32)
                    T2 = S(es_T, "T2", [128, 16, TC], F32)
                    TI = S(es_T, "TI", [128, 16, TC], I32)
                    TT("dve", TA[:], R(3).unsqueeze(2).to_broadcast([128, 16, TC]), NF[:].unsqueeze(1).to_broadcast([128, 16, TC]), ALU.mult,
                       [sc.r, NF.r], [TA.r])
                    rr = [TA.r, T1.r, T2.r, TI.r, TCs.r, TSn.r]
                    sincos(TA[:], TSn[:], TCs[:], T1[:], T2[:], TI[:], rr, rr)
                    P.barrier()

                P.cut(7)
                P.barrier()
                es_P.close()
                es_M = contextlib.ExitStack()
                es_M.__enter__()
                NB2 = 2
                pXr = [PS(es_S, "pXr%d" % i, [128, 512], F32) for i in range(NB2)]
                pXi = [PS(es_S, "pXi%d" % i, [128, 512], F32) for i in range(NB2)]
                pY = PS(es_S, "pY", [128, 512], F32)
                pZ = PS(es_S, "pZ", [128, 512], F32)
                wk = [S(es_M, "wk%d" % i, [128, 8, TC], F32) for i in range(NB2)]
                wk2 = [S(es_M, "wk2%d" % i, [128, 4, TC], F32) for i in range(NB2)]
                H32 = [S(es_M, "H32_%d" % i, [128, 2, TC], F32) for i in range(16)]
                Hb = [S(es_M, "Hb%d" % i, [128, 2, TC], BF16) for i in range(4)]
                CP("dve", rq[:], R(2), [sc.r], [rq.r])

                def glu(N, outs):
                    for kq in range(4):
                        for kc in range(4):
                            MM(pZ[:, 0:N], wg[:, kc, kq * 128:(kq + 1) * 128], gb[kc][:, 0:N], kc == 0, kc == 3, [wg.r, gb[kc].r], [pZ.r])
                        ACT(sg[:, 0:N], pZ[:, 0:N], AF.Sigmoid, [pZ.r, Dp.r], [sg.r], bias=Dp[:, 4 + kq:5 + kq])
                        for (sl, dst, dres) in outs[kq]:
                            TT("pool", dst, g32[kq][:, sl], sg[:, sl], ALU.mult, [g32[kq].r, sg.r], [dres])

                def y_finish(k, N, ucols):
                    STT(y32[:, 0:N], uT[:, k, ucols], Dp[:, k:k + 1], pY[:, 0:N], ALU.mult, ALU.add, [uT_r[0], pY.r, Dp.r], [y32.r])
                    ACT(g32[k][:, 0:N], y32[:, 0:N], AF.Gelu_apprx_tanh, [y32.r], [g32[k].r])
                    CP("pool", gb[k][:, 0:N], g32[k][:, 0:N], [g32[k].r], [gb[k].r])

                uT_all = list(uT_r)
                for m in range(NM):
                    c0 = m * TC
                    for k in range(4):
                        for il in range(4):
                            i = 4 * k + il
                            b = i % NB2
                            W = wk[b]
                            W2 = wk2[b]
                            MM(pXr[b][:, 0:TC], BT[32 * il:32 * il + 32, k, 0, :], uT[32 * il:32 * il + 32, k, c0:c0 + TC], True, True,
                               [BT.r] + uT_all, [pXr[b].r], tp=(32 * il, 0))
                            MM(pXi[b][:, 0:TC], BT[32 * il:32 * il + 32, k, 1, :], uT[32 * il:32 * il + 32, k, c0:c0 + TC], True, True,
                               [BT.r] + uT_all, [pXi[b].r], tp=(32 * il, 0))
                            cs_, sn_ = TCs[:, i, :], TSn[:, i, :]
                            rd = [pXr[b].r, pXi[b].r, TCs.r, TSn.r, W.r]
                            TT("dve", W[:, 0, :], pXr[b][:, 0:TC], cs_, ALU.mult, rd, [W.r])
                            TT("dve", W[:, 1, :], pXi[b][:, 0:TC], sn_, ALU.mult, rd, [W.r])
                            TT("dve", W[:, 4, :], W[:, 0, :], W[:, 1, :], ALU.add, rd, [W.r])
                            TT("dve", W[:, 2, :], pXi[b][:, 0:TC], cs_, ALU.mult, rd, [W.r])
                            TT("dve", W[:, 3, :], pXr[b][:, 0:TC], sn_, ALU.mult, rd, [W.r])
                            TT("dve", W[:, 5, :], W[:, 2, :], W[:, 3, :], ALU.subtract, rd, [W.r])
                            for ri in range(2):
                                init = 0.0 if m == 0 else H32[i][:, ri, TC - 1:TC]
                                P.op("dve", lambda e, W=W, ri=ri, init=init, i=i: e.tensor_tensor_scan(
                                    out=W[:, 6 + ri, :], data0=rq[:, i:i + 1].to_broadcast([128, TC]), data1=W[:, 4 + ri, :],
                                    initial=init, op0=ALU.mult, op1=ALU.add), reads=[W.r, rq.r, H32[i].r], writes=[W.r])
                            rd2 = [W.r, W2.r, TCs.r, TSn.r]
                            TT("pool", W2[:, 0, :], W[:, 6, :], cs_, ALU.mult, rd2, [W2.r])
                            TT("pool", W2[:, 1, :], W[:, 7, :], sn_, ALU.mult, rd2, [W2.r])
                            TT("pool", H32[i][:, 0, :], W2[:, 0, :], W2[:, 1, :], ALU.subtract, rd2 + [H32[i].r], [H32[i].r])
                            TT("pool", W2[:, 2, :], W[:, 7, :], cs_, ALU.mult, rd2, [W2.r])
                            TT("pool", W2[:, 3, :], W[:, 6, :], sn_, ALU.mult, rd2, [W2.r])
                            TT("pool", H32[i][:, 1, :], W2[:, 2, :], W2[:, 3, :], ALU.add, rd2 + [H32[i].r], [H32[i].r])
                            CP("act", Hb[il][:], H32[i][:], [H32[i].r], [Hb[il].r])
                        n = 0
                        for il in range(4):
                            for ri in range(2):
                                MM(pY[:, 0:TC], CT[:, 4 * k + il, ri, :], Hb[il][:, ri, :], n == 0, n == 7, [CT.r, Hb[il].r], [pY.r])
                                n += 1
                        y_finish(k, TC, slice(c0, c0 + TC))
                    outs = []
                    for kq in range(4):
                        if m == 0:
                            o = [(slice(0, 16), mixT[:, 4 + kq, 0:16], mix_r[4 + kq]),
                                 (slice(16, TC), mixT[:, 4 + kq, 128:128 + TC - 16], mix_r[4 + kq])]
                        else:
                            d0 = 128 + c0 - 16
                            o = [(slice(0, TC), mixT[:, 4 + kq, d0:d0 + TC], mix_r[4 + kq])]
                        outs.append(o)
                    glu(TC, outs)
                P.cut(9)
                FP = S(es_M, "FP", [128, 2, 16], F32)
                for i in range(16):
                    CP("dve", FP[:, :, i:i + 1], H32[i][:, :, TC - 1:TC], [H32[i].r, FP.r], [FP.r])
                fpo = S(es_M, "fpo", [16, 2, 128], F32)
                for ri in range(2):
                    TR(ps0[0:16, ri * 128:(ri + 1) * 128], FP[:, ri, :], ident_f[:], [FP.r, ident_f.r], [ps0.r])
                CP("act", fpo[:].rearrange("p a b -> p (a b)"), ps0[0:16, 0:256], [ps0.r], [fpo.r])
                ld("sp", srp, fpo[:, 0, :], [], reads=[fpo.r])
                ld("sp", sip, fpo[:, 1, :], [], reads=[fpo.r])

                P.cut(10)
                P.barrier()
                es_M.close()
                NSC = NS * TS
                XS = S(es_S, "XS", [128, 2, 16, NSC], F32)
                banks = [pXr[0], pXr[1], pXi[0], pXi[1]]
                for ri in range(2):
                    for il in range(4):
                        for k in range(4):
                            MM(banks[il][:, k * NSC:(k + 1) * NSC], BT[32 * il:32 * il + 32, k, ri, :], uT[32 * il:32 * il + 32, k, L:L + NSC],
                               True, True, [BT.r] + uT_all, [banks[il].r], tp=(32 * il, 0))
                    for il in range(4):
                        CP("act", XS[:, ri, :, :].rearrange("p (k il) c -> p k il c", il=4)[:, :, il, :],
                           banks[il][:, 0:4 * NSC].rearrange("p (a b) -> p a b", a=4), [banks[il].r, XS.r], [XS.r])
                P.cut(10.1)
                H0l = S(es_S, "H0l", [16, 2, 2048], F32)
                ld("sp", H0l[:, 0, :], s_re0, [H0l.r])
                ld("sp", H0l[:, 1, :], s_im0, [H0l.r])
                HS = S(es_S, "HS", [128, 2, 16, NS, TS + 1], F32)
                for ri in range(2):
                    for i in range(16):
                        TR(ps0[:, i * 16:(i + 1) * 16], H0l[:, ri, i * 128:(i + 1) * 128], ident_f[0:16, 0:16], [H0l.r, ident_f.r], [ps0.r])
                    CP("act", HS[:, ri, :, :, 0], ps0[:, 0:256].rearrange("p (a b) -> p a b", a=16), [ps0.r, HS.r], [HS.r])
                P.cut(10.2)
                M4 = S(es_S, "M4", [128, 4, 16, NS], F32)
                abc = lambda a: a.unsqueeze(2).to_broadcast([128, 16, NS])
                XSv = lambda ri, t: XS[:, ri, :, :].rearrange("p a (s t) -> p a s t", t=TS)[:, :, :, t]
                rdh = [HS.r, M4.r, XS.r, sc.r]
                for t in range(TS):
                    hr, hi = HS[:, 0, :, :, t], HS[:, 1, :, :, t]
                    TT("dve", M4[:, 0], hr, abc(AR), ALU.mult, rdh, [M4.r])
                    TT("dve", M4[:, 1], hi, abc(AI), ALU.mult, rdh, [M4.r])
                    TT("dve", M4[:, 0], M4[:, 0], M4[:, 1], ALU.subtract, rdh, [M4.r])
                    TT("dve", HS[:, 0, :, :, t + 1], M4[:, 0], XSv(0, t), ALU.add, rdh, [HS.r])
                    TT("dve", M4[:, 2], hi, abc(AR), ALU.mult, rdh, [M4.r])
                    TT("dve", M4[:, 3], hr, abc(AI), ALU.mult, rdh, [M4.r])
                    TT("dve", M4[:, 2], M4[:, 2], M4[:, 3], ALU.add, rdh, [M4.r])
                    TT("dve", HS[:, 1, :, :, t + 1], M4[:, 2], XSv(1, t), ALU.add, rdh, [HS.r])
                P.cut(10.3)
                HSb = S(es_S, "HSb", [128, 2, 16, NS, TS], BF16)
                for ri in range(2):
                    CP("pool", HSb[:, ri], HS[:, ri, :, :, 1:TS + 1], [HS.r, HSb.r], [HSb.r])
                P.cut(10.4)
                for k in range(4):
                    n = 0
                    for il in range(4):
                        for ri in range(2):
                            MM(pY[:, 0:NSC], CT[:, 4 * k + il, ri, :], HSb[:, ri, 4 * k + il].rearrange("p s t -> p (s t)"), n == 0, n == 7,
                               [CT.r, HSb.r], [pY.r])
                            n += 1
                    y_finish(k, NSC, slice(L, L + NSC))
                glu(NSC, [[(slice(0, NSC), mixT[:, 4 + kq, 32:32 + NSC], mix_r[4 + kq])] for kq in range(4)])
                P.cut(10.5)
                fso = S(es_S, "fso", [16, 2, 2048], F32)
                for ri in range(2):
                    for q4 in range(4):
                        for i4 in range(4):
                            i = q4 * 4 + i4
                            TR(ps0[0:16, i4 * 128:(i4 + 1) * 128], HS[:, ri, i, :, TS], ident_f[:], [HS.r, ident_f.r], [ps0.r])
                        CP("act", fso[:, ri, q4 * 512:(q4 + 1) * 512], ps0[0:16, :], [ps0.r, fso.r], [fso.r])
                ld("sp", srs, fso[:, 0, :], [], reads=[fso.r])
                ld("sp", sis, fso[:, 1, :], [], reads=[fso.r])
                P.barrier()
            P.cut(300)
            es_U.close()
            es_C = contextlib.ExitStack()
            with es_C:
                x1 = S(es_C, "x1", [128, NT, D], F32)
                x1_r = [Res("x1_%d" % i) for i in range(NT)]
                xn2T = S(es_C, "xn2T", [128, 8, NCOL], BF16)
                xn2_r = [Res("xn2_%d" % i) for i in range(NT)]
                cwb = S(es_C, "cwb", [128, 4, NFC], F32)
                es_C1 = contextlib.ExitStack()
                with es_C1:
                    g2b = S(es_C1, "g2b", [128, D], F32)
                    ld("sp", g2b[:], rawap(norm2.tensor, 0, [[0, 128], [1, D]]), [g2b.r])
                    wo = S(es_C1, "wo", [128, 8, D], BF16)
                    ld("pool", wo[:], w_out.rearrange("(kc p) n -> p kc n", p=128), [wo.r])
                    cwl = S(es_C1, "cwl", [NFC, 4, 128], F32)
                    for j in range(3):
                        ld("sp", cwl[:, j, :], conv_w[j:j + 1, :].rearrange("o (c f) -> (o c) f", f=128), [cwl.r])
                    ld("sp", cwl[:, 3, :], conv_b.rearrange("o (c f) -> (o c) f", f=128), [cwl.r])
                    pc0 = PS(es_C1, "pcw0", [128, 512], F32)
                    for j in range(4):
                        TR(pc0[:, j * NFC:(j + 1) * NFC], cwl[:, j, :], ident_f[0:NFC, 0:NFC], [cwl.r, ident_f.r], [pc0.r])
                    CP("act", cwb[:].rearrange("p a b -> p (a b)"), pc0[:, 0:4 * NFC], [pc0.r], [cwb.r])
                    NB = 2
                    xt = [S(es_C1, "cxt%d" % i, [128, D], F32) for i in range(NB)]
                    junk = S(es_C1, "cjunk", [128, D], BF16)
                    ss = [S(es_C1, "cssq%d" % i, [128, 4], F32) for i in range(NB)]
                    xn = [S(es_C1, "cxn%d" % i, [128, D], BF16) for i in range(NB)]
                    pw = [PS(es_C1, "pw%d" % i, [128, 1024], F32) for i in range(2)]
                    pT2 = [PS(es_C1, "pT2%d" % i, [128, 8, 128], BF16) for i in range(2)]
                    pw_r = [[Res("pwlo%d" % i), Res("pwhi%d" % i)] for i in range(2)]
                    for tt in range(NT):
                        b = tt % NB
                        X = xt[b]
                        if tt == 0:
                            MS("pool", X[:], 0.0, [X.r])
                            ld("sp", X[0:16, :], meta, [X.r])
                            r2 = Res("cx0b")
                            ld("sp", X[32:96, :], xs, [r2], reads=[X.r])
                            xr = [X.r, r2]
                        else:
                            ld("sp", X[:], xp[(tt - 1) * 128:tt * 128, :], [X.r])
                            xr = [X.r]
                        pw_ = pw[tt % 2]
                        pwr = pw_r[tt % 2]
                        for nb in range(2):
                            for kc in range(8):
                                MM(pw_[:, nb * 512:(nb + 1) * 512], mixT[:, kc, tt * 128:(tt + 1) * 128], wo[:, kc, nb * 512:(nb + 1) * 512],
                                   kc == 0, kc == 7, [wo.r] + mix_r, [pwr[nb]])
                        for nb in range(2):
                            sl = slice(nb * 512, (nb + 1) * 512)
                            TT("dve", x1[:, tt, sl], pw_[:, sl], X[:, sl], ALU.add, [pwr[nb]] + xr + [x1_r[tt]], [x1_r[tt]])
                        P.op("act", lambda e, tt=tt, b=b: e.activation(out=junk[:], in_=x1[:, tt, :], func=AF.Square, accum_out=ss[b][:, 0:1]),
                             reads=[x1_r[tt]], writes=[junk.r, ss[b].r])
                        TSC("dve", ss[b][:, 1:2], ss[b][:, 0:1], 1.0 / D, EPS, ALU.mult, ALU.add, [ss[b].r], [ss[b].r])
                        P.op("act", lambda e, b=b: e.sqrt(out=ss[b][:, 2:3], in_=ss[b][:, 1:2]), reads=[ss[b].r], writes=[ss[b].r])
                        P.op("dve", lambda e, b=b: e.reciprocal(out=ss[b][:, 3:4], in_=ss[b][:, 2:3]), reads=[ss[b].r], writes=[ss[b].r])
                        STT(xn[b][:], x1[:, tt, :], ss[b][:, 3:4], g2b[:], ALU.mult, ALU.mult, [x1_r[tt], ss[b].r, g2b.r], [xn[b].r])
                        pt_ = pT2[tt % 2]
                        for kc in range(8):
                            TR(pt_[:, kc, :], xn[b][:, kc * 128:(kc + 1) * 128], ident_b[:], [xn[b].r, ident_b.r], [pt_.r])
                        CP("act", xn2T[:, :, tt * 128:(tt + 1) * 128], pt_[:], [pt_.r], [xn2_r[tt]])
                    P.barrier()
                P.cut(350)
                parts = [list(range(0, 8)), list(range(8, 15)), list(range(15, 22))]
                hT = mixT
                wd = S(es_C, "wd", [128, 8, D], BF16)
                Ab2 = [S(es_C, "Ab%d" % i, [128, L + 2], F32) for i in range(2)]
                As2 = [S(es_C, "As%d" % i, [128, NS, TS + 2], F32) for i in range(2)]
                Gt = S(es_C, "Gt", [128, NCOL], F32)
                wgu = [S(es_C, "wgu%d" % i, [128, 2, 8, 256], BF16) for i in range(2)]
                wgu_r2 = [Res("wgu2_%d" % i) for i in range(2)]
                cst = S(es_C, "cst", [128, NFC, 2], F32)
                css = S(es_C, "css", [128, NFC, NS, 2], F32)
                hist = S(es_C, "hist", [NS * 2, 128], F32)
                for _a in Ab2:
                    MS("pool", _a[:, 0:2], 0.0, [_a.r])
                MS("pool", Gt[:], 0.0, [Gt.r])
                pa_ = [PS(es_C, "pa%d" % i, [128, 512], F32) for i in range(2)]
                pc_ = [PS(es_C, "pc%d" % i, [128, 512], F32) for i in range(2)]
                pd = [PS(es_C, "pd%d" % i, [128, 1024], F32) for i in range(1)]
                ph = PS(es_C, "ph", [128, 512], F32)
                ost = [S(es_C, "ost%d" % i, [128, D], F32) for i in range(1)]
                stg = [S(es_C, "stg%d" % i, [NS * 2, 512], F32) for i in range(2)]
                w_gate_v = w_gate.rearrange("(kc p) n -> p kc n", p=128)
                w_up_v = w_up.rearrange("(kc p) n -> p kc n", p=128)
                w_down_v = w_down.rearrange("(fc p) n -> p fc n", p=128)
                xn2_all = list(xn2_r)
                hT_r = Res("hT")
                colgroups = [(0, 512), (512, 512), (1024, 512), (1536, 512), (2048, 128)]
                for pi, fcs in enumerate(parts):
                    for j, fc in enumerate(fcs):
                        ld("pool", wd[:, j, :], w_down_v[:, fc, :], [wd.r])
                    wrd_of = {}

                    def stageA(j, fc):
                        Ab, As = Ab2[fc % 2], As2[fc % 2]
                        wbuf = wgu[(fc // 2) % 2]
                        rwb2 = wgu_r2[(fc // 2) % 2]
                        if fc % 2 == 0:
                            ncol = 256 if fc + 1 < NFC else 128
                            ld("pool", wbuf[:, 0, :, 0:ncol], w_gate_v[:, :, fc * 128:fc * 128 + ncol], [wbuf.r], reads=[rwb2])
                            ld("pool", wbuf[:, 1, :, 0:ncol], w_up_v[:, :, fc * 128:fc * 128 + ncol], [rwb2], reads=[wbuf.r])
                        wb = wbuf[:, :, :, (fc % 2) * 128:(fc % 2 + 1) * 128]
                        wrd = [wbuf.r, rwb2]
                        ld("sp", hist[:], conv0[:, fc * 128:(fc + 1) * 128], [hist.r])
                        TR(ph[:, 0:NS * 2], hist[:], ident_f[0:NS * 2, 0:NS * 2], [hist.r, ident_f.r], [ph.r])
                        CP("act", As[:, :, 0:2], ph[:, 0:NS * 2].rearrange("p (s j) -> p s j", j=2), [ph.r, As.r], [As.r])
                        for gi, (c0, N) in enumerate(colgroups):
                            pa = pa_[gi % 2]
                            for kc in range(8):
                                MM(pa[:, 0:N], wb[:, 0, kc, :], xn2T[:, kc, c0:c0 + N], kc == 0, kc == 7, wrd + xn2_all, [pa.r])
                            if gi == 0:
                                CP("act", Ab[:, 2:18], pa[:, 0:16], [pa.r, Ab.r], [Ab.r])
                                CP("act", As[:, :, 2:2 + TS], pa[:, 32:96].rearrange("p (s t) -> p s t", t=TS), [pa.r, As.r], [As.r])
                                CP("act", Ab[:, 18:18 + 384], pa[:, 128:512], [pa.r, Ab.r], [Ab.r])
                            else:
                                CP("act", Ab[:, c0 - 110:c0 - 110 + N], pa[:, 0:N], [pa.r, Ab.r], [Ab.r])
                        wrd_of[fc] = wrd

                    def stageB(j, fc):
                        Ab, As = Ab2[fc % 2], As2[fc % 2]
                        wb = wgu[(fc // 2) % 2][:, :, :, (fc % 2) * 128:(fc % 2 + 1) * 128]
                        wrd = wrd_of[fc]
                        w0, w1, w2, bb = (cwb[:, q, fc:fc + 1] for q in range(4))
                        rdc = [Ab.r, Gt.r, cwb.r]
                        ACT(Gt[:, 112:112 + L], Ab[:, 0:L], AF.Identity, rdc, [Gt.r], bias=bb, scale=w0)
                        STT(Gt[:, 112:112 + L], Ab[:, 1:L + 1], w1, Gt[:, 112:112 + L], ALU.mult, ALU.add, rdc, [Gt.r])
                        STT(Gt[:, 112:112 + L], Ab[:, 2:L + 2], w2, Gt[:, 112:112 + L], ALU.mult, ALU.add, rdc, [Gt.r])
                        ACT(Gt[:, 112:112 + L], Gt[:, 112:112 + L], AF.Gelu_apprx_tanh, rdc, [Gt.r])
                        CP("pool", Gt[:, 0:16], Gt[:, 112:128], [Gt.r], [Gt.r])
                        Gs = Gt[:, 32:96].rearrange("p (s t) -> p s t", t=TS)
                        rds = [As.r, Gt.r, cwb.r]
                        ACT(Gs, As[:, :, 0:TS], AF.Identity, rds, [Gt.r], bias=bb, scale=w0)
                        STT(Gs, As[:, :, 1:TS + 1], w1, Gs, ALU.mult, ALU.add, rds, [Gt.r])
                        STT(Gs, As[:, :, 2:TS + 2], w2, Gs, ALU.mult, ALU.add, rds, [Gt.r])
                        ACT(Gs, Gs, AF.Gelu_apprx_tanh, rds, [Gt.r])
                        for gi, (c0, N) in enumerate(colgroups):
                            pc = pc_[gi % 2]
                            for kc in range(8):
                                MM(pc[:, 0:N], wb[:, 1, kc, :], xn2T[:, kc, c0:c0 + N], kc == 0, kc == 7, wrd + xn2_all, [pc.r])
                            TT("dve", hT[:, j, c0:c0 + N], Gt[:, c0:c0 + N], pc[:, 0:N], ALU.mult, [Gt.r, pc.r, hT_r], [hT_r])
                        CP("pool", cst[:, fc, :], Ab[:, L:L + 2], [Ab.r, cst.r], [cst.r])
                        CP("pool", css[:, fc, :, :], As[:, :, TS:TS + 2], [As.r, css.r], [css.r])

                    stageA(0, fcs[0])
                    for j, fc in enumerate(fcs):
                        if j + 1 < len(fcs):
                            stageA(j + 1, fcs[j + 1])
                        stageB(j, fc)
                    lastp = (pi == len(parts) - 1)
                    pdr = [Res("pd_lo"), Res("pd_hi")]
                    for tt in range(NT):
                        pd_ = pd[0]
                        o_ = ost[0]
                        for nb in range(2):
                            sl = slice(nb * 512, (nb + 1) * 512)
                            for j in range(len(fcs)):
                                MM(pd_[:, sl], hT[:, j, tt * 128:(tt + 1) * 128], wd[:, j, sl],
                                   j == 0, j == len(fcs) - 1, [hT_r, wd.r], [pdr[nb]])
                            if not lastp:
                                TT("dve", x1[:, tt, sl], pd_[:, sl], x1[:, tt, sl], ALU.add, [pdr[nb], x1_r[tt]], [x1_r[tt]])
                            else:
                                TT("dve", o_[:, sl], pd_[:, sl], x1[:, tt, sl], ALU.add, [pdr[nb], x1_r[tt], o_.r], [o_.r])
                        if lastp:
                            if tt == 0:
                                ld("sp", ys[:, :], o_[32:96, :], [], reads=[o_.r])
                            else:
                                ld("sp", yp[(tt - 1) * 128:tt * 128, :], o_[:], [], reads=[o_.r])
                for q4 in range(6):
                    nch = min(4, NFC - 4 * q4)
                    for j in range(nch):
                        fc = 4 * q4 + j
                        TR(ph[0:2, j * 128:(j + 1) * 128], cst[:, fc, :], ident_f[:], [cst.r, ident_f.r], [ph.r])
                    CP("act", stg[0][0:2, 0:nch * 128], ph[0:2, 0:nch * 128], [ph.r, stg[0].r], [stg[0].r])
                    ld("sp", cp[:, q4 * 512:q4 * 512 + nch * 128], stg[0][0:2, 0:nch * 128], [], reads=[stg[0].r])
                    for j in range(nch):
                        fc = 4 * q4 + j
                        TR(pa_[0][0:NS * 2, j * 128:(j + 1) * 128], css[:, fc, :, :].rearrange("p s j -> p (s j)"), ident_f[:],
                           [css.r, ident_f.r], [pa_[0].r])
                    CP("act", stg[1][:, 0:nch * 128], pa_[0][0:NS * 2, 0:nch * 128], [pa_[0].r, stg[1].r], [stg[1].r])
                    ld("sp", cs[:, q4 * 512:q4 * 512 + nch * 128], stg[1][:, 0:nch * 128], [], reads=[stg[1].r])
                P.barrier()
        P.barrier()
        P._free = [d for d in P._dmap.values()]
        P._dmap = {}

    for vc in range(nv):
        emit_vc(vc)
    if True:
        P.run()
    return nc


def make_in_maps(inp, n_cores, npool, nv=1):
    f = lambda a: np.ascontiguousarray(np.asarray(a))
    ckv = np.concatenate([np.asarray(inp["cache_k"]).reshape(npool * 128, 512),
                          np.asarray(inp["cache_v"]).reshape(npool * 128, 512)], axis=1)
    shared = {
        "meta": f(inp["meta_tokens"]), "ckv": ckv,
        "norm1": f(inp["norm1"]).reshape(1, D), "norm2": f(inp["norm2"]).reshape(1, D),
        "w_in": f(inp["w_in"])[0], "w_out": f(inp["w_out"])[0],
        "q_norm": f(inp["q_norm"]).reshape(1, 128), "k_norm": f(inp["k_norm"]).reshape(1, 128),
        "lam_q": f(inp["lam_q"]).reshape(1, 128), "lam_k": f(inp["lam_k"]).reshape(1, 128),
        "sub_norm": f(inp["sub_norm"]).reshape(1, 128),
        "a_re": f(inp["ssm_a_re"]).reshape(16, 128), "a_im": f(inp["ssm_a_im"]).reshape(16, 128),
        "log_dt": f(inp["ssm_log_dt"]).reshape(16, 2),
        "b_re": f(inp["ssm_b_re"]).reshape(16, 2048), "b_im": f(inp["ssm_b_im"]).reshape(16, 2048),
        "c_re": f(inp["ssm_c_re"]).reshape(16, 2048), "c_im": f(inp["ssm_c_im"]).reshape(16, 2048),
        "ssm_d": f(inp["ssm_d"]).reshape(4, 128),
        "w_glu": f(inp["w_glu"])[0], "b_glu": f(inp["b_glu"]).reshape(4, 128),
        "w_gate": f(inp["w_gate"])[0], "w_up": f(inp["w_up"])[0], "w_down": f(inp["w_down"])[0],
        "conv_w": f(inp["ffn_conv_w"])[0], "conv_b": f(inp["ffn_conv_b"]).reshape(1, DFF),
    }
    maps = []
    for c in range(n_cores):
        m = dict(shared)
        for vc in range(nv):
            g = c * nv + vc
            sfx = "_v%d" % vc
            m["xp" + sfx] = f(inp["x_prompt"][g])
            m["xs" + sfx] = f(inp["x_sample"][g * NS:(g + 1) * NS]).reshape(NS * TS, D)
            m["pt" + sfx] = f(inp["page_table"][g * NS:(g + 1) * NS]).reshape(1, NS * NPG).astype(np.int32)
            m["s_re0" + sfx] = f(inp["state_ssm_re"][0, g * NS:(g + 1) * NS]).reshape(NS, 2048)
            m["s_im0" + sfx] = f(inp["state_ssm_im"][0, g * NS:(g + 1) * NS]).reshape(NS, 2048)
            m["conv0" + sfx] = f(inp["state_ffn_conv"][0, g * NS:(g + 1) * NS]).reshape(NS * 2, DFF)
        maps.append(m)
    return maps


def assemble(results, n_cores, nv=1):
    n = n_cores * nv
    cat = lambda k: np.stack([np.asarray(results[c][k + "_v%d" % vc]) for c in range(n_cores) for vc in range(nv)])
    y_prompt = cat("yp").reshape(n, SEQ, D)
    y_sample = cat("ys").reshape(n * NS, TS, D)
    k_prompt = cat("kp").reshape(1, n, L, 4, 128)
    v_prompt = cat("vp").reshape(1, n, L, 4, 128)
    k_sample = cat("ks").reshape(1, n * NS, TS, 4, 128)
    v_sample = cat("vs").reshape(1, n * NS, TS, 4, 128)
    srp = cat("srp").reshape(1, n, 32, 64)
    sip = cat("sip").reshape(1, n, 32, 64)
    srs = cat("srs").reshape(1, n * NS, 32, 64)
    sis = cat("sis").reshape(1, n * NS, 32, 64)
    cpo = cat("cp").reshape(1, n, 2, DFF)
    cso = cat("cs").reshape(1, n * NS, 2, DFF)
    return tuple(np.ascontiguousarray(a.astype(np.float32)) for a in
                 (y_prompt, y_sample, k_prompt, v_prompt, k_sample, v_sample, srp, sip, srs, sis, cpo, cso))


N_CORES = 8
N_VC = 1


def kernel(**inputs):
    npool = int(np.asarray(inputs["cache_k"]).shape[1])
    nc = build(npool, N_VC)
    maps = make_in_maps(inputs, N_CORES, npool, N_VC)
    res = run_bass_kernel_spmd(nc, maps, core_ids=list(range(N_CORES)))
    return assemble(res.results, N_CORES, N_VC)
```

```python
import contextlib
import math
import numpy as np
import concourse.bass as bass
import concourse.mybir as mybir
from concourse.bass_utils import run_bass_kernel_spmd

F32 = mybir.dt.float32
BF16 = mybir.dt.bfloat16
I32 = mybir.dt.int32
AF = mybir.ActivationFunctionType
ALU = mybir.AluOpType
AX = mybir.AxisListType

D = 1024
SEQ = 2048
NMETA = 16
L = SEQ + NMETA
NT = 17
NCOL = NT * 128
NS = 16
TS = 4
NPG = 16
DFF = 2816
NFC = DFF // 128
EPS = 1e-6
LAM_INIT = 0.8 - 0.6 * math.exp(-0.3 * 0)
Q = 8
NJ = L // Q
ENGS = ("pe", "act", "dve", "pool", "sp")


class Res:
    __slots__ = ("name", "writer", "readers")

    def __init__(self, name=""):
        self.name = name
        self.writer = None
        self.readers = []


class DSem:
    def __init__(self, sem):
        self.sem = sem
        self.val = 0


class Prog:
    def __init__(self, nc):
        self.nc = nc
        self.q = {e: [] for e in ENGS}
        self.sem = {e: nc.alloc_semaphore(name="c_" + e) for e in ENGS}
        self.cnt = {e: 0 for e in ENGS}
        self.seen = {e: {} for e in ENGS}
        self.dsems = []
        self.n_inst = 0
        self.dead = False
        import os as _o
        self.cutv = float(_o.environ.get('DBG_S', 1e9))
        _ss = _o.environ.get("DBG_NOSER", "") == ""
        self.serial_same = {"act": _ss, "dve": _ss, "pool": _ss, "pe": False, "sp": False}

    def _deps(self, eng, reads, writes):
        need = {}

        def add(sv):
            if sv is None:
                return
            s, v = sv
            if need.get(s, 0) < v:
                need[s] = v
        for r in reads:
            add(r.writer)
        for w in writes:
            add(w.writer)
            for rd in w.readers:
                add(rd)
        waits = []
        for s, v in need.items():
            if s is self.sem[eng] and not self.serial_same[eng]:
                continue
            if self.seen[eng].get(s, 0) >= v:
                continue
            self.seen[eng][s] = v
            waits.append((s, v))
        return waits

    def _mark(self, reads, writes, sv):
        for r in reads:
            r.readers.append(sv)
            if len(r.readers) > 64:
                best = {}
                for s, v in r.readers:
                    if best.get(s, 0) < v:
                        best[s] = v
                r.readers = list(best.items())
        for w in writes:
            w.writer = sv
            w.readers = []

    def cut(self, x):
        if self.cutv < x:
            self.dead = True

    def op(self, eng, fn, reads=(), writes=()):
        if self.dead:
            return
        waits = self._deps(eng, reads, writes)
        self.cnt[eng] += 1
        sem = self.sem[eng]
        sv = (sem, self.cnt[eng])

        def emit(e, fn=fn, waits=waits, sem=sem):
            for s, v in waits:
                e.wait_ge(s, v)
            fn(e).then_inc(sem, 1)
        self.q[eng].append(emit)
        self._mark(reads, writes, sv)
        self.n_inst += 1

    def dma(self, eng, fn, reads=(), writes=(), dsem=None):
        if self.dead:
            return
        waits = self._deps(eng, reads, writes)
        dsem.val += 16
        sv = (dsem.sem, dsem.val)

        def emit(e, fn=fn, waits=waits, s=dsem.sem):
            for ws, wv in waits:
                e.wait_ge(ws, wv)
            fn(e).then_inc(s, 16)
        self.q[eng].append(emit)
        self._mark(reads, writes, sv)
        self.n_inst += 1

    def new_dsem(self, name):
        d = DSem(self.nc.alloc_semaphore(name=name + "_%d" % len(self.dsems)))
        self.dsems.append(d)
        return d

    def barrier(self):
        if self.dead:
            return
        for e in ENGS:
            waits = []
            for e2 in ENGS:
                if e2 != e and self.cnt[e2] > self.seen[e].get(self.sem[e2], 0):
                    self.seen[e][self.sem[e2]] = self.cnt[e2]
                    waits.append((self.sem[e2], self.cnt[e2]))
            for d in self.dsems:
                if d.val > self.seen[e].get(d.sem, 0):
                    self.seen[e][d.sem] = d.val
                    waits.append((d.sem, d.val))

            def emit(en, waits=waits):
                for s, v in waits:
                    en.wait_ge(s, v)
            self.q[e].append(emit)

    def run(self):
        nc = self.nc
        finals = [(d.sem, d.val) for d in self.dsems if d.val > 0]

        def fin(e):
            for s, v in finals:
                e.wait_ge(s, v)
        self.q["sp"].append(fin)
        with nc.Block() as block:
            @block.tensor
            def _(e):
                for f in self.q["pe"]:
                    f(e)

            @block.scalar
            def _(e):
                for f in self.q["act"]:
                    f(e)

            @block.vector
            def _(e):
                for f in self.q["dve"]:
                    f(e)

            @block.gpsimd
            def _(e):
                for f in self.q["pool"]:
                    f(e)

            @block.sync
            def _(e):
                for f in self.q["sp"]:
                    f(e)


class T:
    def __init__(self, t, name):
        self.t = t
        self.r = Res(name)

    def __getitem__(self, k):
        return self.t[k]


def rawap(t, off, dims):
    return bass.AP(t.t if isinstance(t, T) else t, off, [list(d) for d in dims])


def build(npool, nv=1):
    nc = bass.Bass("TRN2", target_bir_lowering=False)
    P = Prog(nc)

    def din(name, shape, dt=F32):
        return nc.dram_tensor(name, list(shape), dt, kind="ExternalInput").ap()

    def dout(name, shape, dt=F32):
        return nc.dram_tensor(name, list(shape), dt, kind="ExternalOutput").ap()

    meta = din("meta", [NMETA, D])
    ckv = din("ckv", [npool * 128, 1024])
    norm1 = din("norm1", [1, D]); norm2 = din("norm2", [1, D])
    w_in = din("w_in", [D, 2048]); w_out = din("w_out", [D, D])
    q_norm = din("q_norm", [1, 128]); k_norm = din("k_norm", [1, 128])
    lam_q = din("lam_q", [1, 128]); lam_k = din("lam_k", [1, 128])
    sub_norm = din("sub_norm", [1, 128])
    a_re = din("a_re", [16, 128]); a_im = din("a_im", [16, 128]); log_dt = din("log_dt", [16, 2])
    b_re = din("b_re", [16, 2048]); b_im = din("b_im", [16, 2048])
    c_re = din("c_re", [16, 2048]); c_im = din("c_im", [16, 2048])
    ssm_d = din("ssm_d", [4, 128])
    w_glu = din("w_glu", [512, 512]); b_glu = din("b_glu", [4, 128])
    w_gate = din("w_gate", [D, DFF]); w_up = din("w_up", [D, DFF]); w_down = din("w_down", [DFF, D])
    conv_w = din("conv_w", [3, DFF]); conv_b = din("conv_b", [1, DFF])
    SFX = {"s": ""}

    def emit_vc(vc):
        sfx = "_v%d" % vc
        SFX["s"] = sfx
        xp = din("xp" + sfx, [SEQ, D]); xs = din("xs" + sfx, [NS * TS, D])
        pt = din("pt" + sfx, [1, NS * NPG], I32)
        s_re0 = din("s_re0" + sfx, [NS, 2048]); s_im0 = din("s_im0" + sfx, [NS, 2048])
        conv0 = din("conv0" + sfx, [NS * 2, DFF])
        yp = dout("yp" + sfx, [SEQ, D]); ys = dout("ys" + sfx, [NS * TS, D])
        kp = dout("kp" + sfx, [L, 512]); vp = dout("vp" + sfx, [L, 512])
        ks = dout("ks" + sfx, [NS * TS, 512]); vs = dout("vs" + sfx, [NS * TS, 512])
        srp = dout("srp" + sfx, [16, 128]); sip = dout("sip" + sfx, [16, 128])
        srs = dout("srs" + sfx, [NS, 2048]); sis = dout("sis" + sfx, [NS, 2048])
        cp = dout("cp" + sfx, [2, DFF]); cs = dout("cs" + sfx, [NS * 2, DFF])
        es_all = contextlib.ExitStack()

        def S(es, name, shape, dt):
            return T(es.enter_context(nc.sbuf_tensor(name + SFX["s"], list(shape), dt)), name)

        def PS(es, name, shape, dt):
            return T(es.enter_context(nc.psum_tensor(name + SFX["s"], list(shape), dt)), name)

        dq = {"n": 0}

        def ld(eng, out_ap, in_ap, writes, reads=(), name=None):
            w0 = writes[0] if writes else reads[0]
            if not hasattr(P, "_dmap"):
                P._dmap = {}
            key = id(w0)
            if key not in P._dmap:
                fr = getattr(P, "_free", [])
                P._dmap[key] = fr.pop() if fr else P.new_dsem("d%d" % len(P.dsems))
                P._keep = getattr(P, "_keep", []) + [w0]
            P.dma(eng, lambda e: e.dma_start(out=out_ap, in_=in_ap), reads=reads, writes=writes, dsem=P._dmap[key])


        def TT(eng, out, a, b, op, rd, wr):
            P.op(eng, lambda e: e.tensor_tensor(out=out, in0=a, in1=b, op=op), reads=rd, writes=wr)

        def TSC(eng, out, a, s1, s2, op0, op1, rd, wr):
            if s2 is None:
                P.op(eng, lambda e: e.tensor_scalar(out=out, in0=a, scalar1=s1, scalar2=None, op0=op0), reads=rd, writes=wr)
            else:
                P.op(eng, lambda e: e.tensor_scalar(out=out, in0=a, scalar1=s1, scalar2=s2, op0=op0, op1=op1), reads=rd, writes=wr)

        def STT(out, a, sc, b, op0, op1, rd, wr):
            P.op("dve", lambda e: e.scalar_tensor_tensor(out=out, in0=a, scalar=sc, in1=b, op0=op0, op1=op1), reads=rd, writes=wr)

        def ACT(out, in_, func, rd, wr, bias=None, scale=None):
            kw = {}
            if bias is not None:
                kw["bias"] = bias
            if scale is not None:
                kw["scale"] = scale
            P.op("act", lambda e: e.activation(out=out, in_=in_, func=func, **kw), reads=rd, writes=wr)

        def CP(eng, out, in_, rd, wr):
            if eng == "act":
                P.op("act", lambda e: e.copy(out=out, in_=in_), reads=rd, writes=wr)
            else:
                P.op(eng, lambda e: e.tensor_copy(out=out, in_=in_), reads=rd, writes=wr)

        def MM(out, lhsT, rhs, start, stop, rd, wr, tp=None):
            if tp is None:
                P.op("pe", lambda e: e.matmul(out, lhsT=lhsT, rhs=rhs, start=start, stop=stop), reads=rd, writes=wr)
            else:
                P.op("pe", lambda e: e.matmul(out, lhsT=lhsT, rhs=rhs, start=start, stop=stop, tile_position=tp), reads=rd, writes=wr)

        def TR(out, in_, ident, rd, wr):
            P.op("pe", lambda e: e.transpose(out=out, in_=in_, identity=ident), reads=rd, writes=wr)

        def MS(eng, out, val, wr):
            P.op(eng, lambda e: e.memset(out, val), writes=wr)

        with es_all:
            ident_f = S(es_all, "ident_f", [128, 128], F32)
            ident_b = S(es_all, "ident_b", [128, 128], BF16)
            ones_b = S(es_all, "ones_b", [128, 128], BF16)
            P.op("pool", lambda e: e.memset(ident_f[:], 1.0), writes=[ident_f.r])
            P.op("pool", lambda e: e.affine_select(out=ident_f[:], in_=ident_f[:], pattern=[[-1, 128]],
                                                   compare_op=ALU.is_equal, fill=0.0, base=0, channel_multiplier=1),
                 reads=[ident_f.r], writes=[ident_f.r])
            P.op("pool", lambda e: e.tensor_copy(out=ident_b[:], in_=ident_f[:]), reads=[ident_f.r], writes=[ident_b.r])
            P.op("pool", lambda e: e.memset(ones_b[:], 1.0), writes=[ones_b.r])

            mixT = S(es_all, "mixT", [128, 8, NCOL], BF16)
            mix_r = [Res("mix%d" % i) for i in range(8)]
            P.op("pool", lambda e: e.memset(mixT[:], 0.0), writes=mix_r)
            lamt = S(es_all, "lamt", [128, 8], F32)
            sgp = S(es_all, "sgp", [128, 2], F32)
            maskT = S(es_all, "maskT", [128, 128], BF16)
            ones_f = S(es_all, "ones_f", [128, 128], F32)
            es_U = contextlib.ExitStack()
            es_all.enter_context(es_U)
            uT = S(es_U, "uT", [128, 4, L + NS * TS], BF16)
            es_B = contextlib.ExitStack()
            es_all.enter_context(es_B)
            qT = S(es_B, "qT", [128, 4, NCOL], BF16)
            kT = S(es_B, "kT", [128, 4, NCOL], BF16)
            vb = S(es_B, "vb", [128, NT, 512], BF16)

            es_A = contextlib.ExitStack()
            with es_A:
                g1b = S(es_A, "g1b", [128, D], F32)
                gqb = S(es_A, "gqb", [128, 512], F32)
                gkb = S(es_A, "gkb", [128, 512], F32)
                wi = S(es_A, "wi", [128, 8, 2048], BF16)
                ld("sp", g1b[:], rawap(norm1.tensor, 0, [[0, 128], [1, D]]), [g1b.r])
                ld("sp", gqb[:].rearrange("p (h e) -> p h e", h=4), rawap(q_norm.tensor, 0, [[0, 128], [0, 4], [1, 128]]), [gqb.r])
                ld("sp", gkb[:].rearrange("p (h e) -> p h e", h=4), rawap(k_norm.tensor, 0, [[0, 128], [0, 4], [1, 128]]), [gkb.r])
                P.op("act", lambda e: e.mul(out=gqb[:], in_=gqb[:], mul=0.125), reads=[gqb.r], writes=[gqb.r])
                wi_r = [Res("wi%d" % k) for k in range(8)]
                w_in_v = w_in.rearrange("(kc p) n -> p kc n", p=128)
                for kc in range(8):
                    ld("pool", wi[:, kc, :], w_in_v[:, kc, :], [wi_r[kc]])

                NB = 2
                xt = [S(es_A, "xt%d" % i, [128, D], F32) for i in range(NB)]
                junk = S(es_A, "junk", [128, D], BF16)
                ss = [S(es_A, "ss%d" % i, [128, 4], F32) for i in range(NB)]
                xn = [S(es_A, "xn%d" % i, [128, D], BF16) for i in range(NB)]
                xnT = [S(es_A, "xnT%d" % i, [128, 8, 128], BF16) for i in range(NB)]
                sqb = [S(es_A, "sqb%d" % i, [128, 1024], F32) for i in range(NB)]
                s8 = [S(es_A, "s8%d" % i, [128, 16 * 3], F32) for i in range(NB)]
                qk32 = [S(es_A, "qk32%d" % i, [128, 1024], F32) for i in range(NB)]
                k32 = [S(es_A, "k32%d" % i, [128, 512], F32) for i in range(NB)]
                qkb = [S(es_A, "qkb%d" % i, [128, 1024], BF16) for i in range(NB)]
                v32 = [S(es_A, "v32%d" % i, [128, 512], F32) for i in range(NB)]
                pT = [PS(es_A, "pT%d" % i, [128, 8, 128], BF16) for i in range(2)]
                ps_qk = PS(es_A, "ps_qk", [128, 1024], F32)
                psqk_r = [Res("psq"), Res("psk")]
                ps_v = PS(es_A, "ps_v", [128, 512], F32)
                ps_u = PS(es_A, "ps_u", [128, 4, 128], F32)
                pQK = PS(es_A, "pQK", [128, 8, 128], BF16)
                qT_r = [Res("qT%d" % i) for i in range(NT)]
                kT_r = [Res("kT%d" % i) for i in range(NT)]
                vb_r = [Res("vb%d" % i) for i in range(NT)]
                uT_r = [Res("uT%d" % i) for i in range(NT)]

                import os as _os
                _NTR = int(_os.environ.get('DBG_NT', NT)); _CUT = float(_os.environ.get('DBG_CUT', 99))
                for tt in range(_NTR):
                    b = tt % NB
                    X = xt[b]
                    if tt == 0:
                        P.op("pool", lambda e, X=X: e.memset(X[:], 0.0), writes=[X.r])
                        ld("sp", X[0:16, :], meta, [X.r], name="x0a")
                        r2 = Res("x0b")
                        ld("sp", X[32:96, :], xs, [r2], reads=[X.r])
                        xr = [X.r, r2]
                    else:
                        ld("sp", X[:], xp[(tt - 1) * 128:tt * 128, :], [X.r])
                        xr = [X.r]
                    P.op("act", lambda e, X=X, b=b: e.activation(out=junk[:], in_=X[:], func=AF.Square, accum_out=ss[b][:, 0:1]),
                         reads=xr, writes=[junk.r, ss[b].r])
                    P.op("dve", lambda e, b=b: e.tensor_scalar(out=ss[b][:, 1:2], in0=ss[b][:, 0:1], scalar1=1.0 / D, scalar2=EPS,
                                                               op0=ALU.mult, op1=ALU.add), reads=[ss[b].r], writes=[ss[b].r])
                    P.op("act", lambda e, b=b: e.sqrt(out=ss[b][:, 2:3], in_=ss[b][:, 1:2]), reads=[ss[b].r], writes=[ss[b].r])
                    P.op("dve", lambda e, b=b: e.reciprocal(out=ss[b][:, 3:4], in_=ss[b][:, 2:3]), reads=[ss[b].r], writes=[ss[b].r])
                    P.op("dve", lambda e, X=X, b=b: e.scalar_tensor_tensor(out=xn[b][:], in0=X[:], scalar=ss[b][:, 3:4], in1=g1b[:],
                                                                          op0=ALU.mult, op1=ALU.mult),
                         reads=xr + [ss[b].r, g1b.r], writes=[xn[b].r])
                    if _CUT < 1:
                        continue
                    pt_ = pT[tt % 2]
                    for kc in range(8):
                        P.op("pe", lambda e, b=b, kc=kc, pt_=pt_: e.transpose(out=pt_[:, kc, :], in_=xn[b][:, kc * 128:(kc + 1) * 128],
                                                                              identity=ident_b[:]),
                             reads=[xn[b].r, ident_b.r], writes=[pt_.r])
                    P.op("act", lambda e, b=b, pt_=pt_: e.copy(out=xnT[b][:], in_=pt_[:]), reads=[pt_.r], writes=[xnT[b].r])
                    if _CUT < 2:
                        continue
                    for nb in range(2):
                        for kc in range(8):
                            P.op("pe", lambda e, b=b, kc=kc, nb=nb: e.matmul(ps_qk[:, nb * 512:(nb + 1) * 512], lhsT=xnT[b][:, kc, :],
                                                                              rhs=wi[:, kc, nb * 512:(nb + 1) * 512],
                                                                              start=(kc == 0), stop=(kc == 7)),
                                 reads=[xnT[b].r, wi_r[kc]], writes=[psqk_r[nb]])
                    for kc in range(8):
                        P.op("pe", lambda e, b=b, kc=kc: e.matmul(ps_v[:], lhsT=xnT[b][:, kc, :], rhs=wi[:, kc, 1024:1536],
                                                                  start=(kc == 0), stop=(kc == 7)),
                             reads=[xnT[b].r, wi_r[kc]], writes=[ps_v.r])
                    for c in range(4):
                        for kc in range(8):
                            P.op("pe", lambda e, b=b, kc=kc, c=c: e.matmul(ps_u[:, c, :], lhsT=wi[:, kc, 1536 + c * 128:1536 + (c + 1) * 128],
                                                                            rhs=xnT[b][:, kc, :], start=(kc == 0), stop=(kc == 7)),
                                 reads=[xnT[b].r, wi_r[kc]], writes=[ps_u.r])
                    if _CUT < 3:
                        continue
                    for hf in range(2):
                        sl = slice(hf * 512, (hf + 1) * 512)
                        P.op("act", lambda e, b=b, sl=sl: e.activation(out=sqb[b][:, sl], in_=ps_qk[:, sl], func=AF.Square),
                             reads=[psqk_r[hf], sqb[b].r], writes=[sqb[b].r])
                    if _CUT < 3.1:
                        continue
                    P.op("dve", lambda e, b=b: e.tensor_reduce(out=s8[b][:, 0:16], in_=sqb[b][:].rearrange("p (c d) -> p c d", d=64),
                                                               axis=AX.X, op=ALU.add), reads=[sqb[b].r], writes=[s8[b].r])
                    P.op("dve", lambda e, b=b: e.tensor_scalar(out=s8[b][:, 16:32], in0=s8[b][:, 0:16], scalar1=1.0 / 64, scalar2=EPS,
                                                               op0=ALU.mult, op1=ALU.add), reads=[s8[b].r], writes=[s8[b].r])
                    P.op("act", lambda e, b=b: e.sqrt(out=s8[b][:, 32:48], in_=s8[b][:, 16:32]), reads=[s8[b].r], writes=[s8[b].r])
                    P.op("dve", lambda e, b=b: e.reciprocal(out=s8[b][:, 0:16], in_=s8[b][:, 32:48]), reads=[s8[b].r], writes=[s8[b].r])
                    if _CUT < 3.2:
                        continue
                    for hf in range(2):
                        sl = slice(hf * 512, (hf + 1) * 512)
                        P.op("dve", lambda e, b=b, sl=sl, hf=hf: e.tensor_tensor(
                            out=qk32[b][:, sl].rearrange("p (c d) -> p c d", d=64),
                            in0=ps_qk[:, sl].rearrange("p (c d) -> p c d", d=64),
                            in1=s8[b][:, hf * 8:hf * 8 + 8].unsqueeze(2).to_broadcast([128, 8, 64]), op=ALU.mult),
                            reads=[psqk_r[hf], s8[b].r, qk32[b].r], writes=[qk32[b].r])
                    if _CUT < 3.3:
                        continue
                    P.op("pool", lambda e, b=b: e.tensor_tensor(out=qkb[b][:, 0:512], in0=qk32[b][:, 0:512], in1=gqb[:], op=ALU.mult),
                         reads=[qk32[b].r, gqb.r], writes=[qkb[b].r])
                    P.op("dve", lambda e, b=b: e.tensor_tensor(out=k32[b][:], in0=qk32[b][:, 512:1024], in1=gkb[:], op=ALU.mult),
                         reads=[qk32[b].r, gkb.r], writes=[k32[b].r])
                    rkb = Res("kb")
                    P.op("pool", lambda e, b=b: e.tensor_copy(out=qkb[b][:, 512:1024], in_=k32[b][:]), reads=[k32[b].r, qkb[b].r], writes=[qkb[b].r])
                    if _CUT < 3.4:
                        continue
                    for h in range(8):
                        P.op("pe", lambda e, b=b, h=h: e.transpose(out=pQK[:, h, :], in_=qkb[b][:, h * 128:(h + 1) * 128], identity=ident_b[:]),
                             reads=[qkb[b].r, ident_b.r], writes=[pQK.r])
                    if _CUT < 3.5:
                        continue
                    c0 = tt * 128
                    if _CUT >= 3.6:
                        P.op("act", lambda e, c0=c0: e.copy(out=qT[:, :, c0:c0 + 128], in_=pQK[:, 0:4, :]), reads=[pQK.r], writes=[qT_r[tt]])
                    if _CUT >= 3.7:
                        P.op("act", lambda e, c0=c0: e.copy(out=kT[:, :, c0:c0 + 128], in_=pQK[:, 4:8, :]), reads=[pQK.r], writes=[kT_r[tt]])
                    if _CUT < 4:
                        continue
                    P.op("act", lambda e, b=b: e.copy(out=v32[b][:], in_=ps_v[:]), reads=[ps_v.r], writes=[v32[b].r])
                    P.op("pool", lambda e, b=b, tt=tt: e.tensor_copy(out=vb[:, tt, :], in_=v32[b][:]), reads=[v32[b].r], writes=[vb_r[tt]])
                    if tt == 0:
                        P.op("dve", lambda e: e.tensor_copy(out=uT[:, :, 0:16], in_=ps_u[:, :, 0:16]), reads=[ps_u.r], writes=[uT_r[0]])
                        P.op("dve", lambda e: e.tensor_copy(out=uT[:, :, L:L + 64], in_=ps_u[:, :, 32:96]), reads=[ps_u.r, uT_r[0]], writes=[uT_r[0]])
                    else:
                        o0 = 16 + (tt - 1) * 128
                        P.op("act", lambda e, o0=o0: e.copy(out=uT[:, :, o0:o0 + 128], in_=ps_u[:]), reads=[ps_u.r], writes=[uT_r[tt]])
                    if tt == 0:
                        ld("sp", kp[0:16, :], k32[b][0:16, :], [], reads=[k32[b].r])
                        ld("sp", ks[:, :], k32[b][32:96, :], [], reads=[k32[b].r])
                        ld("sp", vp[0:16, :], v32[b][0:16, :], [], reads=[v32[b].r])
                        ld("sp", vs[:, :], v32[b][32:96, :], [], reads=[v32[b].r])
                    else:
                        o0 = 16 + (tt - 1) * 128
                        ld("sp", kp[o0:o0 + 128, :], k32[b][:], [], reads=[k32[b].r])
                        ld("sp", vp[o0:o0 + 128, :], v32[b][:], [], reads=[v32[b].r])
                P.barrier()
            P.cut(100)
            es_L2 = contextlib.ExitStack()
            with es_L2:
                lq = S(es_L2, "lq", [128, 2, 128], F32)
                ld("sp", lq[:, 0, :], rawap(lam_q.tensor, 0, [[0, 128], [1, 128]]), [lq.r])
                ld("sp", lq[:, 1, :], rawap(lam_k.tensor, 0, [[0, 128], [1, 128]]), [lq.r])
                TT("dve", lq[:, 0, :], lq[:, 0, :], lq[:, 1, :], ALU.mult, [lq.r], [lq.r])
                P.op("dve", lambda e: e.tensor_reduce(out=lamt[:, 0:2], in_=lq[:, 0, :].rearrange("p (c d) -> p c d", c=2), axis=AX.X, op=ALU.add),
                     reads=[lq.r], writes=[lamt.r])
                ACT(lamt[:, 2:4], lamt[:, 0:2], AF.Exp, [lamt.r], [lamt.r])
                TT("dve", lamt[:, 5:6], lamt[:, 2:3], lamt[:, 3:4], ALU.subtract, [lamt.r], [lamt.r])
                TSC("dve", lamt[:, 5:6], lamt[:, 5:6], LAM_INIT, None, ALU.add, None, [lamt.r], [lamt.r])
                TSC("dve", lamt[:, 4:5], lamt[:, 5:6], -1.0, None, ALU.mult, None, [lamt.r], [lamt.r])
                ld("sp", sgp[:, 0:1], rawap(sub_norm.tensor, 0, [[1, 128], [1, 1]]), [sgp.r])
                TSC("dve", sgp[:, 1:2], sgp[:, 0:1], 1.0 - LAM_INIT, None, ALU.mult, None, [sgp.r], [sgp.r])
                MS("pool", maskT[:], 1.0, [maskT.r])
                P.op("pool", lambda e: e.affine_select(out=maskT[:], in_=maskT[:], pattern=[[1, 128]], compare_op=ALU.is_ge, fill=0.0,
                                                       base=0, channel_multiplier=-1), reads=[maskT.r], writes=[maskT.r])
                MS("pool", ones_f[:], 1.0, [ones_f.r])
                P.barrier()
            NLAM = lamt[:, 4:5]
            es_At = contextlib.ExitStack()
            with es_At:
                sA = [PS(es_At, "sA%d" % i, [128, 512], F32) for i in range(2)]
                sB = [PS(es_At, "sB%d" % i, [128, 512], F32) for i in range(2)]
                po = [PS(es_At, "po%d" % i, [128, 512], F32) for i in range(2)]
                pss = [PS(es_At, "pss%d" % i, [128, 512], F32) for i in range(2)]
                ptb = [[S(es_At, "ptb%d_%d" % (c, i), [128, 512], BF16) for i in range(2)] for c in range(2)]
                wa = [S(es_At, "wa%d" % i, [128, 512], F32) for i in range(4)]
                kv_all = list(kT_r) + list(qT_r) + list(vb_r)
                groups = [(0, 16, [0])] + [(128 * (4 * g + 1), 512, list(range(0, 4 * g + 5))) for g in range(4)]
                steps = []
                for (q0, NQ, kts) in groups:
                    for h in range(4):
                        for kt in kts:
                            if NQ == 16:
                                off, N, diag = 0, 16, True
                            elif kt * 128 < q0:
                                off, N, diag = 0, 512, False
                            else:
                                off = kt * 128 - q0
                                N, diag = 512 - off, True
                            steps.append((q0, NQ, h, kt, off, N, diag, kt == kts[0], kt == kts[-1]))

                def stage1(i):
                    (q0, NQ, h, kt, off, N, diag, first, last) = steps[i]
                    b = i % 2
                    sb = (sA[b], sB[b])
                    for c in range(2):
                        MM(sb[c][:, 0:N], kT[64 * c:64 * c + 64, h, kt * 128:(kt + 1) * 128], qT[64 * c:64 * c + 64, h, q0 + off:q0 + off + N],
                           True, True, kv_all, [sb[c].r])
                    for c in range(2):
                        pt_ = ptb[c][b]
                        ACT(pt_[:, 0:N], sb[c][:, 0:N], AF.Exp, [sb[c].r], [pt_.r])
                        if diag:
                            nd = min(N, 128)
                            TT("dve" if c == 0 else "pool", pt_[:, 0:nd], pt_[:, 0:nd], maskT[:, 0:nd], ALU.mult, [pt_.r, maskT.r], [pt_.r])
                        elif kt == 0:
                            TSC("dve" if c == 0 else "pool", pt_[:, 0:N], pt_[:, 0:N], maskT[:, 15:16], None, ALU.mult, None,
                                [pt_.r, maskT.r], [pt_.r])

                def stage2(i):
                    (q0, NQ, h, kt, off, N, diag, first, last) = steps[i]
                    b = i % 2
                    for c in range(2):
                        pt_ = ptb[c][b]
                        MM(po[c][:, off:off + N], vb[:, kt, h * 128:(h + 1) * 128], pt_[:, 0:N], first, last, [pt_.r] + kv_all, [po[c].r])
                        MM(pss[c][:, off:off + N], ones_b[:], pt_[:, 0:N], first, last, [pt_.r, ones_b.r], [pss[c].r])
                    if not last:
                        return
                    W0, W1, W2, W3 = wa
                    P.op("dve", lambda e: e.reciprocal(out=W0[:, 0:NQ], in_=pss[0][:, 0:NQ]), reads=[pss[0].r, W0.r], writes=[W0.r])
                    TT("dve", W0[:, 0:NQ], po[0][:, 0:NQ], W0[:, 0:NQ], ALU.mult, [po[0].r, W0.r], [W0.r])
                    P.op("dve", lambda e: e.reciprocal(out=W1[:, 0:NQ], in_=pss[1][:, 0:NQ]), reads=[pss[1].r, W1.r], writes=[W1.r])
                    TT("dve", W1[:, 0:NQ], po[1][:, 0:NQ], W1[:, 0:NQ], ALU.mult, [po[1].r, W1.r], [W1.r])
                    STT(W2[:, 0:NQ], W1[:, 0:NQ], NLAM, W0[:, 0:NQ], ALU.mult, ALU.add, [W0.r, W1.r, lamt.r, W2.r], [W2.r])
                    ACT(W3[:, 0:NQ], W2[:, 0:NQ], AF.Square, [W2.r, W3.r], [W3.r])
                    P.op("pe", lambda e: e.matmul(pss[0][:, 0:NQ], lhsT=ones_f[:], rhs=W3[:, 0:NQ], start=True, stop=True),
                         reads=[W3.r, ones_f.r], writes=[pss[0].r])
                    TSC("dve", W0[:, 0:NQ], pss[0][:, 0:NQ], 1.0 / 128, EPS, ALU.mult, ALU.add, [pss[0].r, W0.r], [W0.r])
                    P.op("act", lambda e: e.sqrt(out=W0[:, 0:NQ], in_=W0[:, 0:NQ]), reads=[W0.r], writes=[W0.r])
                    P.op("dve", lambda e: e.reciprocal(out=W0[:, 0:NQ], in_=W0[:, 0:NQ]), reads=[W0.r], writes=[W0.r])
                    STT(mixT[:, h, q0:q0 + NQ], W2[:, 0:NQ], sgp[:, 1:2], W0[:, 0:NQ], ALU.mult, ALU.mult, [W2.r, W0.r, sgp.r], [mix_r[h]])

                stage1(0)
                for i in range(len(steps)):
                    if i + 1 < len(steps):
                        stage1(i + 1)
                    stage2(i)
                P.barrier()
            P.cut(200)
            es_Q = contextlib.ExitStack()
            with es_Q:
                ptbc = S(es_Q, "ptbc", [128, NS * NPG], I32)
                iop = S(es_Q, "iop", [128, 1], I32)
                idx = S(es_Q, "idx", [128, NS * NPG], I32)
                ld("sp", ptbc[:], rawap(pt.tensor, 0, [[0, 128], [1, NS * NPG]]), [ptbc.r])
                P.op("pool", lambda e: e.iota(iop[:], pattern=[[0, 1]], base=0, channel_multiplier=1), writes=[iop.r])
                TSC("dve", idx[:], ptbc[:], 128, None, ALU.mult, None, [ptbc.r], [idx.r])
                TSC("dve", idx[:], idx[:], iop[:, 0:1], None, ALU.add, None, [idx.r, iop.r], [idx.r])
                Qblk = S(es_Q, "Qblk", [128, 4, 2, NS * TS], BF16)
                MS("pool", Qblk[:], 0.0, [Qblk.r])
                CP("act", Qblk[0:64, :, 0, :], qT[0:64, :, 32:96], [Qblk.r] + list(qT_r), [Qblk.r])
                CP("act", Qblk[64:128, :, 1, :], qT[64:128, :, 32:96], [Qblk.r] + list(qT_r), [Qblk.r])
                vnew2 = [S(es_Q, "vnew%d" % i, [128, 512], BF16) for i in range(2)]
                for _v in vnew2:
                    MS("pool", _v[:], 0.0, [_v.r])
                pnew = S(es_Q, "pnew", [128, 32], BF16)
                MS("pool", pnew[:], 0.0, [pnew.r])
                msk4 = S(es_Q, "msk4", [4, 32], F32)
                MS("pool", msk4[:], 1.0, [msk4.r])
                P.op("pool", lambda e: e.affine_select(out=msk4[:], in_=msk4[:], pattern=[[0, 2], [0, 4], [1, 4]], compare_op=ALU.is_ge, fill=0.0,
                                                       base=0, channel_multiplier=-1), reads=[msk4.r], writes=[msk4.r])
                e4 = S(es_Q, "e4", [4, 32], F32)
                att = S(es_Q, "att", [NS * TS, 512], F32)
                kvpg = [S(es_Q, "kvpg%d" % i, [128, 4, 1024], F32) for i in range(3)]
                vpg = [S(es_Q, "vpg%d" % i, [128, 4, 512], BF16) for i in range(6)]
                kTs = [S(es_Q, "kTs%d" % i, [128, 4, 128], BF16) for i in range(3)]
                pexp = [S(es_Q, "pexp%d" % i, [128, 512], BF16) for i in range(2)]
                rs = [S(es_Q, "rs%d" % i, [16, 4], F32) for i in range(2)]
                t0 = [S(es_Q, "t0_%d" % i, [16, 512], F32) for i in range(2)]
                o16 = [S(es_Q, "o16_%d" % i, [16, 512], F32) for i in range(2)]
                kvd = [P.new_dsem("kvd%d" % i) for i in range(3)]
                es_Q1 = contextlib.ExitStack()
                with es_Q1:
                    pk = [PS(es_Q1, "pk%d" % i, [128, 512], F32) for i in range(2)]
                    pS = [PS(es_Q1, "pS%d" % i, [128, 512], F32) for i in range(2)]
                    pSn = PS(es_Q1, "pSn", [128, 512], F32)
                    poc = [PS(es_Q1, "poc%d" % i, [128, 512], F32) for i in range(2)]
                    psm = PS(es_Q1, "psm", [128, 512], F32)

                    def gather(dst, sem, src, cols):
                        if P.dead:
                            return
                        waits = P._deps("pool", [idx.r], [dst.r])
                        fns = []
                        for n, col in enumerate(cols):
                            fns.append((n, col))
                        sem.val += 16 * len(cols)

                        def emit(e, waits=waits, fns=fns, dst=dst, sem=sem, src=src):
                            for ws, wv in waits:
                                e.wait_ge(ws, wv)
                            for n, col in fns:
                                e.indirect_dma_start(out=dst[:, n, :], out_offset=None, in_=src,
                                                     in_offset=bass.IndirectOffsetOnAxis(ap=idx[:, col:col + 1], axis=0)).then_inc(sem.sem, 16)
                        P.q["pool"].append(emit)
                        P._mark([idx.r], [dst.r], (sem.sem, sem.val))

                    pages = []
                    for s in range(NS):
                        for g4 in range(4):
                            gi_ = s * 4 + g4
                            for n in range(4):
                                pages.append((s, g4, n, gi_))

                    def stage1(j):
                        (s, g4, n, gi_) = pages[j]
                        kvb = kvpg[gi_ % 3]
                        if n == 0:
                            gather(kvb, kvd[gi_ % 3], ckv, [s * NPG + g4 * 4 + q for q in range(4)])
                        pk_ = pk[j % 2]
                        kt_ = kTs[j % 3]
                        for h in range(4):
                            TR(pk_[:, h * 128:(h + 1) * 128], kvb[:, n, h * 128:(h + 1) * 128], ident_f[:], [kvb.r, ident_f.r], [pk_.r])
                        CP("act" if j % 2 == 0 else "dve", kt_[:].rearrange("p h k -> p (h k)"), pk_[:], [pk_.r], [kt_.r])

                    def stage2(j):
                        (s, g4, n, gi_) = pages[j]
                        ps_ = pS[s % 2]
                        kt_ = kTs[j % 3]
                        base = (g4 * 4 + n) * 32
                        for h in range(4):
                            for c in range(2):
                                o0 = base + c * 16 + h * 4
                                MM(ps_[:, o0:o0 + 4], kt_[:, h, :], Qblk[:, h, c, 4 * s:4 * s + 4], True, True, [kt_.r, Qblk.r], [ps_.r])
                        if n == 3:
                            CP("dve", vpg[gi_ % 6][:], kvpg[gi_ % 3][:, :, 512:1024], [kvpg[gi_ % 3].r], [vpg[gi_ % 6].r])

                    def tail(s):
                        ps_ = pS[s % 2]
                        vnew = vnew2[s % 2]
                        MM(pSn[0:4, :], ident_b[:, 32 + 4 * s:36 + 4 * s], vb[:, 0, :], True, True, [ident_b.r] + list(vb_r), [pSn.r])
                        CP("act", vnew[0:4, :], pSn[0:4, :], [pSn.r, vnew.r], [vnew.r])
                        for h in range(4):
                            for c in range(2):
                                o0 = c * 16 + h * 4
                                MM(pSn[0:4, o0:o0 + 4], kT[:, h, 32 + 4 * s:36 + 4 * s], Qblk[:, h, c, 4 * s:4 * s + 4],
                                   True, True, [Qblk.r] + list(kT_r), [pSn.r])
                        ACT(e4[:], pSn[0:4, 0:32], AF.Exp, [pSn.r, e4.r], [e4.r])
                        TT("dve", pnew[0:4, :], e4[:], msk4[:], ALU.mult, [e4.r, msk4.r, pnew.r], [pnew.r])
                        px = pexp[s % 2]
                        ACT(px[:], ps_[:], AF.Exp, [ps_.r], [px.r])
                        for c in range(2):
                            for n in range(16):
                                vbf = vpg[(s * 4 + n // 4) % 6]
                                lh = px[:, n * 32 + c * 16:n * 32 + c * 16 + 16]
                                MM(poc[c][0:16, :], lh, vbf[:, n % 4, :], n == 0, False, [px.r, vbf.r], [poc[c].r])
                                MM(psm[0:16, c:c + 1], lh, ones_b[:, 0:1], n == 0, False, [px.r, ones_b.r], [psm.r])
                            lh = pnew[:, c * 16:(c + 1) * 16]
                            MM(poc[c][0:16, :], lh, vnew[:], False, True, [pnew.r, vnew.r], [poc[c].r])
                            MM(psm[0:16, c:c + 1], lh, ones_b[:, 0:1], False, True, [pnew.r, ones_b.r], [psm.r])
                        r_, t_, o_ = rs[s % 2], t0[s % 2], o16[s % 2]
                        P.op("dve", lambda e: e.reciprocal(out=r_[:, 0:2], in_=psm[0:16, 0:2]), reads=[psm.r, r_.r], writes=[r_.r])
                        TT("dve", r_[:, 2:3], r_[:, 1:2], lamt[0:16, 4:5], ALU.mult, [r_.r, lamt.r], [r_.r])
                        TSC("dve", t_[:], poc[0][0:16, :], r_[:, 0:1], None, ALU.mult, None, [poc[0].r, r_.r, t_.r], [t_.r])
                        STT(o_[:], poc[1][0:16, :], r_[:, 2:3], t_[:], ALU.mult, ALU.add, [poc[1].r, r_.r, t_.r, o_.r], [o_.r])
                        for h in range(4):
                            ld("sp", att[4 * s:4 * s + 4, h * 128:(h + 1) * 128], o_[4 * h:4 * h + 4, h * 128:(h + 1) * 128], [], reads=[o_.r])

                    stage1(0)
                    for j in range(len(pages)):
                        if j + 1 < len(pages):
                            stage1(j + 1)
                        stage2(j)
                        s, g4, n, gi_ = pages[j]
                        if g4 == 0 and n == 3 and s > 0:
                            tail(s - 1)
                    tail(NS - 1)
                    P.barrier()
                es_Q2 = contextlib.ExitStack()
                with es_Q2:
                    NQ4 = NS * TS
                    sq4 = S(es_Q2, "sq4", [NQ4, 4, 128], F32)
                    s4 = S(es_Q2, "s4", [NQ4, 3, 4], F32)
                    sg4 = S(es_Q2, "sg4", [NQ4, 128], F32)
                    attb = S(es_Q2, "attb", [NQ4, 512], BF16)
                    pT4 = PS(es_Q2, "pT4", [128, 4, NQ4], BF16)
                    ld("sp", sg4[:], rawap(sub_norm.tensor, 0, [[0, NQ4], [1, 128]]), [sg4.r])
                    TSC("dve", sg4[:], sg4[:], 1.0 - LAM_INIT, None, ALU.mult, None, [sg4.r], [sg4.r])
                    attv = att[:].rearrange("p (h e) -> p h e", h=4)
                    ACT(sq4[:], attv, AF.Square, [att.r], [sq4.r])
                    P.op("dve", lambda e: e.tensor_reduce(out=s4[:, 0, :], in_=sq4[:], axis=AX.X, op=ALU.add), reads=[sq4.r], writes=[s4.r])
                    TSC("dve", s4[:, 1, :], s4[:, 0, :], 1.0 / 128, EPS, ALU.mult, ALU.add, [s4.r], [s4.r])
                    P.op("act", lambda e: e.sqrt(out=s4[:, 2, :], in_=s4[:, 1, :]), reads=[s4.r], writes=[s4.r])
                    P.op("dve", lambda e: e.reciprocal(out=s4[:, 0, :], in_=s4[:, 2, :]), reads=[s4.r], writes=[s4.r])
                    TT("dve", sq4[:], attv, s4[:, 0, :].unsqueeze(2).to_broadcast([NQ4, 4, 128]), ALU.mult, [att.r, s4.r, sq4.r], [sq4.r])
                    TT("dve", attb[:].rearrange("p (h e) -> p h e", h=4), sq4[:], sg4[:].unsqueeze(1).to_broadcast([NQ4, 4, 128]), ALU.mult,
                       [sq4.r, sg4.r], [attb.r])
                    for h in range(4):
                        TR(pT4[:, h, :], attb[:, h * 128:(h + 1) * 128], ident_b[0:NQ4, 0:NQ4], [attb.r, ident_b.r], [pT4.r])
                    CP("act", mixT[:, 0:4, 32:96], pT4[:], [pT4.r], mix_r[0:4])
                    P.barrier()
            P.cut(50)
            es_B.close()
            TC = 258
            NM = L // TC
            PI = math.pi
            es_S = contextlib.ExitStack()
            with es_S:
                prm = S(es_S, "prm", [128, 3, 16], F32)
                sc = S(es_S, "sc", [128, 24, 16], F32)
                sci = S(es_S, "sci", [128, 16], I32)
                BT = S(es_S, "BT", [128, 4, 2, 128], BF16)
                CT = S(es_S, "CT", [128, 16, 2, 128], BF16)
                Dp = S(es_S, "Dp", [128, 8], F32)
                wg = S(es_S, "wg", [128, 4, 512], BF16)
                TCs = S(es_S, "TCs", [128, 16, TC], F32)
                TSn = S(es_S, "TSn", [128, 16, TC], F32)
                rq = S(es_S, "rq", [128, 16], F32)
                wkr = [[Res("wkr%d_%d" % (bb, r)) for r in range(8)] for bb in range(2)]
                wk2r = [[Res("wk2r%d_%d" % (bb, r)) for r in range(4)] for bb in range(2)]
                h32r = [[Res("h32r%d_%d" % (ii, r)) for r in range(2)] for ii in range(16)]
                y32 = S(es_S, "y32", [128, TC], F32)
                g32 = [S(es_S, "g32_%d" % i, [128, TC], F32) for i in range(4)]
                gb = [S(es_S, "gb_%d" % i, [128, TC], BF16) for i in range(4)]
                sg = S(es_S, "sg", [128, TC], F32)
                es_P = contextlib.ExitStack()
                es_P.__enter__()
                pa = S(es_P, "pa", [16, 3, 128], F32)
                ldt = S(es_P, "ldt", [16, 2], F32)
                ld("sp", pa[:, 0, :], a_re, [pa.r])
                ld("sp", pa[:, 1, :], a_im, [pa.r])
                ld("sp", ldt[:], log_dt, [ldt.r])
                CP("dve", pa[:, 2, :].rearrange("p (g q) -> p g q", g=2), ldt[:].unsqueeze(2).to_broadcast([16, 2, 64]), [ldt.r, pa.r], [pa.r])
                ps0 = PS(es_S, "ps0", [128, 512], F32)
                for j in range(3):
                    TR(ps0[:, j * 16:(j + 1) * 16], pa[:, j, :], ident_f[0:16, 0:16], [pa.r, ident_f.r], [ps0.r])
                CP("act", prm[:].rearrange("p a b -> p (a b)"), ps0[:, 0:48], [ps0.r], [prm.r])
                P.cut(1)
                R = lambda k: sc[:, k, :]
                are, aim, ldtp = prm[:, 0, :], prm[:, 1, :], prm[:, 2, :]
                scr = [sc.r, prm.r]
                ACT(R(0), ldtp, AF.Exp, scr, [sc.r])
                TT("dve", R(1), are, R(0), ALU.mult, scr, [sc.r])
                ACT(R(2), R(1), AF.Exp, scr, [sc.r])
                TT("dve", R(3), aim, R(0), ALU.mult, scr, [sc.r])

                def sincos(x, s_out, c_out, t1, t2, ti, rd, wr):
                    C1 = 6.28125
                    C2 = 2 * PI - C1
                    TSC("dve", t1, x, 1.0 / (2 * PI), None, ALU.mult, None, rd, wr)
                    CP("dve", ti, t1, rd, wr)
                    CP("dve", t1, ti, rd, wr)
                    STT(t2, t1, -C1, x, ALU.mult, ALU.add, rd, wr)
                    STT(t2, t1, -C2, t2, ALU.mult, ALU.add, rd, wr)
                    TSC("dve", t1, t2, PI, -2 * PI, ALU.is_gt, ALU.mult, rd, wr)
                    TT("dve", t2, t2, t1, ALU.add, rd, wr)
                    TSC("dve", t1, t2, -PI, 2 * PI, ALU.is_lt, ALU.mult, rd, wr)
                    TT("dve", t2, t2, t1, ALU.add, rd, wr)
                    ACT(s_out, t2, AF.Sin, rd, wr)
                    TSC("dve", t2, t2, PI / 2, None, ALU.add, None, rd, wr)
                    TSC("dve", t1, t2, PI, -2 * PI, ALU.is_gt, ALU.mult, rd, wr)
                    TT("dve", t2, t2, t1, ALU.add, rd, wr)
                    ACT(c_out, t2, AF.Sin, rd, wr)
                sincos(R(3), R(4), R(5), R(6), R(7), sci[:], scr + [sci.r], [sc.r, sci.r])
                AR, AI = R(8), R(9)
                TT("dve", AR, R(2), R(5), ALU.mult, scr, [sc.r])
                TT("dve", AI, R(2), R(4), ALU.mult, scr, [sc.r])
                TT("dve", R(10), are, are, ALU.mult, scr, [sc.r])
                TT("dve", R(11), aim, aim, ALU.mult, scr, [sc.r])
                TT("dve", R(10), R(10), R(11), ALU.add, scr, [sc.r])
                P.op("dve", lambda e: e.reciprocal(out=R(10), in_=R(10)), reads=scr, writes=[sc.r])
                TSC("dve", R(11), AR, -1.0, None, ALU.add, None, scr, [sc.r])
                TT("dve", R(12), R(11), are, ALU.mult, scr, [sc.r])
                TT("dve", R(13), AI, aim, ALU.mult, scr, [sc.r])
                TT("dve", R(12), R(12), R(13), ALU.add, scr, [sc.r])
                TT("dve", R(12), R(12), R(10), ALU.mult, scr, [sc.r])
                TT("dve", R(13), AI, are, ALU.mult, scr, [sc.r])
                TT("dve", R(14), R(11), aim, ALU.mult, scr, [sc.r])
                TT("dve", R(13), R(13), R(14), ALU.subtract, scr, [sc.r])
                TT("dve", R(13), R(13), R(10), ALU.mult, scr, [sc.r])
                GR, GI = R(12), R(13)

                P.cut(2)
                BN = S(es_P, "BN", [128, 4, 256], F32)
                es_L = contextlib.ExitStack()
                with es_L:
                    Bld = S(es_L, "Bld", [16, 4, 2048], F32)
                    Cl2 = S(es_L, "Cl2", [16, 2, 2048], F32)
                    for j, src in enumerate((b_re, b_im, c_re, c_im)):
                        ld("sp", Bld[:, j, :], src, [Bld.r])
                    for ri in range(2):
                        CP("pool", Cl2[:, ri, :].rearrange("p (c g q) -> p c g q", c=16, g=2),
                           Bld[:, 2 + ri, :].rearrange("p (g c q) -> p c g q", g=2, c=16), [Bld.r, Cl2.r], [Cl2.r])
                    for j in range(4):
                        for c in range(16):
                            if j < 2:
                                src_ap = rawap(Bld, j * 2048 + c, [[4 * 2048, 16], [16, 128]])
                            else:
                                src_ap = Cl2[:, j - 2, c * 128:(c + 1) * 128]
                            TR(ps0[:, c * 16:(c + 1) * 16], src_ap, ident_f[0:16, 0:16], [Bld.r, Cl2.r, ident_f.r], [ps0.r])
                        CP("act", BN[:, j, :], ps0[:, 0:256], [ps0.r], [BN.r])
                    P.barrier()
                P.cut(3)
                Bv = lambda j: BN[:, j, :].rearrange("p (c i) -> p c i", c=16)
                BB = S(es_P, "BB", [128, 4, 256], F32)
                BBv = lambda j: BB[:, j, :].rearrange("p (c i) -> p c i", c=16)
                gbc = lambda g: g.unsqueeze(1).to_broadcast([128, 16, 16])
                rdB = [BN.r, BB.r, sc.r]
                TT("dve", BBv(0), Bv(0), gbc(GR), ALU.mult, rdB, [BB.r])
                TT("dve", BBv(2), Bv(1), gbc(GI), ALU.mult, rdB, [BB.r])
                TT("dve", BBv(0), BBv(0), BBv(2), ALU.subtract, rdB, [BB.r])
                TT("dve", BBv(1), Bv(1), gbc(GR), ALU.mult, rdB, [BB.r])
                TT("dve", BBv(2), Bv(0), gbc(GI), ALU.mult, rdB, [BB.r])
                TT("dve", BBv(1), BBv(1), BBv(2), ALU.add, rdB, [BB.r])
                MASK = S(es_P, "MASK", [128, 16, 2, 16], F32)
                MS("pool", MASK[:], 0.0, [MASK.r])
                MS("pool", MASK[0:64, :, 0, :], 1.0, [MASK.r])
                MS("pool", MASK[64:128, :, 1, :], 1.0, [MASK.r])
                Z4 = S(es_P, "Z4", [128, 16, 2, 16], F32)
                for ri in range(2):
                    TT("dve", Z4[:], rawap(BB, ri * 256, [[1024, 128], [1, 16], [0, 2], [16, 16]]), MASK[:], ALU.mult,
                       [BB.r, MASK.r, Z4.r], [Z4.r])
                    for k in range(4):
                        TR(ps0[:, k * 128:(k + 1) * 128], Z4[:, 4 * k:4 * k + 4, :, :].rearrange("p a b c -> p (a b c)"), ident_f[:],
                           [Z4.r, ident_f.r], [ps0.r])
                    CP("act", BT[:, :, ri, :], ps0[:].rearrange("p (k m) -> p k m", k=4), [ps0.r], [BT.r])
                P.cut(4)
                CTf = S(es_P, "CTf", [128, 16, 2, 128], F32)
                MS("pool", CTf[:], 0.0, [CTf.r])
                for ri in range(2):
                    for il in range(4):
                        outv = rawap(CTf, ri * 128 + il * 32 + il * 256, [[4096, 128], [4 * 256, 4], [16, 2], [1, 16]])
                        inv = rawap(BN, (2 + ri) * 256 + il, [[1024, 128], [4, 4], [0, 2], [16, 16]])
                        mk = MASK[:, 0:4, :, :]
                        TT("dve", outv, inv, mk, ALU.mult, [BN.r, MASK.r, CTf.r], [CTf.r])
                CP("pool", CT[:, :, 0, :], CTf[:, :, 0, :], [CTf.r], [CT.r])
                TSC("dve", CT[:, :, 1, :], CTf[:, :, 1, :], -1.0, None, ALU.mult, None, [CTf.r, CT.r], [CT.r])
                P.cut(5)
                dld = S(es_P, "dld", [4, 2, 128], F32)
                ld("sp", dld[:, 0, :], ssm_d, [dld.r])
                ld("sp", dld[:, 1, :], b_glu, [dld.r])
                TR(ps0[:, 0:4], dld[:, 0, :], ident_f[0:4, 0:4], [dld.r, ident_f.r], [ps0.r])
                TR(ps0[:, 4:8], dld[:, 1, :], ident_f[0:4, 0:4], [dld.r, ident_f.r], [ps0.r])
                CP("act", Dp[:], ps0[:, 0:8], [ps0.r], [Dp.r])
                ld("pool", wg[:], w_glu.rearrange("(kc p) n -> p kc n", p=128), [wg.r])
                P.cut(6)
                es_T = contextlib.ExitStack()
                with es_T:
                    NI = S(es_T, "NI", [128, TC], I32)
                    NF = S(es_T, "NF", [128, TC], F32)
                    P.op("pool", lambda e: e.iota(NI[:], pattern=[[1, TC]], base=1, channel_multiplier=0), writes=[NI.r])
                    CP("dve", NF[:], NI[:], [NI.r], [NF.r])
                    TA = S(es_T, "TA", [128, 16, TC], F32)
                    T1 = S(es_T, "T1", [128, 16, TC], F32)
                    T2 = S(es_T, "T2", [128, 16, TC], F32)
                    TI = S(es_T, "TI", [128, 16, TC], I32)
                    TT("dve", TA[:], R(3).unsqueeze(2).to_broadcast([128, 16, TC]), NF[:].unsqueeze(1).to_broadcast([128, 16, TC]), ALU.mult,
                       [sc.r, NF.r], [TA.r])
                    rr = [TA.r, T1.r, T2.r, TI.r, TCs.r, TSn.r]
                    sincos(TA[:], TSn[:], TCs[:], T1[:], T2[:], TI[:], rr, rr)
                    P.barrier()

                P.cut(7)
                P.barrier()
                es_P.close()
                es_M = contextlib.ExitStack()
                es_M.__enter__()
                NB2 = 2
                pXr = [PS(es_S, "pXr%d" % i, [128, 512], F32) for i in range(NB2)]
                pXi = [PS(es_S, "pXi%d" % i, [128, 512], F32) for i in range(NB2)]
                pY = PS(es_S, "pY", [128, 512], F32)
                pZ = PS(es_S, "pZ", [128, 512], F32)
                wk = [S(es_M, "wk%d" % i, [128, 8, TC], F32) for i in range(NB2)]
                wk2 = [S(es_M, "wk2%d" % i, [128, 4, TC], F32) for i in range(NB2)]
                H32 = [S(es_M, "H32_%d" % i, [128, 2, TC], F32) for i in range(16)]
                Hb = [S(es_M, "Hb%d" % i, [128, 2, TC], BF16) for i in range(4)]
                CP("dve", rq[:], R(2), [sc.r], [rq.r])

                def glu(N, outs):
                    for kq in range(4):
                        for kc in range(4):
                            MM(pZ[:, 0:N], wg[:, kc, kq * 128:(kq + 1) * 128], gb[kc][:, 0:N], kc == 0, kc == 3, [wg.r, gb[kc].r], [pZ.r])
                        ACT(sg[:, 0:N], pZ[:, 0:N], AF.Sigmoid, [pZ.r, Dp.r], [sg.r], bias=Dp[:, 4 + kq:5 + kq])
                        for (sl, dst, dres) in outs[kq]:
                            TT("pool", dst, g32[kq][:, sl], sg[:, sl], ALU.mult, [g32[kq].r, sg.r], [dres])

                def y_finish(k, N, ucols):
                    STT(y32[:, 0:N], uT[:, k, ucols], Dp[:, k:k + 1], pY[:, 0:N], ALU.mult, ALU.add, [uT_r[0], pY.r, Dp.r], [y32.r])
                    ACT(g32[k][:, 0:N], y32[:, 0:N], AF.Gelu_apprx_tanh, [y32.r], [g32[k].r])
                    CP("pool", gb[k][:, 0:N], g32[k][:, 0:N], [g32[k].r], [gb[k].r])

                uT_all = list(uT_r)
                for m in range(NM):
                    c0 = m * TC
                    for k in range(4):
                        for il in range(4):
                            i = 4 * k + il
                            b = i % NB2
                            W = wk[b]
                            W2 = wk2[b]
                            MM(pXr[b][:, 0:TC], BT[32 * il:32 * il + 32, k, 0, :], uT[32 * il:32 * il + 32, k, c0:c0 + TC], True, True,
                               [BT.r] + uT_all, [pXr[b].r], tp=(32 * il, 0))
                            MM(pXi[b][:, 0:TC], BT[32 * il:32 * il + 32, k, 1, :], uT[32 * il:32 * il + 32, k, c0:c0 + TC], True, True,
                               [BT.r] + uT_all, [pXi[b].r], tp=(32 * il, 0))
                            cs_, sn_ = TCs[:, i, :], TSn[:, i, :]
                            rd = [pXr[b].r, pXi[b].r, TCs.r, TSn.r, W.r]
                            wr_, w2r_, hr_ = wkr[b], wk2r[b], h32r[i]
                            tb_ = [TCs.r, TSn.r]
                            TT("dve", W[:, 0, :], pXr[b][:, 0:TC], cs_, ALU.mult, [pXr[b].r] + tb_, [wr_[0]])
                            TT("dve", W[:, 1, :], pXi[b][:, 0:TC], sn_, ALU.mult, [pXi[b].r] + tb_, [wr_[1]])
                            TT("dve", W[:, 2, :], pXi[b][:, 0:TC], cs_, ALU.mult, [pXi[b].r] + tb_, [wr_[2]])
                            TT("dve", W[:, 3, :], pXr[b][:, 0:TC], sn_, ALU.mult, [pXr[b].r] + tb_, [wr_[3]])
                            TT("dve", W[:, 4, :], W[:, 0, :], W[:, 1, :], ALU.add, [wr_[0], wr_[1]], [wr_[4]])
                            TT("dve", W[:, 5, :], W[:, 2, :], W[:, 3, :], ALU.subtract, [wr_[2], wr_[3]], [wr_[5]])
                            for ri in range(2):
                                init = 0.0 if m == 0 else H32[i][:, ri, TC - 1:TC]
                                P.op("dve", lambda e, W=W, ri=ri, init=init, i=i: e.tensor_tensor_scan(
                                    out=W[:, 6 + ri, :], data0=rq[:, i:i + 1].to_broadcast([128, TC]), data1=W[:, 4 + ri, :],
                                    initial=init, op0=ALU.mult, op1=ALU.add), reads=[wr_[4 + ri], rq.r, hr_[ri]], writes=[wr_[6 + ri]])
                            TT("pool", W2[:, 0, :], W[:, 6, :], cs_, ALU.mult, [wr_[6]] + tb_, [w2r_[0]])
                            TT("pool", W2[:, 1, :], W[:, 7, :], sn_, ALU.mult, [wr_[7]] + tb_, [w2r_[1]])
                            TT("pool", W2[:, 2, :], W[:, 7, :], cs_, ALU.mult, [wr_[7]] + tb_, [w2r_[2]])
                            TT("pool", W2[:, 3, :], W[:, 6, :], sn_, ALU.mult, [wr_[6]] + tb_, [w2r_[3]])
                            TT("pool", H32[i][:, 0, :], W2[:, 0, :], W2[:, 1, :], ALU.subtract, [w2r_[0], w2r_[1], hr_[0]], [hr_[0]])
                            TT("pool", H32[i][:, 1, :], W2[:, 2, :], W2[:, 3, :], ALU.add, [w2r_[2], w2r_[3], hr_[1]], [hr_[1]])
                            CP("act", Hb[il][:], H32[i][:], [hr_[0], hr_[1]], [Hb[il].r])
                        n = 0
                        for il in range(4):
                            for ri in range(2):
                                MM(pY[:, 0:TC], CT[:, 4 * k + il, ri, :], Hb[il][:, ri, :], n == 0, n == 7, [CT.r, Hb[il].r], [pY.r])
                                n += 1
                        y_finish(k, TC, slice(c0, c0 + TC))
                    outs = []
                    for kq in range(4):
                        if m == 0:
                            o = [(slice(0, 16), mixT[:, 4 + kq, 0:16], mix_r[4 + kq]),
                                 (slice(16, TC), mixT[:, 4 + kq, 128:128 + TC - 16], mix_r[4 + kq])]
                        else:
                            d0 = 128 + c0 - 16
                            o = [(slice(0, TC), mixT[:, 4 + kq, d0:d0 + TC], mix_r[4 + kq])]
                        outs.append(o)
                    glu(TC, outs)
                P.cut(9)
                FP = S(es_M, "FP", [128, 2, 16], F32)
                for i in range(16):
                    CP("dve", FP[:, :, i:i + 1], H32[i][:, :, TC - 1:TC], [h32r[i][0], h32r[i][1], FP.r], [FP.r])
                fpo = S(es_M, "fpo", [16, 2, 128], F32)
                for ri in range(2):
                    TR(ps0[0:16, ri * 128:(ri + 1) * 128], FP[:, ri, :], ident_f[:], [FP.r, ident_f.r], [ps0.r])
                CP("act", fpo[:].rearrange("p a b -> p (a b)"), ps0[0:16, 0:256], [ps0.r], [fpo.r])
                ld("sp", srp, fpo[:, 0, :], [], reads=[fpo.r])
                ld("sp", sip, fpo[:, 1, :], [], reads=[fpo.r])

                P.cut(10)
                P.barrier()
                es_M.close()
                NSC = NS * TS
                XS = S(es_S, "XS", [128, 2, 16, NSC], F32)
                banks = [pXr[0], pXr[1], pXi[0], pXi[1]]
                for ri in range(2):
                    for il in range(4):
                        for k in range(4):
                            MM(banks[il][:, k * NSC:(k + 1) * NSC], BT[32 * il:32 * il + 32, k, ri, :], uT[32 * il:32 * il + 32, k, L:L + NSC],
                               True, True, [BT.r] + uT_all, [banks[il].r], tp=(32 * il, 0))
                    for il in range(4):
                        CP("act", XS[:, ri, :, :].rearrange("p (k il) c -> p k il c", il=4)[:, :, il, :],
                           banks[il][:, 0:4 * NSC].rearrange("p (a b) -> p a b", a=4), [banks[il].r, XS.r], [XS.r])
                P.cut(10.1)
                H0l = S(es_S, "H0l", [16, 2, 2048], F32)
                ld("sp", H0l[:, 0, :], s_re0, [H0l.r])
                ld("sp", H0l[:, 1, :], s_im0, [H0l.r])
                HS = S(es_S, "HS", [128, 2, 16, NS, TS + 1], F32)
                for ri in range(2):
                    for i in range(16):
                        TR(ps0[:, i * 16:(i + 1) * 16], H0l[:, ri, i * 128:(i + 1) * 128], ident_f[0:16, 0:16], [H0l.r, ident_f.r], [ps0.r])
                    CP("act", HS[:, ri, :, :, 0], ps0[:, 0:256].rearrange("p (a b) -> p a b", a=16), [ps0.r, HS.r], [HS.r])
                P.cut(10.2)
                M4 = S(es_S, "M4", [128, 4, 16, NS], F32)
                abc = lambda a: a.unsqueeze(2).to_broadcast([128, 16, NS])
                XSv = lambda ri, t: XS[:, ri, :, :].rearrange("p a (s t) -> p a s t", t=TS)[:, :, :, t]
                rdh = [HS.r, M4.r, XS.r, sc.r]
                for t in range(TS):
                    hr, hi = HS[:, 0, :, :, t], HS[:, 1, :, :, t]
                    TT("dve", M4[:, 0], hr, abc(AR), ALU.mult, rdh, [M4.r])
                    TT("dve", M4[:, 1], hi, abc(AI), ALU.mult, rdh, [M4.r])
                    TT("dve", M4[:, 0], M4[:, 0], M4[:, 1], ALU.subtract, rdh, [M4.r])
                    TT("dve", HS[:, 0, :, :, t + 1], M4[:, 0], XSv(0, t), ALU.add, rdh, [HS.r])
                    TT("dve", M4[:, 2], hi, abc(AR), ALU.mult, rdh, [M4.r])
                    TT("dve", M4[:, 3], hr, abc(AI), ALU.mult, rdh, [M4.r])
                    TT("dve", M4[:, 2], M4[:, 2], M4[:, 3], ALU.add, rdh, [M4.r])
                    TT("dve", HS[:, 1, :, :, t + 1], M4[:, 2], XSv(1, t), ALU.add, rdh, [HS.r])
                P.cut(10.3)
                HSb = S(es_S, "HSb", [128, 2, 16, NS, TS], BF16)
                for ri in range(2):
                    CP("pool", HSb[:, ri], HS[:, ri, :, :, 1:TS + 1], [HS.r, HSb.r], [HSb.r])
                P.cut(10.4)
                for k in range(4):
                    n = 0
                    for il in range(4):
                        for ri in range(2):
                            MM(pY[:, 0:NSC], CT[:, 4 * k + il, ri, :], HSb[:, ri, 4 * k + il].rearrange("p s t -> p (s t)"), n == 0, n == 7,
                               [CT.r, HSb.r], [pY.r])
                            n += 1
                    y_finish(k, NSC, slice(L, L + NSC))
                glu(NSC, [[(slice(0, NSC), mixT[:, 4 + kq, 32:32 + NSC], mix_r[4 + kq])] for kq in range(4)])
                P.cut(10.5)
                fso = S(es_S, "fso", [16, 2, 2048], F32)
                for ri in range(2):
                    for q4 in range(4):
                        for i4 in range(4):
                            i = q4 * 4 + i4
                            TR(ps0[0:16, i4 * 128:(i4 + 1) * 128], HS[:, ri, i, :, TS], ident_f[:], [HS.r, ident_f.r], [ps0.r])
                        CP("act", fso[:, ri, q4 * 512:(q4 + 1) * 512], ps0[0:16, :], [ps0.r, fso.r], [fso.r])
                ld("sp", srs, fso[:, 0, :], [], reads=[fso.r])
                ld("sp", sis, fso[:, 1, :], [], reads=[fso.r])
                P.barrier()
            P.cut(300)
            es_U.close()
            es_C = contextlib.ExitStack()
            with es_C:
                x1 = S(es_C, "x1", [128, NT, D], F32)
                x1_r = [Res("x1_%d" % i) for i in range(NT)]
                xn2T = S(es_C, "xn2T", [128, 8, NCOL], BF16)
                xn2_r = [Res("xn2_%d" % i) for i in range(NT)]
                cwb = S(es_C, "cwb", [128, 4, NFC], F32)
                es_C1 = contextlib.ExitStack()
                with es_C1:
                    g2b = S(es_C1, "g2b", [128, D], F32)
                    ld("sp", g2b[:], rawap(norm2.tensor, 0, [[0, 128], [1, D]]), [g2b.r])
                    wo = S(es_C1, "wo", [128, 8, D], BF16)
                    ld("pool", wo[:], w_out.rearrange("(kc p) n -> p kc n", p=128), [wo.r])
                    cwl = S(es_C1, "cwl", [NFC, 4, 128], F32)
                    for j in range(3):
                        ld("sp", cwl[:, j, :], conv_w[j:j + 1, :].rearrange("o (c f) -> (o c) f", f=128), [cwl.r])
                    ld("sp", cwl[:, 3, :], conv_b.rearrange("o (c f) -> (o c) f", f=128), [cwl.r])
                    pc0 = PS(es_C1, "pcw0", [128, 512], F32)
                    for j in range(4):
                        TR(pc0[:, j * NFC:(j + 1) * NFC], cwl[:, j, :], ident_f[0:NFC, 0:NFC], [cwl.r, ident_f.r], [pc0.r])
                    CP("act", cwb[:].rearrange("p a b -> p (a b)"), pc0[:, 0:4 * NFC], [pc0.r], [cwb.r])
                    NB = 2
                    xt = [S(es_C1, "cxt%d" % i, [128, D], F32) for i in range(NB)]
                    junk = S(es_C1, "cjunk", [128, D], BF16)
                    ss = [S(es_C1, "cssq%d" % i, [128, 4], F32) for i in range(NB)]
                    xn = [S(es_C1, "cxn%d" % i, [128, D], BF16) for i in range(NB)]
                    pw = [PS(es_C1, "pw%d" % i, [128, 1024], F32) for i in range(2)]
                    pT2 = [PS(es_C1, "pT2%d" % i, [128, 8, 128], BF16) for i in range(2)]
                    pw_r = [[Res("pwlo%d" % i), Res("pwhi%d" % i)] for i in range(2)]
                    for tt in range(NT):
                        b = tt % NB
                        X = xt[b]
                        if tt == 0:
                            MS("pool", X[:], 0.0, [X.r])
                            ld("sp", X[0:16, :], meta, [X.r])
                            r2 = Res("cx0b")
                            ld("sp", X[32:96, :], xs, [r2], reads=[X.r])
                            xr = [X.r, r2]
                        else:
                            ld("sp", X[:], xp[(tt - 1) * 128:tt * 128, :], [X.r])
                            xr = [X.r]
                        pw_ = pw[tt % 2]
                        pwr = pw_r[tt % 2]
                        for nb in range(2):
                            for kc in range(8):
                                MM(pw_[:, nb * 512:(nb + 1) * 512], mixT[:, kc, tt * 128:(tt + 1) * 128], wo[:, kc, nb * 512:(nb + 1) * 512],
                                   kc == 0, kc == 7, [wo.r] + mix_r, [pwr[nb]])
                        for nb in range(2):
                            sl = slice(nb * 512, (nb + 1) * 512)
                            TT("dve", x1[:, tt, sl], pw_[:, sl], X[:, sl], ALU.add, [pwr[nb]] + xr + [x1_r[tt]], [x1_r[tt]])
                        P.op("act", lambda e, tt=tt, b=b: e.activation(out=junk[:], in_=x1[:, tt, :], func=AF.Square, accum_out=ss[b][:, 0:1]),
                             reads=[x1_r[tt]], writes=[junk.r, ss[b].r])
                        TSC("dve", ss[b][:, 1:2], ss[b][:, 0:1], 1.0 / D, EPS, ALU.mult, ALU.add, [ss[b].r], [ss[b].r])
                        P.op("act", lambda e, b=b: e.sqrt(out=ss[b][:, 2:3], in_=ss[b][:, 1:2]), reads=[ss[b].r], writes=[ss[b].r])
                        P.op("dve", lambda e, b=b: e.reciprocal(out=ss[b][:, 3:4], in_=ss[b][:, 2:3]), reads=[ss[b].r], writes=[ss[b].r])
                        STT(xn[b][:], x1[:, tt, :], ss[b][:, 3:4], g2b[:], ALU.mult, ALU.mult, [x1_r[tt], ss[b].r, g2b.r], [xn[b].r])
                        pt_ = pT2[tt % 2]
                        for kc in range(8):
                            TR(pt_[:, kc, :], xn[b][:, kc * 128:(kc + 1) * 128], ident_b[:], [xn[b].r, ident_b.r], [pt_.r])
                        CP("act", xn2T[:, :, tt * 128:(tt + 1) * 128], pt_[:], [pt_.r], [xn2_r[tt]])
                    P.barrier()
                P.cut(350)
                parts = [list(range(0, 8)), list(range(8, 15)), list(range(15, 22))]
                hT = mixT
                wd = S(es_C, "wd", [128, 8, D], BF16)
                Ab2 = [S(es_C, "Ab%d" % i, [128, L + 2], F32) for i in range(2)]
                As2 = [S(es_C, "As%d" % i, [128, NS, TS + 2], F32) for i in range(2)]
                Gt = S(es_C, "Gt", [128, NCOL], F32)
                wgu = [S(es_C, "wgu%d" % i, [128, 2, 8, 256], BF16) for i in range(2)]
                wgu_r2 = [Res("wgu2_%d" % i) for i in range(2)]
                cst = S(es_C, "cst", [128, NFC, 2], F32)
                css = S(es_C, "css", [128, NFC, NS, 2], F32)
                hist = S(es_C, "hist", [NS * 2, 128], F32)
                for _a in Ab2:
                    MS("pool", _a[:, 0:2], 0.0, [_a.r])
                MS("pool", Gt[:], 0.0, [Gt.r])
                pa_ = [PS(es_C, "pa%d" % i, [128, 512], F32) for i in range(2)]
                pc_ = [PS(es_C, "pc%d" % i, [128, 512], F32) for i in range(2)]
                pd = [PS(es_C, "pd%d" % i, [128, 1024], F32) for i in range(1)]
                ph = PS(es_C, "ph", [128, 512], F32)
                ost = [S(es_C, "ost%d" % i, [128, D], F32) for i in range(1)]
                stg = [S(es_C, "stg%d" % i, [NS * 2, 512], F32) for i in range(2)]
                w_gate_v = w_gate.rearrange("(kc p) n -> p kc n", p=128)
                w_up_v = w_up.rearrange("(kc p) n -> p kc n", p=128)
                w_down_v = w_down.rearrange("(fc p) n -> p fc n", p=128)
                xn2_all = list(xn2_r)
                hT_r = Res("hT")
                colgroups = [(0, 512), (512, 512), (1024, 512), (1536, 512), (2048, 128)]
                for pi, fcs in enumerate(parts):
                    for j, fc in enumerate(fcs):
                        ld("pool", wd[:, j, :], w_down_v[:, fc, :], [wd.r])
                    wrd_of = {}

                    def stageA(j, fc):
                        Ab, As = Ab2[fc % 2], As2[fc % 2]
                        wbuf = wgu[(fc // 2) % 2]
                        rwb2 = wgu_r2[(fc // 2) % 2]
                        if fc % 2 == 0:
                            ncol = 256 if fc + 1 < NFC else 128
                            ld("pool", wbuf[:, 0, :, 0:ncol], w_gate_v[:, :, fc * 128:fc * 128 + ncol], [wbuf.r], reads=[rwb2])
                            ld("pool", wbuf[:, 1, :, 0:ncol], w_up_v[:, :, fc * 128:fc * 128 + ncol], [rwb2], reads=[wbuf.r])
                        wb = wbuf[:, :, :, (fc % 2) * 128:(fc % 2 + 1) * 128]
                        wrd = [wbuf.r, rwb2]
                        ld("sp", hist[:], conv0[:, fc * 128:(fc + 1) * 128], [hist.r])
                        TR(ph[:, 0:NS * 2], hist[:], ident_f[0:NS * 2, 0:NS * 2], [hist.r, ident_f.r], [ph.r])
                        CP("act", As[:, :, 0:2], ph[:, 0:NS * 2].rearrange("p (s j) -> p s j", j=2), [ph.r, As.r], [As.r])
                        for gi, (c0, N) in enumerate(colgroups):
                            pa = pa_[gi % 2]
                            for kc in range(8):
                                MM(pa[:, 0:N], wb[:, 0, kc, :], xn2T[:, kc, c0:c0 + N], kc == 0, kc == 7, wrd + xn2_all, [pa.r])
                            if gi == 0:
                                CP("act", Ab[:, 2:18], pa[:, 0:16], [pa.r, Ab.r], [Ab.r])
                                CP("act", As[:, :, 2:2 + TS], pa[:, 32:96].rearrange("p (s t) -> p s t", t=TS), [pa.r, As.r], [As.r])
                                CP("act", Ab[:, 18:18 + 384], pa[:, 128:512], [pa.r, Ab.r], [Ab.r])
                            else:
                                CP("act", Ab[:, c0 - 110:c0 - 110 + N], pa[:, 0:N], [pa.r, Ab.r], [Ab.r])
                        wrd_of[fc] = wrd

                    def stageB(j, fc):
                        Ab, As = Ab2[fc % 2], As2[fc % 2]
                        wb = wgu[(fc // 2) % 2][:, :, :, (fc % 2) * 128:(fc % 2 + 1) * 128]
                        wrd = wrd_of[fc]
                        w0, w1, w2, bb = (cwb[:, q, fc:fc + 1] for q in range(4))
                        rdc = [Ab.r, Gt.r, cwb.r]
                        ACT(Gt[:, 112:112 + L], Ab[:, 0:L], AF.Identity, rdc, [Gt.r], bias=bb, scale=w0)
                        STT(Gt[:, 112:112 + L], Ab[:, 1:L + 1], w1, Gt[:, 112:112 + L], ALU.mult, ALU.add, rdc, [Gt.r])
                        STT(Gt[:, 112:112 + L], Ab[:, 2:L + 2], w2, Gt[:, 112:112 + L], ALU.mult, ALU.add, rdc, [Gt.r])
                        ACT(Gt[:, 112:112 + L], Gt[:, 112:112 + L], AF.Gelu_apprx_tanh, rdc, [Gt.r])
                        CP("pool", Gt[:, 0:16], Gt[:, 112:128], [Gt.r], [Gt.r])
                        Gs = Gt[:, 32:96].rearrange("p (s t) -> p s t", t=TS)
                        rds = [As.r, Gt.r, cwb.r]
                        ACT(Gs, As[:, :, 0:TS], AF.Identity, rds, [Gt.r], bias=bb, scale=w0)
                        STT(Gs, As[:, :, 1:TS + 1], w1, Gs, ALU.mult, ALU.add, rds, [Gt.r])
                        STT(Gs, As[:, :, 2:TS + 2], w2, Gs, ALU.mult, ALU.add, rds, [Gt.r])
                        ACT(Gs, Gs, AF.Gelu_apprx_tanh, rds, [Gt.r])
                        for gi, (c0, N) in enumerate(colgroups):
                            pc = pc_[gi % 2]
                            for kc in range(8):
                                MM(pc[:, 0:N], wb[:, 1, kc, :], xn2T[:, kc, c0:c0 + N], kc == 0, kc == 7, wrd + xn2_all, [pc.r])
                            TT("dve", hT[:, j, c0:c0 + N], Gt[:, c0:c0 + N], pc[:, 0:N], ALU.mult, [Gt.r, pc.r, hT_r], [hT_r])
                        CP("pool", cst[:, fc, :], Ab[:, L:L + 2], [Ab.r, cst.r], [cst.r])
                        CP("pool", css[:, fc, :, :], As[:, :, TS:TS + 2], [As.r, css.r], [css.r])

                    stageA(0, fcs[0])
                    for j, fc in enumerate(fcs):
                        if j + 1 < len(fcs):
                            stageA(j + 1, fcs[j + 1])
                        stageB(j, fc)
                    lastp = (pi == len(parts) - 1)
                    pdr = [Res("pd_lo"), Res("pd_hi")]
                    for tt in range(NT):
                        pd_ = pd[0]
                        o_ = ost[0]
                        for nb in range(2):
                            sl = slice(nb * 512, (nb + 1) * 512)
                            for j in range(len(fcs)):
                                MM(pd_[:, sl], hT[:, j, tt * 128:(tt + 1) * 128], wd[:, j, sl],
                                   j == 0, j == len(fcs) - 1, [hT_r, wd.r], [pdr[nb]])
                            if not lastp:
                                TT("dve", x1[:, tt, sl], pd_[:, sl], x1[:, tt, sl], ALU.add, [pdr[nb], x1_r[tt]], [x1_r[tt]])
                            else:
                                TT("dve", o_[:, sl], pd_[:, sl], x1[:, tt, sl], ALU.add, [pdr[nb], x1_r[tt], o_.r], [o_.r])
                        if lastp:
                            if tt == 0:
                                ld("sp", ys[:, :], o_[32:96, :], [], reads=[o_.r])
                            else:
                                ld("sp", yp[(tt - 1) * 128:tt * 128, :], o_[:], [], reads=[o_.r])
                for q4 in range(6):
                    nch = min(4, NFC - 4 * q4)
                    for j in range(nch):
                        fc = 4 * q4 + j
                        TR(ph[0:2, j * 128:(j + 1) * 128], cst[:, fc, :], ident_f[:], [cst.r, ident_f.r], [ph.r])
                    CP("act", stg[0][0:2, 0:nch * 128], ph[0:2, 0:nch * 128], [ph.r, stg[0].r], [stg[0].r])
                    ld("sp", cp[:, q4 * 512:q4 * 512 + nch * 128], stg[0][0:2, 0:nch * 128], [], reads=[stg[0].r])
                    for j in range(nch):
                        fc = 4 * q4 + j
                        TR(pa_[0][0:NS * 2, j * 128:(j + 1) * 128], css[:, fc, :, :].rearrange("p s j -> p (s j)"), ident_f[:],
                           [css.r, ident_f.r], [pa_[0].r])
                    CP("act", stg[1][:, 0:nch * 128], pa_[0][0:NS * 2, 0:nch * 128], [pa_[0].r, stg[1].r], [stg[1].r])
                    ld("sp", cs[:, q4 * 512:q4 * 512 + nch * 128], stg[1][:, 0:nch * 128], [], reads=[stg[1].r])
                P.barrier()
        P.barrier()
        P._free = [d for d in P._dmap.values()]
        P._dmap = {}

    for vc in range(nv):
        emit_vc(vc)
    if True:
        P.run()
    return nc


def make_in_maps(inp, n_cores, npool, nv=1):
    f = lambda a: np.ascontiguousarray(np.asarray(a))
    ckv = np.concatenate([np.asarray(inp["cache_k"]).reshape(npool * 128, 512),
                          np.asarray(inp["cache_v"]).reshape(npool * 128, 512)], axis=1)
    shared = {
        "meta": f(inp["meta_tokens"]), "ckv": ckv,
        "norm1": f(inp["norm1"]).reshape(1, D), "norm2": f(inp["norm2"]).reshape(1, D),
        "w_in": f(inp["w_in"])[0], "w_out": f(inp["w_out"])[0],
        "q_norm": f(inp["q_norm"]).reshape(1, 128), "k_norm": f(inp["k_norm"]).reshape(1, 128),
        "lam_q": f(inp["lam_q"]).reshape(1, 128), "lam_k": f(inp["lam_k"]).reshape(1, 128),
        "sub_norm": f(inp["sub_norm"]).reshape(1, 128),
        "a_re": f(inp["ssm_a_re"]).reshape(16, 128), "a_im": f(inp["ssm_a_im"]).reshape(16, 128),
        "log_dt": f(inp["ssm_log_dt"]).reshape(16, 2),
        "b_re": f(inp["ssm_b_re"]).reshape(16, 2048), "b_im": f(inp["ssm_b_im"]).reshape(16, 2048),
        "c_re": f(inp["ssm_c_re"]).reshape(16, 2048), "c_im": f(inp["ssm_c_im"]).reshape(16, 2048),
        "ssm_d": f(inp["ssm_d"]).reshape(4, 128),
        "w_glu": f(inp["w_glu"])[0], "b_glu": f(inp["b_glu"]).reshape(4, 128),
        "w_gate": f(inp["w_gate"])[0], "w_up": f(inp["w_up"])[0], "w_down": f(inp["w_down"])[0],
        "conv_w": f(inp["ffn_conv_w"])[0], "conv_b": f(inp["ffn_conv_b"]).reshape(1, DFF),
    }
    maps = []
    for c in range(n_cores):
        m = dict(shared)
        for vc in range(nv):
            g = c * nv + vc
            sfx = "_v%d" % vc
            m["xp" + sfx] = f(inp["x_prompt"][g])
            m["xs" + sfx] = f(inp["x_sample"][g * NS:(g + 1) * NS]).reshape(NS * TS, D)
            m["pt" + sfx] = f(inp["page_table"][g * NS:(g + 1) * NS]).reshape(1, NS * NPG).astype(np.int32)
            m["s_re0" + sfx] = f(inp["state_ssm_re"][0, g * NS:(g + 1) * NS]).reshape(NS, 2048)
            m["s_im0" + sfx] = f(inp["state_ssm_im"][0, g * NS:(g + 1) * NS]).reshape(NS, 2048)
            m["conv0" + sfx] = f(inp["state_ffn_conv"][0, g * NS:(g + 1) * NS]).reshape(NS * 2, DFF)
        maps.append(m)
    return maps


def assemble(results, n_cores, nv=1):
    n = n_cores * nv
    cat = lambda k: np.stack([np.asarray(results[c][k + "_v%d" % vc]) for c in range(n_cores) for vc in range(nv)])
    y_prompt = cat("yp").reshape(n, SEQ, D)
    y_sample = cat("ys").reshape(n * NS, TS, D)
    k_prompt = cat("kp").reshape(1, n, L, 4, 128)
    v_prompt = cat("vp").reshape(1, n, L, 4, 128)
    k_sample = cat("ks").reshape(1, n * NS, TS, 4, 128)
    v_sample = cat("vs").reshape(1, n * NS, TS, 4, 128)
    srp = cat("srp").reshape(1, n, 32, 64)
    sip = cat("sip").reshape(1, n, 32, 64)
    srs = cat("srs").reshape(1, n * NS, 32, 64)
    sis = cat("sis").reshape(1, n * NS, 32, 64)
    cpo = cat("cp").reshape(1, n, 2, DFF)
    cso = cat("cs").reshape(1, n * NS, 2, DFF)
    return tuple(np.ascontiguousarray(a.astype(np.float32)) for a in
                 (y_prompt, y_sample, k_prompt, v_prompt, k_sample, v_sample, srp, sip, srs, sis, cpo, cso))


N_CORES = 8
N_VC = 1


def kernel(**inputs):
    npool = int(np.asarray(inputs["cache_k"]).shape[1])
    nc = build(npool, N_VC)
    maps = make_in_maps(inputs, N_CORES, npool, N_VC)
    res = run_bass_kernel_spmd(nc, maps, core_ids=list(range(N_CORES)))
    return assemble(res.results, N_CORES, N_VC)
```
